# Optimizing a Trainium2 kernel written in Bass

```python
import math
import jax, jax.numpy as jnp
from jax import lax
import numpy as np

D_MODEL = 1024
BATCH = 8
SEQ = 2048
DEPTH = 1
DEC_BATCH = 32
DEC_SEQ = 8
PAST_LEN = 8192
PAGE_SIZE = 128

MIX_WIDTH = D_MODEL
ATT_HEADS = 8
HEAD_DIM = 64
ATT_WIDTH = ATT_HEADS * HEAD_DIM
BLOCK = 256
TOPK = 3
QUERY_ROWS = 128
ROPE_THETA = 10000.0
SSM_HEADS = 8
SSM_HEAD_DIM = 64
SSM_INNER = SSM_HEADS * SSM_HEAD_DIM
SSM_GROUPS = 2
SSM_STATE = 128
CONV_W = 4
CONV_DIM = SSM_INNER + 2 * SSM_GROUPS * SSM_STATE
SSD_CHUNK = 128
IN_COLS = 3 * ATT_WIDTH + SSM_INNER + CONV_DIM + SSM_HEADS
MEM_LEN = 256
X_HEADS = 4
X_HEAD_DIM = 128
X_WIDTH = X_HEADS * X_HEAD_DIM
D_FF = 4 * D_MODEL
EPS = 1e-6

kernel_name = 'moba_ssd_parallel_hybrid_step'


def rmsnorm(x, g):
    xf = x.astype(jnp.float32)
    y = xf * lax.rsqrt(jnp.mean(xf * xf, axis=-1, keepdims=True) + EPS)
    return (y * g.astype(jnp.float32)).astype(x.dtype)


def rotary(x, pos):
    half = x.shape[-1] // 2
    inv_freq = ROPE_THETA ** (-jnp.arange(half, dtype=jnp.float32) / half)
    ang = pos.astype(jnp.float32)[:, None] * inv_freq[None, :]
    cos = jnp.cos(ang)[None, :, None, :]
    sin = jnp.sin(ang)[None, :, None, :]
    xf = x.astype(jnp.float32)
    x1, x2 = xf[..., :half], xf[..., half:]
    return jnp.concatenate([x1 * cos - x2 * sin, x2 * cos + x1 * sin], axis=-1).astype(x.dtype)


def moba_attention(q, k, v, q_pos):
    b, t, h, dh = q.shape
    l = k.shape[1]
    nbp = -(-l // BLOCK)
    nb_full = l // BLOCK
    n_score = max(nb_full, TOPK)
    kpad = ((0, 0), (0, nbp * BLOCK - l), (0, 0), (0, 0))
    kb = jnp.pad(k, kpad).reshape(b, nbp, BLOCK, h, dh)
    vb = jnp.pad(v, kpad).reshape(b, nbp, BLOCK, h, dh)
    kmean = jnp.mean(kb[:, :nb_full].astype(jnp.float32), axis=2)
    kmean = jnp.pad(kmean, ((0, 0), (0, n_score - nb_full), (0, 0), (0, 0)))
    own = q_pos // BLOCK
    gate = jnp.einsum('bthd,bjhd->bhtj', q.astype(jnp.float32), kmean)
    gate = jnp.where(jnp.arange(n_score)[None, :] < own[:, None], gate, -jnp.inf)
    _, sel = lax.top_k(gate, TOPK)
    sel_ok = sel < own[:, None]
    sel = jnp.where(sel_ok, sel, 0)
    own_b = jnp.broadcast_to(own[:, None], (b, h, t, 1)).astype(sel.dtype)
    blocks = jnp.concatenate([sel, own_b], axis=-1)
    block_ok = jnp.concatenate([sel_ok, jnp.ones((b, h, t, 1), bool)], axis=-1)
    n_sel = TOPK + 1

    qc = max(1, min(t, QUERY_ROWS // b))
    nq = -(-t // qc)
    padq = nq * qc - t
    qs = jnp.pad(q.astype(jnp.float32) * (dh ** -0.5), ((0, 0), (0, padq), (0, 0), (0, 0)))
    qs = qs.reshape(b, nq, qc, h, dh).transpose(1, 0, 3, 2, 4)
    blk = jnp.pad(blocks, ((0, 0), (0, 0), (0, padq), (0, 0)))
    blk = blk.reshape(b, h, nq, qc, n_sel).transpose(2, 0, 1, 3, 4)
    okc = jnp.pad(block_ok, ((0, 0), (0, 0), (0, padq), (0, 0)), constant_values=True)
    okc = okc.reshape(b, h, nq, qc, n_sel).transpose(2, 0, 1, 3, 4)
    posc = jnp.pad(q_pos, (0, padq), mode='edge').reshape(nq, qc)
    bi = jnp.arange(b)[:, None, None, None]
    hi = jnp.arange(h)[None, :, None, None]
    offs = jnp.arange(BLOCK)

    def chunk(args):
        qq, bc, oc, pc = args
        kg = kb[bi, bc, :, hi].astype(jnp.float32)
        vg = vb[bi, bc, :, hi].astype(jnp.float32)
        kpos = bc[..., None] * BLOCK + offs
        mask = oc[..., None] & (kpos <= pc[None, None, :, None, None])
        s = jnp.einsum('bhqd,bhqnkd->bhqnk', qq, kg)
        s = jnp.where(mask, s, -jnp.inf).reshape(b, h, qc, n_sel * BLOCK)
        p = jax.nn.softmax(s, axis=-1).reshape(b, h, qc, n_sel, BLOCK)
        return jnp.einsum('bhqnk,bhqnkd->bhqd', p, vg)

    o = lax.map(chunk, (qs, blk, okc, posc))
    o = o.transpose(1, 0, 3, 2, 4).reshape(b, nq * qc, h, dh)[:, :t]
    return o.astype(q.dtype)


def causal_conv(xbc, buf, w, bias):
    t = xbc.shape[1]
    full = jnp.concatenate([buf.astype(xbc.dtype), xbc], axis=1)
    y = bias
    for i in range(CONV_W):
        y = y + full[:, i:i + t] * w[i]
    return y, full[:, full.shape[1] - (CONV_W - 1):]


def ssd_scan(x, dt, a, bm, cm, h0):
    b, t = x.shape[:2]
    cl = min(SSD_CHUNK, t)
    nc = -(-t // cl)
    pad = nc * cl - t
    rep = SSM_HEADS // SSM_GROUPS
    bh = jnp.repeat(bm.astype(jnp.float32), rep, axis=2)
    ch = jnp.repeat(cm.astype(jnp.float32), rep, axis=2)
    xdt = x.astype(jnp.float32) * dt[..., None]
    da = dt * a.astype(jnp.float32)

    def chunked(z):
        z = jnp.pad(z, ((0, 0), (0, pad)) + ((0, 0),) * (z.ndim - 2))
        return jnp.moveaxis(z.reshape((b, nc, cl) + z.shape[2:]), 1, 0)

    causal = jnp.tril(jnp.ones((cl, cl), bool))[None, :, :, None]

    def step(hprev, inp):
        xc, dac, bc, cc = inp
        cum = jnp.cumsum(dac, axis=1)
        seg = cum[:, :, None, :] - cum[:, None, :, :]
        decay = jnp.exp(jnp.where(causal, seg, -jnp.inf))
        scores = jnp.einsum('blhn,bshn->blsh', cc, bc) * decay
        y = jnp.einsum('blsh,bshp->blhp', scores, xc)
        y = y + jnp.einsum('blhn,bhpn->blhp', cc, hprev) * jnp.exp(cum)[..., None]
        tail = jnp.exp(cum[:, -1:, :] - cum)
        hnew = hprev * jnp.exp(cum[:, -1])[:, :, None, None] + jnp.einsum('bshn,bsh,bshp->bhpn', bc, tail, xc)
        return hnew, y

    h_last, ys = lax.scan(step, h0.astype(jnp.float32), (chunked(xdt), chunked(da), chunked(bh), chunked(ch)))
    y = jnp.moveaxis(ys, 0, 1).reshape(b, nc * cl, SSM_HEADS, SSM_HEAD_DIM)[:, :t]
    return y, h_last


def gated_group_norm(y, z, g):
    yz = y * jax.nn.silu(z.astype(jnp.float32))
    shp = yz.shape
    yg = yz.reshape(shp[:-1] + (SSM_GROUPS, SSM_INNER // SSM_GROUPS))
    yg = yg * lax.rsqrt(jnp.mean(yg * yg, axis=-1, keepdims=True) + EPS)
    return yg.reshape(shp) * g.astype(jnp.float32)


def memory_kv(mem, lp):
    b, m, _ = mem.shape
    mn = rmsnorm(mem, lp['ln_mem_g'])
    mk = rmsnorm((mn @ lp['wk_x']).reshape(b, m, X_HEADS, X_HEAD_DIM), lp['kx_norm_g'])
    mv = (mn @ lp['wv_x']).reshape(b, m, X_HEADS, X_HEAD_DIM)
    return mk, mv


def cross_attn(h, mem_k, mem_v, lp):
    b, t, _ = h.shape
    q = rmsnorm((h @ lp['wq_x']).reshape(b, t, X_HEADS, X_HEAD_DIM), lp['qx_norm_g'])
    s = jnp.einsum('bthd,bmhd->bhtm', q.astype(jnp.float32), mem_k.astype(jnp.float32)) * (X_HEAD_DIM ** -0.5)
    p = jax.nn.softmax(s, axis=-1)
    o = jnp.einsum('bhtm,bmhd->bthd', p, mem_v.astype(jnp.float32)).reshape(b, t, X_WIDTH)
    return o.astype(h.dtype) @ lp['wo_x']


def layer(x, pos, k_past, v_past, conv_buf, h0, mem_k, mem_v, lp):
    b, t, _ = x.shape
    hn = rmsnorm(x, lp['ln_mix_g'])
    proj = hn @ lp['w_in']
    q, k, v, z, xbc, dt_raw = jnp.split(
        proj, [ATT_WIDTH, 2 * ATT_WIDTH, 3 * ATT_WIDTH, 3 * ATT_WIDTH + SSM_INNER,
               3 * ATT_WIDTH + SSM_INNER + CONV_DIM], axis=-1)
    q = rotary(rmsnorm(q.reshape(b, t, ATT_HEADS, HEAD_DIM), lp['q_norm_g']), pos)
    k = rotary(rmsnorm(k.reshape(b, t, ATT_HEADS, HEAD_DIM), lp['k_norm_g']), pos)
    v = v.reshape(b, t, ATT_HEADS, HEAD_DIM)
    if k_past is None:
        k_all, v_all = k, v
    else:
        k_all = jnp.concatenate([k_past.astype(k.dtype), k], axis=1)
        v_all = jnp.concatenate([v_past.astype(v.dtype), v], axis=1)
    att = moba_attention(q, k_all, v_all, pos).reshape(b, t, ATT_WIDTH)
    xbc_c, new_conv = causal_conv(xbc, conv_buf, lp['conv_w'], lp['conv_b'])
    xbc_c = jax.nn.silu(xbc_c)
    xs, bm, cm = jnp.split(xbc_c, [SSM_INNER, SSM_INNER + SSM_GROUPS * SSM_STATE], axis=-1)
    xs = xs.reshape(b, t, SSM_HEADS, SSM_HEAD_DIM)
    dt = jax.nn.softplus(dt_raw.astype(jnp.float32) + lp['dt_bias'].astype(jnp.float32))
    a = -jnp.exp(lp['a_log'].astype(jnp.float32))
    y, h_new = ssd_scan(xs, dt, a, bm.reshape(b, t, SSM_GROUPS, SSM_STATE),
                        cm.reshape(b, t, SSM_GROUPS, SSM_STATE), h0)
    y = y + lp['d_skip'].astype(jnp.float32)[:, None] * xs.astype(jnp.float32)
    y = gated_group_norm(y.reshape(b, t, SSM_INNER), z, lp['ssm_norm_g'])
    x = x + jnp.concatenate([att, y.astype(x.dtype)], axis=-1) @ lp['w_out']
    x = x + cross_attn(rmsnorm(x, lp['ln_x_g']), mem_k, mem_v, lp)
    hm = rmsnorm(x, lp['ln_mlp_g'])
    x = x + jnp.square(jax.nn.relu(hm @ lp['w_up'])) @ lp['w_down']
    return x, k, v, new_conv, h_new


def setup_inputs(seed: int = 0) -> dict:
    key = jax.random.key(seed)
    ks = jax.random.split(key, 40)
    f32 = jnp.float32
    n_pages = PAST_LEN // PAGE_SIZE
    used = DEC_BATCH * n_pages
    n_pool = used + max(1, used // 4)

    def nrm(k, shape, scale):
        return scale * jax.random.normal(k, shape, f32)

    def gain(k, n):
        return 1.0 + 0.01 * jax.random.normal(k, (DEPTH, n), f32)

    page_table = jax.random.permutation(ks[0], n_pool)[:used].reshape(DEC_BATCH, n_pages).astype(jnp.int32)
    dt0 = jnp.exp(jax.random.uniform(ks[1], (DEPTH, SSM_HEADS), f32, math.log(1e-3), math.log(1e-1)))
    return {
        'x_prompt': nrm(ks[2], (BATCH, SEQ, D_MODEL), 1.0),
        'x_sample': nrm(ks[3], (DEC_BATCH, DEC_SEQ, D_MODEL), 1.0),
        'mem_prompt': nrm(ks[4], (BATCH, MEM_LEN, D_MODEL), 1.0),
        'cache_k': nrm(ks[5], (DEPTH, n_pool, PAGE_SIZE, ATT_HEADS, HEAD_DIM), 1.0),
        'cache_v': nrm(ks[6], (DEPTH, n_pool, PAGE_SIZE, ATT_HEADS, HEAD_DIM), 1.0),
        'page_table': page_table,
        'state_conv': nrm(ks[7], (DEPTH, DEC_BATCH, CONV_W - 1, CONV_DIM), 1.0),
        'state_ssm': nrm(ks[8], (DEPTH, DEC_BATCH, SSM_HEADS, SSM_HEAD_DIM, SSM_STATE), 0.5),
        'cache_mem_k': nrm(ks[9], (DEPTH, DEC_BATCH, MEM_LEN, X_HEADS, X_HEAD_DIM), 1.0),
        'cache_mem_v': nrm(ks[10], (DEPTH, DEC_BATCH, MEM_LEN, X_HEADS, X_HEAD_DIM), 1.0),
        'ln_mix_g': gain(ks[11], D_MODEL),
        'w_in': nrm(ks[12], (DEPTH, D_MODEL, IN_COLS), D_MODEL ** -0.5),
        'q_norm_g': gain(ks[13], HEAD_DIM),
        'k_norm_g': gain(ks[14], HEAD_DIM),
        'conv_w': nrm(ks[15], (DEPTH, CONV_W, CONV_DIM), CONV_W ** -0.5),
        'conv_b': nrm(ks[16], (DEPTH, CONV_DIM), 0.01),
        'dt_bias': dt0 + jnp.log(-jnp.expm1(-dt0)),
        'a_log': jnp.log(jax.random.uniform(ks[17], (DEPTH, SSM_HEADS), f32, 1.0, 16.0)),
        'd_skip': gain(ks[18], SSM_HEADS),
        'ssm_norm_g': gain(ks[19], SSM_INNER),
        'w_out': nrm(ks[20], (DEPTH, MIX_WIDTH, D_MODEL), MIX_WIDTH ** -0.5),
        'ln_x_g': gain(ks[21], D_MODEL),
        'ln_mem_g': gain(ks[22], D_MODEL),
        'wq_x': nrm(ks[23], (DEPTH, D_MODEL, X_WIDTH), D_MODEL ** -0.5),
        'wk_x': nrm(ks[24], (DEPTH, D_MODEL, X_WIDTH), D_MODEL ** -0.5),
        'wv_x': nrm(ks[25], (DEPTH, D_MODEL, X_WIDTH), D_MODEL ** -0.5),
        'qx_norm_g': gain(ks[26], X_HEAD_DIM),
        'kx_norm_g': gain(ks[27], X_HEAD_DIM),
        'wo_x': nrm(ks[28], (DEPTH, X_WIDTH, D_MODEL), X_WIDTH ** -0.5),
        'ln_mlp_g': gain(ks[29], D_MODEL),
        'w_up': nrm(ks[30], (DEPTH, D_MODEL, D_FF), D_MODEL ** -0.5),
        'w_down': nrm(ks[31], (DEPTH, D_FF, D_MODEL), D_FF ** -0.5),
    }


def reference(x_prompt, x_sample, mem_prompt, cache_k, cache_v, page_table, state_conv, state_ssm,
              cache_mem_k, cache_mem_v, ln_mix_g, w_in, q_norm_g, k_norm_g, conv_w, conv_b, dt_bias,
              a_log, d_skip, ssm_norm_g, w_out, ln_x_g, ln_mem_g, wq_x, wk_x, wv_x, qx_norm_g,
              kx_norm_g, wo_x, ln_mlp_g, w_up, w_down):
    bp, tp, _ = x_prompt.shape
    bs, ts, _ = x_sample.shape
    past_len = page_table.shape[1] * cache_k.shape[2]
    pos_p = jnp.arange(tp, dtype=jnp.int32)
    pos_s = past_len + jnp.arange(ts, dtype=jnp.int32)
    hp, hs = x_prompt, x_sample
    kp_l, vp_l, cp_l, sp_l, mkp_l, mvp_l = [], [], [], [], [], []
    ks_l, vs_l, cs_l, ss_l = [], [], [], []
    for l in range(DEPTH):
        lp = {'ln_mix_g': ln_mix_g[l], 'w_in': w_in[l], 'q_norm_g': q_norm_g[l], 'k_norm_g': k_norm_g[l],
              'conv_w': conv_w[l], 'conv_b': conv_b[l], 'dt_bias': dt_bias[l], 'a_log': a_log[l],
              'd_skip': d_skip[l], 'ssm_norm_g': ssm_norm_g[l], 'w_out': w_out[l], 'ln_x_g': ln_x_g[l],
              'ln_mem_g': ln_mem_g[l], 'wq_x': wq_x[l], 'wk_x': wk_x[l], 'wv_x': wv_x[l],
              'qx_norm_g': qx_norm_g[l], 'kx_norm_g': kx_norm_g[l], 'wo_x': wo_x[l],
              'ln_mlp_g': ln_mlp_g[l], 'w_up': w_up[l], 'w_down': w_down[l]}
        mk_p, mv_p = memory_kv(mem_prompt, lp)
        conv0 = jnp.zeros((bp, CONV_W - 1, CONV_DIM), x_prompt.dtype)
        h0 = jnp.zeros((bp, SSM_HEADS, SSM_HEAD_DIM, SSM_STATE), jnp.float32)
        hp, kp, vp, cp, sp = layer(hp, pos_p, None, None, conv0, h0, mk_p, mv_p, lp)
        k_past = cache_k[l][page_table].reshape(bs, -1, ATT_HEADS, HEAD_DIM)
        v_past = cache_v[l][page_table].reshape(bs, -1, ATT_HEADS, HEAD_DIM)
        hs, kn, vn, cn, sn = layer(hs, pos_s, k_past, v_past, state_conv[l], state_ssm[l],
                                   cache_mem_k[l], cache_mem_v[l], lp)
        kp_l.append(kp); vp_l.append(vp); cp_l.append(cp); sp_l.append(sp)
        mkp_l.append(mk_p); mvp_l.append(mv_p)
        ks_l.append(kn); vs_l.append(vn); cs_l.append(cn); ss_l.append(sn)
    new_k_prompt = jnp.stack(kp_l)
    new_v_prompt = jnp.stack(vp_l)
    new_conv_prompt = jnp.stack(cp_l)
    new_ssm_prompt = jnp.stack(sp_l)
    new_mem_k_prompt = jnp.stack(mkp_l)
    new_mem_v_prompt = jnp.stack(mvp_l)
    new_k_sample = jnp.stack(ks_l)
    new_v_sample = jnp.stack(vs_l)
    new_conv_sample = jnp.stack(cs_l)
    new_ssm_sample = jnp.stack(ss_l)
    return (hp, hs, new_k_prompt, new_v_prompt, new_conv_prompt, new_ssm_prompt, new_mem_k_prompt,
            new_mem_v_prompt, new_k_sample, new_v_sample, new_conv_sample, new_ssm_sample)
```

```python
import os
import numpy as np
import ml_dtypes
from contextlib import ExitStack
import concourse.bass as bass
import concourse.mybir as mybir
from concourse.bass_utils import run_bass_kernel_spmd

F32 = mybir.dt.float32
BF16 = mybir.dt.bfloat16
I32 = mybir.dt.int32
AF = mybir.ActivationFunctionType
ALU = mybir.AluOpType
AX = mybir.AxisListType

NEG = -30000.0
EPS = 1e-6
NTOK = 2080
NPOOL = 2560


class Reg:
    __slots__ = ("w", "rs", "name", "excl")

    def __init__(self, name="", excl=False):
        self.w = None
        self.rs = []
        self.name = name
        self.excl = excl


class Op:
    __slots__ = ("eng", "emit", "deps", "signal", "isdma", "sem", "target", "prev_target", "bar")

    def __init__(self, eng, emit, isdma):
        self.eng = eng
        self.emit = emit
        self.deps = []
        self.signal = False
        self.isdma = isdma
        self.sem = None
        self.target = 0
        self.prev_target = 0
        self.bar = 0


ENGS = ["pe", "act", "dve", "pool", "sp"]


class Sched:
    def __init__(self, nc, n_dma_sems=12):
        self.nc = nc
        self.ops = {e: [] for e in ENGS}
        self.n_dma_sems = n_dma_sems
        self.barriers = [[]]
        self.since_bar_dma = []

    def op(self, eng, emit, r=(), w=(), dma=False):
        o = Op(eng, emit, dma)
        self.count = getattr(self, "count", 0) + 1
        if self.count > int(os.environ.get("K_MAXOPS", "100000000")):
            return o
        if os.environ.get("K_TRACE"):
            import sys as _sys
            f = _sys._getframe(2)
            print("OP", self.count, eng, "dma" if dma else "", f.f_lineno, f.f_code.co_name, "<-", f.f_back.f_lineno)
        o.bar = len(self.barriers) - 1
        deps = {}
        if any(reg.excl for reg in r):
            w = list(w) + [reg for reg in r if reg.excl and reg not in w]
            r = [reg for reg in r if not reg.excl]
        for reg in r:
            if reg.w is not None:
                deps[id(reg.w)] = reg.w
        for reg in w:
            if reg.w is not None:
                deps[id(reg.w)] = reg.w
            for x in reg.rs:
                deps[id(x)] = x
        for d in deps.values():
            if d.eng == "pe" and eng == "pe" and not d.isdma and not dma:
                continue
            o.deps.append(d)
            d.signal = True
        for reg in r:
            reg.rs.append(o)
        for reg in w:
            reg.w = o
            reg.rs = []
        self.ops[eng].append(o)
        if dma:
            self.since_bar_dma.append(o)
        return o

    def barrier(self):
        deps = list(self.since_bar_dma)
        for e in ENGS:
            for o in reversed(self.ops[e]):
                if not o.isdma:
                    o.signal = True
                    deps.append(o)
                    break
        self.barriers.append(deps)
        self.since_bar_dma = []

    def finalize(self, stack):
        nc = self.nc
        self.esem = {}
        self.dsems = {}
        for e in ENGS:
            self.esem[e] = stack.enter_context(nc.semaphore("s_" + e))
            self.dsems[e] = [stack.enter_context(nc.semaphore("d_%s_%d" % (e, i)))
                             for i in range(self.n_dma_sems)]
        self.final_dma = {}
        for e in ENGS:
            cnt = 0
            dcnt = [0] * self.n_dma_sems
            k = 0
            for o in self.ops[e]:
                if o.isdma:
                    i = k % self.n_dma_sems
                    k += 1
                    o.sem = self.dsems[e][i]
                    o.prev_target = dcnt[i]
                    dcnt[i] += 16
                    o.target = dcnt[i]
                elif o.signal:
                    cnt += 1
                    o.sem = self.esem[e]
                    o.target = cnt
            self.final_dma[e] = [o for o in self.ops[e] if o.isdma]

    def emit_engine(self, ename, e):
        waited = {}

        def wait(sem, val):
            key = id(sem)
            if waited.get(key, 0) >= val:
                return
            e.wait_ge(sem, val)
            waited[key] = val

        cur_bar = 0
        for o in self.ops[ename]:
            while cur_bar < o.bar:
                cur_bar += 1
                for d in self.barriers[cur_bar]:
                    wait(d.sem, d.target)
            for d in o.deps:
                wait(d.sem, d.target)
            if o.isdma and o.prev_target > 0:
                wait(o.sem, o.prev_target)
            ins = o.emit(e)
            if o.isdma:
                ins.then_inc(o.sem, 16)
            elif o.signal:
                ins.then_inc(o.sem, 1)
        for o in self.final_dma[ename]:
            wait(o.sem, o.target)

    def run(self, stack):
        self.finalize(stack)
        block = stack.enter_context(self.nc.Block())
        S = self

        @block.tensor
        def _(e):
            S.emit_engine("pe", e)

        @block.scalar
        def _(e):
            S.emit_engine("act", e)

        @block.vector
        def _(e):
            S.emit_engine("dve", e)

        @block.gpsimd
        def _(e):
            S.emit_engine("pool", e)

        @block.sync
        def _(e):
            S.emit_engine("sp", e)


class Arena:
    def __init__(self, t, nbytes):
        self.t = t
        self.cap = nbytes
        self.top = 0

    def alloc(self, shape, dt):
        n = 1
        for d in shape[1:]:
            n *= d
        esz = 2 if dt == BF16 else 4
        size = (n * esz + 31) // 32 * 32
        off = self.top
        self.top += size
        assert self.top <= self.cap, ("SBUF arena overflow", self.top, self.cap)
        v = self.t[0:shape[0], off // 2: off // 2 + (n * esz) // 2]
        if dt != BF16:
            v = v.bitcast(dt)
        if len(shape) == 3:
            v = v.rearrange("p (a b) -> p a b", a=shape[1])
        elif len(shape) == 4:
            v = v.rearrange("p (a b c) -> p a b c", a=shape[1], b=shape[2])
        elif len(shape) == 5:
            v = v.rearrange("p (a b c d) -> p a b c d", a=shape[1], b=shape[2], c=shape[3])
        return v


class Ring:
    def __init__(self, items):
        self.items = items
        self.i = 0

    def next(self):
        x = self.items[self.i % len(self.items)]
        self.i += 1
        return x


def host_consts():
    c = {}
    c["c_ident"] = np.eye(128, dtype=np.float32)
    half = 32
    inv_freq = (10000.0 ** (-np.arange(half, dtype=np.float32) / half)).astype(np.float32)
    cs = np.zeros((128, 17, 64), np.float32)
    p = np.arange(128)
    for ti in range(16):
        pos = (ti * 128 + p).astype(np.float32)
        ang = pos[:, None] * inv_freq[None, :]
        cs[:, ti, 0:32] = np.cos(ang)
        cs[:, ti, 32:64] = np.sin(ang)
    pos = (8192 + (p % 8)).astype(np.float32)
    ang = pos[:, None] * inv_freq[None, :]
    cs[:, 16, 0:32] = np.cos(ang)
    cs[:, 16, 32:64] = np.sin(ang)
    c["c_cs"] = cs
    nm = np.zeros((128, 4, 512), np.float32)
    f = np.arange(512)
    for r in range(4):
        nm[:, r, :] = np.where((r * 128 + p)[:, None] > f[None, :], NEG, 0.0)
    c["c_negmask"] = nm
    ind = np.zeros((8, 2048), np.float32)
    for j in range(8):
        ind[j, j * 256:(j + 1) * 256] = 1.0
    c["c_ind"] = ind
    cb0 = np.zeros((8, 8, 128), np.float32)
    for j in range(8):
        for own in range(8):
            cb0[j, own, :] = 0.0 if j <= own else NEG
    c["c_bias0"] = cb0
    t = np.arange(128)
    c["c_triU"] = (t[:, None] <= t[None, :]).astype(np.float32)
    c["c_mneg"] = np.where(t[None, :] < t[:, None], NEG, 0.0).astype(np.float32)
    c["c_ones"] = np.ones((128, 128), np.float32)
    t32 = np.arange(32)
    same = (t32[:, None] // 8) == (t32[None, :] // 8)
    c["c_triU_s"] = ((t32[:, None] <= t32[None, :]) & same).astype(np.float32)
    c["c_mneg_s"] = np.where((t32[None, :] < t32[:, None]) | (~same), NEG, 0.0).astype(np.float32)
    c["c_onesbd_s"] = same.astype(np.float32)
    seqsel = np.zeros((32, 4, 128), np.float32)
    for b in range(4):
        seqsel[b * 8:(b + 1) * 8, b, :] = 1.0
    c["c_seqsel"] = seqsel
    colmask = np.zeros((128, 4, 32), np.float32)
    for b in range(4):
        colmask[:, b, b * 8:(b + 1) * 8] = 1.0
    c["c_colmask"] = colmask
    rowmask = np.zeros((32, 4), np.float32)
    for b in range(4):
        rowmask[b * 8:(b + 1) * 8, b] = 1.0
    c["c_rowmask"] = rowmask
    Z = np.zeros((128, 63), np.float32)
    Z[:, 31] = 1.0
    c["c_Z"] = Z
    selE = np.zeros((32, 32, 128), np.float32)
    for j in range(32):
        selE[j, j, :] = 1.0
    c["c_selE"] = selE
    bd = np.zeros((64, 512), np.float32)
    for h in range(8):
        bd[h * 8:(h + 1) * 8, h * 64:(h + 1) * 64] = 1.0
    c["c_bdmask"] = bd
    negcs = np.zeros((32, 4, 64), np.float32)
    for key in range(32):
        for b in range(4):
            for tq in range(8):
                ok = (key // 8 == b) and (key % 8 <= tq)
                negcs[key, b, tq::8] = 0.0 if ok else NEG
    negcs2 = np.zeros((32, 4, 64), np.float32)
    for key in range(32):
        for b in range(4):
            for h in range(8):
                for tq in range(8):
                    ok = (key // 8 == b) and (key % 8 <= tq)
                    negcs2[key, b, h * 8 + tq] = 0.0 if ok else NEG
    c["c_negcs"] = negcs2
    c["c_iota"] = np.arange(128, dtype=np.float32).reshape(128, 1)
    return c


CONST_SHAPES = None


def build_program(consts, stages=("A", "S", "M", "C")):
    nc = bass.Bass("TRN2", target_bir_lowering=False)

    def din(name, shape, dt=F32):
        return nc.dram_tensor(name, list(shape), dt, kind="ExternalInput").ap()

    def dout(name, shape):
        return nc.dram_tensor(name, list(shape), F32, kind="ExternalOutput").ap()

    x_d = din("x", [NTOK, 1024])
    mem_d = din("mem", [256, 1024])
    ck_d = din("ck", [NPOOL * 128, 512])
    cv_d = din("cv", [NPOOL * 128, 512])
    pt_d = din("pt", [4, 64], I32)
    sconv_d = din("sconv", [12, 1024])
    sssm_d = din("sssm", [4, 512, 128])
    cmk_d = din("cmk", [4, 256, 512])
    cmv_d = din("cmv", [4, 256, 512])
    w_in_d = din("w_in", [1024, 3080])
    w_out_d = din("w_out", [1024, 1024])
    wq_d = din("wq_x", [1024, 512])
    wk_d = din("wk_x", [1024, 512])
    wv_d = din("wv_x", [1024, 512])
    wo_d = din("wo_x", [512, 1024])
    wup_d = din("w_up", [1024, 4096])
    wdn_d = din("w_down", [4096, 1024])
    g_mix_d = din("g_mix", [128, 1024])
    g_x_d = din("g_x", [128, 1024])
    g_mem_d = din("g_mem", [128, 1024])
    g_mlp_d = din("g_mlp", [128, 1024])
    gq_d = din("gq", [128, 64])
    gk_d = din("gk", [128, 64])
    gqx_d = din("gqx", [128, 128])
    gkx_d = din("gkx", [128, 128])
    gssm_d = din("gssm", [128, 512])
    dtb_d = din("dtb", [128, 8])
    alog_d = din("alog", [128, 8])
    dskip_d = din("dskip", [128, 8])
    cw_d = din("cw", [128, 8, 4])
    cb_d = din("cb", [128, 8])
    cd = {k: din(k, v.shape) for k, v in consts.items()}

    y_d = dout("y", [NTOK, 1024])
    newk_d = dout("newk", [NTOK, 512])
    newv_d = dout("newv", [NTOK, 512])
    newconv_d = dout("newconv", [15, 1024])
    newssm_d = dout("newssm", [5, 512, 128])
    newmk_d = dout("newmk", [256, 512])
    newmv_d = dout("newmv", [256, 512])
    mix_d = nc.dram_tensor("mixscr", [NTOK, 1024], BF16, kind="Internal").ap()

    with ExitStack() as st:
        S = Sched(nc)
        ARENA_BYTES = 192 * 1024
        arena_t = st.enter_context(nc.sbuf_tensor("arena", [128, ARENA_BYTES // 2], BF16))
        A = Arena(arena_t, ARENA_BYTES)
        banks = [st.enter_context(nc.psum_tensor("bank%d" % i, [128, 512], F32)) for i in range(8)]
        banks_b = [b.bitcast(BF16) for b in banks]
        bank_reg = [Reg("bank%d" % i, excl=True) for i in range(8)]

        def mm(out, lhsT, rhs, r, w, start=True, stop=True):
            S.op("pe", lambda e: e.matmul(out, lhsT=lhsT, rhs=rhs, start=start, stop=stop), r, w)

        def tp(out, in_, idn, r, w):
            S.op("pe", lambda e: e.transpose(out=out, in_=in_, identity=idn), r, w)

        def act(out, in_, func, r, w, **kw):
            S.op("act", lambda e: e.activation(out=out, in_=in_, func=func, **kw), r, w)

        def tt(out, a, b, op, r, w, eng="dve"):
            S.op(eng, lambda e: e.tensor_tensor(out=out, in0=a, in1=b, op=op), r, w)

        def tsc(out, a, s1, op0, r, w, s2=None, op1=None):
            if op1 is None:
                S.op("dve", lambda e: e.tensor_scalar(out=out, in0=a, scalar1=s1, scalar2=None, op0=op0), r, w)
            else:
                S.op("dve", lambda e: e.tensor_scalar(out=out, in0=a, scalar1=s1, scalar2=s2, op0=op0, op1=op1), r, w)

        def stt(out, a, s, b, op0, op1, r, w, accum=None):
            if accum is None:
                S.op("dve", lambda e: e.scalar_tensor_tensor(out=out, in0=a, scalar=s, in1=b, op0=op0, op1=op1), r, w)
            else:
                S.op("dve", lambda e: e.scalar_tensor_tensor(out=out, in0=a, scalar=s, in1=b, op0=op0, op1=op1,
                                                             accum_out=accum), r, w)

        def cp(out, in_, r, w, eng="dve"):
            if eng == "act":
                S.op("act", lambda e: e.activation(out=out, in_=in_, func=AF.Copy), r, w)
            else:
                S.op(eng, lambda e: e.tensor_copy(out=out, in_=in_), r, w)

        def red(out, in_, r, w, op=ALU.add):
            S.op("dve", lambda e: e.tensor_reduce(out=out, in_=in_, axis=AX.X, op=op), r, w)

        def rcp(out, in_, r, w):
            S.op("dve", lambda e: e.reciprocal(out=out, in_=in_), r, w)

        def mset(ap, val, w, eng="dve"):
            S.op(eng, lambda e: e.memset(ap, val), (), w)

        def dma(q, out, in_, r, w):
            S.op(q, lambda e: e.dma_start(out=out, in_=in_), r, w, dma=True)

        def gather(out, table, idx, r, w):
            S.op("pool", lambda e: e.indirect_dma_start(out=out, out_offset=None, in_=table,
                                                        in_offset=bass.IndirectOffsetOnAxis(ap=idx, axis=0)),
                 r, w, dma=True)

        def bc(ap, shape):
            return ap.to_broadcast(list(shape))

        identF = A.alloc([128, 128], F32); r_idF = Reg()
        identB = A.alloc([128, 128], BF16); r_idB = Reg()
        cs = A.alloc([128, 17, 64], F32); r_cs = Reg()
        dma("sp", identF, cd["c_ident"], [], [r_idF])
        dma("pool", identB, cd["c_ident"], [], [r_idB])
        dma("sp", cs, cd["c_cs"], [], [r_cs])
        eps_t = A.alloc([128, 1], F32); r_eps = Reg()
        mset(eps_t, EPS, [r_eps])
        one_t = A.alloc([128, 1], F32); r_one = Reg()
        mset(one_t, 1.0, [r_one])

        QsT = A.alloc([128, 4, 32], BF16)
        KnT = A.alloc([128, 4, 32], BF16)
        vb = A.alloc([32, 512], BF16)

        def load_const(name, shape, dt, q=None):
            t_ = A.alloc(shape, dt)
            r_ = Reg(name)
            if q is None:
                q = "pool" if dt == BF16 else "sp"
            dma(q, t_, name if not isinstance(name, str) else cd[name], [], [r_])
            return t_, r_

        def load_in(d_ap, shape, dt=F32, q="sp"):
            t_ = A.alloc(shape, dt)
            r_ = Reg()
            dma(q if dt == F32 or dt == I32 else "pool", t_, d_ap, [], [r_])
            return t_, r_

        def small(shape, dt=F32):
            return A.alloc(shape, dt), Reg()

        def rmsnorm_fm(x_ap, r_x, P, gbc, r_g, outFM, r_out, col0, wk):
            junk, r_junk, hn, r_hn, ssq, r_ssq, psb = wk
            stt(junk[:P], x_ap, 1.0, x_ap, ALU.mult, ALU.mult, [r_x], [r_junk, r_ssq], accum=ssq[:P, 0:1])
            act(ssq[:P, 1:2], ssq[:P, 0:1], AF.Sqrt, [r_ssq, r_eps], [r_ssq], scale=1.0 / 1024, bias=eps_t[:P, :])
            rcp(ssq[:P, 2:3], ssq[:P, 1:2], [r_ssq], [r_ssq])
            stt(hn[:P], x_ap, ssq[:P, 2:3], gbc[:P], ALU.mult, ALU.mult, [r_x, r_ssq, r_g], [r_hn])
            bi = psb.next()
            for k in range(8):
                tp(banks_b[bi][:, k * P:(k + 1) * P], hn[:P, k * 128:(k + 1) * 128], identB[:P, :P],
                   [r_hn, r_idB], [bank_reg[bi]])
            cp(outFM[:, :, col0:col0 + P], banks_b[bi][:, 0:8 * P].rearrange("p (k q) -> p k q", k=8),
               [bank_reg[bi]], [r_out], eng="act")

        def headnorm(ps_ap, r_ps, P, nh, hd, qs, r_qs, sq, r_sq, st8, r_st8):
            cp(qs[:P], ps_ap, [r_ps], [r_qs], eng="act")
            act(sq[:P], ps_ap, AF.Square, [r_ps], [r_sq])
            red(st8[:P, 0, 0:nh], sq[:P].rearrange("p (h d) -> p h d", h=nh), [r_sq], [r_st8])
            act(st8[:P, 1, 0:nh], st8[:P, 0, 0:nh], AF.Sqrt, [r_st8, r_eps], [r_st8], scale=1.0 / hd, bias=eps_t[:P, :])
            rcp(st8[:P, 2, 0:nh], st8[:P, 1, 0:nh], [r_st8], [r_st8])
            q3 = qs[:P].rearrange("p (h d) -> p h d", h=nh)
            tt(q3, q3, bc(st8[:P, 2, 0:nh].unsqueeze(2), [P, nh, hd]), ALU.mult, [r_qs, r_st8], [r_qs])

        markA = A.top
        Win = A.alloc([128, 8, 3080], BF16); r_WinA = Reg(); r_WinB = Reg()
        w_in_v = w_in_d.rearrange("(k p) c -> p k c", p=128)
        dma("pool", Win[:, :, 0:1536], w_in_v[:, :, 0:1536], [], [r_WinA])
        dma("pool", Win[:, :, 1536:3080], w_in_v[:, :, 1536:3080], [], [r_WinB])
        g_mix, r_gmix = load_in(g_mix_d, [128, 1024])
        gq, r_gq = load_in(gq_d, [128, 64])
        gk, r_gk = load_in(gk_d, [128, 64])
        gssm, r_gssm = load_in(gssm_d, [128, 512])
        dtb, r_dtb = load_in(dtb_d, [128, 8])
        a_bc, r_abc = load_in(alog_d, [128, 8])
        dskip, r_dskip = load_in(dskip_d, [128, 8])
        cw, r_cw = load_in(cw_d, [128, 8, 4])
        cb, r_cb = load_in(cb_d, [128, 8])
        act(a_bc, a_bc, AF.Exp, [r_abc], [r_abc])
        tsc(a_bc, a_bc, -1.0, ALU.mult, [r_abc], [r_abc])
        triU, r_triU = load_const("c_triU", [128, 128], F32)
        mneg, r_mneg = load_const("c_mneg", [128, 128], F32)
        ones, r_ones = load_const("c_ones", [128, 128], F32)
        triU_s, r_triUs = load_const("c_triU_s", [32, 32], F32)
        mneg_s, r_mnegs = load_const("c_mneg_s", [32, 32], F32)
        onesbd_s, r_onesbds = load_const("c_onesbd_s", [32, 32], F32)
        seqsel, r_seqsel = load_const("c_seqsel", [32, 4, 128], F32)
        colmask, r_colmask = load_const("c_colmask", [128, 4, 32], BF16)
        rowmask, r_rowmask = load_const("c_rowmask", [32, 4], F32)

        KT = A.alloc([128, 8, 2048], BF16); r_KT = [Reg() for _ in range(16)]; r_KTind = Reg()
        for h in range(8):
            dma("pool", KT[64:72, h, :], cd["c_ind"], [], [r_KTind])
        VA = A.alloc([128, 16, 8, 66], BF16); r_VA = [Reg() for _ in range(16)]; r_VAone = Reg()
        mset(VA[:, :, :, 64:65], 1.0, [r_VAone])
        hnFM = A.alloc([128, 8, 512], BF16); r_hnFM = Reg()
        xc = A.alloc([128, 8, 512], BF16); r_xc = Reg()
        carry = A.alloc([128, 8, 3], F32); r_carry = Reg()
        mset(carry, 0.0, [r_carry])
        hT = [A.alloc([128, 512], F32)]
        hTb = [A.alloc([128, 512], BF16)]
        r_hT = [Reg() for _ in range(5)]
        r_hTb = [Reg() for _ in range(5)]
        mset(hT[0], 0.0, [r_hT[0]])
        mset(hTb[0], 0.0, [r_hTb[0]])

        xt_ring = Ring([(A.alloc([128, 1024], F32), Reg()) for _ in range(1)])
        junk = A.alloc([128, 1024], BF16); r_junk = Reg()
        hn = A.alloc([128, 1024], BF16); r_hn = Reg()
        ssq = A.alloc([128, 4], F32); r_ssq = Reg()
        psT = Ring([0, 1])
        psM = Ring([2, 3])
        nwk = (junk, r_junk, hn, r_hn, ssq, r_ssq, psT)
        qs_ring = Ring([(A.alloc([128, 512], F32), Reg()) for _ in range(1)])
        sq = A.alloc([128, 512], F32); r_sq = Reg()
        st8 = A.alloc([128, 3, 8], F32); r_st8 = Reg()
        tab = A.alloc([128, 4, 32], F32); r_tab = Reg()
        rt = [A.alloc([128, 8, 32], F32) for _ in range(2)]; r_rt = [Reg() for _ in range(2)]
        rt = rt + rt; r_rt = r_rt + r_rt
        ko_ring = Ring([(A.alloc([128, 512], F32), Reg()) for _ in range(1)])
        qb = A.alloc([128, 512], BF16); r_qb = Reg()
        vs_ring = ko_ring
        zs = A.alloc([128, 512], F32); r_zs = Reg()
        dtt = A.alloc([128, 4, 8], F32); r_dtt = Reg()
        xr_ring = Ring([(A.alloc([128, 3 + 512], F32), Reg()) for _ in range(1)])
        acc = A.alloc([128, 512], F32); r_acc = Reg()
        xl, r_xl = xt_ring.items[0]
        xdt = A.alloc([128, 8, 64], BF16); r_xdt = Reg()
        xsT = A.alloc([128, 512], F32); r_xsT = Reg()
        Btm = A.alloc([128, 2, 128], BF16); r_Btm = Reg()
        Bm = A.alloc([128, 2, 128], BF16); r_Bm = Reg()
        cumt = A.alloc([128, 6, 8], F32); r_cumt = Reg()
        etot = A.alloc([128, 4, 8], F32); r_etot = Reg()
        dab_ring = Ring([(A.alloc([128, 128], F32), Reg()) for _ in range(2)])
        dec_ring = Ring([(A.alloc([128, 128], BF16), Reg()) for _ in range(2)])
        LT_ring = Ring([(A.alloc([128, 128], BF16), Reg()) for _ in range(2)])
        CTm = A.alloc([128, 4, 2, 32], BF16); r_CTm = Reg()
        y2s = A.alloc([128, 512], F32); r_y2s = Reg()
        yy = A.alloc([128, 512], F32); r_yy = Reg()
        ysq = sq; r_ysq = r_sq
        ss2 = A.alloc([128, 3, 2], F32); r_ss2 = Reg()
        mixs = A.alloc([128, 512], BF16); r_mixs = Reg()
        xdtt = A.alloc([128, 512], BF16); r_xdtt = Reg()
        hout = yy.rearrange("p (c n) -> p c n", c=4); r_hout = r_yy
        mark_attn = A.top
        negmask, r_negmask = load_const("c_negmask", [128, 4, 512], BF16)
        cbias0, r_cbias0 = A.alloc([128, 8, 128], BF16), Reg()
        dma("pool", cbias0[64:72], cd["c_bias0"], [], [r_cbias0])
        QT = A.alloc([128, 8, 512], BF16); r_QT = Reg(); r_QTb = Reg()
        PT_ring = Ring([(A.alloc([128, 512], BF16), Reg()) for _ in range(3)])
        attb = A.alloc([128, 4, 512], BF16); r_attb = [Reg() for _ in range(4)]
        rinv = A.alloc([128, 4], F32); r_rinv = Reg()
        m8 = A.alloc([128, 8, 8], F32); r_m8 = Reg()
        bsel = A.alloc([128, 8, 8], F32); r_bsel = Reg()
        biasm = A.alloc([128, 8, 8], BF16); r_biasm = Reg()
        ksumT = A.alloc([128, 8, 8], BF16); r_ksum = Reg()
        ksf = A.alloc([128, 8], F32); r_ksf = Reg()
        gsb = A.alloc([128, 8, 8], F32); r_gsb = Reg()
        mset(gsb, -1e30, [r_gsb])
        r_sconvT = Reg(); r_QsT = Reg(); r_KnT = Reg(); r_vb = Reg()
        print("phase A arena top", A.top)

        def qk_post(ps_ap, r_ps, P, ti, g_bc, r_g, is_k, row0):
            qs, r_qs = qs_ring.next()
            headnorm(ps_ap, r_ps, P, 8, 64, qs, r_qs, sq, r_sq, st8, r_st8)
            cos = cs[:P, ti, 0:32]
            sin = cs[:P, ti, 32:64]
            tt(tab[:P, 0, :], cos, g_bc[:P, 0:32], ALU.mult, [r_cs, r_g], [r_tab])
            tt(tab[:P, 1, :], sin, g_bc[:P, 32:64], ALU.mult, [r_cs, r_g], [r_tab])
            tt(tab[:P, 2, :], cos, g_bc[:P, 32:64], ALU.mult, [r_cs, r_g], [r_tab])
            tt(tab[:P, 3, :], sin, g_bc[:P, 0:32], ALU.mult, [r_cs, r_g], [r_tab])
            q3 = qs[:P].rearrange("p (h d) -> p h d", h=8)
            x1 = q3[:, :, 0:32]
            x2 = q3[:, :, 32:64]
            tb = [bc(tab[:P, i, :].unsqueeze(1), [P, 8, 32]) for i in range(4)]
            ko, r_ko = ko_ring.next()
            o3 = ko[:P].rearrange("p (h d) -> p h d", h=8)
            tt(rt[0][:P], x1, tb[0], ALU.mult, [r_qs, r_tab], [r_rt[0]])
            tt(rt[1][:P], x2, tb[1], ALU.mult, [r_qs, r_tab], [r_rt[1]])
            tt(o3[:, :, 0:32], rt[0][:P], rt[1][:P], ALU.subtract, [r_rt[0], r_rt[1]], [r_ko])
            tt(rt[2][:P], x2, tb[2], ALU.mult, [r_qs, r_tab], [r_rt[2]])
            tt(rt[3][:P], x1, tb[3], ALU.mult, [r_qs, r_tab], [r_rt[3]])
            tt(o3[:, :, 32:64], rt[2][:P], rt[3][:P], ALU.add, [r_rt[2], r_rt[3]], [r_ko])
            if is_k:
                dma("sp", newk_d[row0:row0 + P, :], ko[:P], [r_ko], [])
            cp(qb[:P], ko[:P], [r_ko], [r_qb], eng="act")
            return ko, r_ko

        def ssd_chunk(L, nseq, c0, seqs, row0, tri_c, r_tri, mneg_c, r_mn, ones_c, r_on):
            dt_ = dtt[:L, 1, :]
            da_ = dtt[:L, 2, :]
            bi = psT.next()
            for c in range(4):
                tp(banks_b[bi][:L, c * 128:(c + 1) * 128], xc[:, c, c0:c0 + L], identB, [r_xc, r_idB], [bank_reg[bi]])
            psv = banks_b[bi][:L, 0:512].rearrange("p (h d) -> p h d", h=8)
            tt(xdt[:L], psv, bc(dt_.unsqueeze(2), [L, 8, 64]), ALU.mult, [bank_reg[bi], r_dtt], [r_xdt])
            cp(xsT[:L], banks_b[bi][:L, 0:512], [bank_reg[bi]], [r_xsT], eng="act")
            bi = psT.next()
            for g in range(2):
                tp(banks_b[bi][:L, g * 128:(g + 1) * 128], xc[:, 4 + g, c0:c0 + L], identB, [r_xc, r_idB], [bank_reg[bi]])
            cp(Btm[:L], banks_b[bi][:L, 0:256].rearrange("p (g n) -> p g n", g=2), [bank_reg[bi]], [r_Btm], eng="act")
            b5 = 5
            mm(banks[b5][:L, 0:8], tri_c[:L, :L], da_, [r_tri, r_dtt], [bank_reg[b5]])
            mm(banks[b5][:L, 8:16], ones_c[:L, :L], da_, [r_on, r_dtt], [bank_reg[b5]])
            for bidx in range(nseq):
                lhs = ones[:L, :] if nseq == 1 else seqsel[:L, bidx, :]
                mm(banks[b5][:, 16 + bidx * 8:24 + bidx * 8], lhs, da_, [r_ones, r_seqsel, r_dtt], [bank_reg[b5]])
            cp(cumt[:L, 0, :], banks[b5][:L, 0:8], [bank_reg[b5]], [r_cumt])
            tsc(cumt[:L, 1, :], banks[b5][:L, 0:8], -1.0, ALU.mult, [bank_reg[b5]], [r_cumt])
            tt(cumt[:L, 2, :], banks[b5][:L, 8:16], cumt[:L, 0, :], ALU.subtract, [bank_reg[b5], r_cumt], [r_cumt])
            act(cumt[:L, 3, :], cumt[:L, 2, :], AF.Exp, [r_cumt], [r_cumt])
            act(cumt[:L, 4, :], cumt[:L, 0, :], AF.Exp, [r_cumt], [r_cumt])
            act(etot[:, 0:nseq, :], banks[b5][:, 16:16 + 8 * nseq].rearrange("p (b h) -> p b h", b=nseq), AF.Exp,
                [bank_reg[b5]], [r_etot])
            b4 = 4
            for g in range(2):
                mm(banks[b4][:L, g * 128:g * 128 + L], xc[:, 4 + g, c0:c0 + L], xc[:, 6 + g, c0:c0 + L],
                   [r_xc], [bank_reg[b4]])
            b6, b7 = 6, 7
            for h in range(8):
                g = h // 4
                bi = psM.next()
                dab, r_dab = dab_ring.next()
                cp(dab[:L, 0:L], bc(dtt[:L, 2, h:h + 1], [L, L]), [r_dtt], [r_dab])
                mm(banks[bi][:L, 0:L], dab[:L, 0:L], tri_c[:L, :L], [r_dab, r_tri], [bank_reg[bi]], start=True, stop=False)
                mm(banks[bi][:L, 0:L], identF[:L, :L], mneg_c[:L, :L], [r_idF, r_mn], [bank_reg[bi]], start=False, stop=True)
                dec, r_dec = dec_ring.next()
                act(dec[:L, :L], banks[bi][:L, 0:L], AF.Exp, [bank_reg[bi], r_cumt], [r_dec], bias=cumt[:L, 1, h:h + 1])
                LT, r_LT = LT_ring.next()
                tt(LT[:L, :L], banks[b4][:L, g * 128:g * 128 + L], dec[:L, :L], ALU.mult, [bank_reg[b4], r_dec], [r_LT])
                mm(banks[b6][:L, h * 64:(h + 1) * 64], LT[:L, :L], xdt[:L, h, :], [r_LT, r_xdt], [bank_reg[b6]])
            if nseq > 1:
                for bidx in range(nseq):
                    for g in range(2):
                        tt(CTm[:, bidx, g, :], xc[:, 6 + g, c0:c0 + L], colmask[:, bidx, :], ALU.mult,
                           [r_xc, r_colmask], [r_CTm])
            for g in range(2):
                for bidx, sq_ in enumerate(seqs):
                    lhs = xc[:, 6 + g, c0:c0 + L] if nseq == 1 else CTm[:, bidx, g, :]
                    mm(banks[b7][:L, g * 256:(g + 1) * 256], lhs, hTb[sq_][:, g * 256:(g + 1) * 256],
                       [r_xc, r_CTm, r_hTb[sq_]], [bank_reg[b7]], start=(bidx == 0), stop=(bidx == nseq - 1))
            tt(y2s[:L].rearrange("p (h d) -> p h d", h=8), banks[b7][:L, :].rearrange("p (h d) -> p h d", h=8),
               bc(cumt[:L, 4, :].unsqueeze(2), [L, 8, 64]), ALU.mult, [bank_reg[b7], r_cumt], [r_y2s])
            tt(yy[:L], banks[b6][:L, :], y2s[:L], ALU.add, [bank_reg[b6], r_y2s], [r_yy])
            tt(y2s[:L].rearrange("p (h d) -> p h d", h=8), xsT[:L].rearrange("p (h d) -> p h d", h=8),
               bc(dskip[:L, :].unsqueeze(2), [L, 8, 64]), ALU.mult, [r_xsT, r_dskip], [r_y2s])
            tt(yy[:L], yy[:L], y2s[:L], ALU.add, [r_yy, r_y2s], [r_yy])
            tt(yy[:L], yy[:L], zs[:L], ALU.mult, [r_yy, r_zs], [r_yy])
            tt(ysq[:L], yy[:L], yy[:L], ALU.mult, [r_yy], [r_ysq])
            red(ss2[:L, 0, :], ysq[:L].rearrange("p (g d) -> p g d", g=2), [r_ysq], [r_ss2])
            act(ss2[:L, 1, :], ss2[:L, 0, :], AF.Sqrt, [r_ss2, r_eps], [r_ss2], scale=1.0 / 256, bias=eps_t[:L, :])
            rcp(ss2[:L, 2, :], ss2[:L, 1, :], [r_ss2], [r_ss2])
            tt(yy[:L].rearrange("p (g d) -> p g d", g=2), yy[:L].rearrange("p (g d) -> p g d", g=2),
               bc(ss2[:L, 2, :].unsqueeze(2), [L, 2, 256]), ALU.mult, [r_yy, r_ss2], [r_yy])
            tt(mixs[:L], yy[:L], gssm[:L], ALU.mult, [r_yy, r_gssm], [r_mixs])
            dma("sp", mix_d[row0:row0 + L, 512:1024], mixs[:L], [r_mixs], [])
            tt(xdtt[:L].rearrange("p (h d) -> p h d", h=8), xdt[:L], bc(cumt[:L, 3, :].unsqueeze(2), [L, 8, 64]),
               ALU.mult, [r_xdt, r_cumt], [r_xdtt])
            for bidx, sq_ in enumerate(seqs):
                if nseq > 1:
                    tsc(Bm[:L].rearrange("p g n -> p (g n)"), Btm[:L].rearrange("p g n -> p (g n)"),
                        rowmask[:L, bidx:bidx + 1], ALU.mult, [r_Btm, r_rowmask], [r_Bm])
                    Bsrc, r_Bs = Bm, r_Bm
                else:
                    Bsrc, r_Bs = Btm, r_Btm
                bi = psM.next()
                for g in range(2):
                    mm(banks[bi][:, g * 256:(g + 1) * 256], Bsrc[:L, g, :], xdtt[:L, g * 256:(g + 1) * 256],
                       [r_Bs, r_xdtt], [bank_reg[bi]])
                h3 = hT[sq_].rearrange("p (h d) -> p h d", h=8)
                tt(h3, h3, bc(etot[:, bidx, :].unsqueeze(2), [128, 8, 64]), ALU.mult, [r_hT[sq_], r_etot], [r_hT[sq_]])
                tt(hT[sq_], hT[sq_], banks[bi][:, :], ALU.add, [r_hT[sq_], bank_reg[bi]], [r_hT[sq_]])
                cp(hTb[sq_], hT[sq_], [r_hT[sq_]], [r_hTb[sq_]], eng="act")

        def ssm_out(sq_):
            bi = psM.next()
            for c in range(4):
                tp(banks[bi][:, c * 128:(c + 1) * 128], hT[sq_][:, c * 128:(c + 1) * 128], identF, [r_hT[sq_], r_idF],
                   [bank_reg[bi]])
            cp(hout, banks[bi][:, :].rearrange("p (c n) -> p c n", c=4), [bank_reg[bi]], [r_hout], eng="act")
            dma("sp", newssm_d[sq_].rearrange("(c p) n -> p c n", p=128), hout, [r_hout], [])

        def phaseA_tile(tok0, P, nsub, nseq, is_sample, tidx):
            W = P * nsub
            L = W // nseq
            for s in range(nsub):
                xt, r_xt = xt_ring.next()
                dma("sp", xt[:P], x_d[tok0 + s * P: tok0 + (s + 1) * P, :], [], [r_xt])
                rmsnorm_fm(xt[:P], r_xt, P, g_mix, r_gmix, hnFM, r_hnFM, s * P, nwk)
            for c in range(8):
                bi = psM.next()
                for k in range(8):
                    mm(banks[bi][:, 0:W], Win[:, k, 2048 + c * 128: 2048 + (c + 1) * 128], hnFM[:, k, 0:W],
                       [r_WinB, r_hnFM], [bank_reg[bi]], start=(k == 0), stop=(k == 7))
                xr, r_xr = xr_ring.next()
                xr3 = xr[:, 0:nseq * (3 + L)].rearrange("p (b l) -> p b l", b=nseq)
                if is_sample:
                    cp(xr3[:, :, 0:3], sconvT[:, c, :].rearrange("p (b r) -> p b r", b=4), [r_sconvT], [r_xr])
                else:
                    cp(xr3[:, :, 0:3], carry[:, c, :].unsqueeze(1), [r_carry], [r_xr])
                cp(xr3[:, :, 3:3 + L], banks[bi][:, 0:W].rearrange("p (b l) -> p b l", b=nseq), [bank_reg[bi]], [r_xr],
                   eng="act")
                acc3 = acc[:, 0:W].rearrange("p (b l) -> p b l", b=nseq)
                tsc(acc3, xr3[:, :, 0:L], cw[:, c, 0:1], ALU.mult, [r_xr, r_cw, r_cb], [r_acc], s2=cb[:, c:c + 1], op1=ALU.add)
                for i in range(1, 4):
                    stt(acc3, xr3[:, :, i:i + L], cw[:, c, i:i + 1], acc3, ALU.mult, ALU.add, [r_xr, r_cw, r_acc], [r_acc])
                act(xc[:, c, 0:W], acc[:, 0:W], AF.Silu, [r_acc], [r_xc])
                if not is_sample:
                    cp(carry[:, c, :], xr[:, W:W + 3], [r_xr], [r_carry])
            for s in range(nsub):
                row0 = tok0 + s * P
                ti = 16 if is_sample else (tok0 // 128 + s)
                cols = slice(s * P, (s + 1) * P)

                def proj(c0, c1, r_w):
                    bi_ = psM.next()
                    for k in range(8):
                        mm(banks[bi_][:P, 0:c1 - c0], hnFM[:, k, cols], Win[:, k, c0:c1], [r_hnFM, r_w], [bank_reg[bi_]],
                           start=(k == 0), stop=(k == 7))
                    return bi_
                bi = proj(512, 1024, r_WinA)
                qk_post(banks[bi][:P, :], bank_reg[bi], P, ti, gk, r_gk, True, row0)
                if not is_sample:
                    bt = psT.next()
                    for h in range(8):
                        tp(banks_b[bt][0:64, h * 128:(h + 1) * 128], qb[:, h * 64:(h + 1) * 64], identB, [r_qb, r_idB],
                           [bank_reg[bt]])
                    kti = tok0 // 128 + s
                    cp(KT[0:64, :, kti * 128:(kti + 1) * 128], banks_b[bt][0:64, 0:1024].rearrange("p (h q) -> p h q", h=8),
                       [bank_reg[bt]], [r_KT[kti]], eng="act")
                else:
                    bt = psT.next()
                    for c in range(4):
                        tp(banks_b[bt][:, c * 32:(c + 1) * 32], qb[:32, c * 128:(c + 1) * 128], identB[:32, :32],
                           [r_qb, r_idB], [bank_reg[bt]])
                    cp(KnT, banks_b[bt][:, 0:128].rearrange("p (c q) -> p c q", c=4), [bank_reg[bt]], [r_KnT], eng="act")
                bi = proj(0, 512, r_WinA)
                qk_post(banks[bi][:P, :], bank_reg[bi], P, ti, gq, r_gq, False, row0)
                if not is_sample:
                    bt = psT.next()
                    for h in range(8):
                        tp(banks_b[bt][0:64, h * 128:(h + 1) * 128], qb[:, h * 64:(h + 1) * 64], identB, [r_qb, r_idB],
                           [bank_reg[bt]])
                    cp(QT[0:64, :, cols], banks_b[bt][0:64, 0:1024].rearrange("p (h q) -> p h q", h=8),
                       [bank_reg[bt]], [r_QT], eng="act")
                else:
                    bt = psT.next()
                    for c in range(4):
                        tp(banks_b[bt][:, c * 32:(c + 1) * 32], qb[:32, c * 128:(c + 1) * 128], identB[:32, :32],
                           [r_qb, r_idB], [bank_reg[bt]])
                    cp(QsT, banks_b[bt][:, 0:128].rearrange("p (c q) -> p c q", c=4), [bank_reg[bt]], [r_QsT], eng="act")
                bi = proj(1024, 1536, r_WinA)
                vs, r_vs = vs_ring.next()
                cp(vs[:P], banks[bi][:P, :], [bank_reg[bi]], [r_vs], eng="act")
                dma("sp", newv_d[row0:row0 + P, :], vs[:P], [r_vs], [])
                if not is_sample:
                    kti = tok0 // 128 + s
                    cp(VA[:, kti, :, 0:64], banks[bi][:, :].rearrange("p (h d) -> p h d", h=8), [bank_reg[bi]], [r_VA[kti]])
                else:
                    cp(vb, banks[bi][:32, :], [bank_reg[bi]], [r_vb])
                bi = proj(1536, 2048, r_WinB)
                act(zs[:P], banks[bi][:P, :], AF.Silu, [bank_reg[bi]], [r_zs])
                bi = proj(3072, 3080, r_WinB)
                tt(dtt[:P, 0, :], banks[bi][:P, 0:8], dtb[:P], ALU.add, [bank_reg[bi], r_dtb], [r_dtt])
                act(dtt[:P, 0, :], dtt[:P, 0, :], AF.Exp, [r_dtt], [r_dtt])
                act(dtt[:P, 1, :], dtt[:P, 0, :], AF.Ln, [r_dtt, r_one], [r_dtt], bias=one_t[:P, :])
                tt(dtt[:P, 2, :], dtt[:P, 1, :], a_bc[:P], ALU.mult, [r_dtt, r_abc], [r_dtt])
                if is_sample or (tok0 == 1536 and s == 3):
                    for half in range(2):
                        bi = proj(2048 + half * 512, 2048 + (half + 1) * 512, r_WinB)
                        cp(xl[:P, half * 512:(half + 1) * 512], banks[bi][:P, :], [bank_reg[bi]], [r_xl], eng="act")
                    if is_sample:
                        for b_ in range(4):
                            dma("sp", newconv_d[3 + 3 * b_: 6 + 3 * b_, :], xl[8 * b_ + 5: 8 * b_ + 8, :], [r_xl], [])
                    else:
                        dma("sp", newconv_d[0:3, :], xl[125:128, :], [r_xl], [])
                if is_sample:
                    ssd_chunk(32, 4, 0, [1, 2, 3, 4], row0, triU_s, r_triUs, mneg_s, r_mnegs, onesbd_s, r_onesbds)
                else:
                    ssd_chunk(128, 1, s * 128, [0], row0, triU, r_triU, mneg, r_mneg, ones, r_ones)

        def attention_tile(i):
            tok0 = i * 512
            for j in (2 * i, 2 * i + 1):
                red(ksf[0:64, :], KT[0:64, :, j * 256:(j + 1) * 256], [r_KT[2 * j], r_KT[2 * j + 1]], [r_ksf])
                cp(ksumT[0:64, :, j], ksf[0:64, :], [r_ksf], [r_ksum])
            for s in range(4):
                own = (tok0 + s * 128) // 256
                cols = slice(s * 128, (s + 1) * 128)
                if own <= 3:
                    cp(QT[64:72, :, cols], bc(cbias0[64:72, own, :].unsqueeze(1), [8, 8, 128]), [r_cbias0], [r_QTb])
                else:
                    bi = psM.next()
                    for h in range(8):
                        mm(banks[bi][:, h * 8:h * 8 + own], QT[0:64, h, cols], ksumT[0:64, h, 0:own], [r_QT, r_ksum],
                           [bank_reg[bi]])
                    cp(gsb[:, :, 0:own], banks[bi][:, 0:64].rearrange("p (h j) -> p h j", h=8)[:, :, 0:own],
                       [bank_reg[bi]], [r_gsb])
                    for h in range(8):
                        S.op("dve", (lambda o_, i_: (lambda e: e.max(out=o_, in_=i_)))(m8[:, h, :], gsb[:, h, :]),
                             [r_gsb], [r_m8])
                    tt(bsel, gsb, bc(m8[:, :, 2:3], [128, 8, 8]), ALU.is_lt, [r_gsb, r_m8], [r_bsel])
                    tsc(biasm, bsel, NEG, ALU.mult, [r_bsel], [r_biasm])
                    mset(biasm[:, :, own:own + 1], 0.0, [r_biasm])
                    bt = psT.next()
                    for h in range(8):
                        tp(banks_b[bt][64:72, h * 128:(h + 1) * 128], biasm[:, h, :], identB, [r_biasm, r_idB],
                           [bank_reg[bt]])
                    cp(QT[64:72, :, cols], banks_b[bt][64:72, 0:1024].rearrange("p (h q) -> p h q", h=8),
                       [bank_reg[bt]], [r_QTb], eng="act")
            obank = [4, 5, 6, 7]
            for h in range(8):
                nk = 4 * i + 4
                for kc in range(nk):
                    diag = kc >= 4 * i
                    bi = psM.next()
                    mm(banks[bi][:, :], KT[0:72, h, kc * 128:(kc + 1) * 128], QT[0:72, h, :],
                       [r_KT[kc], r_KTind, r_QT, r_QTb], [bank_reg[bi]], start=True, stop=not diag)
                    if diag:
                        mm(banks[bi][:, :], identB, negmask[:, kc - 4 * i, :], [r_idB, r_negmask], [bank_reg[bi]],
                           start=False, stop=True)
                    PT, r_PT = PT_ring.next()
                    act(PT, banks[bi][:, :], AF.Exp, [bank_reg[bi]], [r_PT], scale=0.125)
                    for s in range(4):
                        last = 4 * i + s
                        if kc > last:
                            continue
                        ob = obank[s]
                        mm(banks[ob][:, 0:65], PT[:, s * 128:(s + 1) * 128], VA[:, kc, h, 0:65], [r_PT, r_VA[kc], r_VAone],
                           [bank_reg[ob]], start=(kc == 0), stop=(kc == last))
                for s in range(4):
                    ob = obank[s]
                    rcp(rinv[:, s:s + 1], banks[ob][:, 64:65], [bank_reg[ob]], [r_rinv])
                    tsc(attb[:, s, h * 64:(h + 1) * 64], banks[ob][:, 0:64], rinv[:, s:s + 1], ALU.mult,
                        [bank_reg[ob], r_rinv], [r_attb[s]])
            for s in range(4):
                dma("sp", mix_d[tok0 + s * 128: tok0 + (s + 1) * 128, 0:512], attb[:, s, :], [r_attb[s]], [])

        if "A" in stages:
            for i in range(int(os.environ.get('K_NTILES', '4'))):
                phaseA_tile(i * 512, 128, 4, 1, False, i)
                if "B" in stages:
                    attention_tile(i)
            ssm_out(0)
            if os.environ.get('K_SAMPLE', '1') == '1':
                S.barrier()
                A.top = mark_attn
                for _ in range(4):
                    hT.append(A.alloc([128, 512], F32))
                    hTb.append(A.alloc([128, 512], BF16))
                sconvT = A.alloc([128, 8, 12], F32)
                sct, r_sct = xt_ring.next()
                dma("sp", sct[:12], sconv_d, [], [r_sct])
                bi = psM.next()
                for c in range(8):
                    tp(banks[bi][:, c * 12:(c + 1) * 12], sct[:12, c * 128:(c + 1) * 128], identF[:12, :12], [r_sct, r_idF],
                       [bank_reg[bi]])
                cp(sconvT, banks[bi][:, 0:96].rearrange("p (c r) -> p c r", c=8), [bank_reg[bi]], [r_sconvT], eng="act")
                for b_ in range(4):
                    dma("sp", hout, sssm_d[b_].rearrange("(c p) n -> p c n", p=128), [], [r_hout])
                    bi = psM.next()
                    for c in range(4):
                        tp(banks[bi][:, c * 128:(c + 1) * 128], hout[:, c, :], identF, [r_hout, r_idF], [bank_reg[bi]])
                    cp(hT[1 + b_], banks[bi][:, :], [bank_reg[bi]], [r_hT[1 + b_]], eng="act")
                    cp(hTb[1 + b_], banks[bi][:, :], [bank_reg[bi]], [r_hTb[1 + b_]])
                phaseA_tile(2048, 32, 1, 4, True, 4)
                for b_ in range(4):
                    ssm_out(1 + b_)


        if "S" in stages:
            S.barrier()
            A.top = markA
            psT = Ring([0, 1]); psM = Ring([2, 3])
            KTall = A.alloc([128, 4, 8192], BF16); r_KTall = [Reg() for _ in range(64)]
            Kp_ring = Ring([(A.alloc([128, 512], BF16), Reg()) for _ in range(3)])
            Vp_ring = Ring([(A.alloc([128, 512], BF16), Reg()) for _ in range(3)])
            Zc, r_Zc = load_const("c_Z", [128, 63], BF16)
            selE, r_selE = load_const("c_selE", [32, 32, 128], BF16)
            bdmask, r_bd = load_const("c_bdmask", [64, 512], F32)
            negcs, r_negcs = load_const("c_negcs", [32, 4, 64], BF16)
            iota, r_iota = load_const("c_iota", [128, 1], F32)
            onesB = A.alloc([128, 2], BF16); r_onesB = Reg()
            mset(onesB, 1.0, [r_onesB])
            pti = A.alloc([128, 64], I32); r_pti = Reg()
            ptf = A.alloc([128, 64], F32); r_ptf = Reg()
            idxi = A.alloc([128, 64], I32); r_idx = Reg()
            Qbd = A.alloc([128, 4, 64], BF16); r_Qbd = Reg()
            mset(Qbd, 0.0, [r_Qbd])
            km = A.alloc([32, 512], F32); r_km = Reg()
            kmT = A.alloc([128, 4, 32], BF16); r_kmT = Reg()
            gs = A.alloc([64, 32], F32); r_gs = Reg()
            m8s = A.alloc([64, 8], F32); r_m8s = Reg()
            bm = A.alloc([64, 32], BF16); r_bm = Reg()
            biasT = A.alloc([32, 64], BF16); r_biasT = Reg()
            PTs_ring = Ring([(A.alloc([128, 64], BF16), Reg()) for _ in range(2)])
            PTn = A.alloc([32, 64], BF16); r_PTn = Reg()
            om = A.alloc([64, 512], F32); r_om = Reg()
            osm = A.alloc([64, 64], F32); r_osm = Reg()
            rl = A.alloc([64, 1], F32); r_rl = Reg()
            o2 = A.alloc([64, 64], BF16); r_o2 = Reg()
            for b in range(4):
                dma("sp", pti, pt_d[b:b + 1, :].to_broadcast([128, 64]), [], [r_pti])
                cp(ptf, pti, [r_pti], [r_ptf])
                tsc(ptf, ptf, 128.0, ALU.mult, [r_ptf, r_iota], [r_ptf], s2=iota[:, 0:1], op1=ALU.add)
                cp(idxi, ptf, [r_ptf], [r_idx])
                for c in range(4):
                    cp(Qbd[0:64, c, (2 * c) * 8:(2 * c) * 8 + 8], QsT[0:64, c, b * 8:(b + 1) * 8], [r_QsT], [r_Qbd])
                    cp(Qbd[64:128, c, (2 * c + 1) * 8:(2 * c + 1) * 8 + 8], QsT[64:128, c, b * 8:(b + 1) * 8], [r_QsT], [r_Qbd])
                for pg in range(64):
                    j = pg // 2
                    Kp, r_Kp = Kp_ring.next()
                    gather(Kp, ck_d, idxi[:, pg:pg + 1], [r_idx], [r_Kp])
                    mm(banks[4][0:32, :], Zc[:, 31 - j:63 - j], Kp, [r_Zc, r_Kp], [bank_reg[4]], start=(pg == 0), stop=(pg == 63))
                    bt = psT.next()
                    for c in range(4):
                        tp(banks_b[bt][:, c * 128:(c + 1) * 128], Kp[:, c * 128:(c + 1) * 128], identB, [r_Kp, r_idB], [bank_reg[bt]])
                    cp(KTall[:, :, pg * 128:(pg + 1) * 128], banks_b[bt][:, 0:512].rearrange("p (c q) -> p c q", c=4),
                       [bank_reg[bt]], [r_KTall[pg]], eng=("act" if pg % 2 == 0 else "dve"))
                cp(km, banks[4][0:32, :], [bank_reg[4]], [r_km], eng="act")
                for c in range(4):
                    tp(banks[7][:, c * 32:(c + 1) * 32], km[0:32, c * 128:(c + 1) * 128], identF[0:32, 0:32], [r_km, r_idF], [bank_reg[7]])
                cp(kmT, banks[7][:, 0:128].rearrange("p (c j) -> p c j", c=4), [bank_reg[7]], [r_kmT], eng="act")
                for c in range(4):
                    mm(banks[7][0:64, 256:288], Qbd[:, c, :], kmT[:, c, :], [r_Qbd, r_kmT], [bank_reg[7]], start=(c == 0), stop=(c == 3))
                cp(gs, banks[7][0:64, 256:288], [bank_reg[7]], [r_gs])
                S.op("dve", lambda e: e.max(out=m8s, in_=gs), [r_gs], [r_m8s])
                tsc(bm, gs, m8s[:, 2:3], ALU.is_lt, [r_gs, r_m8s], [r_bm], s2=NEG, op1=ALU.mult)
                bt = psT.next()
                tp(banks_b[bt][0:32, 0:64], bm[0:64, 0:32], identB[0:64, 0:64], [r_bm, r_idB], [bank_reg[bt]])
                cp(biasT, banks_b[bt][0:32, 0:64], [bank_reg[bt]], [r_biasT], eng="act")
                for pg in range(64):
                    j = pg // 2
                    Vp, r_Vp = Vp_ring.next()
                    gather(Vp, cv_d, idxi[:, pg:pg + 1], [r_idx], [r_Vp])
                    bi = psM.next()
                    mm(banks[bi][:, 0:64], selE[0:32, j, :], biasT[0:32, :], [r_selE, r_biasT], [bank_reg[bi]], start=True, stop=False)
                    for c in range(4):
                        mm(banks[bi][:, c * 16:(c + 1) * 16], KTall[:, c, pg * 128:(pg + 1) * 128], Qbd[:, c, c * 16:(c + 1) * 16],
                           [r_KTall[pg], r_Qbd], [bank_reg[bi]], start=False, stop=(c == 3))
                    PTs, r_PTs = PTs_ring.next()
                    act(PTs, banks[bi][:, 0:64], AF.Exp, [bank_reg[bi]], [r_PTs], scale=0.125)
                    mm(banks[5][0:64, :], PTs, Vp, [r_PTs, r_Vp], [bank_reg[5]], start=(pg == 0), stop=False)
                    mm(banks[6][0:64, 0:2], PTs, onesB[:, 0:2], [r_PTs, r_onesB], [bank_reg[6]], start=(pg == 0), stop=False)
                bi = psM.next()
                mm(banks[bi][0:32, 0:64], identB[0:32, 0:32], negcs[0:32, b, :], [r_idB, r_negcs], [bank_reg[bi]], start=True, stop=False)
                for c in range(4):
                    mm(banks[bi][0:32, c * 16:(c + 1) * 16], KnT[:, c, 0:32], Qbd[:, c, c * 16:(c + 1) * 16], [r_KnT, r_Qbd],
                       [bank_reg[bi]], start=False, stop=(c == 3))
                act(PTn, banks[bi][0:32, 0:64], AF.Exp, [bank_reg[bi]], [r_PTn], scale=0.125)
                mm(banks[5][0:64, :], PTn, vb[0:32, :], [r_PTn, r_vb], [bank_reg[5]], start=False, stop=True)
                mm(banks[6][0:64, 0:2], PTn, onesB[0:32, 0:2], [r_PTn, r_onesB], [bank_reg[6]], start=False, stop=True)
                tt(om, banks[5][0:64, :], bdmask, ALU.mult, [bank_reg[5], r_bd], [r_om])
                red(osm, om.rearrange("p (h d) -> p d h", h=8), [r_om], [r_osm])
                rcp(rl, banks[6][0:64, 0:1], [bank_reg[6]], [r_rl])
                tsc(o2, osm, rl[:, 0:1], ALU.mult, [r_osm, r_rl], [r_o2])
                for h in range(8):
                    dma("sp", mix_d[2048 + 8 * b: 2048 + 8 * b + 8, h * 64:(h + 1) * 64], o2[h * 8:(h + 1) * 8, :], [r_o2], [])

        if "C" in stages:
            S.barrier()
            A.top = markA
            psT = Ring([0, 1]); psM = Ring([2, 3])
            wout = A.alloc([128, 8, 1024], BF16); r_wout = Reg()
            dma("pool", wout, w_out_d.rearrange("(k p) c -> p k c", p=128), [], [r_wout])
            wq = A.alloc([128, 8, 512], BF16); r_wq = Reg()
            dma("pool", wq, wq_d.rearrange("(k p) c -> p k c", p=128), [], [r_wq])
            wo = A.alloc([128, 4, 1024], BF16); r_wo = Reg()
            dma("pool", wo, wo_d.rearrange("(k p) c -> p k c", p=128), [], [r_wo])
            g_x, r_gx = load_in(g_x_d, [128, 1024])
            g_mlp, r_gmlp = load_in(g_mlp_d, [128, 1024])
            gqx, r_gqx = load_in(gqx_d, [128, 128])
            gkx, r_gkx = load_in(gkx_d, [128, 128])
            MKT = A.alloc([128, 5, 4, 256], BF16); r_MKT = Reg()
            MV = A.alloc([128, 5, 2, 4, 130], BF16); r_MV = Reg(); r_MVone = Reg()
            mset(MV[:, :, :, :, 128:129], 1.0, [r_MVone])
            xt = A.alloc([128, 1024], F32); r_xt = Reg()
            junk = A.alloc([128, 1024], BF16); r_junk = Reg()
            hn = A.alloc([128, 1024], BF16); r_hn = Reg()
            ssq = A.alloc([128, 4], F32); r_ssq = Reg()
            nwk = (junk, r_junk, hn, r_hn, ssq, r_ssq, psT)
            qs = A.alloc([128, 512], F32); r_qs = Reg()
            sq = A.alloc([128, 512], F32); r_sq = Reg()
            st8 = A.alloc([128, 3, 8], F32); r_st8 = Reg()
            mb = A.alloc([128, 512], BF16); r_mb = Reg()
            markM = A.top
            wk = A.alloc([128, 8, 512], BF16); r_wk = Reg()
            dma("pool", wk, wk_d.rearrange("(k p) c -> p k c", p=128), [], [r_wk])
            wv = A.alloc([128, 8, 512], BF16); r_wv = Reg()
            dma("pool", wv, wv_d.rearrange("(k p) c -> p k c", p=128), [], [r_wv])
            g_mem, r_gmem = load_in(g_mem_d, [128, 1024])
            mnFM = A.alloc([128, 8, 256], BF16); r_mnFM = Reg()
            vs = A.alloc([128, 512], F32); r_vs = Reg()
            for mc in range(2):
                dma("sp", xt, mem_d[mc * 128:(mc + 1) * 128, :], [], [r_xt])
                rmsnorm_fm(xt, r_xt, 128, g_mem, r_gmem, mnFM, r_mnFM, mc * 128, nwk)
            for mc in range(2):
                bi = psM.next()
                for k in range(8):
                    mm(banks[bi][:, :], mnFM[:, k, mc * 128:(mc + 1) * 128], wk[:, k, :], [r_mnFM, r_wk], [bank_reg[bi]],
                       start=(k == 0), stop=(k == 7))
                headnorm(banks[bi][:, :], bank_reg[bi], 128, 4, 128, qs, r_qs, sq, r_sq, st8, r_st8)
                q3 = qs.rearrange("p (h d) -> p h d", h=4)
                tt(q3, q3, bc(gkx.unsqueeze(1), [128, 4, 128]), ALU.mult, [r_qs, r_gkx], [r_qs])
                dma("sp", newmk_d[mc * 128:(mc + 1) * 128, :], qs, [r_qs], [])
                cp(mb, qs, [r_qs], [r_mb])
                bt = psT.next()
                for h in range(4):
                    tp(banks_b[bt][:, h * 128:(h + 1) * 128], mb[:, h * 128:(h + 1) * 128], identB, [r_mb, r_idB], [bank_reg[bt]])
                cp(MKT[:, 0, :, mc * 128:(mc + 1) * 128], banks_b[bt][:, 0:512].rearrange("p (h q) -> p h q", h=4),
                   [bank_reg[bt]], [r_MKT], eng="act")
                bi = psM.next()
                for k in range(8):
                    mm(banks[bi][:, :], mnFM[:, k, mc * 128:(mc + 1) * 128], wv[:, k, :], [r_mnFM, r_wv], [bank_reg[bi]],
                       start=(k == 0), stop=(k == 7))
                cp(vs, banks[bi][:, :], [bank_reg[bi]], [r_vs], eng="act")
                dma("sp", newmv_d[mc * 128:(mc + 1) * 128, :], vs, [r_vs], [])
                cp(MV[:, 0, mc, :, 0:128], banks[bi][:, :].rearrange("p (h d) -> p h d", h=4), [bank_reg[bi]], [r_MV])
            for b in range(4):
                for mc in range(2):
                    dma("pool", mb, cmk_d[b, mc * 128:(mc + 1) * 128, :], [], [r_mb])
                    bt = psT.next()
                    for h in range(4):
                        tp(banks_b[bt][:, h * 128:(h + 1) * 128], mb[:, h * 128:(h + 1) * 128], identB, [r_mb, r_idB], [bank_reg[bt]])
                    cp(MKT[:, 1 + b, :, mc * 128:(mc + 1) * 128], banks_b[bt][:, 0:512].rearrange("p (h q) -> p h q", h=4),
                       [bank_reg[bt]], [r_MKT], eng="act")
                    dma("pool", MV[:, 1 + b, mc, :, 0:128], cmv_d[b, mc * 128:(mc + 1) * 128, :].rearrange("p (h d) -> p h d", h=4),
                        [], [r_MV])
            S.barrier()
            A.top = markM
            xr4 = A.alloc([128, 4, 1024], F32); r_xr4 = [Reg() for _ in range(4)]
            mt = A.alloc([128, 1024], BF16); r_mt = Reg()
            mixFM = A.alloc([128, 8, 512], BF16); r_mixFM = Reg()
            hxFM = A.alloc([128, 8, 512], BF16); r_hxFM = Reg()
            QxT = A.alloc([128, 4, 512], BF16); r_QxT = Reg()
            oxb = A.alloc([128, 4, 512], BF16); r_oxb = [Reg() for _ in range(4)]
            oxFM = A.alloc([128, 4, 512], BF16); r_oxFM = Reg()
            PTx_ring = Ring([(A.alloc([128, 512], BF16), Reg()) for _ in range(2)])
            PTxb = A.alloc([128, 4, 32], BF16); r_PTxb = Reg()
            mset(PTxb, 0.0, [r_PTxb])
            rinv = A.alloc([128, 4], F32); r_rinv = Reg()
            hF_ring = Ring([(A.alloc([128, 4, 512], BF16), Reg()) for _ in range(2)])
            wu_ring = Ring([(A.alloc([128, 8, 512], BF16), Reg()) for _ in range(2)])
            wd_ring = Ring([(A.alloc([128, 4, 1024], BF16), Reg()) for _ in range(2)])
            tmpr = A.alloc([128, 512], BF16); r_tmpr = Reg()
            print("phase C arena top", A.top)
            obank = [4, 5, 6, 7]
            xscale = float(128 ** -0.5)
            wup_v = wup_d.rearrange("(k p) c -> p k c", p=128)

            def phaseC_tile(tok0, P, nsub, is_sample):
                W = P * nsub
                for s in range(nsub):
                    rows = slice(tok0 + s * P, tok0 + (s + 1) * P)
                    cols = slice(s * P, (s + 1) * P)
                    xs_ = xr4[:P, s, :]
                    dma("sp", mt[:P], mix_d[rows, :], [], [r_mt])
                    dma("sp", xs_, x_d[rows, :], [], [r_xr4[s]])
                    bt = psT.next()
                    for k in range(8):
                        tp(banks_b[bt][:, k * P:(k + 1) * P], mt[:P, k * 128:(k + 1) * 128], identB[:P, :P], [r_mt, r_idB], [bank_reg[bt]])
                    cp(mixFM[:, :, cols], banks_b[bt][:, 0:8 * P].rearrange("p (k q) -> p k q", k=8), [bank_reg[bt]], [r_mixFM], eng="act")
                    for half in range(2):
                        hc = slice(half * 512, (half + 1) * 512)
                        bi = psM.next()
                        for k in range(8):
                            mm(banks[bi][:P, :], mixFM[:, k, cols], wout[:, k, hc], [r_mixFM, r_wout], [bank_reg[bi]],
                               start=(k == 0), stop=(k == 7))
                        tt(xr4[:P, s, hc], banks[bi][:P, :], xr4[:P, s, hc], ALU.add, [bank_reg[bi], r_xr4[s]], [r_xr4[s]])
                    rmsnorm_fm(xs_, r_xr4[s], P, g_x, r_gx, hxFM, r_hxFM, s * P, nwk)
                    bi = psM.next()
                    for k in range(8):
                        mm(banks[bi][:P, :], hxFM[:, k, cols], wq[:, k, :], [r_hxFM, r_wq], [bank_reg[bi]], start=(k == 0), stop=(k == 7))
                    headnorm(banks[bi][:P, :], bank_reg[bi], P, 4, 128, qs, r_qs, sq, r_sq, st8, r_st8)
                    q3 = qs[:P].rearrange("p (h d) -> p h d", h=4)
                    tt(q3, q3, bc(gqx[:P].unsqueeze(1), [P, 4, 128]), ALU.mult, [r_qs, r_gqx], [r_qs])
                    cp(mb[:P], qs[:P], [r_qs], [r_mb])
                    bt = psT.next()
                    for h in range(4):
                        tp(banks_b[bt][:, h * P:(h + 1) * P], mb[:P, h * 128:(h + 1) * 128], identB[:P, :P], [r_mb, r_idB], [bank_reg[bt]])
                    cp(QxT[:, :, cols], banks_b[bt][:, 0:4 * P].rearrange("p (h q) -> p h q", h=4), [bank_reg[bt]], [r_QxT], eng="act")
                for h in range(4):
                    for mc in range(2):
                        mcs = slice(mc * 128, (mc + 1) * 128)
                        bi = psM.next()
                        if not is_sample:
                            mm(banks[bi][:, 0:W], MKT[:, 0, h, mcs], QxT[:, h, 0:W], [r_MKT, r_QxT], [bank_reg[bi]])
                            PTx, r_PTx = PTx_ring.next()
                            act(PTx[:, 0:W], banks[bi][:, 0:W], AF.Exp, [bank_reg[bi]], [r_PTx], scale=xscale)
                            for s in range(nsub):
                                mm(banks[obank[s]][:P, 0:129], PTx[:, s * P:(s + 1) * P], MV[:, 0, mc, h, 0:129], [r_PTx, r_MV, r_MVone],
                                   [bank_reg[obank[s]]], start=(mc == 0), stop=(mc == 1))
                        else:
                            for b in range(4):
                                mm(banks[bi][:, b * 8:(b + 1) * 8], MKT[:, 1 + b, h, mcs], QxT[:, h, b * 8:(b + 1) * 8], [r_MKT, r_QxT],
                                   [bank_reg[bi]])
                            for b in range(4):
                                act(PTxb[:, b, b * 8:(b + 1) * 8], banks[bi][:, b * 8:(b + 1) * 8], AF.Exp, [bank_reg[bi]], [r_PTxb],
                                    scale=xscale)
                            for b in range(4):
                                mm(banks[obank[0]][:32, 0:129], PTxb[:, b, :], MV[:, 1 + b, mc, h, 0:129], [r_PTxb, r_MV, r_MVone],
                                   [bank_reg[obank[0]]], start=(mc == 0 and b == 0), stop=(mc == 1 and b == 3))
                    for s in range(nsub):
                        ob = obank[s]
                        rcp(rinv[:P, s:s + 1], banks[ob][:P, 128:129], [bank_reg[ob]], [r_rinv])
                        tsc(oxb[:P, s, h * 128:(h + 1) * 128], banks[ob][:P, 0:128], rinv[:P, s:s + 1], ALU.mult,
                            [bank_reg[ob], r_rinv], [r_oxb[s]])
                for s in range(nsub):
                    cols = slice(s * P, (s + 1) * P)
                    bt = psT.next()
                    for h in range(4):
                        tp(banks_b[bt][:, h * P:(h + 1) * P], oxb[:P, s, h * 128:(h + 1) * 128], identB[:P, :P], [r_oxb[s], r_idB],
                           [bank_reg[bt]])
                    cp(oxFM[:, :, cols], banks_b[bt][:, 0:4 * P].rearrange("p (h q) -> p h q", h=4), [bank_reg[bt]], [r_oxFM], eng="act")
                    for half in range(2):
                        hc = slice(half * 512, (half + 1) * 512)
                        bi = psM.next()
                        for k in range(4):
                            mm(banks[bi][:P, :], oxFM[:, k, cols], wo[:, k, hc], [r_oxFM, r_wo], [bank_reg[bi]], start=(k == 0), stop=(k == 3))
                        tt(xr4[:P, s, hc], banks[bi][:P, :], xr4[:P, s, hc], ALU.add, [bank_reg[bi], r_xr4[s]], [r_xr4[s]])
                    rmsnorm_fm(xr4[:P, s, :], r_xr4[s], P, g_mlp, r_gmlp, hxFM, r_hxFM, s * P, nwk)
                for fg in range(8):
                    wu, r_wu = wu_ring.next()
                    dma("pool", wu, wup_v[:, :, fg * 512:(fg + 1) * 512], [], [r_wu])
                    wd, r_wd = wd_ring.next()
                    dma("pool", wd, wdn_d[fg * 512:(fg + 1) * 512, :].rearrange("(f p) c -> p f c", p=128), [], [r_wd])
                    hF, r_hF = hF_ring.next()
                    for f4 in range(4):
                        bi = psM.next()
                        for k in range(8):
                            mm(banks[bi][:, 0:W], wu[:, k, f4 * 128:(f4 + 1) * 128], hxFM[:, k, 0:W], [r_wu, r_hxFM], [bank_reg[bi]],
                               start=(k == 0), stop=(k == 7))
                        act(tmpr[:, 0:W], banks[bi][:, 0:W], AF.Relu, [bank_reg[bi]], [r_tmpr])
                        tt(hF[:, f4, 0:W], tmpr[:, 0:W], tmpr[:, 0:W], ALU.mult, [r_tmpr], [r_hF])
                    for s in range(nsub):
                        for half in range(2):
                            hc = slice(half * 512, (half + 1) * 512)
                            bi = obank[(s * 2 + half) % 4]
                            for f4 in range(4):
                                mm(banks[bi][:P, :], hF[:, f4, s * P:(s + 1) * P], wd[:, f4, hc], [r_hF, r_wd], [bank_reg[bi]],
                                   start=(f4 == 0), stop=(f4 == 3))
                            tt(xr4[:P, s, hc], banks[bi][:P, :], xr4[:P, s, hc], ALU.add, [bank_reg[bi], r_xr4[s]], [r_xr4[s]])
                for s in range(nsub):
                    dma("sp", y_d[tok0 + s * P: tok0 + (s + 1) * P, :], xr4[:P, s, :], [r_xr4[s]], [])

            for i in range(int(os.environ.get('K_NTILES', '4'))):
                phaseC_tile(i * 512, 128, 4, False)
            if os.environ.get('K_SAMPLE', '1') == '1':
                phaseC_tile(2048, 32, 1, True)

        S.run(st)
    return nc


def prep_inputs(inputs, consts):
    f32 = np.float32
    xp = np.asarray(inputs["x_prompt"], f32)
    xs = np.asarray(inputs["x_sample"], f32)
    ck = np.ascontiguousarray(np.asarray(inputs["cache_k"], f32)[0].reshape(NPOOL * 128, 512))
    cv = np.ascontiguousarray(np.asarray(inputs["cache_v"], f32)[0].reshape(NPOOL * 128, 512))
    pt = np.asarray(inputs["page_table"], np.int32)

    def bcast(a, n=128):
        a = np.asarray(a, f32).reshape(1, -1)
        return np.ascontiguousarray(np.broadcast_to(a, (n, a.shape[1])))

    shared = {
        "ck": ck, "cv": cv,
        "w_in": np.ascontiguousarray(inputs["w_in"][0], f32), "w_out": np.ascontiguousarray(inputs["w_out"][0], f32),
        "wq_x": np.ascontiguousarray(inputs["wq_x"][0], f32), "wk_x": np.ascontiguousarray(inputs["wk_x"][0], f32),
        "wv_x": np.ascontiguousarray(inputs["wv_x"][0], f32), "wo_x": np.ascontiguousarray(inputs["wo_x"][0], f32),
        "w_up": np.ascontiguousarray(inputs["w_up"][0], f32), "w_down": np.ascontiguousarray(inputs["w_down"][0], f32),
        "g_mix": bcast(inputs["ln_mix_g"][0]), "g_x": bcast(inputs["ln_x_g"][0]),
        "g_mem": bcast(inputs["ln_mem_g"][0]), "g_mlp": bcast(inputs["ln_mlp_g"][0]),
        "gq": bcast(inputs["q_norm_g"][0]), "gk": bcast(inputs["k_norm_g"][0]),
        "gqx": bcast(inputs["qx_norm_g"][0]), "gkx": bcast(inputs["kx_norm_g"][0]),
        "gssm": bcast(inputs["ssm_norm_g"][0]), "dtb": bcast(inputs["dt_bias"][0]),
        "alog": bcast(inputs["a_log"][0]), "dskip": bcast(inputs["d_skip"][0]),
        "cw": np.ascontiguousarray(np.asarray(inputs["conv_w"][0], f32).reshape(4, 8, 128).transpose(2, 1, 0)),
        "cb": np.ascontiguousarray(np.asarray(inputs["conv_b"][0], f32).reshape(8, 128).T),
    }
    shared.update(consts)
    in_maps = []
    for c in range(8):
        m = dict(shared)
        m["x"] = np.ascontiguousarray(np.concatenate([xp[c], xs[4 * c:4 * c + 4].reshape(32, 1024)], axis=0))
        m["mem"] = np.ascontiguousarray(inputs["mem_prompt"][c], f32)
        m["pt"] = np.ascontiguousarray(pt[4 * c:4 * c + 4])
        m["sconv"] = np.ascontiguousarray(np.asarray(inputs["state_conv"], f32)[0, 4 * c:4 * c + 4].reshape(12, 1024))
        m["sssm"] = np.ascontiguousarray(np.asarray(inputs["state_ssm"], f32)[0, 4 * c:4 * c + 4].reshape(4, 512, 128))
        m["cmk"] = np.ascontiguousarray(np.asarray(inputs["cache_mem_k"], f32)[0, 4 * c:4 * c + 4].reshape(4, 256, 512))
        m["cmv"] = np.ascontiguousarray(np.asarray(inputs["cache_mem_v"], f32)[0, 4 * c:4 * c + 4].reshape(4, 256, 512))
        in_maps.append(m)
    return in_maps


def assemble(results):
    f32 = np.float32
    y_p = np.stack([r["y"][:2048] for r in results]).astype(f32)
    y_s = np.concatenate([r["y"][2048:].reshape(4, 8, 1024) for r in results]).astype(f32)
    nk_p = np.stack([r["newk"][:2048].reshape(2048, 8, 64) for r in results])[None].astype(f32)
    nv_p = np.stack([r["newv"][:2048].reshape(2048, 8, 64) for r in results])[None].astype(f32)
    nc_p = np.stack([r["newconv"][0:3] for r in results])[None].astype(f32)
    ns_p = np.stack([r["newssm"][0].reshape(8, 64, 128) for r in results])[None].astype(f32)
    mk_p = np.stack([r["newmk"].reshape(256, 4, 128) for r in results])[None].astype(f32)
    mv_p = np.stack([r["newmv"].reshape(256, 4, 128) for r in results])[None].astype(f32)
    nk_s = np.concatenate([r["newk"][2048:].reshape(4, 8, 8, 64) for r in results])[None].astype(f32)
    nv_s = np.concatenate([r["newv"][2048:].reshape(4, 8, 8, 64) for r in results])[None].astype(f32)
    nc_s = np.concatenate([r["newconv"][3:15].reshape(4, 3, 1024) for r in results])[None].astype(f32)
    ns_s = np.concatenate([r["newssm"][1:5].reshape(4, 8, 64, 128) for r in results])[None].astype(f32)
    return (y_p, y_s, nk_p, nv_p, nc_p, ns_p, mk_p, mv_p, nk_s, nv_s, nc_s, ns_s)


def kernel(**inputs):
    consts = host_consts()
    nc = build_program(consts, stages=("A", "B", "S", "M", "C"))
    in_maps = prep_inputs(inputs, consts)
    res = run_bass_kernel_spmd(nc, in_maps, core_ids=list(range(8)))
    return assemble(res.results)
```

```python
import os
import numpy as np
import ml_dtypes
from contextlib import ExitStack
import concourse.bass as bass
import concourse.mybir as mybir
from concourse.bass_utils import run_bass_kernel_spmd

F32 = mybir.dt.float32
BF16 = mybir.dt.bfloat16
I32 = mybir.dt.int32
AF = mybir.ActivationFunctionType
ALU = mybir.AluOpType
AX = mybir.AxisListType

NEG = -30000.0
EPS = 1e-6
NTOK = 2080
NPOOL = 2560


class Reg:
    __slots__ = ("w", "rs", "name", "excl")

    def __init__(self, name="", excl=False):
        self.w = None
        self.rs = []
        self.name = name
        self.excl = excl


class Op:
    __slots__ = ("eng", "emit", "deps", "signal", "isdma", "sem", "target", "prev_target", "bar")

    def __init__(self, eng, emit, isdma):
        self.eng = eng
        self.emit = emit
        self.deps = []
        self.signal = False
        self.isdma = isdma
        self.sem = None
        self.target = 0
        self.prev_target = 0
        self.bar = 0


ENGS = ["pe", "act", "dve", "pool", "sp"]


class Sched:
    def __init__(self, nc, n_dma_sems=12):
        self.nc = nc
        self.ops = {e: [] for e in ENGS}
        self.n_dma_sems = n_dma_sems
        self.barriers = [[]]
        self.since_bar_dma = []

    def op(self, eng, emit, r=(), w=(), dma=False):
        o = Op(eng, emit, dma)
        self.count = getattr(self, "count", 0) + 1
        if self.count > int(os.environ.get("K_MAXOPS", "100000000")):
            return o
        if os.environ.get("K_TRACE"):
            import sys as _sys
            f = _sys._getframe(2)
            print("OP", self.count, eng, "dma" if dma else "", f.f_lineno, f.f_code.co_name, "<-", f.f_back.f_lineno)
        o.bar = len(self.barriers) - 1
        deps = {}
        if any(reg.excl for reg in r):
            w = list(w) + [reg for reg in r if reg.excl and reg not in w]
            r = [reg for reg in r if not reg.excl]
        for reg in r:
            if reg.w is not None:
                deps[id(reg.w)] = reg.w
        for reg in w:
            if reg.w is not None:
                deps[id(reg.w)] = reg.w
            for x in reg.rs:
                deps[id(x)] = x
        for d in deps.values():
            if d.eng == "pe" and eng == "pe" and not d.isdma and not dma:
                continue
            o.deps.append(d)
            d.signal = True
        for reg in r:
            reg.rs.append(o)
        for reg in w:
            reg.w = o
            reg.rs = []
        self.ops[eng].append(o)
        if dma:
            self.since_bar_dma.append(o)
        return o

    def barrier(self):
        deps = list(self.since_bar_dma)
        for e in ENGS:
            for o in reversed(self.ops[e]):
                if not o.isdma:
                    o.signal = True
                    deps.append(o)
                    break
        self.barriers.append(deps)
        self.since_bar_dma = []

    def finalize(self, stack):
        nc = self.nc
        self.esem = {}
        self.dsems = {}
        for e in ENGS:
            self.esem[e] = stack.enter_context(nc.semaphore("s_" + e))
            self.dsems[e] = [stack.enter_context(nc.semaphore("d_%s_%d" % (e, i)))
                             for i in range(self.n_dma_sems)]
        self.final_dma = {}
        for e in ENGS:
            cnt = 0
            dcnt = [0] * self.n_dma_sems
            k = 0
            for o in self.ops[e]:
                if o.isdma:
                    i = k % self.n_dma_sems
                    k += 1
                    o.sem = self.dsems[e][i]
                    o.prev_target = dcnt[i]
                    dcnt[i] += 16
                    o.target = dcnt[i]
                elif o.signal:
                    cnt += 1
                    o.sem = self.esem[e]
                    o.target = cnt
            self.final_dma[e] = [o for o in self.ops[e] if o.isdma]

    def emit_engine(self, ename, e):
        waited = {}

        def wait(sem, val):
            key = id(sem)
            if waited.get(key, 0) >= val:
                return
            e.wait_ge(sem, val)
            waited[key] = val

        cur_bar = 0
        for o in self.ops[ename]:
            while cur_bar < o.bar:
                cur_bar += 1
                for d in self.barriers[cur_bar]:
                    wait(d.sem, d.target)
            for d in o.deps:
                wait(d.sem, d.target)
            if o.isdma and o.prev_target > 0:
                wait(o.sem, o.prev_target)
            ins = o.emit(e)
            if o.isdma:
                ins.then_inc(o.sem, 16)
            elif o.signal:
                ins.then_inc(o.sem, 1)
        for o in self.final_dma[ename]:
            wait(o.sem, o.target)

    def run(self, stack):
        self.finalize(stack)
        block = stack.enter_context(self.nc.Block())
        S = self

        @block.tensor
        def _(e):
            S.emit_engine("pe", e)

        @block.scalar
        def _(e):
            S.emit_engine("act", e)

        @block.vector
        def _(e):
            S.emit_engine("dve", e)

        @block.gpsimd
        def _(e):
            S.emit_engine("pool", e)

        @block.sync
        def _(e):
            S.emit_engine("sp", e)


class Arena:
    def __init__(self, t, nbytes):
        self.t = t
        self.cap = nbytes
        self.top = 0

    def alloc(self, shape, dt):
        n = 1
        for d in shape[1:]:
            n *= d
        esz = 2 if dt == BF16 else 4
        size = (n * esz + 31) // 32 * 32
        off = self.top
        self.top += size
        assert self.top <= self.cap, ("SBUF arena overflow", self.top, self.cap)
        v = self.t[0:shape[0], off // 2: off // 2 + (n * esz) // 2]
        if dt != BF16:
            v = v.bitcast(dt)
        if len(shape) == 3:
            v = v.rearrange("p (a b) -> p a b", a=shape[1])
        elif len(shape) == 4:
            v = v.rearrange("p (a b c) -> p a b c", a=shape[1], b=shape[2])
        elif len(shape) == 5:
            v = v.rearrange("p (a b c d) -> p a b c d", a=shape[1], b=shape[2], c=shape[3])
        return v


class Ring:
    def __init__(self, items):
        self.items = items
        self.i = 0

    def next(self):
        x = self.items[self.i % len(self.items)]
        self.i += 1
        return x


def host_consts():
    c = {}
    c["c_ident"] = np.eye(128, dtype=np.float32)
    half = 32
    inv_freq = (10000.0 ** (-np.arange(half, dtype=np.float32) / half)).astype(np.float32)
    cs = np.zeros((128, 17, 64), np.float32)
    p = np.arange(128)
    for ti in range(16):
        pos = (ti * 128 + p).astype(np.float32)
        ang = pos[:, None] * inv_freq[None, :]
        cs[:, ti, 0:32] = np.cos(ang)
        cs[:, ti, 32:64] = np.sin(ang)
    pos = (8192 + (p % 8)).astype(np.float32)
    ang = pos[:, None] * inv_freq[None, :]
    cs[:, 16, 0:32] = np.cos(ang)
    cs[:, 16, 32:64] = np.sin(ang)
    c["c_cs"] = cs
    nm = np.zeros((128, 4, 512), np.float32)
    f = np.arange(512)
    for r in range(4):
        nm[:, r, :] = np.where((r * 128 + p)[:, None] > f[None, :], NEG, 0.0)
    c["c_negmask"] = nm
    ind = np.zeros((8, 2048), np.float32)
    for j in range(8):
        ind[j, j * 256:(j + 1) * 256] = 1.0
    c["c_ind"] = ind
    cb0 = np.zeros((8, 8, 128), np.float32)
    for j in range(8):
        for own in range(8):
            cb0[j, own, :] = 0.0 if j <= own else NEG
    c["c_bias0"] = cb0
    t = np.arange(128)
    c["c_triU"] = (t[:, None] <= t[None, :]).astype(np.float32)
    c["c_mneg"] = np.where(t[None, :] < t[:, None], NEG, 0.0).astype(np.float32)
    c["c_ones"] = np.ones((128, 128), np.float32)
    t32 = np.arange(32)
    same = (t32[:, None] // 8) == (t32[None, :] // 8)
    c["c_triU_s"] = ((t32[:, None] <= t32[None, :]) & same).astype(np.float32)
    c["c_mneg_s"] = np.where((t32[None, :] < t32[:, None]) | (~same), NEG, 0.0).astype(np.float32)
    c["c_onesbd_s"] = same.astype(np.float32)
    seqsel = np.zeros((32, 4, 128), np.float32)
    for b in range(4):
        seqsel[b * 8:(b + 1) * 8, b, :] = 1.0
    c["c_seqsel"] = seqsel
    colmask = np.zeros((128, 4, 32), np.float32)
    for b in range(4):
        colmask[:, b, b * 8:(b + 1) * 8] = 1.0
    c["c_colmask"] = colmask
    rowmask = np.zeros((32, 4), np.float32)
    for b in range(4):
        rowmask[b * 8:(b + 1) * 8, b] = 1.0
    c["c_rowmask"] = rowmask
    Z = np.zeros((128, 63), np.float32)
    Z[:, 31] = 1.0
    c["c_Z"] = Z
    selE = np.zeros((32, 32, 128), np.float32)
    for j in range(32):
        selE[j, j, :] = 1.0
    c["c_selE"] = selE
    bd = np.zeros((64, 512), np.float32)
    for h in range(8):
        bd[h * 8:(h + 1) * 8, h * 64:(h + 1) * 64] = 1.0
    c["c_bdmask"] = bd
    negcs = np.zeros((32, 4, 64), np.float32)
    for key in range(32):
        for b in range(4):
            for tq in range(8):
                ok = (key // 8 == b) and (key % 8 <= tq)
                negcs[key, b, tq::8] = 0.0 if ok else NEG
    negcs2 = np.zeros((32, 4, 64), np.float32)
    for key in range(32):
        for b in range(4):
            for h in range(8):
                for tq in range(8):
                    ok = (key // 8 == b) and (key % 8 <= tq)
                    negcs2[key, b, h * 8 + tq] = 0.0 if ok else NEG
    c["c_negcs"] = negcs2
    c["c_iota"] = np.arange(128, dtype=np.float32).reshape(128, 1)
    return c


CONST_SHAPES = None


def build_program(consts, stages=("A", "S", "M", "C")):
    nc = bass.Bass("TRN2", target_bir_lowering=False)

    def din(name, shape, dt=F32):
        return nc.dram_tensor(name, list(shape), dt, kind="ExternalInput").ap()

    def dout(name, shape):
        return nc.dram_tensor(name, list(shape), F32, kind="ExternalOutput").ap()

    x_d = din("x", [NTOK, 1024])
    mem_d = din("mem", [256, 1024])
    ck_d = din("ck", [NPOOL * 128, 512])
    cv_d = din("cv", [NPOOL * 128, 512])
    pt_d = din("pt", [4, 64], I32)
    sconv_d = din("sconv", [12, 1024])
    sssm_d = din("sssm", [4, 512, 128])
    cmk_d = din("cmk", [4, 256, 512])
    cmv_d = din("cmv", [4, 256, 512])
    w_in_d = din("w_in", [1024, 3080])
    w_out_d = din("w_out", [1024, 1024])
    wq_d = din("wq_x", [1024, 512])
    wk_d = din("wk_x", [1024, 512])
    wv_d = din("wv_x", [1024, 512])
    wo_d = din("wo_x", [512, 1024])
    wup_d = din("w_up", [1024, 4096])
    wdn_d = din("w_down", [4096, 1024])
    g_mix_d = din("g_mix", [128, 1024])
    g_x_d = din("g_x", [128, 1024])
    g_mem_d = din("g_mem", [128, 1024])
    g_mlp_d = din("g_mlp", [128, 1024])
    gq_d = din("gq", [128, 64])
    gk_d = din("gk", [128, 64])
    gqx_d = din("gqx", [128, 128])
    gkx_d = din("gkx", [128, 128])
    gssm_d = din("gssm", [128, 512])
    dtb_d = din("dtb", [128, 8])
    alog_d = din("alog", [128, 8])
    dskip_d = din("dskip", [128, 8])
    cw_d = din("cw", [128, 8, 4])
    cb_d = din("cb", [128, 8])
    cd = {k: din(k, v.shape) for k, v in consts.items()}

    y_d = dout("y", [NTOK, 1024])
    newk_d = dout("newk", [NTOK, 512])
    newv_d = dout("newv", [NTOK, 512])
    newconv_d = dout("newconv", [15, 1024])
    newssm_d = dout("newssm", [5, 512, 128])
    newmk_d = dout("newmk", [256, 512])
    newmv_d = dout("newmv", [256, 512])
    mix_d = nc.dram_tensor("mixscr", [NTOK, 1024], BF16, kind="Internal").ap()
    hm_d = nc.dram_tensor("hmscr", [128, 8, NTOK], BF16, kind="Internal").ap()

    with ExitStack() as st:
        S = Sched(nc)
        ARENA_BYTES = 192 * 1024
        arena_t = st.enter_context(nc.sbuf_tensor("arena", [128, ARENA_BYTES // 2], BF16))
        A = Arena(arena_t, ARENA_BYTES)
        banks = [st.enter_context(nc.psum_tensor("bank%d" % i, [128, 512], F32)) for i in range(8)]
        banks_b = [b.bitcast(BF16) for b in banks]
        bank_reg = [Reg("bank%d" % i, excl=True) for i in range(8)]

        def mm(out, lhsT, rhs, r, w, start=True, stop=True):
            S.op("pe", lambda e: e.matmul(out, lhsT=lhsT, rhs=rhs, start=start, stop=stop), r, w)

        def tp(out, in_, idn, r, w):
            S.op("pe", lambda e: e.transpose(out=out, in_=in_, identity=idn), r, w)

        def act(out, in_, func, r, w, **kw):
            S.op("act", lambda e: e.activation(out=out, in_=in_, func=func, **kw), r, w)

        def tt(out, a, b, op, r, w, eng="dve"):
            S.op(eng, lambda e: e.tensor_tensor(out=out, in0=a, in1=b, op=op), r, w)

        def tsc(out, a, s1, op0, r, w, s2=None, op1=None):
            if op1 is None:
                S.op("dve", lambda e: e.tensor_scalar(out=out, in0=a, scalar1=s1, scalar2=None, op0=op0), r, w)
            else:
                S.op("dve", lambda e: e.tensor_scalar(out=out, in0=a, scalar1=s1, scalar2=s2, op0=op0, op1=op1), r, w)

        def stt(out, a, s, b, op0, op1, r, w, accum=None):
            if accum is None:
                S.op("dve", lambda e: e.scalar_tensor_tensor(out=out, in0=a, scalar=s, in1=b, op0=op0, op1=op1), r, w)
            else:
                S.op("dve", lambda e: e.scalar_tensor_tensor(out=out, in0=a, scalar=s, in1=b, op0=op0, op1=op1,
                                                             accum_out=accum), r, w)

        def cp(out, in_, r, w, eng="dve"):
            if eng == "act":
                S.op("act", lambda e: e.activation(out=out, in_=in_, func=AF.Copy), r, w)
            else:
                S.op(eng, lambda e: e.tensor_copy(out=out, in_=in_), r, w)

        def red(out, in_, r, w, op=ALU.add):
            S.op("dve", lambda e: e.tensor_reduce(out=out, in_=in_, axis=AX.X, op=op), r, w)

        def rcp(out, in_, r, w):
            S.op("dve", lambda e: e.reciprocal(out=out, in_=in_), r, w)

        def mset(ap, val, w, eng="dve"):
            S.op(eng, lambda e: e.memset(ap, val), (), w)

        def dma(q, out, in_, r, w):
            S.op(q, lambda e: e.dma_start(out=out, in_=in_), r, w, dma=True)

        def gather(out, table, idx, r, w):
            S.op("pool", lambda e: e.indirect_dma_start(out=out, out_offset=None, in_=table,
                                                        in_offset=bass.IndirectOffsetOnAxis(ap=idx, axis=0)),
                 r, w, dma=True)

        def bc(ap, shape):
            return ap.to_broadcast(list(shape))

        identF = A.alloc([128, 128], F32); r_idF = Reg()
        identB = A.alloc([128, 128], BF16); r_idB = Reg()
        cs = A.alloc([128, 17, 64], F32); r_cs = Reg()
        dma("sp", identF, cd["c_ident"], [], [r_idF])
        dma("pool", identB, cd["c_ident"], [], [r_idB])
        dma("sp", cs, cd["c_cs"], [], [r_cs])
        eps_t = A.alloc([128, 1], F32); r_eps = Reg()
        mset(eps_t, EPS, [r_eps])
        one_t = A.alloc([128, 1], F32); r_one = Reg()
        mset(one_t, 1.0, [r_one])

        QsT = A.alloc([128, 4, 32], BF16)
        KnT = A.alloc([128, 4, 32], BF16)
        vb = A.alloc([32, 512], BF16)

        def load_const(name, shape, dt, q=None):
            t_ = A.alloc(shape, dt)
            r_ = Reg(name)
            if q is None:
                q = "pool" if dt == BF16 else "sp"
            dma(q, t_, name if not isinstance(name, str) else cd[name], [], [r_])
            return t_, r_

        def load_in(d_ap, shape, dt=F32, q="sp"):
            t_ = A.alloc(shape, dt)
            r_ = Reg()
            dma(q if dt == F32 or dt == I32 else "pool", t_, d_ap, [], [r_])
            return t_, r_

        def small(shape, dt=F32):
            return A.alloc(shape, dt), Reg()

        def rmsnorm_fm(x_ap, r_x, P, gbc, r_g, outFM, r_out, col0, wk):
            junk, r_junk, hn, r_hn, ssq, r_ssq, psb = wk
            stt(junk[:P], x_ap, 1.0, x_ap, ALU.mult, ALU.mult, [r_x], [r_junk, r_ssq], accum=ssq[:P, 0:1])
            act(ssq[:P, 1:2], ssq[:P, 0:1], AF.Sqrt, [r_ssq, r_eps], [r_ssq], scale=1.0 / 1024, bias=eps_t[:P, :])
            rcp(ssq[:P, 2:3], ssq[:P, 1:2], [r_ssq], [r_ssq])
            stt(hn[:P], x_ap, ssq[:P, 2:3], gbc[:P], ALU.mult, ALU.mult, [r_x, r_ssq, r_g], [r_hn])
            bi = psb.next()
            for k in range(8):
                tp(banks_b[bi][:, k * P:(k + 1) * P], hn[:P, k * 128:(k + 1) * 128], identB[:P, :P],
                   [r_hn, r_idB], [bank_reg[bi]])
            cp(outFM[:, :, col0:col0 + P], banks_b[bi][:, 0:8 * P].rearrange("p (k q) -> p k q", k=8),
               [bank_reg[bi]], [r_out], eng="act")

        def headnorm(ps_ap, r_ps, P, nh, hd, qs, r_qs, sq, r_sq, st8, r_st8):
            cp(qs[:P], ps_ap, [r_ps], [r_qs], eng="act")
            act(sq[:P], ps_ap, AF.Square, [r_ps], [r_sq])
            red(st8[:P, 0, 0:nh], sq[:P].rearrange("p (h d) -> p h d", h=nh), [r_sq], [r_st8])
            act(st8[:P, 1, 0:nh], st8[:P, 0, 0:nh], AF.Sqrt, [r_st8, r_eps], [r_st8], scale=1.0 / hd, bias=eps_t[:P, :])
            rcp(st8[:P, 2, 0:nh], st8[:P, 1, 0:nh], [r_st8], [r_st8])
            q3 = qs[:P].rearrange("p (h d) -> p h d", h=nh)
            tt(q3, q3, bc(st8[:P, 2, 0:nh].unsqueeze(2), [P, nh, hd]), ALU.mult, [r_qs, r_st8], [r_qs])

        markA = A.top
        Win = A.alloc([128, 8, 3080], BF16); r_WinA = Reg(); r_WinB = Reg()
        w_in_v = w_in_d.rearrange("(k p) c -> p k c", p=128)
        dma("pool", Win[:, :, 0:1536], w_in_v[:, :, 0:1536], [], [r_WinA])
        dma("pool", Win[:, :, 1536:3080], w_in_v[:, :, 1536:3080], [], [r_WinB])
        g_mix, r_gmix = load_in(g_mix_d, [128, 1024])
        gq, r_gq = load_in(gq_d, [128, 64])
        gk, r_gk = load_in(gk_d, [128, 64])
        gssm, r_gssm = load_in(gssm_d, [128, 512])
        dtb, r_dtb = load_in(dtb_d, [128, 8])
        a_bc, r_abc = load_in(alog_d, [128, 8])
        dskip, r_dskip = load_in(dskip_d, [128, 8])
        cw, r_cw = load_in(cw_d, [128, 8, 4])
        cb, r_cb = load_in(cb_d, [128, 8])
        act(a_bc, a_bc, AF.Exp, [r_abc], [r_abc])
        tsc(a_bc, a_bc, -1.0, ALU.mult, [r_abc], [r_abc])
        triU, r_triU = load_const("c_triU", [128, 128], F32)
        mneg, r_mneg = load_const("c_mneg", [128, 128], F32)
        ones, r_ones = load_const("c_ones", [128, 128], F32)
        triU_s, r_triUs = load_const("c_triU_s", [32, 32], F32)
        mneg_s, r_mnegs = load_const("c_mneg_s", [32, 32], F32)
        onesbd_s, r_onesbds = load_const("c_onesbd_s", [32, 32], F32)
        seqsel, r_seqsel = load_const("c_seqsel", [32, 4, 128], F32)
        colmask, r_colmask = load_const("c_colmask", [128, 4, 32], BF16)
        rowmask, r_rowmask = load_const("c_rowmask", [32, 4], F32)

        KT = A.alloc([128, 8, 2048], BF16); r_KT = [Reg() for _ in range(16)]; r_KTind = Reg()
        for h in range(8):
            dma("pool", KT[64:72, h, :], cd["c_ind"], [], [r_KTind])
        VA = A.alloc([128, 16, 8, 66], BF16); r_VA = [Reg() for _ in range(16)]; r_VAone = Reg()
        mset(VA[:, :, :, 64:65], 1.0, [r_VAone])
        hnFM = A.alloc([128, 8, 512], BF16); r_hnFM = Reg()
        xc = A.alloc([128, 8, 512], BF16); r_xc = Reg()
        carry = A.alloc([128, 8, 3], F32); r_carry = Reg()
        mset(carry, 0.0, [r_carry])
        hT = [A.alloc([128, 512], F32)]
        hTb = [A.alloc([128, 512], BF16)]
        r_hT = [Reg() for _ in range(5)]
        r_hTb = [Reg() for _ in range(5)]
        mset(hT[0], 0.0, [r_hT[0]])
        mset(hTb[0], 0.0, [r_hTb[0]])

        xt_ring = Ring([(A.alloc([128, 1024], F32), Reg()) for _ in range(1)])
        junk = A.alloc([128, 1024], BF16); r_junk = Reg()
        hn = A.alloc([128, 1024], BF16); r_hn = Reg()
        ssq = A.alloc([128, 4], F32); r_ssq = Reg()
        psT = Ring([0, 1, 2, 3])
        psM = psT
        nwk = (junk, r_junk, hn, r_hn, ssq, r_ssq, psT)
        qs_ring = Ring([(A.alloc([128, 512], F32), Reg()) for _ in range(1)])
        sq = A.alloc([128, 512], F32); r_sq = Reg()
        st8 = A.alloc([128, 3, 8], F32); r_st8 = Reg()
        tab = A.alloc([128, 4, 32], F32); r_tab = Reg()
        rt = [A.alloc([128, 8, 32], F32) for _ in range(2)]; r_rt = [Reg() for _ in range(2)]
        rt = rt + rt; r_rt = r_rt + r_rt
        ko_ring = Ring([(A.alloc([128, 512], F32), Reg()) for _ in range(1)])
        qb = A.alloc([128, 512], BF16); r_qb = Reg()
        vs_ring = ko_ring
        zs = A.alloc([128, 512], F32); r_zs = Reg()
        dtt = A.alloc([128, 4, 8], F32); r_dtt = Reg()
        xr_ring = Ring([(A.alloc([128, 3 + 512], F32), Reg()) for _ in range(1)])
        acc = A.alloc([128, 512], F32); r_acc = Reg()
        xl, r_xl = xt_ring.items[0]
        xdt = A.alloc([128, 8, 64], BF16); r_xdt = Reg()
        xsT = A.alloc([128, 512], F32); r_xsT = Reg()
        Btm = A.alloc([128, 2, 128], BF16); r_Btm = Reg()
        Bm = A.alloc([128, 2, 128], BF16); r_Bm = Reg()
        cumt = A.alloc([128, 6, 8], F32); r_cumt = Reg()
        etot = A.alloc([128, 4, 8], F32); r_etot = Reg()
        dab_ring = Ring([(A.alloc([128, 128], F32), Reg()) for _ in range(2)])
        dec_ring = Ring([(A.alloc([128, 128], BF16), Reg()) for _ in range(2)])
        LT_ring = Ring([(A.alloc([128, 128], BF16), Reg()) for _ in range(2)])
        CTm = A.alloc([128, 4, 2, 32], BF16); r_CTm = Reg()
        y2s = A.alloc([128, 512], F32); r_y2s = Reg()
        yy = A.alloc([128, 512], F32); r_yy = Reg()
        ysq = sq; r_ysq = r_sq
        ss2 = A.alloc([128, 3, 2], F32); r_ss2 = Reg()
        mixs = A.alloc([128, 512], BF16); r_mixs = Reg()
        xdtt = A.alloc([128, 512], BF16); r_xdtt = Reg()
        hout = yy.rearrange("p (c n) -> p c n", c=4); r_hout = r_yy
        mark_attn = A.top
        negmask, r_negmask = load_const("c_negmask", [128, 4, 512], BF16)
        cbias0, r_cbias0 = A.alloc([128, 8, 128], BF16), Reg()
        dma("pool", cbias0[64:72], cd["c_bias0"], [], [r_cbias0])
        QT = A.alloc([128, 8, 512], BF16); r_QT = Reg(); r_QTb = Reg()
        PT_ring = Ring([(A.alloc([128, 512], BF16), Reg()) for _ in range(3)])
        attb = A.alloc([128, 4, 512], BF16); r_attb = [Reg() for _ in range(4)]
        rinv = A.alloc([128, 4], F32); r_rinv = Reg()
        m8 = A.alloc([128, 8, 8], F32); r_m8 = Reg()
        bsel = A.alloc([128, 8, 8], F32); r_bsel = Reg()
        biasm = A.alloc([128, 8, 8], BF16); r_biasm = Reg()
        ksumT = A.alloc([128, 8, 8], BF16); r_ksum = Reg()
        ksf = A.alloc([128, 8], F32); r_ksf = Reg()
        gsb = A.alloc([128, 8, 8], F32); r_gsb = Reg()
        mset(gsb, -1e30, [r_gsb])
        r_sconvT = Reg(); r_QsT = Reg(); r_KnT = Reg(); r_vb = Reg()
        print("phase A arena top", A.top)

        def qk_post(ps_ap, r_ps, P, ti, g_bc, r_g, is_k, row0):
            qs, r_qs = qs_ring.next()
            headnorm(ps_ap, r_ps, P, 8, 64, qs, r_qs, sq, r_sq, st8, r_st8)
            cos = cs[:P, ti, 0:32]
            sin = cs[:P, ti, 32:64]
            tt(tab[:P, 0, :], cos, g_bc[:P, 0:32], ALU.mult, [r_cs, r_g], [r_tab])
            tt(tab[:P, 1, :], sin, g_bc[:P, 32:64], ALU.mult, [r_cs, r_g], [r_tab])
            tt(tab[:P, 2, :], cos, g_bc[:P, 32:64], ALU.mult, [r_cs, r_g], [r_tab])
            tt(tab[:P, 3, :], sin, g_bc[:P, 0:32], ALU.mult, [r_cs, r_g], [r_tab])
            q3 = qs[:P].rearrange("p (h d) -> p h d", h=8)
            x1 = q3[:, :, 0:32]
            x2 = q3[:, :, 32:64]
            tb = [bc(tab[:P, i, :].unsqueeze(1), [P, 8, 32]) for i in range(4)]
            ko, r_ko = ko_ring.next()
            o3 = ko[:P].rearrange("p (h d) -> p h d", h=8)
            tt(rt[0][:P], x1, tb[0], ALU.mult, [r_qs, r_tab], [r_rt[0]])
            tt(rt[1][:P], x2, tb[1], ALU.mult, [r_qs, r_tab], [r_rt[1]])
            tt(o3[:, :, 0:32], rt[0][:P], rt[1][:P], ALU.subtract, [r_rt[0], r_rt[1]], [r_ko])
            tt(rt[2][:P], x2, tb[2], ALU.mult, [r_qs, r_tab], [r_rt[2]])
            tt(rt[3][:P], x1, tb[3], ALU.mult, [r_qs, r_tab], [r_rt[3]])
            tt(o3[:, :, 32:64], rt[2][:P], rt[3][:P], ALU.add, [r_rt[2], r_rt[3]], [r_ko])
            if is_k:
                dma("sp", newk_d[row0:row0 + P, :], ko[:P], [r_ko], [])
            cp(qb[:P], ko[:P], [r_ko], [r_qb], eng="act")
            return ko, r_ko

        def ssd_chunk(L, nseq, c0, seqs, row0, tri_c, r_tri, mneg_c, r_mn, ones_c, r_on):
            dt_ = dtt[:L, 1, :]
            da_ = dtt[:L, 2, :]
            bi = psT.next()
            for c in range(4):
                tp(banks_b[bi][:L, c * 128:(c + 1) * 128], xc[:, c, c0:c0 + L], identB, [r_xc, r_idB], [bank_reg[bi]])
            psv = banks_b[bi][:L, 0:512].rearrange("p (h d) -> p h d", h=8)
            tt(xdt[:L], psv, bc(dt_.unsqueeze(2), [L, 8, 64]), ALU.mult, [bank_reg[bi], r_dtt], [r_xdt])
            cp(xsT[:L], banks_b[bi][:L, 0:512], [bank_reg[bi]], [r_xsT], eng="act")
            bi = psT.next()
            for g in range(2):
                tp(banks_b[bi][:L, g * 128:(g + 1) * 128], xc[:, 4 + g, c0:c0 + L], identB, [r_xc, r_idB], [bank_reg[bi]])
            cp(Btm[:L], banks_b[bi][:L, 0:256].rearrange("p (g n) -> p g n", g=2), [bank_reg[bi]], [r_Btm], eng="act")
            b5 = 5
            mm(banks[b5][:L, 0:8], tri_c[:L, :L], da_, [r_tri, r_dtt], [bank_reg[b5]])
            mm(banks[b5][:L, 8:16], ones_c[:L, :L], da_, [r_on, r_dtt], [bank_reg[b5]])
            for bidx in range(nseq):
                lhs = ones[:L, :] if nseq == 1 else seqsel[:L, bidx, :]
                mm(banks[b5][:, 16 + bidx * 8:24 + bidx * 8], lhs, da_, [r_ones, r_seqsel, r_dtt], [bank_reg[b5]])
            cp(cumt[:L, 0, :], banks[b5][:L, 0:8], [bank_reg[b5]], [r_cumt])
            tsc(cumt[:L, 1, :], banks[b5][:L, 0:8], -1.0, ALU.mult, [bank_reg[b5]], [r_cumt])
            tt(cumt[:L, 2, :], banks[b5][:L, 8:16], cumt[:L, 0, :], ALU.subtract, [bank_reg[b5], r_cumt], [r_cumt])
            act(cumt[:L, 3, :], cumt[:L, 2, :], AF.Exp, [r_cumt], [r_cumt])
            act(cumt[:L, 4, :], cumt[:L, 0, :], AF.Exp, [r_cumt], [r_cumt])
            act(etot[:, 0:nseq, :], banks[b5][:, 16:16 + 8 * nseq].rearrange("p (b h) -> p b h", b=nseq), AF.Exp,
                [bank_reg[b5]], [r_etot])
            b4 = 4
            for g in range(2):
                mm(banks[b4][:L, g * 128:g * 128 + L], xc[:, 4 + g, c0:c0 + L], xc[:, 6 + g, c0:c0 + L],
                   [r_xc], [bank_reg[b4]])
            b6, b7 = 6, 7
            for h in range(8):
                g = h // 4
                bi = psM.next()
                dab, r_dab = dab_ring.next()
                cp(dab[:L, 0:L], bc(dtt[:L, 2, h:h + 1], [L, L]), [r_dtt], [r_dab])
                mm(banks[bi][:L, 0:L], dab[:L, 0:L], tri_c[:L, :L], [r_dab, r_tri], [bank_reg[bi]], start=True, stop=False)
                mm(banks[bi][:L, 0:L], identF[:L, :L], mneg_c[:L, :L], [r_idF, r_mn], [bank_reg[bi]], start=False, stop=True)
                dec, r_dec = dec_ring.next()
                act(dec[:L, :L], banks[bi][:L, 0:L], AF.Exp, [bank_reg[bi], r_cumt], [r_dec], bias=cumt[:L, 1, h:h + 1])
                LT, r_LT = LT_ring.next()
                tt(LT[:L, :L], banks[b4][:L, g * 128:g * 128 + L], dec[:L, :L], ALU.mult, [bank_reg[b4], r_dec], [r_LT])
                mm(banks[b6][:L, h * 64:(h + 1) * 64], LT[:L, :L], xdt[:L, h, :], [r_LT, r_xdt], [bank_reg[b6]])
            if nseq > 1:
                for bidx in range(nseq):
                    for g in range(2):
                        tt(CTm[:, bidx, g, :], xc[:, 6 + g, c0:c0 + L], colmask[:, bidx, :], ALU.mult,
                           [r_xc, r_colmask], [r_CTm])
            for g in range(2):
                for bidx, sq_ in enumerate(seqs):
                    lhs = xc[:, 6 + g, c0:c0 + L] if nseq == 1 else CTm[:, bidx, g, :]
                    mm(banks[b7][:L, g * 256:(g + 1) * 256], lhs, hTb[sq_][:, g * 256:(g + 1) * 256],
                       [r_xc, r_CTm, r_hTb[sq_]], [bank_reg[b7]], start=(bidx == 0), stop=(bidx == nseq - 1))
            tt(y2s[:L].rearrange("p (h d) -> p h d", h=8), banks[b7][:L, :].rearrange("p (h d) -> p h d", h=8),
               bc(cumt[:L, 4, :].unsqueeze(2), [L, 8, 64]), ALU.mult, [bank_reg[b7], r_cumt], [r_y2s])
            tt(yy[:L], banks[b6][:L, :], y2s[:L], ALU.add, [bank_reg[b6], r_y2s], [r_yy])
            tt(y2s[:L].rearrange("p (h d) -> p h d", h=8), xsT[:L].rearrange("p (h d) -> p h d", h=8),
               bc(dskip[:L, :].unsqueeze(2), [L, 8, 64]), ALU.mult, [r_xsT, r_dskip], [r_y2s])
            tt(yy[:L], yy[:L], y2s[:L], ALU.add, [r_yy, r_y2s], [r_yy])
            tt(yy[:L], yy[:L], zs[:L], ALU.mult, [r_yy, r_zs], [r_yy])
            tt(ysq[:L], yy[:L], yy[:L], ALU.mult, [r_yy], [r_ysq])
            red(ss2[:L, 0, :], ysq[:L].rearrange("p (g d) -> p g d", g=2), [r_ysq], [r_ss2])
            act(ss2[:L, 1, :], ss2[:L, 0, :], AF.Sqrt, [r_ss2, r_eps], [r_ss2], scale=1.0 / 256, bias=eps_t[:L, :])
            rcp(ss2[:L, 2, :], ss2[:L, 1, :], [r_ss2], [r_ss2])
            tt(yy[:L].rearrange("p (g d) -> p g d", g=2), yy[:L].rearrange("p (g d) -> p g d", g=2),
               bc(ss2[:L, 2, :].unsqueeze(2), [L, 2, 256]), ALU.mult, [r_yy, r_ss2], [r_yy])
            tt(mixs[:L], yy[:L], gssm[:L], ALU.mult, [r_yy, r_gssm], [r_mixs])
            dma("sp", mix_d[row0:row0 + L, 512:1024], mixs[:L], [r_mixs], [])
            tt(xdtt[:L].rearrange("p (h d) -> p h d", h=8), xdt[:L], bc(cumt[:L, 3, :].unsqueeze(2), [L, 8, 64]),
               ALU.mult, [r_xdt, r_cumt], [r_xdtt])
            for bidx, sq_ in enumerate(seqs):
                if nseq > 1:
                    tsc(Bm[:L].rearrange("p g n -> p (g n)"), Btm[:L].rearrange("p g n -> p (g n)"),
                        rowmask[:L, bidx:bidx + 1], ALU.mult, [r_Btm, r_rowmask], [r_Bm])
                    Bsrc, r_Bs = Bm, r_Bm
                else:
                    Bsrc, r_Bs = Btm, r_Btm
                bi = psM.next()
                for g in range(2):
                    mm(banks[bi][:, g * 256:(g + 1) * 256], Bsrc[:L, g, :], xdtt[:L, g * 256:(g + 1) * 256],
                       [r_Bs, r_xdtt], [bank_reg[bi]])
                h3 = hT[sq_].rearrange("p (h d) -> p h d", h=8)
                tt(h3, h3, bc(etot[:, bidx, :].unsqueeze(2), [128, 8, 64]), ALU.mult, [r_hT[sq_], r_etot], [r_hT[sq_]])
                tt(hT[sq_], hT[sq_], banks[bi][:, :], ALU.add, [r_hT[sq_], bank_reg[bi]], [r_hT[sq_]])
                cp(hTb[sq_], hT[sq_], [r_hT[sq_]], [r_hTb[sq_]], eng="act")

        def ssm_out(sq_):
            bi = psM.next()
            for c in range(4):
                tp(banks[bi][:, c * 128:(c + 1) * 128], hT[sq_][:, c * 128:(c + 1) * 128], identF, [r_hT[sq_], r_idF],
                   [bank_reg[bi]])
            cp(hout, banks[bi][:, :].rearrange("p (c n) -> p c n", c=4), [bank_reg[bi]], [r_hout], eng="act")
            dma("sp", newssm_d[sq_].rearrange("(c p) n -> p c n", p=128), hout, [r_hout], [])

        def phaseA_tile(tok0, P, nsub, nseq, is_sample, tidx):
            W = P * nsub
            L = W // nseq
            for s in range(nsub):
                xt, r_xt = xt_ring.next()
                dma("sp", xt[:P], x_d[tok0 + s * P: tok0 + (s + 1) * P, :], [], [r_xt])
                rmsnorm_fm(xt[:P], r_xt, P, g_mix, r_gmix, hnFM, r_hnFM, s * P, nwk)
            for c in range(8):
                bi = psM.next()
                for k in range(8):
                    mm(banks[bi][:, 0:W], Win[:, k, 2048 + c * 128: 2048 + (c + 1) * 128], hnFM[:, k, 0:W],
                       [r_WinB, r_hnFM], [bank_reg[bi]], start=(k == 0), stop=(k == 7))
                xr, r_xr = xr_ring.next()
                xr3 = xr[:, 0:nseq * (3 + L)].rearrange("p (b l) -> p b l", b=nseq)
                if is_sample:
                    cp(xr3[:, :, 0:3], sconvT[:, c, :].rearrange("p (b r) -> p b r", b=4), [r_sconvT], [r_xr])
                else:
                    cp(xr3[:, :, 0:3], carry[:, c, :].unsqueeze(1), [r_carry], [r_xr])
                cp(xr3[:, :, 3:3 + L], banks[bi][:, 0:W].rearrange("p (b l) -> p b l", b=nseq), [bank_reg[bi]], [r_xr],
                   eng="act")
                acc3 = acc[:, 0:W].rearrange("p (b l) -> p b l", b=nseq)
                tsc(acc3, xr3[:, :, 0:L], cw[:, c, 0:1], ALU.mult, [r_xr, r_cw, r_cb], [r_acc], s2=cb[:, c:c + 1], op1=ALU.add)
                for i in range(1, 4):
                    stt(acc3, xr3[:, :, i:i + L], cw[:, c, i:i + 1], acc3, ALU.mult, ALU.add, [r_xr, r_cw, r_acc], [r_acc])
                act(xc[:, c, 0:W], acc[:, 0:W], AF.Silu, [r_acc], [r_xc])
                if not is_sample:
                    cp(carry[:, c, :], xr[:, W:W + 3], [r_xr], [r_carry])
            for s in range(nsub):
                row0 = tok0 + s * P
                ti = 16 if is_sample else (tok0 // 128 + s)
                cols = slice(s * P, (s + 1) * P)

                def proj(c0, c1, r_w):
                    bi_ = psM.next()
                    for k in range(8):
                        mm(banks[bi_][:P, 0:c1 - c0], hnFM[:, k, cols], Win[:, k, c0:c1], [r_hnFM, r_w], [bank_reg[bi_]],
                           start=(k == 0), stop=(k == 7))
                    return bi_
                bk_ = proj(512, 1024, r_WinA)
                bq_ = proj(0, 512, r_WinA)
                bv_ = proj(1024, 1536, r_WinA)
                bz_ = proj(1536, 2048, r_WinB)
                bi = bk_
                qk_post(banks[bi][:P, :], bank_reg[bi], P, ti, gk, r_gk, True, row0)
                if not is_sample:
                    bt = psT.next()
                    for h in range(8):
                        tp(banks_b[bt][0:64, h * 128:(h + 1) * 128], qb[:, h * 64:(h + 1) * 64], identB, [r_qb, r_idB],
                           [bank_reg[bt]])
                    kti = tok0 // 128 + s
                    cp(KT[0:64, :, kti * 128:(kti + 1) * 128], banks_b[bt][0:64, 0:1024].rearrange("p (h q) -> p h q", h=8),
                       [bank_reg[bt]], [r_KT[kti]], eng="act")
                else:
                    bt = psT.next()
                    for c in range(4):
                        tp(banks_b[bt][:, c * 32:(c + 1) * 32], qb[:32, c * 128:(c + 1) * 128], identB[:32, :32],
                           [r_qb, r_idB], [bank_reg[bt]])
                    cp(KnT, banks_b[bt][:, 0:128].rearrange("p (c q) -> p c q", c=4), [bank_reg[bt]], [r_KnT], eng="act")
                bi = bq_
                qk_post(banks[bi][:P, :], bank_reg[bi], P, ti, gq, r_gq, False, row0)
                if not is_sample:
                    bt = psT.next()
                    for h in range(8):
                        tp(banks_b[bt][0:64, h * 128:(h + 1) * 128], qb[:, h * 64:(h + 1) * 64], identB, [r_qb, r_idB],
                           [bank_reg[bt]])
                    cp(QT[0:64, :, cols], banks_b[bt][0:64, 0:1024].rearrange("p (h q) -> p h q", h=8),
                       [bank_reg[bt]], [r_QT], eng="act")
                else:
                    bt = psT.next()
                    for c in range(4):
                        tp(banks_b[bt][:, c * 32:(c + 1) * 32], qb[:32, c * 128:(c + 1) * 128], identB[:32, :32],
                           [r_qb, r_idB], [bank_reg[bt]])
                    cp(QsT, banks_b[bt][:, 0:128].rearrange("p (c q) -> p c q", c=4), [bank_reg[bt]], [r_QsT], eng="act")
                bi = bv_
                vs, r_vs = vs_ring.next()
                cp(vs[:P], banks[bi][:P, :], [bank_reg[bi]], [r_vs], eng="act")
                dma("sp", newv_d[row0:row0 + P, :], vs[:P], [r_vs], [])
                if not is_sample:
                    kti = tok0 // 128 + s
                    cp(VA[:, kti, :, 0:64], banks[bi][:, :].rearrange("p (h d) -> p h d", h=8), [bank_reg[bi]], [r_VA[kti]])
                else:
                    cp(vb, banks[bi][:32, :], [bank_reg[bi]], [r_vb])
                bi = bz_
                act(zs[:P], banks[bi][:P, :], AF.Silu, [bank_reg[bi]], [r_zs])
                bi = proj(3072, 3080, r_WinB)
                tt(dtt[:P, 0, :], banks[bi][:P, 0:8], dtb[:P], ALU.add, [bank_reg[bi], r_dtb], [r_dtt])
                act(dtt[:P, 0, :], dtt[:P, 0, :], AF.Exp, [r_dtt], [r_dtt])
                act(dtt[:P, 1, :], dtt[:P, 0, :], AF.Ln, [r_dtt, r_one], [r_dtt], bias=one_t[:P, :])
                tt(dtt[:P, 2, :], dtt[:P, 1, :], a_bc[:P], ALU.mult, [r_dtt, r_abc], [r_dtt])
                if is_sample or (tok0 == 1536 and s == 3):
                    for half in range(2):
                        bi = proj(2048 + half * 512, 2048 + (half + 1) * 512, r_WinB)
                        cp(xl[:P, half * 512:(half + 1) * 512], banks[bi][:P, :], [bank_reg[bi]], [r_xl], eng="act")
                    if is_sample:
                        for b_ in range(4):
                            dma("sp", newconv_d[3 + 3 * b_: 6 + 3 * b_, :], xl[8 * b_ + 5: 8 * b_ + 8, :], [r_xl], [])
                    else:
                        dma("sp", newconv_d[0:3, :], xl[125:128, :], [r_xl], [])
                if is_sample:
                    ssd_chunk(32, 4, 0, [1, 2, 3, 4], row0, triU_s, r_triUs, mneg_s, r_mnegs, onesbd_s, r_onesbds)
                else:
                    ssd_chunk(128, 1, s * 128, [0], row0, triU, r_triU, mneg, r_mneg, ones, r_ones)

        def attention_tile(i):
            tok0 = i * 512
            for j in (2 * i, 2 * i + 1):
                red(ksf[0:64, :], KT[0:64, :, j * 256:(j + 1) * 256], [r_KT[2 * j], r_KT[2 * j + 1]], [r_ksf])
                cp(ksumT[0:64, :, j], ksf[0:64, :], [r_ksf], [r_ksum])
            for s in range(4):
                own = (tok0 + s * 128) // 256
                cols = slice(s * 128, (s + 1) * 128)
                if own <= 3:
                    cp(QT[64:72, :, cols], bc(cbias0[64:72, own, :].unsqueeze(1), [8, 8, 128]), [r_cbias0], [r_QTb])
                else:
                    bi = psM.next()
                    for h in range(8):
                        mm(banks[bi][:, h * 8:h * 8 + own], QT[0:64, h, cols], ksumT[0:64, h, 0:own], [r_QT, r_ksum],
                           [bank_reg[bi]])
                    cp(gsb[:, :, 0:own], banks[bi][:, 0:64].rearrange("p (h j) -> p h j", h=8)[:, :, 0:own],
                       [bank_reg[bi]], [r_gsb])
                    for h in range(8):
                        S.op("dve", (lambda o_, i_: (lambda e: e.max(out=o_, in_=i_)))(m8[:, h, :], gsb[:, h, :]),
                             [r_gsb], [r_m8])
                    tt(bsel, gsb, bc(m8[:, :, 2:3], [128, 8, 8]), ALU.is_lt, [r_gsb, r_m8], [r_bsel])
                    tsc(biasm, bsel, NEG, ALU.mult, [r_bsel], [r_biasm])
                    mset(biasm[:, :, own:own + 1], 0.0, [r_biasm])
                    bt = psT.next()
                    for h in range(8):
                        tp(banks_b[bt][64:72, h * 128:(h + 1) * 128], biasm[:, h, :], identB, [r_biasm, r_idB],
                           [bank_reg[bt]])
                    cp(QT[64:72, :, cols], banks_b[bt][64:72, 0:1024].rearrange("p (h q) -> p h q", h=8),
                       [bank_reg[bt]], [r_QTb], eng="act")
            obank = [4, 5, 6, 7]
            for h in range(8):
                nk = 4 * i + 4
                for kc in range(nk):
                    diag = kc >= 4 * i
                    bi = psM.next()
                    mm(banks[bi][:, :], KT[0:72, h, kc * 128:(kc + 1) * 128], QT[0:72, h, :],
                       [r_KT[kc], r_KTind, r_QT, r_QTb], [bank_reg[bi]], start=True, stop=not diag)
                    if diag:
                        mm(banks[bi][:, :], identB, negmask[:, kc - 4 * i, :], [r_idB, r_negmask], [bank_reg[bi]],
                           start=False, stop=True)
                    PT, r_PT = PT_ring.next()
                    act(PT, banks[bi][:, :], AF.Exp, [bank_reg[bi]], [r_PT], scale=0.125)
                    for s in range(4):
                        last = 4 * i + s
                        if kc > last:
                            continue
                        ob = obank[s]
                        mm(banks[ob][:, 0:65], PT[:, s * 128:(s + 1) * 128], VA[:, kc, h, 0:65], [r_PT, r_VA[kc], r_VAone],
                           [bank_reg[ob]], start=(kc == 0), stop=(kc == last))
                for s in range(4):
                    ob = obank[s]
                    rcp(rinv[:, s:s + 1], banks[ob][:, 64:65], [bank_reg[ob]], [r_rinv])
                    tsc(attb[:, s, h * 64:(h + 1) * 64], banks[ob][:, 0:64], rinv[:, s:s + 1], ALU.mult,
                        [bank_reg[ob], r_rinv], [r_attb[s]])
            for s in range(4):
                dma("sp", mix_d[tok0 + s * 128: tok0 + (s + 1) * 128, 0:512], attb[:, s, :], [r_attb[s]], [])

        if "A" in stages:
            for i in range(int(os.environ.get('K_NTILES', '4'))):
                phaseA_tile(i * 512, 128, 4, 1, False, i)
                if "B" in stages:
                    attention_tile(i)
            ssm_out(0)
            if os.environ.get('K_SAMPLE', '1') == '1':
                S.barrier()
                A.top = mark_attn
                for _ in range(4):
                    hT.append(A.alloc([128, 512], F32))
                    hTb.append(A.alloc([128, 512], BF16))
                sconvT = A.alloc([128, 8, 12], F32)
                sct, r_sct = xt_ring.next()
                dma("sp", sct[:12], sconv_d, [], [r_sct])
                bi = psM.next()
                for c in range(8):
                    tp(banks[bi][:, c * 12:(c + 1) * 12], sct[:12, c * 128:(c + 1) * 128], identF[:12, :12], [r_sct, r_idF],
                       [bank_reg[bi]])
                cp(sconvT, banks[bi][:, 0:96].rearrange("p (c r) -> p c r", c=8), [bank_reg[bi]], [r_sconvT], eng="act")
                for b_ in range(4):
                    dma("sp", hout, sssm_d[b_].rearrange("(c p) n -> p c n", p=128), [], [r_hout])
                    bi = psM.next()
                    for c in range(4):
                        tp(banks[bi][:, c * 128:(c + 1) * 128], hout[:, c, :], identF, [r_hout, r_idF], [bank_reg[bi]])
                    cp(hT[1 + b_], banks[bi][:, :], [bank_reg[bi]], [r_hT[1 + b_]], eng="act")
                    cp(hTb[1 + b_], banks[bi][:, :], [bank_reg[bi]], [r_hTb[1 + b_]])
                phaseA_tile(2048, 32, 1, 4, True, 4)
                for b_ in range(4):
                    ssm_out(1 + b_)


        if "S" in stages:
            S.barrier()
            A.top = markA
            psT = Ring([0, 1, 2, 3]); psM = psT
            KTall = A.alloc([128, 4, 8192], BF16); r_KTall = [Reg() for _ in range(64)]
            Kp_ring = Ring([(A.alloc([128, 512], BF16), Reg()) for _ in range(6)])
            Vp_ring = Ring([(A.alloc([128, 512], BF16), Reg()) for _ in range(6)])
            Zc, r_Zc = load_const("c_Z", [128, 63], BF16)
            selE, r_selE = load_const("c_selE", [32, 32, 128], BF16)
            bdmask, r_bd = load_const("c_bdmask", [64, 512], F32)
            negcs, r_negcs = load_const("c_negcs", [32, 4, 64], BF16)
            iota, r_iota = load_const("c_iota", [128, 1], F32)
            onesB = A.alloc([128, 2], BF16); r_onesB = Reg()
            mset(onesB, 1.0, [r_onesB])
            pti = A.alloc([128, 64], I32); r_pti = Reg()
            ptf = A.alloc([128, 64], F32); r_ptf = Reg()
            idxi = A.alloc([128, 64], I32); r_idx = Reg()
            Qbd = A.alloc([128, 4, 64], BF16); r_Qbd = Reg()
            mset(Qbd, 0.0, [r_Qbd])
            km = A.alloc([32, 512], F32); r_km = Reg()
            kmT = A.alloc([128, 4, 32], BF16); r_kmT = Reg()
            gs = A.alloc([64, 32], F32); r_gs = Reg()
            m8s = A.alloc([64, 8], F32); r_m8s = Reg()
            bm = A.alloc([64, 32], BF16); r_bm = Reg()
            biasT = A.alloc([32, 64], BF16); r_biasT = Reg()
            PTs_ring = Ring([(A.alloc([128, 64], BF16), Reg()) for _ in range(4)])
            PTn = A.alloc([32, 64], BF16); r_PTn = Reg()
            om = A.alloc([64, 512], F32); r_om = Reg()
            osm = A.alloc([64, 64], F32); r_osm = Reg()
            rl = A.alloc([64, 1], F32); r_rl = Reg()
            o2 = A.alloc([64, 64], BF16); r_o2 = Reg()
            for b in range(4):
                dma("sp", pti, pt_d[b:b + 1, :].to_broadcast([128, 64]), [], [r_pti])
                cp(ptf, pti, [r_pti], [r_ptf])
                tsc(ptf, ptf, 128.0, ALU.mult, [r_ptf, r_iota], [r_ptf], s2=iota[:, 0:1], op1=ALU.add)
                cp(idxi, ptf, [r_ptf], [r_idx])
                for c in range(4):
                    cp(Qbd[0:64, c, (2 * c) * 8:(2 * c) * 8 + 8], QsT[0:64, c, b * 8:(b + 1) * 8], [r_QsT], [r_Qbd])
                    cp(Qbd[64:128, c, (2 * c + 1) * 8:(2 * c + 1) * 8 + 8], QsT[64:128, c, b * 8:(b + 1) * 8], [r_QsT], [r_Qbd])
                for pg in range(64):
                    j = pg // 2
                    Kp, r_Kp = Kp_ring.next()
                    gather(Kp, ck_d, idxi[:, pg:pg + 1], [r_idx], [r_Kp])
                    mm(banks[4][0:32, :], Zc[:, 31 - j:63 - j], Kp, [r_Zc, r_Kp], [bank_reg[4]], start=(pg == 0), stop=(pg == 63))
                    bt = psT.next()
                    for c in range(4):
                        tp(banks_b[bt][:, c * 128:(c + 1) * 128], Kp[:, c * 128:(c + 1) * 128], identB, [r_Kp, r_idB], [bank_reg[bt]])
                    cp(KTall[:, :, pg * 128:(pg + 1) * 128], banks_b[bt][:, 0:512].rearrange("p (c q) -> p c q", c=4),
                       [bank_reg[bt]], [r_KTall[pg]], eng=("act" if pg % 2 == 0 else "dve"))
                cp(km, banks[4][0:32, :], [bank_reg[4]], [r_km], eng="act")
                for c in range(4):
                    tp(banks[7][:, c * 32:(c + 1) * 32], km[0:32, c * 128:(c + 1) * 128], identF[0:32, 0:32], [r_km, r_idF], [bank_reg[7]])
                cp(kmT, banks[7][:, 0:128].rearrange("p (c j) -> p c j", c=4), [bank_reg[7]], [r_kmT], eng="act")
                for c in range(4):
                    mm(banks[7][0:64, 256:288], Qbd[:, c, :], kmT[:, c, :], [r_Qbd, r_kmT], [bank_reg[7]], start=(c == 0), stop=(c == 3))
                cp(gs, banks[7][0:64, 256:288], [bank_reg[7]], [r_gs])
                S.op("dve", lambda e: e.max(out=m8s, in_=gs), [r_gs], [r_m8s])
                tsc(bm, gs, m8s[:, 2:3], ALU.is_lt, [r_gs, r_m8s], [r_bm], s2=NEG, op1=ALU.mult)
                bt = psT.next()
                tp(banks_b[bt][0:32, 0:64], bm[0:64, 0:32], identB[0:64, 0:64], [r_bm, r_idB], [bank_reg[bt]])
                cp(biasT, banks_b[bt][0:32, 0:64], [bank_reg[bt]], [r_biasT], eng="act")
                for pg in range(64):
                    j = pg // 2
                    Vp, r_Vp = Vp_ring.next()
                    gather(Vp, cv_d, idxi[:, pg:pg + 1], [r_idx], [r_Vp])
                    bi = psM.next()
                    mm(banks[bi][:, 0:64], selE[0:32, j, :], biasT[0:32, :], [r_selE, r_biasT], [bank_reg[bi]], start=True, stop=False)
                    for c in range(4):
                        mm(banks[bi][:, c * 16:(c + 1) * 16], KTall[:, c, pg * 128:(pg + 1) * 128], Qbd[:, c, c * 16:(c + 1) * 16],
                           [r_KTall[pg], r_Qbd], [bank_reg[bi]], start=False, stop=(c == 3))
                    PTs, r_PTs = PTs_ring.next()
                    act(PTs, banks[bi][:, 0:64], AF.Exp, [bank_reg[bi]], [r_PTs], scale=0.125)
                    mm(banks[5][0:64, :], PTs, Vp, [r_PTs, r_Vp], [bank_reg[5]], start=(pg == 0), stop=False)
                    mm(banks[6][0:64, 0:2], PTs, onesB[:, 0:2], [r_PTs, r_onesB], [bank_reg[6]], start=(pg == 0), stop=False)
                bi = psM.next()
                mm(banks[bi][0:32, 0:64], identB[0:32, 0:32], negcs[0:32, b, :], [r_idB, r_negcs], [bank_reg[bi]], start=True, stop=False)
                for c in range(4):
                    mm(banks[bi][0:32, c * 16:(c + 1) * 16], KnT[:, c, 0:32], Qbd[:, c, c * 16:(c + 1) * 16], [r_KnT, r_Qbd],
                       [bank_reg[bi]], start=False, stop=(c == 3))
                act(PTn, banks[bi][0:32, 0:64], AF.Exp, [bank_reg[bi]], [r_PTn], scale=0.125)
                mm(banks[5][0:64, :], PTn, vb[0:32, :], [r_PTn, r_vb], [bank_reg[5]], start=False, stop=True)
                mm(banks[6][0:64, 0:2], PTn, onesB[0:32, 0:2], [r_PTn, r_onesB], [bank_reg[6]], start=False, stop=True)
                tt(om, banks[5][0:64, :], bdmask, ALU.mult, [bank_reg[5], r_bd], [r_om])
                red(osm, om.rearrange("p (h d) -> p d h", h=8), [r_om], [r_osm])
                rcp(rl, banks[6][0:64, 0:1], [bank_reg[6]], [r_rl])
                tsc(o2, osm, rl[:, 0:1], ALU.mult, [r_osm, r_rl], [r_o2])
                for h in range(8):
                    dma("sp", mix_d[2048 + 8 * b: 2048 + 8 * b + 8, h * 64:(h + 1) * 64], o2[h * 8:(h + 1) * 8, :], [r_o2], [])

        if "C" in stages:
            S.barrier()
            A.top = markA
            psT = Ring([0, 1, 2, 3]); psM = psT
            wout = A.alloc([128, 8, 1024], BF16); r_wout = Reg()
            dma("pool", wout, w_out_d.rearrange("(k p) c -> p k c", p=128), [], [r_wout])
            wq = A.alloc([128, 8, 512], BF16); r_wq = Reg()
            dma("pool", wq, wq_d.rearrange("(k p) c -> p k c", p=128), [], [r_wq])
            wo = A.alloc([128, 4, 1024], BF16); r_wo = Reg()
            dma("pool", wo, wo_d.rearrange("(k p) c -> p k c", p=128), [], [r_wo])
            g_x, r_gx = load_in(g_x_d, [128, 1024])
            g_mlp, r_gmlp = load_in(g_mlp_d, [128, 1024])
            gqx, r_gqx = load_in(gqx_d, [128, 128])
            gkx, r_gkx = load_in(gkx_d, [128, 128])
            MKT = A.alloc([128, 5, 4, 256], BF16); r_MKT = Reg()
            MV = A.alloc([128, 5, 2, 4, 130], BF16); r_MV = Reg(); r_MVone = Reg()
            mset(MV[:, :, :, :, 128:129], 1.0, [r_MVone])
            xt = A.alloc([128, 1024], F32); r_xt = Reg()
            junk = A.alloc([128, 1024], BF16); r_junk = Reg()
            hn = A.alloc([128, 1024], BF16); r_hn = Reg()
            ssq = A.alloc([128, 4], F32); r_ssq = Reg()
            nwk = (junk, r_junk, hn, r_hn, ssq, r_ssq, psT)
            qs = A.alloc([128, 512], F32); r_qs = Reg()
            sq = A.alloc([128, 512], F32); r_sq = Reg()
            st8 = A.alloc([128, 3, 8], F32); r_st8 = Reg()
            mb = A.alloc([128, 512], BF16); r_mb = Reg()
            markM = A.top
            wk = A.alloc([128, 8, 512], BF16); r_wk = Reg()
            dma("pool", wk, wk_d.rearrange("(k p) c -> p k c", p=128), [], [r_wk])
            wv = A.alloc([128, 8, 512], BF16); r_wv = Reg()
            dma("pool", wv, wv_d.rearrange("(k p) c -> p k c", p=128), [], [r_wv])
            g_mem, r_gmem = load_in(g_mem_d, [128, 1024])
            mnFM = A.alloc([128, 8, 256], BF16); r_mnFM = Reg()
            vs = A.alloc([128, 512], F32); r_vs = Reg()
            for mc in range(2):
                dma("sp", xt, mem_d[mc * 128:(mc + 1) * 128, :], [], [r_xt])
                rmsnorm_fm(xt, r_xt, 128, g_mem, r_gmem, mnFM, r_mnFM, mc * 128, nwk)
            for mc in range(2):
                bi = psM.next()
                for k in range(8):
                    mm(banks[bi][:, :], mnFM[:, k, mc * 128:(mc + 1) * 128], wk[:, k, :], [r_mnFM, r_wk], [bank_reg[bi]],
                       start=(k == 0), stop=(k == 7))
                headnorm(banks[bi][:, :], bank_reg[bi], 128, 4, 128, qs, r_qs, sq, r_sq, st8, r_st8)
                q3 = qs.rearrange("p (h d) -> p h d", h=4)
                tt(q3, q3, bc(gkx.unsqueeze(1), [128, 4, 128]), ALU.mult, [r_qs, r_gkx], [r_qs])
                dma("sp", newmk_d[mc * 128:(mc + 1) * 128, :], qs, [r_qs], [])
                cp(mb, qs, [r_qs], [r_mb])
                bt = psT.next()
                for h in range(4):
                    tp(banks_b[bt][:, h * 128:(h + 1) * 128], mb[:, h * 128:(h + 1) * 128], identB, [r_mb, r_idB], [bank_reg[bt]])
                cp(MKT[:, 0, :, mc * 128:(mc + 1) * 128], banks_b[bt][:, 0:512].rearrange("p (h q) -> p h q", h=4),
                   [bank_reg[bt]], [r_MKT], eng="act")
                bi = psM.next()
                for k in range(8):
                    mm(banks[bi][:, :], mnFM[:, k, mc * 128:(mc + 1) * 128], wv[:, k, :], [r_mnFM, r_wv], [bank_reg[bi]],
                       start=(k == 0), stop=(k == 7))
                cp(vs, banks[bi][:, :], [bank_reg[bi]], [r_vs], eng="act")
                dma("sp", newmv_d[mc * 128:(mc + 1) * 128, :], vs, [r_vs], [])
                cp(MV[:, 0, mc, :, 0:128], banks[bi][:, :].rearrange("p (h d) -> p h d", h=4), [bank_reg[bi]], [r_MV])
            for b in range(4):
                for mc in range(2):
                    dma("pool", mb, cmk_d[b, mc * 128:(mc + 1) * 128, :], [], [r_mb])
                    bt = psT.next()
                    for h in range(4):
                        tp(banks_b[bt][:, h * 128:(h + 1) * 128], mb[:, h * 128:(h + 1) * 128], identB, [r_mb, r_idB], [bank_reg[bt]])
                    cp(MKT[:, 1 + b, :, mc * 128:(mc + 1) * 128], banks_b[bt][:, 0:512].rearrange("p (h q) -> p h q", h=4),
                       [bank_reg[bt]], [r_MKT], eng="act")
                    dma("pool", MV[:, 1 + b, mc, :, 0:128], cmv_d[b, mc * 128:(mc + 1) * 128, :].rearrange("p (h d) -> p h d", h=4),
                        [], [r_MV])
            S.barrier()
            A.top = markM
            xr4 = A.alloc([128, 4, 1024], F32); r_xr4 = [Reg() for _ in range(4)]
            mt = A.alloc([128, 1024], BF16); r_mt = Reg()
            mixFM = A.alloc([128, 8, 512], BF16); r_mixFM = Reg()
            hxFM = A.alloc([128, 8, 512], BF16); r_hxFM = Reg()
            QxT = A.alloc([128, 4, 512], BF16); r_QxT = Reg()
            oxb = A.alloc([128, 4, 512], BF16); r_oxb = [Reg() for _ in range(4)]
            oxFM = A.alloc([128, 4, 512], BF16); r_oxFM = Reg()
            PTx_ring = Ring([(A.alloc([128, 512], BF16), Reg()) for _ in range(2)])
            PTxb = A.alloc([128, 4, 32], BF16); r_PTxb = Reg()
            mset(PTxb, 0.0, [r_PTxb])
            rinv = A.alloc([128, 4], F32); r_rinv = Reg()
            print("phase C arena top", A.top)
            obank = [4, 5, 6, 7]
            xscale = float(128 ** -0.5)
            wup_v = wup_d.rearrange("(k p) c -> p k c", p=128)

            def phaseC_tile(tok0, P, nsub, is_sample):
                W = P * nsub
                for s in range(nsub):
                    rows = slice(tok0 + s * P, tok0 + (s + 1) * P)
                    cols = slice(s * P, (s + 1) * P)
                    xs_ = xr4[:P, s, :]
                    dma("sp", mt[:P], mix_d[rows, :], [], [r_mt])
                    dma("sp", xs_, x_d[rows, :], [], [r_xr4[s]])
                    bt = psT.next()
                    for k in range(8):
                        tp(banks_b[bt][:, k * P:(k + 1) * P], mt[:P, k * 128:(k + 1) * 128], identB[:P, :P], [r_mt, r_idB], [bank_reg[bt]])
                    cp(mixFM[:, :, cols], banks_b[bt][:, 0:8 * P].rearrange("p (k q) -> p k q", k=8), [bank_reg[bt]], [r_mixFM], eng="act")
                    for half in range(2):
                        hc = slice(half * 512, (half + 1) * 512)
                        bi = psM.next()
                        for k in range(8):
                            mm(banks[bi][:P, :], mixFM[:, k, cols], wout[:, k, hc], [r_mixFM, r_wout], [bank_reg[bi]],
                               start=(k == 0), stop=(k == 7))
                        tt(xr4[:P, s, hc], banks[bi][:P, :], xr4[:P, s, hc], ALU.add, [bank_reg[bi], r_xr4[s]], [r_xr4[s]])
                    rmsnorm_fm(xs_, r_xr4[s], P, g_x, r_gx, hxFM, r_hxFM, s * P, nwk)
                    bi = psM.next()
                    for k in range(8):
                        mm(banks[bi][:P, :], hxFM[:, k, cols], wq[:, k, :], [r_hxFM, r_wq], [bank_reg[bi]], start=(k == 0), stop=(k == 7))
                    headnorm(banks[bi][:P, :], bank_reg[bi], P, 4, 128, qs, r_qs, sq, r_sq, st8, r_st8)
                    q3 = qs[:P].rearrange("p (h d) -> p h d", h=4)
                    tt(q3, q3, bc(gqx[:P].unsqueeze(1), [P, 4, 128]), ALU.mult, [r_qs, r_gqx], [r_qs])
                    cp(mb[:P], qs[:P], [r_qs], [r_mb])
                    bt = psT.next()
                    for h in range(4):
                        tp(banks_b[bt][:, h * P:(h + 1) * P], mb[:P, h * 128:(h + 1) * 128], identB[:P, :P], [r_mb, r_idB], [bank_reg[bt]])
                    cp(QxT[:, :, cols], banks_b[bt][:, 0:4 * P].rearrange("p (h q) -> p h q", h=4), [bank_reg[bt]], [r_QxT], eng="act")
                for h in range(4):
                    for mc in range(2):
                        mcs = slice(mc * 128, (mc + 1) * 128)
                        bi = psM.next()
                        if not is_sample:
                            mm(banks[bi][:, 0:W], MKT[:, 0, h, mcs], QxT[:, h, 0:W], [r_MKT, r_QxT], [bank_reg[bi]])
                            PTx, r_PTx = PTx_ring.next()
                            act(PTx[:, 0:W], banks[bi][:, 0:W], AF.Exp, [bank_reg[bi]], [r_PTx], scale=xscale)
                            for s in range(nsub):
                                mm(banks[obank[s]][:P, 0:129], PTx[:, s * P:(s + 1) * P], MV[:, 0, mc, h, 0:129], [r_PTx, r_MV, r_MVone],
                                   [bank_reg[obank[s]]], start=(mc == 0), stop=(mc == 1))
                        else:
                            for b in range(4):
                                mm(banks[bi][:, b * 8:(b + 1) * 8], MKT[:, 1 + b, h, mcs], QxT[:, h, b * 8:(b + 1) * 8], [r_MKT, r_QxT],
                                   [bank_reg[bi]])
                            for b in range(4):
                                act(PTxb[:, b, b * 8:(b + 1) * 8], banks[bi][:, b * 8:(b + 1) * 8], AF.Exp, [bank_reg[bi]], [r_PTxb],
                                    scale=xscale)
                            for b in range(4):
                                mm(banks[obank[0]][:32, 0:129], PTxb[:, b, :], MV[:, 1 + b, mc, h, 0:129], [r_PTxb, r_MV, r_MVone],
                                   [bank_reg[obank[0]]], start=(mc == 0 and b == 0), stop=(mc == 1 and b == 3))
                    for s in range(nsub):
                        ob = obank[s]
                        rcp(rinv[:P, s:s + 1], banks[ob][:P, 128:129], [bank_reg[ob]], [r_rinv])
                        tsc(oxb[:P, s, h * 128:(h + 1) * 128], banks[ob][:P, 0:128], rinv[:P, s:s + 1], ALU.mult,
                            [bank_reg[ob], r_rinv], [r_oxb[s]])
                for s in range(nsub):
                    cols = slice(s * P, (s + 1) * P)
                    bt = psT.next()
                    for h in range(4):
                        tp(banks_b[bt][:, h * P:(h + 1) * P], oxb[:P, s, h * 128:(h + 1) * 128], identB[:P, :P], [r_oxb[s], r_idB],
                           [bank_reg[bt]])
                    cp(oxFM[:, :, cols], banks_b[bt][:, 0:4 * P].rearrange("p (h q) -> p h q", h=4), [bank_reg[bt]], [r_oxFM], eng="act")
                    for half in range(2):
                        hc = slice(half * 512, (half + 1) * 512)
                        bi = psM.next()
                        for k in range(4):
                            mm(banks[bi][:P, :], oxFM[:, k, cols], wo[:, k, hc], [r_oxFM, r_wo], [bank_reg[bi]], start=(k == 0), stop=(k == 3))
                        tt(xr4[:P, s, hc], banks[bi][:P, :], xr4[:P, s, hc], ALU.add, [bank_reg[bi], r_xr4[s]], [r_xr4[s]])
                    rmsnorm_fm(xr4[:P, s, :], r_xr4[s], P, g_mlp, r_gmlp, hxFM, r_hxFM, s * P, nwk)
                r_x2 = [Reg() for _ in range(nsub)]
                for s in range(nsub):
                    dma("sp", y_d[tok0 + s * P: tok0 + (s + 1) * P, :], xr4[:P, s, :], [r_xr4[s]], [r_y2[(tok0 // 128) + s]])
                dma("sp", hm_d[:, :, tok0:tok0 + W], hxFM[:, :, 0:W], [r_hxFM], [r_hmd[tok0 // 512]])

            r_y2 = [Reg() for _ in range(17)]
            r_hmd = [Reg() for _ in range(5)]
            for i in range(int(os.environ.get('K_NTILES', '4'))):
                phaseC_tile(i * 512, 128, 4, False)
            if os.environ.get('K_SAMPLE', '1') == '1':
                phaseC_tile(2048, 32, 1, True)

            S.barrier()
            A.top = markA
            psM = Ring([0, 1, 2, 3])
            wup_sb = A.alloc([128, 8, 4096], BF16); r_wup = [Reg() for _ in range(8)]
            wdn_sb = A.alloc([128, 32, 1024], BF16); r_wdn = [Reg() for _ in range(8)]
            wdn_v = wdn_d.rearrange("(f p) c -> p f c", p=128)
            for fg in range(8):
                dma("pool", wup_sb[:, :, fg * 512:(fg + 1) * 512], wup_v[:, :, fg * 512:(fg + 1) * 512], [], [r_wup[fg]])
                dma("pool", wdn_sb[:, fg * 4:(fg + 1) * 4, :], wdn_v[:, fg * 4:(fg + 1) * 4, :], [], [r_wdn[fg]])
            hm_ring = Ring([(A.alloc([128, 8, 512], BF16), Reg()) for _ in range(2)])
            x2_ring = Ring([(A.alloc([128, 4, 1024], F32), [Reg() for _ in range(4)]) for _ in range(1)])
            hF_ring = Ring([(A.alloc([128, 4, 512], BF16), Reg()) for _ in range(2)])
            tmp_ring = Ring([(A.alloc([128, 512], BF16), Reg()) for _ in range(2)])
            print("phase D arena top", A.top)

            def phaseD_tile(tok0, P, nsub):
                W = P * nsub
                hm, r_hm = hm_ring.next()
                dma("sp", hm[:, :, 0:W], hm_d[:, :, tok0:tok0 + W], [r_hmd[tok0 // 512]], [r_hm])
                x2, r_x2 = x2_ring.next()
                for s in range(nsub):
                    dma("sp", x2[:P, s, :], y_d[tok0 + s * P: tok0 + (s + 1) * P, :], [r_y2[(tok0 // 128) + s]], [r_x2[s]])
                for fg in range(8):
                    hF, r_hF = hF_ring.next()
                    for f4 in range(4):
                        f = fg * 4 + f4
                        bi = psM.next()
                        for k in range(8):
                            mm(banks[bi][:, 0:W], wup_sb[:, k, f * 128:(f + 1) * 128], hm[:, k, 0:W], [r_wup[fg], r_hm], [bank_reg[bi]],
                               start=(k == 0), stop=(k == 7))
                        tmpr, r_tmpr = tmp_ring.next()
                        act(tmpr[:, 0:W], banks[bi][:, 0:W], AF.Relu, [bank_reg[bi]], [r_tmpr])
                        tt(hF[:, f4, 0:W], tmpr[:, 0:W], tmpr[:, 0:W], ALU.mult, [r_tmpr], [r_hF])
                    for s in range(nsub):
                        for half in range(2):
                            hc = slice(half * 512, (half + 1) * 512)
                            bi = obank[(s * 2 + half) % 4]
                            for f4 in range(4):
                                mm(banks[bi][:P, :], hF[:, f4, s * P:(s + 1) * P], wdn_sb[:, fg * 4 + f4, hc], [r_hF, r_wdn[fg]],
                                   [bank_reg[bi]], start=(f4 == 0), stop=(f4 == 3))
                            tt(x2[:P, s, hc], banks[bi][:P, :], x2[:P, s, hc], ALU.add, [bank_reg[bi], r_x2[s]], [r_x2[s]])
                for s in range(nsub):
                    dma("sp", y_d[tok0 + s * P: tok0 + (s + 1) * P, :], x2[:P, s, :], [r_x2[s]], [r_y2[(tok0 // 128) + s]])

            for i in range(int(os.environ.get('K_NTILES', '4'))):
                phaseD_tile(i * 512, 128, 4)
            if os.environ.get('K_SAMPLE', '1') == '1':
                phaseD_tile(2048, 32, 1)

        S.run(st)
    return nc


def prep_inputs(inputs, consts):
    f32 = np.float32
    xp = np.asarray(inputs["x_prompt"], f32)
    xs = np.asarray(inputs["x_sample"], f32)
    ck = np.ascontiguousarray(np.asarray(inputs["cache_k"], f32)[0].reshape(NPOOL * 128, 512))
    cv = np.ascontiguousarray(np.asarray(inputs["cache_v"], f32)[0].reshape(NPOOL * 128, 512))
    pt = np.asarray(inputs["page_table"], np.int32)

    def bcast(a, n=128):
        a = np.asarray(a, f32).reshape(1, -1)
        return np.ascontiguousarray(np.broadcast_to(a, (n, a.shape[1])))

    shared = {
        "ck": ck, "cv": cv,
        "w_in": np.ascontiguousarray(inputs["w_in"][0], f32), "w_out": np.ascontiguousarray(inputs["w_out"][0], f32),
        "wq_x": np.ascontiguousarray(inputs["wq_x"][0], f32), "wk_x": np.ascontiguousarray(inputs["wk_x"][0], f32),
        "wv_x": np.ascontiguousarray(inputs["wv_x"][0], f32), "wo_x": np.ascontiguousarray(inputs["wo_x"][0], f32),
        "w_up": np.ascontiguousarray(inputs["w_up"][0], f32), "w_down": np.ascontiguousarray(inputs["w_down"][0], f32),
        "g_mix": bcast(inputs["ln_mix_g"][0]), "g_x": bcast(inputs["ln_x_g"][0]),
        "g_mem": bcast(inputs["ln_mem_g"][0]), "g_mlp": bcast(inputs["ln_mlp_g"][0]),
        "gq": bcast(inputs["q_norm_g"][0]), "gk": bcast(inputs["k_norm_g"][0]),
        "gqx": bcast(inputs["qx_norm_g"][0]), "gkx": bcast(inputs["kx_norm_g"][0]),
        "gssm": bcast(inputs["ssm_norm_g"][0]), "dtb": bcast(inputs["dt_bias"][0]),
        "alog": bcast(inputs["a_log"][0]), "dskip": bcast(inputs["d_skip"][0]),
        "cw": np.ascontiguousarray(np.asarray(inputs["conv_w"][0], f32).reshape(4, 8, 128).transpose(2, 1, 0)),
        "cb": np.ascontiguousarray(np.asarray(inputs["conv_b"][0], f32).reshape(8, 128).T),
    }
    shared.update(consts)
    in_maps = []
    for c in range(8):
        m = dict(shared)
        m["x"] = np.ascontiguousarray(np.concatenate([xp[c], xs[4 * c:4 * c + 4].reshape(32, 1024)], axis=0))
        m["mem"] = np.ascontiguousarray(inputs["mem_prompt"][c], f32)
        m["pt"] = np.ascontiguousarray(pt[4 * c:4 * c + 4])
        m["sconv"] = np.ascontiguousarray(np.asarray(inputs["state_conv"], f32)[0, 4 * c:4 * c + 4].reshape(12, 1024))
        m["sssm"] = np.ascontiguousarray(np.asarray(inputs["state_ssm"], f32)[0, 4 * c:4 * c + 4].reshape(4, 512, 128))
        m["cmk"] = np.ascontiguousarray(np.asarray(inputs["cache_mem_k"], f32)[0, 4 * c:4 * c + 4].reshape(4, 256, 512))
        m["cmv"] = np.ascontiguousarray(np.asarray(inputs["cache_mem_v"], f32)[0, 4 * c:4 * c + 4].reshape(4, 256, 512))
        in_maps.append(m)
    return in_maps


def assemble(results):
    f32 = np.float32
    y_p = np.stack([r["y"][:2048] for r in results]).astype(f32)
    y_s = np.concatenate([r["y"][2048:].reshape(4, 8, 1024) for r in results]).astype(f32)
    nk_p = np.stack([r["newk"][:2048].reshape(2048, 8, 64) for r in results])[None].astype(f32)
    nv_p = np.stack([r["newv"][:2048].reshape(2048, 8, 64) for r in results])[None].astype(f32)
    nc_p = np.stack([r["newconv"][0:3] for r in results])[None].astype(f32)
    ns_p = np.stack([r["newssm"][0].reshape(8, 64, 128) for r in results])[None].astype(f32)
    mk_p = np.stack([r["newmk"].reshape(256, 4, 128) for r in results])[None].astype(f32)
    mv_p = np.stack([r["newmv"].reshape(256, 4, 128) for r in results])[None].astype(f32)
    nk_s = np.concatenate([r["newk"][2048:].reshape(4, 8, 8, 64) for r in results])[None].astype(f32)
    nv_s = np.concatenate([r["newv"][2048:].reshape(4, 8, 8, 64) for r in results])[None].astype(f32)
    nc_s = np.concatenate([r["newconv"][3:15].reshape(4, 3, 1024) for r in results])[None].astype(f32)
    ns_s = np.concatenate([r["newssm"][1:5].reshape(4, 8, 64, 128) for r in results])[None].astype(f32)
    return (y_p, y_s, nk_p, nv_p, nc_p, ns_p, mk_p, mv_p, nk_s, nv_s, nc_s, ns_s)


def kernel(**inputs):
    consts = host_consts()
    nc = build_program(consts, stages=("A", "B", "S", "M", "C"))
    in_maps = prep_inputs(inputs, consts)
    res = run_bass_kernel_spmd(nc, in_maps, core_ids=list(range(8)))
    return assemble(res.results)
```

```python
import os
import numpy as np
import ml_dtypes
from contextlib import ExitStack
import concourse.bass as bass
import concourse.mybir as mybir
from concourse.bass_utils import run_bass_kernel_spmd

F32 = mybir.dt.float32
BF16 = mybir.dt.bfloat16
I32 = mybir.dt.int32
AF = mybir.ActivationFunctionType
ALU = mybir.AluOpType
AX = mybir.AxisListType

NEG = -30000.0
EPS = 1e-6
NTOK = 2080
NPOOL = 2560


class Reg:
    __slots__ = ("w", "rs", "name", "excl")

    def __init__(self, name="", excl=False):
        self.w = None
        self.rs = []
        self.name = name
        self.excl = excl


class Op:
    __slots__ = ("eng", "emit", "deps", "signal", "isdma", "sem", "target", "prev_target", "bar")

    def __init__(self, eng, emit, isdma):
        self.eng = eng
        self.emit = emit
        self.deps = []
        self.signal = False
        self.isdma = isdma
        self.sem = None
        self.target = 0
        self.prev_target = 0
        self.bar = 0


ENGS = ["pe", "act", "dve", "pool", "sp"]


class Sched:
    def __init__(self, nc, n_dma_sems=12):
        self.nc = nc
        self.ops = {e: [] for e in ENGS}
        self.n_dma_sems = n_dma_sems
        self.barriers = [[]]
        self.since_bar_dma = []

    def op(self, eng, emit, r=(), w=(), dma=False):
        if getattr(self, "capture", None) is not None:
            self.capture.append((eng, emit, list(r), list(w), dma))
            return None
        o = Op(eng, emit, dma)
        self.count = getattr(self, "count", 0) + 1
        if self.count > int(os.environ.get("K_MAXOPS", "100000000")):
            return o
        if os.environ.get("K_TRACE"):
            import sys as _sys
            f = _sys._getframe(2)
            print("OP", self.count, eng, "dma" if dma else "", f.f_lineno, f.f_code.co_name, "<-", f.f_back.f_lineno)
        o.bar = len(self.barriers) - 1
        deps = {}
        if any(reg.excl for reg in r):
            w = list(w) + [reg for reg in r if reg.excl and reg not in w]
            r = [reg for reg in r if not reg.excl]
        for reg in r:
            if reg.w is not None:
                deps[id(reg.w)] = reg.w
        for reg in w:
            if reg.w is not None:
                deps[id(reg.w)] = reg.w
            for x in reg.rs:
                deps[id(x)] = x
        for d in deps.values():
            if d.eng == "pe" and eng == "pe" and not d.isdma and not dma:
                continue
            o.deps.append(d)
            d.signal = True
        for reg in r:
            reg.rs.append(o)
        for reg in w:
            reg.w = o
            reg.rs = []
        self.ops[eng].append(o)
        if dma:
            self.since_bar_dma.append(o)
        return o

    def record(self, fn):
        assert getattr(self, "capture", None) is None
        self.capture = []
        fn()
        items = self.capture
        self.capture = None
        return items

    def replay(self, a, b=()):
        i = j = 0
        while i < len(a) or j < len(b):
            if j >= len(b) or (i < len(a) and i * len(b) <= j * len(a)):
                self.op(*a[i]); i += 1
            else:
                self.op(*b[j]); j += 1

    def barrier(self):
        deps = list(self.since_bar_dma)
        for e in ENGS:
            for o in reversed(self.ops[e]):
                if not o.isdma:
                    o.signal = True
                    deps.append(o)
                    break
        self.barriers.append(deps)
        self.since_bar_dma = []

    def finalize(self, stack):
        nc = self.nc
        self.esem = {}
        self.dsems = {}
        for e in ENGS:
            self.esem[e] = stack.enter_context(nc.semaphore("s_" + e))
            self.dsems[e] = [stack.enter_context(nc.semaphore("d_%s_%d" % (e, i)))
                             for i in range(self.n_dma_sems)]
        self.final_dma = {}
        for e in ENGS:
            cnt = 0
            dcnt = [0] * self.n_dma_sems
            k = 0
            for o in self.ops[e]:
                if o.isdma:
                    i = k % self.n_dma_sems
                    k += 1
                    o.sem = self.dsems[e][i]
                    o.prev_target = dcnt[i]
                    dcnt[i] += 16
                    o.target = dcnt[i]
                elif o.signal:
                    cnt += 1
                    o.sem = self.esem[e]
                    o.target = cnt
            self.final_dma[e] = [o for o in self.ops[e] if o.isdma]

    def emit_engine(self, ename, e):
        waited = {}

        def wait(sem, val):
            key = id(sem)
            if waited.get(key, 0) >= val:
                return
            e.wait_ge(sem, val)
            waited[key] = val

        cur_bar = 0
        for o in self.ops[ename]:
            while cur_bar < o.bar:
                cur_bar += 1
                for d in self.barriers[cur_bar]:
                    wait(d.sem, d.target)
            for d in o.deps:
                wait(d.sem, d.target)
            if o.isdma and o.prev_target > 0:
                wait(o.sem, o.prev_target)
            ins = o.emit(e)
            if o.isdma:
                ins.then_inc(o.sem, 16)
            elif o.signal:
                ins.then_inc(o.sem, 1)
        for o in self.final_dma[ename]:
            wait(o.sem, o.target)

    def run(self, stack):
        self.finalize(stack)
        block = stack.enter_context(self.nc.Block())
        S = self

        @block.tensor
        def _(e):
            S.emit_engine("pe", e)

        @block.scalar
        def _(e):
            S.emit_engine("act", e)

        @block.vector
        def _(e):
            S.emit_engine("dve", e)

        @block.gpsimd
        def _(e):
            S.emit_engine("pool", e)

        @block.sync
        def _(e):
            S.emit_engine("sp", e)


class Arena:
    def __init__(self, t, nbytes):
        self.t = t
        self.cap = nbytes
        self.top = 0

    def alloc(self, shape, dt):
        n = 1
        for d in shape[1:]:
            n *= d
        esz = 2 if dt == BF16 else 4
        size = (n * esz + 31) // 32 * 32
        off = self.top
        self.top += size
        assert self.top <= self.cap, ("SBUF arena overflow", self.top, self.cap)
        v = self.t[0:shape[0], off // 2: off // 2 + (n * esz) // 2]
        if dt != BF16:
            v = v.bitcast(dt)
        if len(shape) == 3:
            v = v.rearrange("p (a b) -> p a b", a=shape[1])
        elif len(shape) == 4:
            v = v.rearrange("p (a b c) -> p a b c", a=shape[1], b=shape[2])
        elif len(shape) == 5:
            v = v.rearrange("p (a b c d) -> p a b c d", a=shape[1], b=shape[2], c=shape[3])
        return v


class Ring:
    def __init__(self, items):
        self.items = items
        self.i = 0

    def next(self):
        x = self.items[self.i % len(self.items)]
        self.i += 1
        return x


def host_consts():
    c = {}
    c["c_ident"] = np.eye(128, dtype=np.float32)
    half = 32
    inv_freq = (10000.0 ** (-np.arange(half, dtype=np.float32) / half)).astype(np.float32)
    cs = np.zeros((128, 17, 64), np.float32)
    p = np.arange(128)
    for ti in range(16):
        pos = (ti * 128 + p).astype(np.float32)
        ang = pos[:, None] * inv_freq[None, :]
        cs[:, ti, 0:32] = np.cos(ang)
        cs[:, ti, 32:64] = np.sin(ang)
    pos = (8192 + (p % 8)).astype(np.float32)
    ang = pos[:, None] * inv_freq[None, :]
    cs[:, 16, 0:32] = np.cos(ang)
    cs[:, 16, 32:64] = np.sin(ang)
    c["c_cs"] = cs
    nm = np.zeros((128, 4, 512), np.float32)
    f = np.arange(512)
    for r in range(4):
        nm[:, r, :] = np.where((r * 128 + p)[:, None] > f[None, :], NEG, 0.0)
    c["c_negmask"] = nm
    ind = np.zeros((8, 2048), np.float32)
    for j in range(8):
        ind[j, j * 256:(j + 1) * 256] = 1.0
    c["c_ind"] = ind
    cb0 = np.zeros((8, 8, 128), np.float32)
    for j in range(8):
        for own in range(8):
            cb0[j, own, :] = 0.0 if j <= own else NEG
    c["c_bias0"] = cb0
    t = np.arange(128)
    c["c_triU"] = (t[:, None] <= t[None, :]).astype(np.float32)
    c["c_mneg"] = np.where(t[None, :] < t[:, None], NEG, 0.0).astype(np.float32)
    c["c_ones"] = np.ones((128, 128), np.float32)
    t32 = np.arange(32)
    same = (t32[:, None] // 8) == (t32[None, :] // 8)
    c["c_triU_s"] = ((t32[:, None] <= t32[None, :]) & same).astype(np.float32)
    c["c_mneg_s"] = np.where((t32[None, :] < t32[:, None]) | (~same), NEG, 0.0).astype(np.float32)
    c["c_onesbd_s"] = same.astype(np.float32)
    seqsel = np.zeros((32, 4, 128), np.float32)
    for b in range(4):
        seqsel[b * 8:(b + 1) * 8, b, :] = 1.0
    c["c_seqsel"] = seqsel
    colmask = np.zeros((128, 4, 32), np.float32)
    for b in range(4):
        colmask[:, b, b * 8:(b + 1) * 8] = 1.0
    c["c_colmask"] = colmask
    rowmask = np.zeros((32, 4), np.float32)
    for b in range(4):
        rowmask[b * 8:(b + 1) * 8, b] = 1.0
    c["c_rowmask"] = rowmask
    Z = np.zeros((128, 63), np.float32)
    Z[:, 31] = 1.0
    c["c_Z"] = Z
    selE = np.zeros((32, 32, 128), np.float32)
    for j in range(32):
        selE[j, j, :] = 1.0
    c["c_selE"] = selE
    bd = np.zeros((64, 512), np.float32)
    for h in range(8):
        bd[h * 8:(h + 1) * 8, h * 64:(h + 1) * 64] = 1.0
    c["c_bdmask"] = bd
    negcs = np.zeros((32, 4, 64), np.float32)
    for key in range(32):
        for b in range(4):
            for tq in range(8):
                ok = (key // 8 == b) and (key % 8 <= tq)
                negcs[key, b, tq::8] = 0.0 if ok else NEG
    negcs2 = np.zeros((32, 4, 64), np.float32)
    for key in range(32):
        for b in range(4):
            for h in range(8):
                for tq in range(8):
                    ok = (key // 8 == b) and (key % 8 <= tq)
                    negcs2[key, b, h * 8 + tq] = 0.0 if ok else NEG
    c["c_negcs"] = negcs2
    c["c_iota"] = np.arange(128, dtype=np.float32).reshape(128, 1)
    return c


CONST_SHAPES = None


def build_program(consts, stages=("A", "S", "M", "C")):
    nc = bass.Bass("TRN2", target_bir_lowering=False)

    def din(name, shape, dt=F32):
        return nc.dram_tensor(name, list(shape), dt, kind="ExternalInput").ap()

    def dout(name, shape):
        return nc.dram_tensor(name, list(shape), F32, kind="ExternalOutput").ap()

    x_d = din("x", [NTOK, 1024])
    mem_d = din("mem", [256, 1024])
    ck_d = din("ck", [NPOOL * 128, 512])
    cv_d = din("cv", [NPOOL * 128, 512])
    pt_d = din("pt", [4, 64], I32)
    sconv_d = din("sconv", [12, 1024])
    sssm_d = din("sssm", [4, 512, 128])
    cmk_d = din("cmk", [4, 256, 512])
    cmv_d = din("cmv", [4, 256, 512])
    w_in_d = din("w_in", [1024, 3080])
    w_out_d = din("w_out", [1024, 1024])
    wq_d = din("wq_x", [1024, 512])
    wk_d = din("wk_x", [1024, 512])
    wv_d = din("wv_x", [1024, 512])
    wo_d = din("wo_x", [512, 1024])
    wup_d = din("w_up", [1024, 4096])
    wdn_d = din("w_down", [4096, 1024])
    g_mix_d = din("g_mix", [128, 1024])
    g_x_d = din("g_x", [128, 1024])
    g_mem_d = din("g_mem", [128, 1024])
    g_mlp_d = din("g_mlp", [128, 1024])
    gq_d = din("gq", [128, 64])
    gk_d = din("gk", [128, 64])
    gqx_d = din("gqx", [128, 128])
    gkx_d = din("gkx", [128, 128])
    gssm_d = din("gssm", [128, 512])
    dtb_d = din("dtb", [128, 8])
    alog_d = din("alog", [128, 8])
    dskip_d = din("dskip", [128, 8])
    cw_d = din("cw", [128, 8, 4])
    cb_d = din("cb", [128, 8])
    cd = {k: din(k, v.shape) for k, v in consts.items()}

    y_d = dout("y", [NTOK, 1024])
    newk_d = dout("newk", [NTOK, 512])
    newv_d = dout("newv", [NTOK, 512])
    newconv_d = dout("newconv", [15, 1024])
    newssm_d = dout("newssm", [5, 512, 128])
    newmk_d = dout("newmk", [256, 512])
    newmv_d = dout("newmv", [256, 512])
    mix_d = nc.dram_tensor("mixscr", [NTOK, 1024], BF16, kind="Internal").ap()
    hm_d = nc.dram_tensor("hmscr", [128, 8, NTOK], BF16, kind="Internal").ap()

    with ExitStack() as st:
        S = Sched(nc)
        ARENA_BYTES = 192 * 1024
        arena_t = st.enter_context(nc.sbuf_tensor("arena", [128, ARENA_BYTES // 2], BF16))
        A = Arena(arena_t, ARENA_BYTES)
        banks = [st.enter_context(nc.psum_tensor("bank%d" % i, [128, 512], F32)) for i in range(8)]
        banks_b = [b.bitcast(BF16) for b in banks]
        bank_reg = [Reg("bank%d" % i, excl=True) for i in range(8)]

        def mm(out, lhsT, rhs, r, w, start=True, stop=True):
            S.op("pe", lambda e: e.matmul(out, lhsT=lhsT, rhs=rhs, start=start, stop=stop), r, w)

        def tp(out, in_, idn, r, w):
            S.op("pe", lambda e: e.transpose(out=out, in_=in_, identity=idn), r, w)

        def act(out, in_, func, r, w, **kw):
            S.op("act", lambda e: e.activation(out=out, in_=in_, func=func, **kw), r, w)

        def tt(out, a, b, op, r, w, eng="dve"):
            S.op(eng, lambda e: e.tensor_tensor(out=out, in0=a, in1=b, op=op), r, w)

        def tsc(out, a, s1, op0, r, w, s2=None, op1=None):
            if op1 is None:
                S.op("dve", lambda e: e.tensor_scalar(out=out, in0=a, scalar1=s1, scalar2=None, op0=op0), r, w)
            else:
                S.op("dve", lambda e: e.tensor_scalar(out=out, in0=a, scalar1=s1, scalar2=s2, op0=op0, op1=op1), r, w)

        def stt(out, a, s, b, op0, op1, r, w, accum=None):
            if accum is None:
                S.op("dve", lambda e: e.scalar_tensor_tensor(out=out, in0=a, scalar=s, in1=b, op0=op0, op1=op1), r, w)
            else:
                S.op("dve", lambda e: e.scalar_tensor_tensor(out=out, in0=a, scalar=s, in1=b, op0=op0, op1=op1,
                                                             accum_out=accum), r, w)

        def cp(out, in_, r, w, eng="dve"):
            if eng == "act":
                S.op("act", lambda e: e.activation(out=out, in_=in_, func=AF.Copy), r, w)
            else:
                S.op(eng, lambda e: e.tensor_copy(out=out, in_=in_), r, w)

        def red(out, in_, r, w, op=ALU.add):
            S.op("dve", lambda e: e.tensor_reduce(out=out, in_=in_, axis=AX.X, op=op), r, w)

        def rcp(out, in_, r, w):
            S.op("dve", lambda e: e.reciprocal(out=out, in_=in_), r, w)

        def mset(ap, val, w, eng="dve"):
            S.op(eng, lambda e: e.memset(ap, val), (), w)

        def dma(q, out, in_, r, w):
            S.op(q, lambda e: e.dma_start(out=out, in_=in_), r, w, dma=True)

        def gather(out, table, idx, r, w):
            S.op("pool", lambda e: e.indirect_dma_start(out=out, out_offset=None, in_=table,
                                                        in_offset=bass.IndirectOffsetOnAxis(ap=idx, axis=0)),
                 r, w, dma=True)

        def bc(ap, shape):
            return ap.to_broadcast(list(shape))

        identF = A.alloc([128, 128], F32); r_idF = Reg()
        identB = A.alloc([128, 128], BF16); r_idB = Reg()
        cs = A.alloc([128, 17, 64], F32); r_cs = Reg()
        dma("sp", identF, cd["c_ident"], [], [r_idF])
        dma("pool", identB, cd["c_ident"], [], [r_idB])
        dma("sp", cs, cd["c_cs"], [], [r_cs])
        eps_t = A.alloc([128, 1], F32); r_eps = Reg()
        mset(eps_t, EPS, [r_eps])
        one_t = A.alloc([128, 1], F32); r_one = Reg()
        mset(one_t, 1.0, [r_one])

        QsT = A.alloc([128, 4, 32], BF16)
        KnT = A.alloc([128, 4, 32], BF16)
        vb = A.alloc([32, 512], BF16)

        def load_const(name, shape, dt, q=None):
            t_ = A.alloc(shape, dt)
            r_ = Reg(name)
            if q is None:
                q = "pool" if dt == BF16 else "sp"
            dma(q, t_, name if not isinstance(name, str) else cd[name], [], [r_])
            return t_, r_

        def load_in(d_ap, shape, dt=F32, q="sp"):
            t_ = A.alloc(shape, dt)
            r_ = Reg()
            dma(q if dt == F32 or dt == I32 else "pool", t_, d_ap, [], [r_])
            return t_, r_

        def small(shape, dt=F32):
            return A.alloc(shape, dt), Reg()

        def rmsnorm_fm(x_ap, r_x, P, gbc, r_g, outFM, r_out, col0, wk):
            junk, r_junk, hn, r_hn, ssq, r_ssq, psb = wk
            stt(junk[:P], x_ap, 1.0, x_ap, ALU.mult, ALU.mult, [r_x], [r_junk, r_ssq], accum=ssq[:P, 0:1])
            act(ssq[:P, 1:2], ssq[:P, 0:1], AF.Sqrt, [r_ssq, r_eps], [r_ssq], scale=1.0 / 1024, bias=eps_t[:P, :])
            rcp(ssq[:P, 2:3], ssq[:P, 1:2], [r_ssq], [r_ssq])
            stt(hn[:P], x_ap, ssq[:P, 2:3], gbc[:P], ALU.mult, ALU.mult, [r_x, r_ssq, r_g], [r_hn])
            bi = psb.next()
            for k in range(8):
                tp(banks_b[bi][:, k * P:(k + 1) * P], hn[:P, k * 128:(k + 1) * 128], identB[:P, :P],
                   [r_hn, r_idB], [bank_reg[bi]])
            cp(outFM[:, :, col0:col0 + P], banks_b[bi][:, 0:8 * P].rearrange("p (k q) -> p k q", k=8),
               [bank_reg[bi]], [r_out], eng="act")

        def headnorm(ps_ap, r_ps, P, nh, hd, qs, r_qs, sq, r_sq, st8, r_st8):
            cp(qs[:P], ps_ap, [r_ps], [r_qs], eng="act")
            act(sq[:P], ps_ap, AF.Square, [r_ps], [r_sq])
            red(st8[:P, 0, 0:nh], sq[:P].rearrange("p (h d) -> p h d", h=nh), [r_sq], [r_st8])
            act(st8[:P, 1, 0:nh], st8[:P, 0, 0:nh], AF.Sqrt, [r_st8, r_eps], [r_st8], scale=1.0 / hd, bias=eps_t[:P, :])
            rcp(st8[:P, 2, 0:nh], st8[:P, 1, 0:nh], [r_st8], [r_st8])
            q3 = qs[:P].rearrange("p (h d) -> p h d", h=nh)
            tt(q3, q3, bc(st8[:P, 2, 0:nh].unsqueeze(2), [P, nh, hd]), ALU.mult, [r_qs, r_st8], [r_qs])

        markA = A.top
        Win = A.alloc([128, 8, 3080], BF16); r_WinA = Reg(); r_WinB = Reg()
        w_in_v = w_in_d.rearrange("(k p) c -> p k c", p=128)
        dma("pool", Win[:, :, 0:1536], w_in_v[:, :, 0:1536], [], [r_WinA])
        dma("pool", Win[:, :, 1536:3080], w_in_v[:, :, 1536:3080], [], [r_WinB])
        g_mix, r_gmix = load_in(g_mix_d, [128, 1024])
        gq, r_gq = load_in(gq_d, [128, 64])
        gk, r_gk = load_in(gk_d, [128, 64])
        gssm, r_gssm = load_in(gssm_d, [128, 512])
        dtb, r_dtb = load_in(dtb_d, [128, 8])
        a_bc, r_abc = load_in(alog_d, [128, 8])
        dskip, r_dskip = load_in(dskip_d, [128, 8])
        cw, r_cw = load_in(cw_d, [128, 8, 4])
        cb, r_cb = load_in(cb_d, [128, 8])
        act(a_bc, a_bc, AF.Exp, [r_abc], [r_abc])
        tsc(a_bc, a_bc, -1.0, ALU.mult, [r_abc], [r_abc])
        triU, r_triU = load_const("c_triU", [128, 128], F32)
        mneg, r_mneg = load_const("c_mneg", [128, 128], F32)
        ones, r_ones = load_const("c_ones", [128, 128], F32)
        triU_s, r_triUs = load_const("c_triU_s", [32, 32], F32)
        mneg_s, r_mnegs = load_const("c_mneg_s", [32, 32], F32)
        onesbd_s, r_onesbds = load_const("c_onesbd_s", [32, 32], F32)
        seqsel, r_seqsel = load_const("c_seqsel", [32, 4, 128], F32)
        colmask, r_colmask = load_const("c_colmask", [128, 4, 32], BF16)
        rowmask, r_rowmask = load_const("c_rowmask", [32, 4], F32)

        KT = A.alloc([128, 8, 2048], BF16); r_KT = [Reg() for _ in range(16)]; r_KTind = Reg()
        for h in range(8):
            dma("pool", KT[64:72, h, :], cd["c_ind"], [], [r_KTind])
        VA = A.alloc([128, 16, 8, 66], BF16); r_VA = [Reg() for _ in range(16)]; r_VAone = Reg()
        mset(VA[:, :, :, 64:65], 1.0, [r_VAone])
        hnFM = A.alloc([128, 8, 512], BF16); r_hnFM = Reg()
        xc = A.alloc([128, 8, 512], BF16); r_xc = Reg()
        carry = A.alloc([128, 8, 3], F32); r_carry = Reg()
        mset(carry, 0.0, [r_carry])
        hT = [A.alloc([128, 512], F32)]
        hTb = [A.alloc([128, 512], BF16)]
        r_hT = [Reg() for _ in range(5)]
        r_hTb = [Reg() for _ in range(5)]
        mset(hT[0], 0.0, [r_hT[0]])
        mset(hTb[0], 0.0, [r_hTb[0]])

        xt_ring = Ring([(A.alloc([128, 1024], F32), Reg()) for _ in range(1)])
        junk = A.alloc([128, 1024], BF16); r_junk = Reg()
        hn = A.alloc([128, 1024], BF16); r_hn = Reg()
        ssq = A.alloc([128, 4], F32); r_ssq = Reg()
        psT = Ring([0, 1, 2, 3])
        psM = psT
        nwk = (junk, r_junk, hn, r_hn, ssq, r_ssq, psT)
        qs_ring = Ring([(A.alloc([128, 512], F32), Reg()) for _ in range(1)])
        sq = A.alloc([128, 512], F32); r_sq = Reg()
        st8 = A.alloc([128, 3, 8], F32); r_st8 = Reg()
        tab = A.alloc([128, 4, 32], F32); r_tab = Reg()
        rt = [A.alloc([128, 8, 32], F32) for _ in range(2)]; r_rt = [Reg() for _ in range(2)]
        rt = rt + rt; r_rt = r_rt + r_rt
        ko_ring = Ring([(A.alloc([128, 512], F32), Reg()) for _ in range(1)])
        qb = A.alloc([128, 512], BF16); r_qb = Reg()
        vs_ring = ko_ring
        zs = A.alloc([128, 512], F32); r_zs = Reg()
        dtt = A.alloc([128, 4, 8], F32); r_dtt = Reg()
        dtt2 = A.alloc([128, 4, 8], F32); r_dtt2 = Reg()
        xr_ring = Ring([(A.alloc([128, 3 + 512], F32), Reg()) for _ in range(1)])
        acc = A.alloc([128, 512], F32); r_acc = Reg()
        zs_bufs = [(zs, r_zs), (acc, r_acc)]
        dtt_bufs = [(dtt, r_dtt), (dtt2, r_dtt2)]
        xl, r_xl = xt_ring.items[0]
        xdt = A.alloc([128, 8, 64], BF16); r_xdt = Reg()
        xsT = A.alloc([128, 512], F32); r_xsT = Reg()
        Btm = A.alloc([128, 2, 128], BF16); r_Btm = Reg()
        Bm = A.alloc([128, 2, 128], BF16); r_Bm = Reg()
        cumt = A.alloc([128, 6, 8], F32); r_cumt = Reg()
        etot = A.alloc([128, 4, 8], F32); r_etot = Reg()
        dab_ring = Ring([(A.alloc([128, 128], F32), Reg()) for _ in range(2)])
        dec_ring = Ring([(A.alloc([128, 128], BF16), Reg()) for _ in range(2)])
        LT_ring = Ring([(A.alloc([128, 128], BF16), Reg()) for _ in range(2)])
        CTm = A.alloc([128, 4, 2, 32], BF16); r_CTm = Reg()
        y2s = A.alloc([128, 512], F32); r_y2s = Reg()
        yy = A.alloc([128, 512], F32); r_yy = Reg()
        ysq = sq; r_ysq = r_sq
        ss2 = A.alloc([128, 3, 2], F32); r_ss2 = Reg()
        mixs = A.alloc([128, 512], BF16); r_mixs = Reg()
        xdtt = A.alloc([128, 512], BF16); r_xdtt = Reg()
        hout = yy.rearrange("p (c n) -> p c n", c=4); r_hout = r_yy
        mark_attn = A.top
        negmask, r_negmask = load_const("c_negmask", [128, 4, 512], BF16)
        cbias0, r_cbias0 = A.alloc([128, 8, 128], BF16), Reg()
        dma("pool", cbias0[64:72], cd["c_bias0"], [], [r_cbias0])
        QT = A.alloc([128, 8, 512], BF16); r_QT = Reg(); r_QTb = Reg()
        PT_ring = Ring([(A.alloc([128, 512], BF16), Reg()) for _ in range(3)])
        attb = A.alloc([128, 4, 512], BF16); r_attb = [Reg() for _ in range(4)]
        rinv = A.alloc([128, 4], F32); r_rinv = Reg()
        m8 = A.alloc([128, 8, 8], F32); r_m8 = Reg()
        bsel = A.alloc([128, 8, 8], F32); r_bsel = Reg()
        biasm = A.alloc([128, 8, 8], BF16); r_biasm = Reg()
        ksumT = A.alloc([128, 8, 8], BF16); r_ksum = Reg()
        ksf = A.alloc([128, 8], F32); r_ksf = Reg()
        gsb = A.alloc([128, 8, 8], F32); r_gsb = Reg()
        mset(gsb, -1e30, [r_gsb])
        r_sconvT = Reg(); r_QsT = Reg(); r_KnT = Reg(); r_vb = Reg()
        print("phase A arena top", A.top)

        def qk_post(ps_ap, r_ps, P, ti, g_bc, r_g, is_k, row0):
            qs, r_qs = qs_ring.next()
            headnorm(ps_ap, r_ps, P, 8, 64, qs, r_qs, sq, r_sq, st8, r_st8)
            cos = cs[:P, ti, 0:32]
            sin = cs[:P, ti, 32:64]
            tt(tab[:P, 0, :], cos, g_bc[:P, 0:32], ALU.mult, [r_cs, r_g], [r_tab])
            tt(tab[:P, 1, :], sin, g_bc[:P, 32:64], ALU.mult, [r_cs, r_g], [r_tab])
            tt(tab[:P, 2, :], cos, g_bc[:P, 32:64], ALU.mult, [r_cs, r_g], [r_tab])
            tt(tab[:P, 3, :], sin, g_bc[:P, 0:32], ALU.mult, [r_cs, r_g], [r_tab])
            q3 = qs[:P].rearrange("p (h d) -> p h d", h=8)
            x1 = q3[:, :, 0:32]
            x2 = q3[:, :, 32:64]
            tb = [bc(tab[:P, i, :].unsqueeze(1), [P, 8, 32]) for i in range(4)]
            ko, r_ko = ko_ring.next()
            o3 = ko[:P].rearrange("p (h d) -> p h d", h=8)
            tt(rt[0][:P], x1, tb[0], ALU.mult, [r_qs, r_tab], [r_rt[0]])
            tt(rt[1][:P], x2, tb[1], ALU.mult, [r_qs, r_tab], [r_rt[1]])
            tt(o3[:, :, 0:32], rt[0][:P], rt[1][:P], ALU.subtract, [r_rt[0], r_rt[1]], [r_ko])
            tt(rt[2][:P], x2, tb[2], ALU.mult, [r_qs, r_tab], [r_rt[2]])
            tt(rt[3][:P], x1, tb[3], ALU.mult, [r_qs, r_tab], [r_rt[3]])
            tt(o3[:, :, 32:64], rt[2][:P], rt[3][:P], ALU.add, [r_rt[2], r_rt[3]], [r_ko])
            if is_k:
                dma("sp", newk_d[row0:row0 + P, :], ko[:P], [r_ko], [])
            cp(qb[:P], ko[:P], [r_ko], [r_qb], eng="act")
            return ko, r_ko

        ssd_ring = Ring([5, 7])

        def ssd_chunk(L, nseq, c0, seqs, row0, tri_c, r_tri, mneg_c, r_mn, ones_c, r_on, zs, r_zs, dtt, r_dtt):
            dt_ = dtt[:L, 1, :]
            da_ = dtt[:L, 2, :]
            bi = 7
            for c in range(4):
                tp(banks_b[bi][:L, c * 128:(c + 1) * 128], xc[:, c, c0:c0 + L], identB, [r_xc, r_idB], [bank_reg[bi]])
            psv = banks_b[bi][:L, 0:512].rearrange("p (h d) -> p h d", h=8)
            tt(xdt[:L], psv, bc(dt_.unsqueeze(2), [L, 8, 64]), ALU.mult, [bank_reg[bi], r_dtt], [r_xdt])
            cp(xsT[:L], banks_b[bi][:L, 0:512], [bank_reg[bi]], [r_xsT], eng="act")
            bi = 7
            for g in range(2):
                tp(banks_b[bi][:L, g * 128:(g + 1) * 128], xc[:, 4 + g, c0:c0 + L], identB, [r_xc, r_idB], [bank_reg[bi]])
            cp(Btm[:L], banks_b[bi][:L, 0:256].rearrange("p (g n) -> p g n", g=2), [bank_reg[bi]], [r_Btm], eng="act")
            b5 = 5
            mm(banks[b5][:L, 0:8], tri_c[:L, :L], da_, [r_tri, r_dtt], [bank_reg[b5]])
            mm(banks[b5][:L, 8:16], ones_c[:L, :L], da_, [r_on, r_dtt], [bank_reg[b5]])
            for bidx in range(nseq):
                lhs = ones[:L, :] if nseq == 1 else seqsel[:L, bidx, :]
                mm(banks[b5][:, 16 + bidx * 8:24 + bidx * 8], lhs, da_, [r_ones, r_seqsel, r_dtt], [bank_reg[b5]])
            cp(cumt[:L, 0, :], banks[b5][:L, 0:8], [bank_reg[b5]], [r_cumt])
            tsc(cumt[:L, 1, :], banks[b5][:L, 0:8], -1.0, ALU.mult, [bank_reg[b5]], [r_cumt])
            tt(cumt[:L, 2, :], banks[b5][:L, 8:16], cumt[:L, 0, :], ALU.subtract, [bank_reg[b5], r_cumt], [r_cumt])
            act(cumt[:L, 3, :], cumt[:L, 2, :], AF.Exp, [r_cumt], [r_cumt])
            act(cumt[:L, 4, :], cumt[:L, 0, :], AF.Exp, [r_cumt], [r_cumt])
            act(etot[:, 0:nseq, :], banks[b5][:, 16:16 + 8 * nseq].rearrange("p (b h) -> p b h", b=nseq), AF.Exp,
                [bank_reg[b5]], [r_etot])
            b4 = 4
            for g in range(2):
                mm(banks[b4][:L, g * 128:g * 128 + L], xc[:, 4 + g, c0:c0 + L], xc[:, 6 + g, c0:c0 + L],
                   [r_xc], [bank_reg[b4]])
            b6, b7 = 6, 7
            for h in range(8):
                g = h // 4
                bi = ssd_ring.next()
                dab, r_dab = dab_ring.next()
                cp(dab[:L, 0:L], bc(dtt[:L, 2, h:h + 1], [L, L]), [r_dtt], [r_dab])
                mm(banks[bi][:L, 0:L], dab[:L, 0:L], tri_c[:L, :L], [r_dab, r_tri], [bank_reg[bi]], start=True, stop=False)
                mm(banks[bi][:L, 0:L], identF[:L, :L], mneg_c[:L, :L], [r_idF, r_mn], [bank_reg[bi]], start=False, stop=True)
                dec, r_dec = dec_ring.next()
                act(dec[:L, :L], banks[bi][:L, 0:L], AF.Exp, [bank_reg[bi], r_cumt], [r_dec], bias=cumt[:L, 1, h:h + 1])
                LT, r_LT = LT_ring.next()
                tt(LT[:L, :L], banks[b4][:L, g * 128:g * 128 + L], dec[:L, :L], ALU.mult, [bank_reg[b4], r_dec], [r_LT])
                mm(banks[b6][:L, h * 64:(h + 1) * 64], LT[:L, :L], xdt[:L, h, :], [r_LT, r_xdt], [bank_reg[b6]])
            if nseq > 1:
                for bidx in range(nseq):
                    for g in range(2):
                        tt(CTm[:, bidx, g, :], xc[:, 6 + g, c0:c0 + L], colmask[:, bidx, :], ALU.mult,
                           [r_xc, r_colmask], [r_CTm])
            for g in range(2):
                for bidx, sq_ in enumerate(seqs):
                    lhs = xc[:, 6 + g, c0:c0 + L] if nseq == 1 else CTm[:, bidx, g, :]
                    mm(banks[b7][:L, g * 256:(g + 1) * 256], lhs, hTb[sq_][:, g * 256:(g + 1) * 256],
                       [r_xc, r_CTm, r_hTb[sq_]], [bank_reg[b7]], start=(bidx == 0), stop=(bidx == nseq - 1))
            tt(y2s[:L].rearrange("p (h d) -> p h d", h=8), banks[b7][:L, :].rearrange("p (h d) -> p h d", h=8),
               bc(cumt[:L, 4, :].unsqueeze(2), [L, 8, 64]), ALU.mult, [bank_reg[b7], r_cumt], [r_y2s])
            tt(yy[:L], banks[b6][:L, :], y2s[:L], ALU.add, [bank_reg[b6], r_y2s], [r_yy])
            tt(y2s[:L].rearrange("p (h d) -> p h d", h=8), xsT[:L].rearrange("p (h d) -> p h d", h=8),
               bc(dskip[:L, :].unsqueeze(2), [L, 8, 64]), ALU.mult, [r_xsT, r_dskip], [r_y2s])
            tt(yy[:L], yy[:L], y2s[:L], ALU.add, [r_yy, r_y2s], [r_yy])
            tt(yy[:L], yy[:L], zs[:L], ALU.mult, [r_yy, r_zs], [r_yy])
            tt(ysq[:L], yy[:L], yy[:L], ALU.mult, [r_yy], [r_ysq])
            red(ss2[:L, 0, :], ysq[:L].rearrange("p (g d) -> p g d", g=2), [r_ysq], [r_ss2])
            act(ss2[:L, 1, :], ss2[:L, 0, :], AF.Sqrt, [r_ss2, r_eps], [r_ss2], scale=1.0 / 256, bias=eps_t[:L, :])
            rcp(ss2[:L, 2, :], ss2[:L, 1, :], [r_ss2], [r_ss2])
            tt(yy[:L].rearrange("p (g d) -> p g d", g=2), yy[:L].rearrange("p (g d) -> p g d", g=2),
               bc(ss2[:L, 2, :].unsqueeze(2), [L, 2, 256]), ALU.mult, [r_yy, r_ss2], [r_yy])
            tt(mixs[:L], yy[:L], gssm[:L], ALU.mult, [r_yy, r_gssm], [r_mixs])
            dma("sp", mix_d[row0:row0 + L, 512:1024], mixs[:L], [r_mixs], [])
            tt(xdtt[:L].rearrange("p (h d) -> p h d", h=8), xdt[:L], bc(cumt[:L, 3, :].unsqueeze(2), [L, 8, 64]),
               ALU.mult, [r_xdt, r_cumt], [r_xdtt])
            for bidx, sq_ in enumerate(seqs):
                if nseq > 1:
                    tsc(Bm[:L].rearrange("p g n -> p (g n)"), Btm[:L].rearrange("p g n -> p (g n)"),
                        rowmask[:L, bidx:bidx + 1], ALU.mult, [r_Btm, r_rowmask], [r_Bm])
                    Bsrc, r_Bs = Bm, r_Bm
                else:
                    Bsrc, r_Bs = Btm, r_Btm
                bi = ssd_ring.next()
                for g in range(2):
                    mm(banks[bi][:, g * 256:(g + 1) * 256], Bsrc[:L, g, :], xdtt[:L, g * 256:(g + 1) * 256],
                       [r_Bs, r_xdtt], [bank_reg[bi]])
                h3 = hT[sq_].rearrange("p (h d) -> p h d", h=8)
                tt(h3, h3, bc(etot[:, bidx, :].unsqueeze(2), [128, 8, 64]), ALU.mult, [r_hT[sq_], r_etot], [r_hT[sq_]])
                tt(hT[sq_], hT[sq_], banks[bi][:, :], ALU.add, [r_hT[sq_], bank_reg[bi]], [r_hT[sq_]])
                cp(hTb[sq_], hT[sq_], [r_hT[sq_]], [r_hTb[sq_]], eng="act")

        def ssm_out(sq_):
            bi = psM.next()
            for c in range(4):
                tp(banks[bi][:, c * 128:(c + 1) * 128], hT[sq_][:, c * 128:(c + 1) * 128], identF, [r_hT[sq_], r_idF],
                   [bank_reg[bi]])
            cp(hout, banks[bi][:, :].rearrange("p (c n) -> p c n", c=4), [bank_reg[bi]], [r_hout], eng="act")
            dma("sp", newssm_d[sq_].rearrange("(c p) n -> p c n", p=128), hout, [r_hout], [])

        def phaseA_tile(tok0, P, nsub, nseq, is_sample, tidx):
            W = P * nsub
            L = W // nseq
            for s in range(nsub):
                xt, r_xt = xt_ring.next()
                dma("sp", xt[:P], x_d[tok0 + s * P: tok0 + (s + 1) * P, :], [], [r_xt])
                rmsnorm_fm(xt[:P], r_xt, P, g_mix, r_gmix, hnFM, r_hnFM, s * P, nwk)
            for c in range(8):
                bi = psM.next()
                for k in range(8):
                    mm(banks[bi][:, 0:W], Win[:, k, 2048 + c * 128: 2048 + (c + 1) * 128], hnFM[:, k, 0:W],
                       [r_WinB, r_hnFM], [bank_reg[bi]], start=(k == 0), stop=(k == 7))
                xr, r_xr = xr_ring.next()
                xr3 = xr[:, 0:nseq * (3 + L)].rearrange("p (b l) -> p b l", b=nseq)
                if is_sample:
                    cp(xr3[:, :, 0:3], sconvT[:, c, :].rearrange("p (b r) -> p b r", b=4), [r_sconvT], [r_xr])
                else:
                    cp(xr3[:, :, 0:3], carry[:, c, :].unsqueeze(1), [r_carry], [r_xr])
                cp(xr3[:, :, 3:3 + L], banks[bi][:, 0:W].rearrange("p (b l) -> p b l", b=nseq), [bank_reg[bi]], [r_xr],
                   eng="act")
                acc3 = acc[:, 0:W].rearrange("p (b l) -> p b l", b=nseq)
                tsc(acc3, xr3[:, :, 0:L], cw[:, c, 0:1], ALU.mult, [r_xr, r_cw, r_cb], [r_acc], s2=cb[:, c:c + 1], op1=ALU.add)
                for i in range(1, 4):
                    stt(acc3, xr3[:, :, i:i + L], cw[:, c, i:i + 1], acc3, ALU.mult, ALU.add, [r_xr, r_cw, r_acc], [r_acc])
                act(xc[:, c, 0:W], acc[:, 0:W], AF.Silu, [r_acc], [r_xc])
                if not is_sample:
                    cp(carry[:, c, :], xr[:, W:W + 3], [r_xr], [r_carry])
            def front(s):
                row0 = tok0 + s * P
                zs, r_zs = zs_bufs[s % 2]
                dtt, r_dtt = dtt_bufs[s % 2]
                ti = 16 if is_sample else (tok0 // 128 + s)
                cols = slice(s * P, (s + 1) * P)

                def proj(c0, c1, r_w):
                    bi_ = psM.next()
                    for k in range(8):
                        mm(banks[bi_][:P, 0:c1 - c0], hnFM[:, k, cols], Win[:, k, c0:c1], [r_hnFM, r_w], [bank_reg[bi_]],
                           start=(k == 0), stop=(k == 7))
                    return bi_
                bk_ = proj(512, 1024, r_WinA)
                bq_ = proj(0, 512, r_WinA)
                bv_ = proj(1024, 1536, r_WinA)
                bz_ = proj(1536, 2048, r_WinB)
                bi = bk_
                qk_post(banks[bi][:P, :], bank_reg[bi], P, ti, gk, r_gk, True, row0)
                if not is_sample:
                    bt = psT.next()
                    for h in range(8):
                        tp(banks_b[bt][0:64, h * 128:(h + 1) * 128], qb[:, h * 64:(h + 1) * 64], identB, [r_qb, r_idB],
                           [bank_reg[bt]])
                    kti = tok0 // 128 + s
                    cp(KT[0:64, :, kti * 128:(kti + 1) * 128], banks_b[bt][0:64, 0:1024].rearrange("p (h q) -> p h q", h=8),
                       [bank_reg[bt]], [r_KT[kti]], eng="act")
                else:
                    bt = psT.next()
                    for c in range(4):
                        tp(banks_b[bt][:, c * 32:(c + 1) * 32], qb[:32, c * 128:(c + 1) * 128], identB[:32, :32],
                           [r_qb, r_idB], [bank_reg[bt]])
                    cp(KnT, banks_b[bt][:, 0:128].rearrange("p (c q) -> p c q", c=4), [bank_reg[bt]], [r_KnT], eng="act")
                bi = bq_
                qk_post(banks[bi][:P, :], bank_reg[bi], P, ti, gq, r_gq, False, row0)
                if not is_sample:
                    bt = psT.next()
                    for h in range(8):
                        tp(banks_b[bt][0:64, h * 128:(h + 1) * 128], qb[:, h * 64:(h + 1) * 64], identB, [r_qb, r_idB],
                           [bank_reg[bt]])
                    cp(QT[0:64, :, cols], banks_b[bt][0:64, 0:1024].rearrange("p (h q) -> p h q", h=8),
                       [bank_reg[bt]], [r_QT], eng="act")
                else:
                    bt = psT.next()
                    for c in range(4):
                        tp(banks_b[bt][:, c * 32:(c + 1) * 32], qb[:32, c * 128:(c + 1) * 128], identB[:32, :32],
                           [r_qb, r_idB], [bank_reg[bt]])
                    cp(QsT, banks_b[bt][:, 0:128].rearrange("p (c q) -> p c q", c=4), [bank_reg[bt]], [r_QsT], eng="act")
                bi = bv_
                vs, r_vs = vs_ring.next()
                cp(vs[:P], banks[bi][:P, :], [bank_reg[bi]], [r_vs], eng="act")
                dma("sp", newv_d[row0:row0 + P, :], vs[:P], [r_vs], [])
                if not is_sample:
                    kti = tok0 // 128 + s
                    cp(VA[:, kti, :, 0:64], banks[bi][:, :].rearrange("p (h d) -> p h d", h=8), [bank_reg[bi]], [r_VA[kti]])
                else:
                    cp(vb, banks[bi][:32, :], [bank_reg[bi]], [r_vb])
                bi = bz_
                act(zs[:P], banks[bi][:P, :], AF.Silu, [bank_reg[bi]], [r_zs])
                bi = proj(3072, 3080, r_WinB)
                tt(dtt[:P, 0, :], banks[bi][:P, 0:8], dtb[:P], ALU.add, [bank_reg[bi], r_dtb], [r_dtt])
                act(dtt[:P, 0, :], dtt[:P, 0, :], AF.Exp, [r_dtt], [r_dtt])
                act(dtt[:P, 1, :], dtt[:P, 0, :], AF.Ln, [r_dtt, r_one], [r_dtt], bias=one_t[:P, :])
                tt(dtt[:P, 2, :], dtt[:P, 1, :], a_bc[:P], ALU.mult, [r_dtt, r_abc], [r_dtt])
                if is_sample or (tok0 == 1536 and s == 3):
                    for half in range(2):
                        bi = proj(2048 + half * 512, 2048 + (half + 1) * 512, r_WinB)
                        cp(xl[:P, half * 512:(half + 1) * 512], banks[bi][:P, :], [bank_reg[bi]], [r_xl], eng="act")
                    if is_sample:
                        for b_ in range(4):
                            dma("sp", newconv_d[3 + 3 * b_: 6 + 3 * b_, :], xl[8 * b_ + 5: 8 * b_ + 8, :], [r_xl], [])
                    else:
                        dma("sp", newconv_d[0:3, :], xl[125:128, :], [r_xl], [])
            def back(s):
                row0 = tok0 + s * P
                zs, r_zs = zs_bufs[s % 2]
                dtt, r_dtt = dtt_bufs[s % 2]
                if is_sample:
                    ssd_chunk(32, 4, 0, [1, 2, 3, 4], row0, triU_s, r_triUs, mneg_s, r_mnegs, onesbd_s, r_onesbds, zs, r_zs, dtt, r_dtt)
                else:
                    ssd_chunk(128, 1, s * 128, [0], row0, triU, r_triU, mneg, r_mneg, ones, r_ones, zs, r_zs, dtt, r_dtt)

            f_ops = S.record(lambda: front(0))
            S.replay(f_ops)
            for s in range(nsub):
                b_ops = S.record(lambda: back(s))
                if s + 1 < nsub:
                    f_ops = S.record(lambda: front(s + 1))
                    S.replay(f_ops, b_ops)
                else:
                    S.replay(b_ops)

        def attention_tile(i):
            tok0 = i * 512
            for j in (2 * i, 2 * i + 1):
                red(ksf[0:64, :], KT[0:64, :, j * 256:(j + 1) * 256], [r_KT[2 * j], r_KT[2 * j + 1]], [r_ksf])
                cp(ksumT[0:64, :, j], ksf[0:64, :], [r_ksf], [r_ksum])
            for s in range(4):
                own = (tok0 + s * 128) // 256
                cols = slice(s * 128, (s + 1) * 128)
                if own <= 3:
                    cp(QT[64:72, :, cols], bc(cbias0[64:72, own, :].unsqueeze(1), [8, 8, 128]), [r_cbias0], [r_QTb])
                else:
                    bi = psM.next()
                    for h in range(8):
                        mm(banks[bi][:, h * 8:h * 8 + own], QT[0:64, h, cols], ksumT[0:64, h, 0:own], [r_QT, r_ksum],
                           [bank_reg[bi]])
                    cp(gsb[:, :, 0:own], banks[bi][:, 0:64].rearrange("p (h j) -> p h j", h=8)[:, :, 0:own],
                       [bank_reg[bi]], [r_gsb])
                    for h in range(8):
                        S.op("dve", (lambda o_, i_: (lambda e: e.max(out=o_, in_=i_)))(m8[:, h, :], gsb[:, h, :]),
                             [r_gsb], [r_m8])
                    tt(bsel, gsb, bc(m8[:, :, 2:3], [128, 8, 8]), ALU.is_lt, [r_gsb, r_m8], [r_bsel])
                    tsc(biasm, bsel, NEG, ALU.mult, [r_bsel], [r_biasm])
                    mset(biasm[:, :, own:own + 1], 0.0, [r_biasm])
                    bt = psT.next()
                    for h in range(8):
                        tp(banks_b[bt][64:72, h * 128:(h + 1) * 128], biasm[:, h, :], identB, [r_biasm, r_idB],
                           [bank_reg[bt]])
                    cp(QT[64:72, :, cols], banks_b[bt][64:72, 0:1024].rearrange("p (h q) -> p h q", h=8),
                       [bank_reg[bt]], [r_QTb], eng="act")
            obank = [4, 5, 6, 7]
            nk = 4 * i + 4
            units = [(h, kc) for h in range(8) for kc in range(nk)]

            def emit_qk(h, kc):
                diag = kc >= 4 * i
                bi = psM.next()
                mm(banks[bi][:, :], KT[0:72, h, kc * 128:(kc + 1) * 128], QT[0:72, h, :],
                   [r_KT[kc], r_KTind, r_QT, r_QTb], [bank_reg[bi]], start=True, stop=not diag)
                if diag:
                    mm(banks[bi][:, :], identB, negmask[:, kc - 4 * i, :], [r_idB, r_negmask], [bank_reg[bi]],
                       start=False, stop=True)
                PT, r_PT = PT_ring.next()
                act(PT, banks[bi][:, :], AF.Exp, [bank_reg[bi]], [r_PT], scale=0.125)
                return PT, r_PT

            def emit_pv(h, kc, PT, r_PT):
                for s in range(4):
                    last = 4 * i + s
                    if kc > last:
                        continue
                    ob = obank[s]
                    mm(banks[ob][:, 0:65], PT[:, s * 128:(s + 1) * 128], VA[:, kc, h, 0:65], [r_PT, r_VA[kc], r_VAone],
                       [bank_reg[ob]], start=(kc == 0), stop=(kc == last))
                if kc == nk - 1:
                    for s in range(4):
                        ob = obank[s]
                        rcp(rinv[:, s:s + 1], banks[ob][:, 64:65], [bank_reg[ob]], [r_rinv])
                        tsc(attb[:, s, h * 64:(h + 1) * 64], banks[ob][:, 0:64], rinv[:, s:s + 1], ALU.mult,
                            [bank_reg[ob], r_rinv], [r_attb[s]])

            pend = None
            for (h, kc) in units:
                cur = emit_qk(h, kc)
                if pend is not None:
                    emit_pv(*pend)
                pend = (h, kc) + cur
            emit_pv(*pend)
            for s in range(4):
                dma("sp", mix_d[tok0 + s * 128: tok0 + (s + 1) * 128, 0:512], attb[:, s, :], [r_attb[s]], [])

        if "A" in stages:
            for i in range(int(os.environ.get('K_NTILES', '4'))):
                phaseA_tile(i * 512, 128, 4, 1, False, i)
                if "B" in stages:
                    attention_tile(i)
            ssm_out(0)
            if os.environ.get('K_SAMPLE', '1') == '1':
                S.barrier()
                A.top = mark_attn
                for _ in range(4):
                    hT.append(A.alloc([128, 512], F32))
                    hTb.append(A.alloc([128, 512], BF16))
                sconvT = A.alloc([128, 8, 12], F32)
                sct, r_sct = xt_ring.next()
                dma("sp", sct[:12], sconv_d, [], [r_sct])
                bi = psM.next()
                for c in range(8):
                    tp(banks[bi][:, c * 12:(c + 1) * 12], sct[:12, c * 128:(c + 1) * 128], identF[:12, :12], [r_sct, r_idF],
                       [bank_reg[bi]])
                cp(sconvT, banks[bi][:, 0:96].rearrange("p (c r) -> p c r", c=8), [bank_reg[bi]], [r_sconvT], eng="act")
                for b_ in range(4):
                    dma("sp", hout, sssm_d[b_].rearrange("(c p) n -> p c n", p=128), [], [r_hout])
                    bi = psM.next()
                    for c in range(4):
                        tp(banks[bi][:, c * 128:(c + 1) * 128], hout[:, c, :], identF, [r_hout, r_idF], [bank_reg[bi]])
                    cp(hT[1 + b_], banks[bi][:, :], [bank_reg[bi]], [r_hT[1 + b_]], eng="act")
                    cp(hTb[1 + b_], banks[bi][:, :], [bank_reg[bi]], [r_hTb[1 + b_]])
                phaseA_tile(2048, 32, 1, 4, True, 4)
                for b_ in range(4):
                    ssm_out(1 + b_)


        if "S" in stages:
            S.barrier()
            A.top = markA
            psT = Ring([0, 1, 2, 3]); psM = psT
            KTall = A.alloc([128, 4, 8192], BF16); r_KTall = [Reg() for _ in range(64)]
            Kp_ring = Ring([(A.alloc([128, 512], BF16), Reg()) for _ in range(6)])
            Vp_ring = Ring([(A.alloc([128, 512], BF16), Reg()) for _ in range(6)])
            Zc, r_Zc = load_const("c_Z", [128, 63], BF16)
            selE, r_selE = load_const("c_selE", [32, 32, 128], BF16)
            bdmask, r_bd = load_const("c_bdmask", [64, 512], F32)
            negcs, r_negcs = load_const("c_negcs", [32, 4, 64], BF16)
            iota, r_iota = load_const("c_iota", [128, 1], F32)
            onesB = A.alloc([128, 2], BF16); r_onesB = Reg()
            mset(onesB, 1.0, [r_onesB])
            pti = A.alloc([128, 64], I32); r_pti = Reg()
            ptf = A.alloc([128, 64], F32); r_ptf = Reg()
            idxi = A.alloc([128, 64], I32); r_idx = Reg()
            Qbd = A.alloc([128, 4, 64], BF16); r_Qbd = Reg()
            mset(Qbd, 0.0, [r_Qbd])
            km = A.alloc([32, 512], F32); r_km = Reg()
            kmT = A.alloc([128, 4, 32], BF16); r_kmT = Reg()
            gs = A.alloc([64, 32], F32); r_gs = Reg()
            m8s = A.alloc([64, 8], F32); r_m8s = Reg()
            bm = A.alloc([64, 32], BF16); r_bm = Reg()
            biasT = A.alloc([32, 64], BF16); r_biasT = Reg()
            PTs_ring = Ring([(A.alloc([128, 64], BF16), Reg()) for _ in range(4)])
            PTn = A.alloc([32, 64], BF16); r_PTn = Reg()
            om = A.alloc([64, 512], F32); r_om = Reg()
            osm = A.alloc([64, 64], F32); r_osm = Reg()
            rl = A.alloc([64, 1], F32); r_rl = Reg()
            o2 = A.alloc([64, 64], BF16); r_o2 = Reg()
            for b in range(4):
                dma("sp", pti, pt_d[b:b + 1, :].to_broadcast([128, 64]), [], [r_pti])
                cp(ptf, pti, [r_pti], [r_ptf])
                tsc(ptf, ptf, 128.0, ALU.mult, [r_ptf, r_iota], [r_ptf], s2=iota[:, 0:1], op1=ALU.add)
                cp(idxi, ptf, [r_ptf], [r_idx])
                for c in range(4):
                    cp(Qbd[0:64, c, (2 * c) * 8:(2 * c) * 8 + 8], QsT[0:64, c, b * 8:(b + 1) * 8], [r_QsT], [r_Qbd])
                    cp(Qbd[64:128, c, (2 * c + 1) * 8:(2 * c + 1) * 8 + 8], QsT[64:128, c, b * 8:(b + 1) * 8], [r_QsT], [r_Qbd])
                for pg in range(64):
                    j = pg // 2
                    Kp, r_Kp = Kp_ring.next()
                    gather(Kp, ck_d, idxi[:, pg:pg + 1], [r_idx], [r_Kp])
                    mm(banks[4][0:32, :], Zc[:, 31 - j:63 - j], Kp, [r_Zc, r_Kp], [bank_reg[4]], start=(pg == 0), stop=(pg == 63))
                    bt = psT.next()
                    for c in range(4):
                        tp(banks_b[bt][:, c * 128:(c + 1) * 128], Kp[:, c * 128:(c + 1) * 128], identB, [r_Kp, r_idB], [bank_reg[bt]])
                    cp(KTall[:, :, pg * 128:(pg + 1) * 128], banks_b[bt][:, 0:512].rearrange("p (c q) -> p c q", c=4),
                       [bank_reg[bt]], [r_KTall[pg]], eng=("act" if pg % 2 == 0 else "dve"))
                cp(km, banks[4][0:32, :], [bank_reg[4]], [r_km], eng="act")
                for c in range(4):
                    tp(banks[7][:, c * 32:(c + 1) * 32], km[0:32, c * 128:(c + 1) * 128], identF[0:32, 0:32], [r_km, r_idF], [bank_reg[7]])
                cp(kmT, banks[7][:, 0:128].rearrange("p (c j) -> p c j", c=4), [bank_reg[7]], [r_kmT], eng="act")
                for c in range(4):
                    mm(banks[7][0:64, 256:288], Qbd[:, c, :], kmT[:, c, :], [r_Qbd, r_kmT], [bank_reg[7]], start=(c == 0), stop=(c == 3))
                cp(gs, banks[7][0:64, 256:288], [bank_reg[7]], [r_gs])
                S.op("dve", lambda e: e.max(out=m8s, in_=gs), [r_gs], [r_m8s])
                tsc(bm, gs, m8s[:, 2:3], ALU.is_lt, [r_gs, r_m8s], [r_bm], s2=NEG, op1=ALU.mult)
                bt = psT.next()
                tp(banks_b[bt][0:32, 0:64], bm[0:64, 0:32], identB[0:64, 0:64], [r_bm, r_idB], [bank_reg[bt]])
                cp(biasT, banks_b[bt][0:32, 0:64], [bank_reg[bt]], [r_biasT], eng="act")
                def s_qk(pg):
                    j = pg // 2
                    bi = psM.next()
                    mm(banks[bi][:, 0:64], selE[0:32, j, :], biasT[0:32, :], [r_selE, r_biasT], [bank_reg[bi]], start=True, stop=False)
                    for c in range(4):
                        mm(banks[bi][:, c * 16:(c + 1) * 16], KTall[:, c, pg * 128:(pg + 1) * 128], Qbd[:, c, c * 16:(c + 1) * 16],
                           [r_KTall[pg], r_Qbd], [bank_reg[bi]], start=False, stop=(c == 3))
                    PTs, r_PTs = PTs_ring.next()
                    act(PTs, banks[bi][:, 0:64], AF.Exp, [bank_reg[bi]], [r_PTs], scale=0.125)
                    return PTs, r_PTs

                def s_pv(pg, PTs, r_PTs):
                    Vp, r_Vp = Vp_ring.next()
                    gather(Vp, cv_d, idxi[:, pg:pg + 1], [r_idx], [r_Vp])
                    mm(banks[5][0:64, :], PTs, Vp, [r_PTs, r_Vp], [bank_reg[5]], start=(pg == 0), stop=False)
                    mm(banks[6][0:64, 0:2], PTs, onesB[:, 0:2], [r_PTs, r_onesB], [bank_reg[6]], start=(pg == 0), stop=False)

                pend = None
                for pg in range(64):
                    cur = s_qk(pg)
                    if pend is not None:
                        s_pv(*pend)
                    pend = (pg,) + cur
                s_pv(*pend)
                bi = psM.next()
                mm(banks[bi][0:32, 0:64], identB[0:32, 0:32], negcs[0:32, b, :], [r_idB, r_negcs], [bank_reg[bi]], start=True, stop=False)
                for c in range(4):
                    mm(banks[bi][0:32, c * 16:(c + 1) * 16], KnT[:, c, 0:32], Qbd[:, c, c * 16:(c + 1) * 16], [r_KnT, r_Qbd],
                       [bank_reg[bi]], start=False, stop=(c == 3))
                act(PTn, banks[bi][0:32, 0:64], AF.Exp, [bank_reg[bi]], [r_PTn], scale=0.125)
                mm(banks[5][0:64, :], PTn, vb[0:32, :], [r_PTn, r_vb], [bank_reg[5]], start=False, stop=True)
                mm(banks[6][0:64, 0:2], PTn, onesB[0:32, 0:2], [r_PTn, r_onesB], [bank_reg[6]], start=False, stop=True)
                tt(om, banks[5][0:64, :], bdmask, ALU.mult, [bank_reg[5], r_bd], [r_om])
                red(osm, om.rearrange("p (h d) -> p d h", h=8), [r_om], [r_osm])
                rcp(rl, banks[6][0:64, 0:1], [bank_reg[6]], [r_rl])
                tsc(o2, osm, rl[:, 0:1], ALU.mult, [r_osm, r_rl], [r_o2])
                for h in range(8):
                    dma("sp", mix_d[2048 + 8 * b: 2048 + 8 * b + 8, h * 64:(h + 1) * 64], o2[h * 8:(h + 1) * 8, :], [r_o2], [])

        if "C" in stages:
            S.barrier()
            A.top = markA
            psT = Ring([0, 1, 2, 3]); psM = psT
            wout = A.alloc([128, 8, 1024], BF16); r_wout = Reg()
            dma("pool", wout, w_out_d.rearrange("(k p) c -> p k c", p=128), [], [r_wout])
            wq = A.alloc([128, 8, 512], BF16); r_wq = Reg()
            dma("pool", wq, wq_d.rearrange("(k p) c -> p k c", p=128), [], [r_wq])
            wo = A.alloc([128, 4, 1024], BF16); r_wo = Reg()
            dma("pool", wo, wo_d.rearrange("(k p) c -> p k c", p=128), [], [r_wo])
            g_x, r_gx = load_in(g_x_d, [128, 1024])
            g_mlp, r_gmlp = load_in(g_mlp_d, [128, 1024])
            gqx, r_gqx = load_in(gqx_d, [128, 128])
            gkx, r_gkx = load_in(gkx_d, [128, 128])
            MKT = A.alloc([128, 5, 4, 256], BF16); r_MKT = Reg()
            MV = A.alloc([128, 5, 2, 4, 130], BF16); r_MV = Reg(); r_MVone = Reg()
            mset(MV[:, :, :, :, 128:129], 1.0, [r_MVone])
            xt = A.alloc([128, 1024], F32); r_xt = Reg()
            junk = A.alloc([128, 1024], BF16); r_junk = Reg()
            hn = A.alloc([128, 1024], BF16); r_hn = Reg()
            ssq = A.alloc([128, 4], F32); r_ssq = Reg()
            nwk = (junk, r_junk, hn, r_hn, ssq, r_ssq, psT)
            qs = A.alloc([128, 512], F32); r_qs = Reg()
            sq = A.alloc([128, 512], F32); r_sq = Reg()
            st8 = A.alloc([128, 3, 8], F32); r_st8 = Reg()
            mb = A.alloc([128, 512], BF16); r_mb = Reg()
            markM = A.top
            wk = A.alloc([128, 8, 512], BF16); r_wk = Reg()
            dma("pool", wk, wk_d.rearrange("(k p) c -> p k c", p=128), [], [r_wk])
            wv = A.alloc([128, 8, 512], BF16); r_wv = Reg()
            dma("pool", wv, wv_d.rearrange("(k p) c -> p k c", p=128), [], [r_wv])
            g_mem, r_gmem = load_in(g_mem_d, [128, 1024])
            mnFM = A.alloc([128, 8, 256], BF16); r_mnFM = Reg()
            vs = A.alloc([128, 512], F32); r_vs = Reg()
            for mc in range(2):
                dma("sp", xt, mem_d[mc * 128:(mc + 1) * 128, :], [], [r_xt])
                rmsnorm_fm(xt, r_xt, 128, g_mem, r_gmem, mnFM, r_mnFM, mc * 128, nwk)
            for mc in range(2):
                bi = psM.next()
                for k in range(8):
                    mm(banks[bi][:, :], mnFM[:, k, mc * 128:(mc + 1) * 128], wk[:, k, :], [r_mnFM, r_wk], [bank_reg[bi]],
                       start=(k == 0), stop=(k == 7))
                headnorm(banks[bi][:, :], bank_reg[bi], 128, 4, 128, qs, r_qs, sq, r_sq, st8, r_st8)
                q3 = qs.rearrange("p (h d) -> p h d", h=4)
                tt(q3, q3, bc(gkx.unsqueeze(1), [128, 4, 128]), ALU.mult, [r_qs, r_gkx], [r_qs])
                dma("sp", newmk_d[mc * 128:(mc + 1) * 128, :], qs, [r_qs], [])
                cp(mb, qs, [r_qs], [r_mb])
                bt = psT.next()
                for h in range(4):
                    tp(banks_b[bt][:, h * 128:(h + 1) * 128], mb[:, h * 128:(h + 1) * 128], identB, [r_mb, r_idB], [bank_reg[bt]])
                cp(MKT[:, 0, :, mc * 128:(mc + 1) * 128], banks_b[bt][:, 0:512].rearrange("p (h q) -> p h q", h=4),
                   [bank_reg[bt]], [r_MKT], eng="act")
                bi = psM.next()
                for k in range(8):
                    mm(banks[bi][:, :], mnFM[:, k, mc * 128:(mc + 1) * 128], wv[:, k, :], [r_mnFM, r_wv], [bank_reg[bi]],
                       start=(k == 0), stop=(k == 7))
                cp(vs, banks[bi][:, :], [bank_reg[bi]], [r_vs], eng="act")
                dma("sp", newmv_d[mc * 128:(mc + 1) * 128, :], vs, [r_vs], [])
                cp(MV[:, 0, mc, :, 0:128], banks[bi][:, :].rearrange("p (h d) -> p h d", h=4), [bank_reg[bi]], [r_MV])
            for b in range(4):
                for mc in range(2):
                    dma("pool", mb, cmk_d[b, mc * 128:(mc + 1) * 128, :], [], [r_mb])
                    bt = psT.next()
                    for h in range(4):
                        tp(banks_b[bt][:, h * 128:(h + 1) * 128], mb[:, h * 128:(h + 1) * 128], identB, [r_mb, r_idB], [bank_reg[bt]])
                    cp(MKT[:, 1 + b, :, mc * 128:(mc + 1) * 128], banks_b[bt][:, 0:512].rearrange("p (h q) -> p h q", h=4),
                       [bank_reg[bt]], [r_MKT], eng="act")
                    dma("pool", MV[:, 1 + b, mc, :, 0:128], cmv_d[b, mc * 128:(mc + 1) * 128, :].rearrange("p (h d) -> p h d", h=4),
                        [], [r_MV])
            S.barrier()
            A.top = markM
            xr4 = A.alloc([128, 4, 1024], F32); r_xr4 = [Reg() for _ in range(4)]
            mt = A.alloc([128, 1024], BF16); r_mt = Reg()
            mixFM = A.alloc([128, 8, 512], BF16); r_mixFM = Reg()
            hxFM = A.alloc([128, 8, 512], BF16); r_hxFM = Reg()
            QxT = A.alloc([128, 4, 512], BF16); r_QxT = Reg()
            oxb = A.alloc([128, 4, 512], BF16); r_oxb = [Reg() for _ in range(4)]
            oxFM = A.alloc([128, 4, 512], BF16); r_oxFM = Reg()
            PTx_ring = Ring([(A.alloc([128, 512], BF16), Reg()) for _ in range(2)])
            PTxb = A.alloc([128, 4, 32], BF16); r_PTxb = Reg()
            mset(PTxb, 0.0, [r_PTxb])
            rinv = A.alloc([128, 4], F32); r_rinv = Reg()
            print("phase C arena top", A.top)
            obank = [4, 5, 6, 7]
            xscale = float(128 ** -0.5)
            wup_v = wup_d.rearrange("(k p) c -> p k c", p=128)

            def phaseC_tile(tok0, P, nsub, is_sample):
                W = P * nsub
                for s in range(nsub):
                    rows = slice(tok0 + s * P, tok0 + (s + 1) * P)
                    cols = slice(s * P, (s + 1) * P)
                    xs_ = xr4[:P, s, :]
                    dma("sp", mt[:P], mix_d[rows, :], [], [r_mt])
                    dma("sp", xs_, x_d[rows, :], [], [r_xr4[s]])
                    bt = psT.next()
                    for k in range(8):
                        tp(banks_b[bt][:, k * P:(k + 1) * P], mt[:P, k * 128:(k + 1) * 128], identB[:P, :P], [r_mt, r_idB], [bank_reg[bt]])
                    cp(mixFM[:, :, cols], banks_b[bt][:, 0:8 * P].rearrange("p (k q) -> p k q", k=8), [bank_reg[bt]], [r_mixFM], eng="act")
                    for half in range(2):
                        hc = slice(half * 512, (half + 1) * 512)
                        bi = psM.next()
                        for k in range(8):
                            mm(banks[bi][:P, :], mixFM[:, k, cols], wout[:, k, hc], [r_mixFM, r_wout], [bank_reg[bi]],
                               start=(k == 0), stop=(k == 7))
                        tt(xr4[:P, s, hc], banks[bi][:P, :], xr4[:P, s, hc], ALU.add, [bank_reg[bi], r_xr4[s]], [r_xr4[s]])
                    rmsnorm_fm(xs_, r_xr4[s], P, g_x, r_gx, hxFM, r_hxFM, s * P, nwk)
                    bi = psM.next()
                    for k in range(8):
                        mm(banks[bi][:P, :], hxFM[:, k, cols], wq[:, k, :], [r_hxFM, r_wq], [bank_reg[bi]], start=(k == 0), stop=(k == 7))
                    headnorm(banks[bi][:P, :], bank_reg[bi], P, 4, 128, qs, r_qs, sq, r_sq, st8, r_st8)
                    q3 = qs[:P].rearrange("p (h d) -> p h d", h=4)
                    tt(q3, q3, bc(gqx[:P].unsqueeze(1), [P, 4, 128]), ALU.mult, [r_qs, r_gqx], [r_qs])
                    cp(mb[:P], qs[:P], [r_qs], [r_mb])
                    bt = psT.next()
                    for h in range(4):
                        tp(banks_b[bt][:, h * P:(h + 1) * P], mb[:P, h * 128:(h + 1) * 128], identB[:P, :P], [r_mb, r_idB], [bank_reg[bt]])
                    cp(QxT[:, :, cols], banks_b[bt][:, 0:4 * P].rearrange("p (h q) -> p h q", h=4), [bank_reg[bt]], [r_QxT], eng="act")
                for h in range(4):
                    for mc in range(2):
                        mcs = slice(mc * 128, (mc + 1) * 128)
                        bi = psM.next()
                        if not is_sample:
                            mm(banks[bi][:, 0:W], MKT[:, 0, h, mcs], QxT[:, h, 0:W], [r_MKT, r_QxT], [bank_reg[bi]])
                            PTx, r_PTx = PTx_ring.next()
                            act(PTx[:, 0:W], banks[bi][:, 0:W], AF.Exp, [bank_reg[bi]], [r_PTx], scale=xscale)
                            for s in range(nsub):
                                mm(banks[obank[s]][:P, 0:129], PTx[:, s * P:(s + 1) * P], MV[:, 0, mc, h, 0:129], [r_PTx, r_MV, r_MVone],
                                   [bank_reg[obank[s]]], start=(mc == 0), stop=(mc == 1))
                        else:
                            for b in range(4):
                                mm(banks[bi][:, b * 8:(b + 1) * 8], MKT[:, 1 + b, h, mcs], QxT[:, h, b * 8:(b + 1) * 8], [r_MKT, r_QxT],
                                   [bank_reg[bi]])
                            for b in range(4):
                                act(PTxb[:, b, b * 8:(b + 1) * 8], banks[bi][:, b * 8:(b + 1) * 8], AF.Exp, [bank_reg[bi]], [r_PTxb],
                                    scale=xscale)
                            for b in range(4):
                                mm(banks[obank[0]][:32, 0:129], PTxb[:, b, :], MV[:, 1 + b, mc, h, 0:129], [r_PTxb, r_MV, r_MVone],
                                   [bank_reg[obank[0]]], start=(mc == 0 and b == 0), stop=(mc == 1 and b == 3))
                    for s in range(nsub):
                        ob = obank[s]
                        rcp(rinv[:P, s:s + 1], banks[ob][:P, 128:129], [bank_reg[ob]], [r_rinv])
                        tsc(oxb[:P, s, h * 128:(h + 1) * 128], banks[ob][:P, 0:128], rinv[:P, s:s + 1], ALU.mult,
                            [bank_reg[ob], r_rinv], [r_oxb[s]])
                for s in range(nsub):
                    cols = slice(s * P, (s + 1) * P)
                    bt = psT.next()
                    for h in range(4):
                        tp(banks_b[bt][:, h * P:(h + 1) * P], oxb[:P, s, h * 128:(h + 1) * 128], identB[:P, :P], [r_oxb[s], r_idB],
                           [bank_reg[bt]])
                    cp(oxFM[:, :, cols], banks_b[bt][:, 0:4 * P].rearrange("p (h q) -> p h q", h=4), [bank_reg[bt]], [r_oxFM], eng="act")
                    for half in range(2):
                        hc = slice(half * 512, (half + 1) * 512)
                        bi = psM.next()
                        for k in range(4):
                            mm(banks[bi][:P, :], oxFM[:, k, cols], wo[:, k, hc], [r_oxFM, r_wo], [bank_reg[bi]], start=(k == 0), stop=(k == 3))
                        tt(xr4[:P, s, hc], banks[bi][:P, :], xr4[:P, s, hc], ALU.add, [bank_reg[bi], r_xr4[s]], [r_xr4[s]])
                    rmsnorm_fm(xr4[:P, s, :], r_xr4[s], P, g_mlp, r_gmlp, hxFM, r_hxFM, s * P, nwk)
                r_x2 = [Reg() for _ in range(nsub)]
                for s in range(nsub):
                    dma("sp", y_d[tok0 + s * P: tok0 + (s + 1) * P, :], xr4[:P, s, :], [r_xr4[s]], [r_y2[(tok0 // 128) + s]])
                dma("sp", hm_d[:, :, tok0:tok0 + W], hxFM[:, :, 0:W], [r_hxFM], [r_hmd[tok0 // 512]])

            r_y2 = [Reg() for _ in range(17)]
            r_hmd = [Reg() for _ in range(5)]
            for i in range(int(os.environ.get('K_NTILES', '4'))):
                phaseC_tile(i * 512, 128, 4, False)
            if os.environ.get('K_SAMPLE', '1') == '1':
                phaseC_tile(2048, 32, 1, True)

            S.barrier()
            A.top = markA
            psM = Ring([0, 1, 2, 3])
            wup_sb = A.alloc([128, 8, 4096], BF16); r_wup = [Reg() for _ in range(8)]
            wdn_sb = A.alloc([128, 32, 1024], BF16); r_wdn = [Reg() for _ in range(8)]
            wdn_v = wdn_d.rearrange("(f p) c -> p f c", p=128)
            for fg in range(8):
                dma("pool", wup_sb[:, :, fg * 512:(fg + 1) * 512], wup_v[:, :, fg * 512:(fg + 1) * 512], [], [r_wup[fg]])
                dma("pool", wdn_sb[:, fg * 4:(fg + 1) * 4, :], wdn_v[:, fg * 4:(fg + 1) * 4, :], [], [r_wdn[fg]])
            hm_ring = Ring([(A.alloc([128, 8, 512], BF16), Reg()) for _ in range(2)])
            x2_ring = Ring([(A.alloc([128, 4, 1024], F32), [Reg() for _ in range(4)]) for _ in range(1)])
            hF_ring = Ring([(A.alloc([128, 4, 512], BF16), Reg()) for _ in range(2)])
            tmp_ring = Ring([(A.alloc([128, 512], BF16), Reg()) for _ in range(2)])
            print("phase D arena top", A.top)

            def phaseD_tile(tok0, P, nsub):
                W = P * nsub
                hm, r_hm = hm_ring.next()
                dma("sp", hm[:, :, 0:W], hm_d[:, :, tok0:tok0 + W], [r_hmd[tok0 // 512]], [r_hm])
                x2, r_x2 = x2_ring.next()
                for s in range(nsub):
                    dma("sp", x2[:P, s, :], y_d[tok0 + s * P: tok0 + (s + 1) * P, :], [r_y2[(tok0 // 128) + s]], [r_x2[s]])
                for fg in range(8):
                    hF, r_hF = hF_ring.next()
                    for f4 in range(4):
                        f = fg * 4 + f4
                        bi = psM.next()
                        for k in range(8):
                            mm(banks[bi][:, 0:W], wup_sb[:, k, f * 128:(f + 1) * 128], hm[:, k, 0:W], [r_wup[fg], r_hm], [bank_reg[bi]],
                               start=(k == 0), stop=(k == 7))
                        tmpr, r_tmpr = tmp_ring.next()
                        act(tmpr[:, 0:W], banks[bi][:, 0:W], AF.Relu, [bank_reg[bi]], [r_tmpr])
                        tt(hF[:, f4, 0:W], tmpr[:, 0:W], tmpr[:, 0:W], ALU.mult, [r_tmpr], [r_hF])
                    for s in range(nsub):
                        for half in range(2):
                            hc = slice(half * 512, (half + 1) * 512)
                            bi = obank[(s * 2 + half) % 4]
                            for f4 in range(4):
                                mm(banks[bi][:P, :], hF[:, f4, s * P:(s + 1) * P], wdn_sb[:, fg * 4 + f4, hc], [r_hF, r_wdn[fg]],
                                   [bank_reg[bi]], start=(f4 == 0), stop=(f4 == 3))
                            tt(x2[:P, s, hc], banks[bi][:P, :], x2[:P, s, hc], ALU.add, [bank_reg[bi], r_x2[s]], [r_x2[s]])
                for s in range(nsub):
                    dma("sp", y_d[tok0 + s * P: tok0 + (s + 1) * P, :], x2[:P, s, :], [r_x2[s]], [r_y2[(tok0 // 128) + s]])

            for i in range(int(os.environ.get('K_NTILES', '4'))):
                phaseD_tile(i * 512, 128, 4)
            if os.environ.get('K_SAMPLE', '1') == '1':
                phaseD_tile(2048, 32, 1)

        S.run(st)
    return nc


def prep_inputs(inputs, consts):
    f32 = np.float32
    xp = np.asarray(inputs["x_prompt"], f32)
    xs = np.asarray(inputs["x_sample"], f32)
    ck = np.ascontiguousarray(np.asarray(inputs["cache_k"], f32)[0].reshape(NPOOL * 128, 512))
    cv = np.ascontiguousarray(np.asarray(inputs["cache_v"], f32)[0].reshape(NPOOL * 128, 512))
    pt = np.asarray(inputs["page_table"], np.int32)

    def bcast(a, n=128):
        a = np.asarray(a, f32).reshape(1, -1)
        return np.ascontiguousarray(np.broadcast_to(a, (n, a.shape[1])))

    shared = {
        "ck": ck, "cv": cv,
        "w_in": np.ascontiguousarray(inputs["w_in"][0], f32), "w_out": np.ascontiguousarray(inputs["w_out"][0], f32),
        "wq_x": np.ascontiguousarray(inputs["wq_x"][0], f32), "wk_x": np.ascontiguousarray(inputs["wk_x"][0], f32),
        "wv_x": np.ascontiguousarray(inputs["wv_x"][0], f32), "wo_x": np.ascontiguousarray(inputs["wo_x"][0], f32),
        "w_up": np.ascontiguousarray(inputs["w_up"][0], f32), "w_down": np.ascontiguousarray(inputs["w_down"][0], f32),
        "g_mix": bcast(inputs["ln_mix_g"][0]), "g_x": bcast(inputs["ln_x_g"][0]),
        "g_mem": bcast(inputs["ln_mem_g"][0]), "g_mlp": bcast(inputs["ln_mlp_g"][0]),
        "gq": bcast(inputs["q_norm_g"][0]), "gk": bcast(inputs["k_norm_g"][0]),
        "gqx": bcast(inputs["qx_norm_g"][0]), "gkx": bcast(inputs["kx_norm_g"][0]),
        "gssm": bcast(inputs["ssm_norm_g"][0]), "dtb": bcast(inputs["dt_bias"][0]),
        "alog": bcast(inputs["a_log"][0]), "dskip": bcast(inputs["d_skip"][0]),
        "cw": np.ascontiguousarray(np.asarray(inputs["conv_w"][0], f32).reshape(4, 8, 128).transpose(2, 1, 0)),
        "cb": np.ascontiguousarray(np.asarray(inputs["conv_b"][0], f32).reshape(8, 128).T),
    }
    shared.update(consts)
    in_maps = []
    for c in range(8):
        m = dict(shared)
        m["x"] = np.ascontiguousarray(np.concatenate([xp[c], xs[4 * c:4 * c + 4].reshape(32, 1024)], axis=0))
        m["mem"] = np.ascontiguousarray(inputs["mem_prompt"][c], f32)
        m["pt"] = np.ascontiguousarray(pt[4 * c:4 * c + 4])
        m["sconv"] = np.ascontiguousarray(np.asarray(inputs["state_conv"], f32)[0, 4 * c:4 * c + 4].reshape(12, 1024))
        m["sssm"] = np.ascontiguousarray(np.asarray(inputs["state_ssm"], f32)[0, 4 * c:4 * c + 4].reshape(4, 512, 128))
        m["cmk"] = np.ascontiguousarray(np.asarray(inputs["cache_mem_k"], f32)[0, 4 * c:4 * c + 4].reshape(4, 256, 512))
        m["cmv"] = np.ascontiguousarray(np.asarray(inputs["cache_mem_v"], f32)[0, 4 * c:4 * c + 4].reshape(4, 256, 512))
        in_maps.append(m)
    return in_maps


def assemble(results):
    f32 = np.float32
    y_p = np.stack([r["y"][:2048] for r in results]).astype(f32)
    y_s = np.concatenate([r["y"][2048:].reshape(4, 8, 1024) for r in results]).astype(f32)
    nk_p = np.stack([r["newk"][:2048].reshape(2048, 8, 64) for r in results])[None].astype(f32)
    nv_p = np.stack([r["newv"][:2048].reshape(2048, 8, 64) for r in results])[None].astype(f32)
    nc_p = np.stack([r["newconv"][0:3] for r in results])[None].astype(f32)
    ns_p = np.stack([r["newssm"][0].reshape(8, 64, 128) for r in results])[None].astype(f32)
    mk_p = np.stack([r["newmk"].reshape(256, 4, 128) for r in results])[None].astype(f32)
    mv_p = np.stack([r["newmv"].reshape(256, 4, 128) for r in results])[None].astype(f32)
    nk_s = np.concatenate([r["newk"][2048:].reshape(4, 8, 8, 64) for r in results])[None].astype(f32)
    nv_s = np.concatenate([r["newv"][2048:].reshape(4, 8, 8, 64) for r in results])[None].astype(f32)
    nc_s = np.concatenate([r["newconv"][3:15].reshape(4, 3, 1024) for r in results])[None].astype(f32)
    ns_s = np.concatenate([r["newssm"][1:5].reshape(4, 8, 64, 128) for r in results])[None].astype(f32)
    return (y_p, y_s, nk_p, nv_p, nc_p, ns_p, mk_p, mv_p, nk_s, nv_s, nc_s, ns_s)


def kernel(**inputs):
    consts = host_consts()
    nc = build_program(consts, stages=("A", "B", "S", "M", "C"))
    in_maps = prep_inputs(inputs, consts)
    res = run_bass_kernel_spmd(nc, in_maps, core_ids=list(range(8)))
    return assemble(res.results)
```

```python
import os
import numpy as np
import ml_dtypes
from contextlib import ExitStack
import concourse.bass as bass
import concourse.mybir as mybir
from concourse.bass_utils import run_bass_kernel_spmd

F32 = mybir.dt.float32
BF16 = mybir.dt.bfloat16
I32 = mybir.dt.int32
AF = mybir.ActivationFunctionType
ALU = mybir.AluOpType
AX = mybir.AxisListType

NEG = -30000.0
EPS = 1e-6
NTOK = 2080
NPOOL = 2560


class Reg:
    __slots__ = ("w", "rs", "name", "excl")

    def __init__(self, name="", excl=False):
        self.w = None
        self.rs = []
        self.name = name
        self.excl = excl


class Op:
    __slots__ = ("eng", "emit", "deps", "signal", "isdma", "sem", "target", "prev_target", "bar")

    def __init__(self, eng, emit, isdma):
        self.eng = eng
        self.emit = emit
        self.deps = []
        self.signal = False
        self.isdma = isdma
        self.sem = None
        self.target = 0
        self.prev_target = 0
        self.bar = 0


ENGS = ["pe", "act", "dve", "pool", "sp"]


class Sched:
    def __init__(self, nc, n_dma_sems=12):
        self.nc = nc
        self.ops = {e: [] for e in ENGS}
        self.n_dma_sems = n_dma_sems
        self.barriers = [[]]
        self.since_bar_dma = []

    def op(self, eng, emit, r=(), w=(), dma=False):
        if getattr(self, "capture", None) is not None:
            self.capture.append((eng, emit, list(r), list(w), dma))
            return None
        o = Op(eng, emit, dma)
        self.count = getattr(self, "count", 0) + 1
        if self.count > int(os.environ.get("K_MAXOPS", "100000000")):
            return o
        if os.environ.get("K_TRACE"):
            import sys as _sys
            f = _sys._getframe(2)
            print("OP", self.count, eng, "dma" if dma else "", f.f_lineno, f.f_code.co_name, "<-", f.f_back.f_lineno)
        o.bar = len(self.barriers) - 1
        deps = {}
        if any(reg.excl for reg in r):
            w = list(w) + [reg for reg in r if reg.excl and reg not in w]
            r = [reg for reg in r if not reg.excl]
        for reg in r:
            if reg.w is not None:
                deps[id(reg.w)] = reg.w
        for reg in w:
            if reg.w is not None:
                deps[id(reg.w)] = reg.w
            for x in reg.rs:
                deps[id(x)] = x
        for d in deps.values():
            if d.eng == "pe" and eng == "pe" and not d.isdma and not dma:
                continue
            o.deps.append(d)
            d.signal = True
        for reg in r:
            reg.rs.append(o)
        for reg in w:
            reg.w = o
            reg.rs = []
        self.ops[eng].append(o)
        if dma:
            self.since_bar_dma.append(o)
        return o

    def record(self, fn):
        assert getattr(self, "capture", None) is None
        self.capture = []
        fn()
        items = self.capture
        self.capture = None
        return items

    def replay(self, a, b=()):
        i = j = 0
        while i < len(a) or j < len(b):
            if j >= len(b) or (i < len(a) and i * len(b) <= j * len(a)):
                self.op(*a[i]); i += 1
            else:
                self.op(*b[j]); j += 1

    def barrier(self):
        deps = list(self.since_bar_dma)
        for e in ENGS:
            for o in reversed(self.ops[e]):
                if not o.isdma:
                    o.signal = True
                    deps.append(o)
                    break
        self.barriers.append(deps)
        self.since_bar_dma = []

    def finalize(self, stack):
        nc = self.nc
        self.esem = {}
        self.dsems = {}
        for e in ENGS:
            self.esem[e] = stack.enter_context(nc.semaphore("s_" + e))
            self.dsems[e] = [stack.enter_context(nc.semaphore("d_%s_%d" % (e, i)))
                             for i in range(self.n_dma_sems)]
        self.final_dma = {}
        for e in ENGS:
            cnt = 0
            dcnt = [0] * self.n_dma_sems
            k = 0
            for o in self.ops[e]:
                if o.isdma:
                    i = k % self.n_dma_sems
                    k += 1
                    o.sem = self.dsems[e][i]
                    o.prev_target = dcnt[i]
                    dcnt[i] += 16
                    o.target = dcnt[i]
                elif o.signal:
                    cnt += 1
                    o.sem = self.esem[e]
                    o.target = cnt
            self.final_dma[e] = [o for o in self.ops[e] if o.isdma]

    def emit_engine(self, ename, e):
        waited = {}

        def wait(sem, val):
            key = id(sem)
            if waited.get(key, 0) >= val:
                return
            e.wait_ge(sem, val)
            waited[key] = val

        cur_bar = 0
        for o in self.ops[ename]:
            while cur_bar < o.bar:
                cur_bar += 1
                for d in self.barriers[cur_bar]:
                    wait(d.sem, d.target)
            for d in o.deps:
                wait(d.sem, d.target)
            if o.isdma and o.prev_target > 0:
                wait(o.sem, o.prev_target)
            ins = o.emit(e)
            if o.isdma:
                ins.then_inc(o.sem, 16)
            elif o.signal:
                ins.then_inc(o.sem, 1)
        for o in self.final_dma[ename]:
            wait(o.sem, o.target)

    def run(self, stack):
        self.finalize(stack)
        block = stack.enter_context(self.nc.Block())
        S = self

        @block.tensor
        def _(e):
            S.emit_engine("pe", e)

        @block.scalar
        def _(e):
            S.emit_engine("act", e)

        @block.vector
        def _(e):
            S.emit_engine("dve", e)

        @block.gpsimd
        def _(e):
            S.emit_engine("pool", e)

        @block.sync
        def _(e):
            S.emit_engine("sp", e)


class Arena:
    def __init__(self, t, nbytes):
        self.t = t
        self.cap = nbytes
        self.top = 0

    def alloc(self, shape, dt):
        n = 1
        for d in shape[1:]:
            n *= d
        esz = 2 if dt == BF16 else 4
        size = (n * esz + 31) // 32 * 32
        off = self.top
        self.top += size
        assert self.top <= self.cap, ("SBUF arena overflow", self.top, self.cap)
        v = self.t[0:shape[0], off // 2: off // 2 + (n * esz) // 2]
        if dt != BF16:
            v = v.bitcast(dt)
        if len(shape) == 3:
            v = v.rearrange("p (a b) -> p a b", a=shape[1])
        elif len(shape) == 4:
            v = v.rearrange("p (a b c) -> p a b c", a=shape[1], b=shape[2])
        elif len(shape) == 5:
            v = v.rearrange("p (a b c d) -> p a b c d", a=shape[1], b=shape[2], c=shape[3])
        return v


class Ring:
    def __init__(self, items):
        self.items = items
        self.i = 0

    def next(self):
        x = self.items[self.i % len(self.items)]
        self.i += 1
        return x


def host_consts():
    c = {}
    c["c_ident"] = np.eye(128, dtype=np.float32)
    half = 32
    inv_freq = (10000.0 ** (-np.arange(half, dtype=np.float32) / half)).astype(np.float32)
    cs = np.zeros((128, 17, 64), np.float32)
    p = np.arange(128)
    for ti in range(16):
        pos = (ti * 128 + p).astype(np.float32)
        ang = pos[:, None] * inv_freq[None, :]
        cs[:, ti, 0:32] = np.cos(ang)
        cs[:, ti, 32:64] = np.sin(ang)
    pos = (8192 + (p % 8)).astype(np.float32)
    ang = pos[:, None] * inv_freq[None, :]
    cs[:, 16, 0:32] = np.cos(ang)
    cs[:, 16, 32:64] = np.sin(ang)
    c["c_cs"] = cs
    nm = np.zeros((128, 4, 512), np.float32)
    f = np.arange(512)
    for r in range(4):
        nm[:, r, :] = np.where((r * 128 + p)[:, None] > f[None, :], NEG, 0.0)
    c["c_negmask"] = nm
    ind = np.zeros((8, 2048), np.float32)
    for j in range(8):
        ind[j, j * 256:(j + 1) * 256] = 1.0
    c["c_ind"] = ind
    cb0 = np.zeros((8, 8, 128), np.float32)
    for j in range(8):
        for own in range(8):
            cb0[j, own, :] = 0.0 if j <= own else NEG
    c["c_bias0"] = cb0
    t = np.arange(128)
    c["c_triU"] = (t[:, None] <= t[None, :]).astype(np.float32)
    c["c_mneg"] = np.where(t[None, :] < t[:, None], NEG, 0.0).astype(np.float32)
    c["c_ones"] = np.ones((128, 128), np.float32)
    t32 = np.arange(32)
    same = (t32[:, None] // 8) == (t32[None, :] // 8)
    c["c_triU_s"] = ((t32[:, None] <= t32[None, :]) & same).astype(np.float32)
    c["c_mneg_s"] = np.where((t32[None, :] < t32[:, None]) | (~same), NEG, 0.0).astype(np.float32)
    c["c_onesbd_s"] = same.astype(np.float32)
    seqsel = np.zeros((32, 4, 128), np.float32)
    for b in range(4):
        seqsel[b * 8:(b + 1) * 8, b, :] = 1.0
    c["c_seqsel"] = seqsel
    colmask = np.zeros((128, 4, 32), np.float32)
    for b in range(4):
        colmask[:, b, b * 8:(b + 1) * 8] = 1.0
    c["c_colmask"] = colmask
    rowmask = np.zeros((32, 4), np.float32)
    for b in range(4):
        rowmask[b * 8:(b + 1) * 8, b] = 1.0
    c["c_rowmask"] = rowmask
    Z = np.zeros((128, 63), np.float32)
    Z[:, 31] = 1.0
    c["c_Z"] = Z
    selE = np.zeros((32, 32, 128), np.float32)
    for j in range(32):
        selE[j, j, :] = 1.0
    c["c_selE"] = selE
    bd = np.zeros((64, 512), np.float32)
    for h in range(8):
        bd[h * 8:(h + 1) * 8, h * 64:(h + 1) * 64] = 1.0
    c["c_bdmask"] = bd
    negcs = np.zeros((32, 4, 64), np.float32)
    for key in range(32):
        for b in range(4):
            for tq in range(8):
                ok = (key // 8 == b) and (key % 8 <= tq)
                negcs[key, b, tq::8] = 0.0 if ok else NEG
    negcs2 = np.zeros((32, 4, 64), np.float32)
    for key in range(32):
        for b in range(4):
            for h in range(8):
                for tq in range(8):
                    ok = (key // 8 == b) and (key % 8 <= tq)
                    negcs2[key, b, h * 8 + tq] = 0.0 if ok else NEG
    c["c_negcs"] = negcs2
    c["c_iota"] = np.arange(128, dtype=np.float32).reshape(128, 1)
    return c


CONST_SHAPES = None


def build_program(consts, stages=("A", "S", "M", "C")):
    nc = bass.Bass("TRN2", target_bir_lowering=False)

    def din(name, shape, dt=F32):
        return nc.dram_tensor(name, list(shape), dt, kind="ExternalInput").ap()

    def dout(name, shape):
        return nc.dram_tensor(name, list(shape), F32, kind="ExternalOutput").ap()

    x_d = din("x", [NTOK, 1024])
    mem_d = din("mem", [256, 1024])
    ck_d = din("ck", [NPOOL * 128, 512])
    cv_d = din("cv", [NPOOL * 128, 512])
    pt_d = din("pt", [4, 64], I32)
    sconv_d = din("sconv", [12, 1024])
    sssm_d = din("sssm", [4, 512, 128])
    cmk_d = din("cmk", [4, 256, 512])
    cmv_d = din("cmv", [4, 256, 512])
    w_in_d = din("w_in", [1024, 3080])
    w_out_d = din("w_out", [1024, 1024])
    wq_d = din("wq_x", [1024, 512])
    wk_d = din("wk_x", [1024, 512])
    wv_d = din("wv_x", [1024, 512])
    wo_d = din("wo_x", [512, 1024])
    wup_d = din("w_up", [1024, 4096])
    wdn_d = din("w_down", [4096, 1024])
    g_mix_d = din("g_mix", [128, 1024])
    g_x_d = din("g_x", [128, 1024])
    g_mem_d = din("g_mem", [128, 1024])
    g_mlp_d = din("g_mlp", [128, 1024])
    gq_d = din("gq", [128, 64])
    gk_d = din("gk", [128, 64])
    gqx_d = din("gqx", [128, 128])
    gkx_d = din("gkx", [128, 128])
    gssm_d = din("gssm", [128, 512])
    dtb_d = din("dtb", [128, 8])
    alog_d = din("alog", [128, 8])
    dskip_d = din("dskip", [128, 8])
    cw_d = din("cw", [128, 8, 4])
    cb_d = din("cb", [128, 8])
    cd = {k: din(k, v.shape) for k, v in consts.items()}

    y_d = dout("y", [NTOK, 1024])
    newk_d = dout("newk", [NTOK, 512])
    newv_d = dout("newv", [NTOK, 512])
    newconv_d = dout("newconv", [15, 1024])
    newssm_d = dout("newssm", [5, 512, 128])
    newmk_d = dout("newmk", [256, 512])
    newmv_d = dout("newmv", [256, 512])
    mix_d = nc.dram_tensor("mixscr", [NTOK, 1024], BF16, kind="Internal").ap()
    hm_d = nc.dram_tensor("hmscr", [128, 8, NTOK], BF16, kind="Internal").ap()

    with ExitStack() as st:
        S = Sched(nc)
        ARENA_BYTES = 192 * 1024
        arena_t = st.enter_context(nc.sbuf_tensor("arena", [128, ARENA_BYTES // 2], BF16))
        A = Arena(arena_t, ARENA_BYTES)
        banks = [st.enter_context(nc.psum_tensor("bank%d" % i, [128, 512], F32)) for i in range(8)]
        banks_b = [b.bitcast(BF16) for b in banks]
        bank_reg = [Reg("bank%d" % i, excl=True) for i in range(8)]

        def mm(out, lhsT, rhs, r, w, start=True, stop=True):
            S.op("pe", lambda e: e.matmul(out, lhsT=lhsT, rhs=rhs, start=start, stop=stop), r, w)

        def tp(out, in_, idn, r, w):
            S.op("pe", lambda e: e.transpose(out=out, in_=in_, identity=idn), r, w)

        def act(out, in_, func, r, w, **kw):
            S.op("act", lambda e: e.activation(out=out, in_=in_, func=func, **kw), r, w)

        def tt(out, a, b, op, r, w, eng="dve"):
            S.op(eng, lambda e: e.tensor_tensor(out=out, in0=a, in1=b, op=op), r, w)

        def tsc(out, a, s1, op0, r, w, s2=None, op1=None):
            if op1 is None:
                S.op("dve", lambda e: e.tensor_scalar(out=out, in0=a, scalar1=s1, scalar2=None, op0=op0), r, w)
            else:
                S.op("dve", lambda e: e.tensor_scalar(out=out, in0=a, scalar1=s1, scalar2=s2, op0=op0, op1=op1), r, w)

        def stt(out, a, s, b, op0, op1, r, w, accum=None):
            if accum is None:
                S.op("dve", lambda e: e.scalar_tensor_tensor(out=out, in0=a, scalar=s, in1=b, op0=op0, op1=op1), r, w)
            else:
                S.op("dve", lambda e: e.scalar_tensor_tensor(out=out, in0=a, scalar=s, in1=b, op0=op0, op1=op1,
                                                             accum_out=accum), r, w)

        def cp(out, in_, r, w, eng="dve"):
            if eng == "act":
                S.op("act", lambda e: e.activation(out=out, in_=in_, func=AF.Copy), r, w)
            else:
                S.op(eng, lambda e: e.tensor_copy(out=out, in_=in_), r, w)

        def red(out, in_, r, w, op=ALU.add):
            S.op("dve", lambda e: e.tensor_reduce(out=out, in_=in_, axis=AX.X, op=op), r, w)

        def rcp(out, in_, r, w):
            S.op("dve", lambda e: e.reciprocal(out=out, in_=in_), r, w)

        def mset(ap, val, w, eng="dve"):
            S.op(eng, lambda e: e.memset(ap, val), (), w)

        def dma(q, out, in_, r, w):
            S.op(q, lambda e: e.dma_start(out=out, in_=in_), r, w, dma=True)

        def gather(out, table, idx, r, w):
            S.op("pool", lambda e: e.indirect_dma_start(out=out, out_offset=None, in_=table,
                                                        in_offset=bass.IndirectOffsetOnAxis(ap=idx, axis=0)),
                 r, w, dma=True)

        def bc(ap, shape):
            return ap.to_broadcast(list(shape))

        identF = A.alloc([128, 128], F32); r_idF = Reg()
        identB = A.alloc([128, 128], BF16); r_idB = Reg()
        cs = A.alloc([128, 17, 64], F32); r_cs = Reg()
        dma("sp", identF, cd["c_ident"], [], [r_idF])
        dma("pool", identB, cd["c_ident"], [], [r_idB])
        dma("sp", cs, cd["c_cs"], [], [r_cs])
        eps_t = A.alloc([128, 1], F32); r_eps = Reg()
        mset(eps_t, EPS, [r_eps])
        one_t = A.alloc([128, 1], F32); r_one = Reg()
        mset(one_t, 1.0, [r_one])

        QsT = A.alloc([128, 4, 32], BF16)
        KnT = A.alloc([128, 4, 32], BF16)
        vb = A.alloc([32, 512], BF16)

        def load_const(name, shape, dt, q=None):
            t_ = A.alloc(shape, dt)
            r_ = Reg(name)
            if q is None:
                q = "pool" if dt == BF16 else "sp"
            dma(q, t_, name if not isinstance(name, str) else cd[name], [], [r_])
            return t_, r_

        def load_in(d_ap, shape, dt=F32, q="sp"):
            t_ = A.alloc(shape, dt)
            r_ = Reg()
            dma(q if dt == F32 or dt == I32 else "pool", t_, d_ap, [], [r_])
            return t_, r_

        def small(shape, dt=F32):
            return A.alloc(shape, dt), Reg()

        def rmsnorm_fm(x_ap, r_x, P, gbc, r_g, outFM, r_out, col0, wk):
            junk, r_junk, hn, r_hn, ssq, r_ssq, psb = wk
            stt(junk[:P], x_ap, 1.0, x_ap, ALU.mult, ALU.mult, [r_x], [r_junk, r_ssq], accum=ssq[:P, 0:1])
            act(ssq[:P, 1:2], ssq[:P, 0:1], AF.Sqrt, [r_ssq, r_eps], [r_ssq], scale=1.0 / 1024, bias=eps_t[:P, :])
            rcp(ssq[:P, 2:3], ssq[:P, 1:2], [r_ssq], [r_ssq])
            stt(hn[:P], x_ap, ssq[:P, 2:3], gbc[:P], ALU.mult, ALU.mult, [r_x, r_ssq, r_g], [r_hn])
            bi = psb.next()
            for k in range(8):
                tp(banks_b[bi][:, k * P:(k + 1) * P], hn[:P, k * 128:(k + 1) * 128], identB[:P, :P],
                   [r_hn, r_idB], [bank_reg[bi]])
            cp(outFM[:, :, col0:col0 + P], banks_b[bi][:, 0:8 * P].rearrange("p (k q) -> p k q", k=8),
               [bank_reg[bi]], [r_out], eng="act")

        def headnorm(ps_ap, r_ps, P, nh, hd, qs, r_qs, sq, r_sq, st8, r_st8):
            cp(qs[:P], ps_ap, [r_ps], [r_qs], eng="act")
            act(sq[:P], ps_ap, AF.Square, [r_ps], [r_sq])
            red(st8[:P, 0, 0:nh], sq[:P].rearrange("p (h d) -> p h d", h=nh), [r_sq], [r_st8])
            act(st8[:P, 1, 0:nh], st8[:P, 0, 0:nh], AF.Sqrt, [r_st8, r_eps], [r_st8], scale=1.0 / hd, bias=eps_t[:P, :])
            rcp(st8[:P, 2, 0:nh], st8[:P, 1, 0:nh], [r_st8], [r_st8])
            q3 = qs[:P].rearrange("p (h d) -> p h d", h=nh)
            tt(q3, q3, bc(st8[:P, 2, 0:nh].unsqueeze(2), [P, nh, hd]), ALU.mult, [r_qs, r_st8], [r_qs])

        markA = A.top
        Win = A.alloc([128, 8, 3080], BF16); r_WinA = Reg(); r_WinB = Reg()
        w_in_v = w_in_d.rearrange("(k p) c -> p k c", p=128)
        dma("pool", Win[:, :, 0:1536], w_in_v[:, :, 0:1536], [], [r_WinA])
        dma("pool", Win[:, :, 1536:3080], w_in_v[:, :, 1536:3080], [], [r_WinB])
        g_mix, r_gmix = load_in(g_mix_d, [128, 1024])
        gq, r_gq = load_in(gq_d, [128, 64])
        gk, r_gk = load_in(gk_d, [128, 64])
        gssm, r_gssm = load_in(gssm_d, [128, 512])
        dtb, r_dtb = load_in(dtb_d, [128, 8])
        a_bc, r_abc = load_in(alog_d, [128, 8])
        dskip, r_dskip = load_in(dskip_d, [128, 8])
        cw, r_cw = load_in(cw_d, [128, 8, 4])
        cb, r_cb = load_in(cb_d, [128, 8])
        act(a_bc, a_bc, AF.Exp, [r_abc], [r_abc])
        tsc(a_bc, a_bc, -1.0, ALU.mult, [r_abc], [r_abc])
        triU, r_triU = load_const("c_triU", [128, 128], F32)
        mneg, r_mneg = load_const("c_mneg", [128, 128], F32)
        ones, r_ones = load_const("c_ones", [128, 128], F32)
        triU_s, r_triUs = load_const("c_triU_s", [32, 32], F32)
        mneg_s, r_mnegs = load_const("c_mneg_s", [32, 32], F32)
        onesbd_s, r_onesbds = load_const("c_onesbd_s", [32, 32], F32)
        seqsel, r_seqsel = load_const("c_seqsel", [32, 4, 128], F32)
        colmask, r_colmask = load_const("c_colmask", [128, 4, 32], BF16)
        rowmask, r_rowmask = load_const("c_rowmask", [32, 4], F32)

        KT = A.alloc([128, 8, 2048], BF16); r_KT = [Reg() for _ in range(16)]; r_KTind = Reg()
        for h in range(8):
            dma("pool", KT[64:72, h, :], cd["c_ind"], [], [r_KTind])
        VA = A.alloc([128, 16, 8, 66], BF16); r_VA = [Reg() for _ in range(16)]; r_VAone = Reg()
        mset(VA[:, :, :, 64:65], 1.0, [r_VAone])
        hnFM = A.alloc([128, 8, 512], BF16); r_hnFM = Reg()
        xc = A.alloc([128, 8, 512], BF16); r_xc = Reg()
        carry = A.alloc([128, 8, 3], F32); r_carry = Reg()
        mset(carry, 0.0, [r_carry])
        hT = [A.alloc([128, 512], F32)]
        hTb = [A.alloc([128, 512], BF16)]
        r_hT = [Reg() for _ in range(5)]
        r_hTb = [Reg() for _ in range(5)]
        mset(hT[0], 0.0, [r_hT[0]])
        mset(hTb[0], 0.0, [r_hTb[0]])

        xt_ring = Ring([(A.alloc([128, 1024], F32), Reg()) for _ in range(1)])
        junk = A.alloc([128, 1024], BF16); r_junk = Reg()
        hn = A.alloc([128, 1024], BF16); r_hn = Reg()
        ssq = A.alloc([128, 4], F32); r_ssq = Reg()
        psT = Ring([0, 1, 2, 3])
        psM = psT
        nwk = (junk, r_junk, hn, r_hn, ssq, r_ssq, psT)
        qs_ring = Ring([(A.alloc([128, 512], F32), Reg()) for _ in range(1)])
        sq = A.alloc([128, 512], F32); r_sq = Reg()
        st8 = A.alloc([128, 3, 8], F32); r_st8 = Reg()
        tab = A.alloc([128, 4, 32], F32); r_tab = Reg()
        rt = [A.alloc([128, 8, 32], F32) for _ in range(2)]; r_rt = [Reg() for _ in range(2)]
        rt = rt + rt; r_rt = r_rt + r_rt
        ko_ring = Ring([(A.alloc([128, 512], F32), Reg()) for _ in range(1)])
        qb = A.alloc([128, 512], BF16); r_qb = Reg()
        vs_ring = ko_ring
        zs = A.alloc([128, 512], F32); r_zs = Reg()
        dtt = A.alloc([128, 4, 8], F32); r_dtt = Reg()
        dtt2 = A.alloc([128, 4, 8], F32); r_dtt2 = Reg()
        xr_ring = Ring([(A.alloc([128, 3 + 512], F32), Reg()) for _ in range(1)])
        acc = A.alloc([128, 512], F32); r_acc = Reg()
        zs_bufs = [(zs, r_zs), (acc, r_acc)]
        dtt_bufs = [(dtt, r_dtt), (dtt2, r_dtt2)]
        xl, r_xl = xt_ring.items[0]
        xdt = A.alloc([128, 8, 64], BF16); r_xdt = Reg()
        xsT = A.alloc([128, 512], F32); r_xsT = Reg()
        Btm = A.alloc([128, 2, 128], BF16); r_Btm = Reg()
        Bm = A.alloc([128, 2, 128], BF16); r_Bm = Reg()
        cumt = A.alloc([128, 6, 8], F32); r_cumt = Reg()
        etot = A.alloc([128, 4, 8], F32); r_etot = Reg()
        dab_ring = Ring([(A.alloc([128, 128], F32), Reg()) for _ in range(2)])
        dec_ring = Ring([(A.alloc([128, 128], BF16), Reg()) for _ in range(2)])
        LT_ring = Ring([(A.alloc([128, 128], BF16), Reg()) for _ in range(2)])
        CTm = A.alloc([128, 4, 2, 32], BF16); r_CTm = Reg()
        y2s = A.alloc([128, 512], F32); r_y2s = Reg()
        yy = A.alloc([128, 512], F32); r_yy = Reg()
        ysq = sq; r_ysq = r_sq
        ss2 = A.alloc([128, 3, 2], F32); r_ss2 = Reg()
        mixs = A.alloc([128, 512], BF16); r_mixs = Reg()
        xdtt = A.alloc([128, 512], BF16); r_xdtt = Reg()
        hout = yy.rearrange("p (c n) -> p c n", c=4); r_hout = r_yy
        mark_attn = A.top
        negmask, r_negmask = load_const("c_negmask", [128, 4, 512], BF16)
        cbias0, r_cbias0 = A.alloc([128, 8, 128], BF16), Reg()
        dma("pool", cbias0[64:72], cd["c_bias0"], [], [r_cbias0])
        QT = A.alloc([128, 8, 512], BF16); r_QT = Reg(); r_QTb = Reg()
        PT_ring = Ring([(A.alloc([128, 512], BF16), Reg()) for _ in range(3)])
        attb = A.alloc([128, 4, 512], BF16); r_attb = [Reg() for _ in range(4)]
        rinv = A.alloc([128, 4], F32); r_rinv = Reg()
        m8 = A.alloc([128, 8, 8], F32); r_m8 = Reg()
        bsel = A.alloc([128, 8, 8], F32); r_bsel = Reg()
        biasm = A.alloc([128, 8, 8], BF16); r_biasm = Reg()
        ksumT = A.alloc([128, 8, 8], BF16); r_ksum = Reg()
        ksf = A.alloc([128, 8], F32); r_ksf = Reg()
        gsb = A.alloc([128, 8, 8], F32); r_gsb = Reg()
        mset(gsb, -1e30, [r_gsb])
        r_sconvT = Reg(); r_QsT = Reg(); r_KnT = Reg(); r_vb = Reg()
        print("phase A arena top", A.top)

        def qk_post(ps_ap, r_ps, P, ti, g_bc, r_g, is_k, row0):
            qs, r_qs = qs_ring.next()
            headnorm(ps_ap, r_ps, P, 8, 64, qs, r_qs, sq, r_sq, st8, r_st8)
            cos = cs[:P, ti, 0:32]
            sin = cs[:P, ti, 32:64]
            tt(tab[:P, 0, :], cos, g_bc[:P, 0:32], ALU.mult, [r_cs, r_g], [r_tab])
            tt(tab[:P, 1, :], sin, g_bc[:P, 32:64], ALU.mult, [r_cs, r_g], [r_tab])
            tt(tab[:P, 2, :], cos, g_bc[:P, 32:64], ALU.mult, [r_cs, r_g], [r_tab])
            tt(tab[:P, 3, :], sin, g_bc[:P, 0:32], ALU.mult, [r_cs, r_g], [r_tab])
            q3 = qs[:P].rearrange("p (h d) -> p h d", h=8)
            x1 = q3[:, :, 0:32]
            x2 = q3[:, :, 32:64]
            tb = [bc(tab[:P, i, :].unsqueeze(1), [P, 8, 32]) for i in range(4)]
            ko, r_ko = ko_ring.next()
            o3 = ko[:P].rearrange("p (h d) -> p h d", h=8)
            tt(rt[0][:P], x1, tb[0], ALU.mult, [r_qs, r_tab], [r_rt[0]])
            tt(rt[1][:P], x2, tb[1], ALU.mult, [r_qs, r_tab], [r_rt[1]])
            tt(o3[:, :, 0:32], rt[0][:P], rt[1][:P], ALU.subtract, [r_rt[0], r_rt[1]], [r_ko])
            tt(rt[2][:P], x2, tb[2], ALU.mult, [r_qs, r_tab], [r_rt[2]])
            tt(rt[3][:P], x1, tb[3], ALU.mult, [r_qs, r_tab], [r_rt[3]])
            tt(o3[:, :, 32:64], rt[2][:P], rt[3][:P], ALU.add, [r_rt[2], r_rt[3]], [r_ko])
            if is_k:
                dma("sp", newk_d[row0:row0 + P, :], ko[:P], [r_ko], [])
            cp(qb[:P], ko[:P], [r_ko], [r_qb], eng="act")
            return ko, r_ko

        ssd_ring = Ring([5, 7])

        def ssd_chunk(L, nseq, c0, seqs, row0, tri_c, r_tri, mneg_c, r_mn, ones_c, r_on, zs, r_zs, dtt, r_dtt):
            dt_ = dtt[:L, 1, :]
            da_ = dtt[:L, 2, :]
            bi = 7
            for c in range(4):
                tp(banks_b[bi][:L, c * 128:(c + 1) * 128], xc[:, c, c0:c0 + L], identB, [r_xc, r_idB], [bank_reg[bi]])
            psv = banks_b[bi][:L, 0:512].rearrange("p (h d) -> p h d", h=8)
            tt(xdt[:L], psv, bc(dt_.unsqueeze(2), [L, 8, 64]), ALU.mult, [bank_reg[bi], r_dtt], [r_xdt])
            cp(xsT[:L], banks_b[bi][:L, 0:512], [bank_reg[bi]], [r_xsT], eng="act")
            bi = 7
            for g in range(2):
                tp(banks_b[bi][:L, g * 128:(g + 1) * 128], xc[:, 4 + g, c0:c0 + L], identB, [r_xc, r_idB], [bank_reg[bi]])
            cp(Btm[:L], banks_b[bi][:L, 0:256].rearrange("p (g n) -> p g n", g=2), [bank_reg[bi]], [r_Btm], eng="act")
            b5 = 5
            mm(banks[b5][:L, 0:8], tri_c[:L, :L], da_, [r_tri, r_dtt], [bank_reg[b5]])
            mm(banks[b5][:L, 8:16], ones_c[:L, :L], da_, [r_on, r_dtt], [bank_reg[b5]])
            for bidx in range(nseq):
                lhs = ones[:L, :] if nseq == 1 else seqsel[:L, bidx, :]
                mm(banks[b5][:, 16 + bidx * 8:24 + bidx * 8], lhs, da_, [r_ones, r_seqsel, r_dtt], [bank_reg[b5]])
            cp(cumt[:L, 0, :], banks[b5][:L, 0:8], [bank_reg[b5]], [r_cumt])
            tsc(cumt[:L, 1, :], banks[b5][:L, 0:8], -1.0, ALU.mult, [bank_reg[b5]], [r_cumt])
            tt(cumt[:L, 2, :], banks[b5][:L, 8:16], cumt[:L, 0, :], ALU.subtract, [bank_reg[b5], r_cumt], [r_cumt])
            act(cumt[:L, 3, :], cumt[:L, 2, :], AF.Exp, [r_cumt], [r_cumt])
            act(cumt[:L, 4, :], cumt[:L, 0, :], AF.Exp, [r_cumt], [r_cumt])
            act(etot[:, 0:nseq, :], banks[b5][:, 16:16 + 8 * nseq].rearrange("p (b h) -> p b h", b=nseq), AF.Exp,
                [bank_reg[b5]], [r_etot])
            b4 = 4
            for g in range(2):
                mm(banks[b4][:L, g * 128:g * 128 + L], xc[:, 4 + g, c0:c0 + L], xc[:, 6 + g, c0:c0 + L],
                   [r_xc], [bank_reg[b4]])
            b6, b7 = 6, 7
            for h in range(8):
                g = h // 4
                bi = ssd_ring.next()
                dab, r_dab = dab_ring.next()
                cp(dab[:L, 0:L], bc(dtt[:L, 2, h:h + 1], [L, L]), [r_dtt], [r_dab])
                mm(banks[bi][:L, 0:L], dab[:L, 0:L], tri_c[:L, :L], [r_dab, r_tri], [bank_reg[bi]], start=True, stop=False)
                mm(banks[bi][:L, 0:L], identF[:L, :L], mneg_c[:L, :L], [r_idF, r_mn], [bank_reg[bi]], start=False, stop=True)
                dec, r_dec = dec_ring.next()
                act(dec[:L, :L], banks[bi][:L, 0:L], AF.Exp, [bank_reg[bi], r_cumt], [r_dec], bias=cumt[:L, 1, h:h + 1])
                LT, r_LT = LT_ring.next()
                tt(LT[:L, :L], banks[b4][:L, g * 128:g * 128 + L], dec[:L, :L], ALU.mult, [bank_reg[b4], r_dec], [r_LT])
                mm(banks[b6][:L, h * 64:(h + 1) * 64], LT[:L, :L], xdt[:L, h, :], [r_LT, r_xdt], [bank_reg[b6]])
            if nseq > 1:
                for bidx in range(nseq):
                    for g in range(2):
                        tt(CTm[:, bidx, g, :], xc[:, 6 + g, c0:c0 + L], colmask[:, bidx, :], ALU.mult,
                           [r_xc, r_colmask], [r_CTm])
            for g in range(2):
                for bidx, sq_ in enumerate(seqs):
                    lhs = xc[:, 6 + g, c0:c0 + L] if nseq == 1 else CTm[:, bidx, g, :]
                    mm(banks[b7][:L, g * 256:(g + 1) * 256], lhs, hTb[sq_][:, g * 256:(g + 1) * 256],
                       [r_xc, r_CTm, r_hTb[sq_]], [bank_reg[b7]], start=(bidx == 0), stop=(bidx == nseq - 1))
            tt(y2s[:L].rearrange("p (h d) -> p h d", h=8), banks[b7][:L, :].rearrange("p (h d) -> p h d", h=8),
               bc(cumt[:L, 4, :].unsqueeze(2), [L, 8, 64]), ALU.mult, [bank_reg[b7], r_cumt], [r_y2s])
            tt(yy[:L], banks[b6][:L, :], y2s[:L], ALU.add, [bank_reg[b6], r_y2s], [r_yy])
            tt(y2s[:L].rearrange("p (h d) -> p h d", h=8), xsT[:L].rearrange("p (h d) -> p h d", h=8),
               bc(dskip[:L, :].unsqueeze(2), [L, 8, 64]), ALU.mult, [r_xsT, r_dskip], [r_y2s])
            tt(yy[:L], yy[:L], y2s[:L], ALU.add, [r_yy, r_y2s], [r_yy])
            tt(yy[:L], yy[:L], zs[:L], ALU.mult, [r_yy, r_zs], [r_yy])
            tt(ysq[:L], yy[:L], yy[:L], ALU.mult, [r_yy], [r_ysq])
            red(ss2[:L, 0, :], ysq[:L].rearrange("p (g d) -> p g d", g=2), [r_ysq], [r_ss2])
            act(ss2[:L, 1, :], ss2[:L, 0, :], AF.Sqrt, [r_ss2, r_eps], [r_ss2], scale=1.0 / 256, bias=eps_t[:L, :])
            rcp(ss2[:L, 2, :], ss2[:L, 1, :], [r_ss2], [r_ss2])
            tt(yy[:L].rearrange("p (g d) -> p g d", g=2), yy[:L].rearrange("p (g d) -> p g d", g=2),
               bc(ss2[:L, 2, :].unsqueeze(2), [L, 2, 256]), ALU.mult, [r_yy, r_ss2], [r_yy])
            tt(mixs[:L], yy[:L], gssm[:L], ALU.mult, [r_yy, r_gssm], [r_mixs])
            dma("sp", mix_d[row0:row0 + L, 512:1024], mixs[:L], [r_mixs], [])
            tt(xdtt[:L].rearrange("p (h d) -> p h d", h=8), xdt[:L], bc(cumt[:L, 3, :].unsqueeze(2), [L, 8, 64]),
               ALU.mult, [r_xdt, r_cumt], [r_xdtt])
            for bidx, sq_ in enumerate(seqs):
                if nseq > 1:
                    tsc(Bm[:L].rearrange("p g n -> p (g n)"), Btm[:L].rearrange("p g n -> p (g n)"),
                        rowmask[:L, bidx:bidx + 1], ALU.mult, [r_Btm, r_rowmask], [r_Bm])
                    Bsrc, r_Bs = Bm, r_Bm
                else:
                    Bsrc, r_Bs = Btm, r_Btm
                bi = ssd_ring.next()
                for g in range(2):
                    mm(banks[bi][:, g * 256:(g + 1) * 256], Bsrc[:L, g, :], xdtt[:L, g * 256:(g + 1) * 256],
                       [r_Bs, r_xdtt], [bank_reg[bi]])
                h3 = hT[sq_].rearrange("p (h d) -> p h d", h=8)
                tt(h3, h3, bc(etot[:, bidx, :].unsqueeze(2), [128, 8, 64]), ALU.mult, [r_hT[sq_], r_etot], [r_hT[sq_]])
                tt(hT[sq_], hT[sq_], banks[bi][:, :], ALU.add, [r_hT[sq_], bank_reg[bi]], [r_hT[sq_]])
                cp(hTb[sq_], hT[sq_], [r_hT[sq_]], [r_hTb[sq_]], eng="act")

        def ssm_out(sq_):
            bi = psM.next()
            for c in range(4):
                tp(banks[bi][:, c * 128:(c + 1) * 128], hT[sq_][:, c * 128:(c + 1) * 128], identF, [r_hT[sq_], r_idF],
                   [bank_reg[bi]])
            cp(hout, banks[bi][:, :].rearrange("p (c n) -> p c n", c=4), [bank_reg[bi]], [r_hout], eng="act")
            dma("sp", newssm_d[sq_].rearrange("(c p) n -> p c n", p=128), hout, [r_hout], [])

        ringPre = Ring([2, 3])
        ringAtt = Ring([0, 1])
        nwk_pre = (junk, r_junk, hn, r_hn, ssq, r_ssq, ringPre)

        def phaseA_tile(tok0, P, nsub, nseq, is_sample, tidx, part="all"):
            W = P * nsub
            L = W // nseq
            if part in ("all", "pre"):
                phaseA_pre(tok0, P, nsub, nseq, is_sample)
            if part in ("all", "main"):
                phaseA_main(tok0, P, nsub, nseq, is_sample)

        def phaseA_pre(tok0, P, nsub, nseq, is_sample):
            W = P * nsub
            L = W // nseq
            for s in range(nsub):
                xt, r_xt = xt_ring.next()
                dma("sp", xt[:P], x_d[tok0 + s * P: tok0 + (s + 1) * P, :], [], [r_xt])
                rmsnorm_fm(xt[:P], r_xt, P, g_mix, r_gmix, hnFM, r_hnFM, s * P, nwk_pre)
            for c in range(8):
                bi = ringPre.next()
                for k in range(8):
                    mm(banks[bi][:, 0:W], Win[:, k, 2048 + c * 128: 2048 + (c + 1) * 128], hnFM[:, k, 0:W],
                       [r_WinB, r_hnFM], [bank_reg[bi]], start=(k == 0), stop=(k == 7))
                xr, r_xr = xr_ring.next()
                xr3 = xr[:, 0:nseq * (3 + L)].rearrange("p (b l) -> p b l", b=nseq)
                if is_sample:
                    cp(xr3[:, :, 0:3], sconvT[:, c, :].rearrange("p (b r) -> p b r", b=4), [r_sconvT], [r_xr])
                else:
                    cp(xr3[:, :, 0:3], carry[:, c, :].unsqueeze(1), [r_carry], [r_xr])
                cp(xr3[:, :, 3:3 + L], banks[bi][:, 0:W].rearrange("p (b l) -> p b l", b=nseq), [bank_reg[bi]], [r_xr],
                   eng="act")
                acc3 = acc[:, 0:W].rearrange("p (b l) -> p b l", b=nseq)
                tsc(acc3, xr3[:, :, 0:L], cw[:, c, 0:1], ALU.mult, [r_xr, r_cw, r_cb], [r_acc], s2=cb[:, c:c + 1], op1=ALU.add)
                for i in range(1, 4):
                    stt(acc3, xr3[:, :, i:i + L], cw[:, c, i:i + 1], acc3, ALU.mult, ALU.add, [r_xr, r_cw, r_acc], [r_acc])
                act(xc[:, c, 0:W], acc[:, 0:W], AF.Silu, [r_acc], [r_xc])
                if not is_sample:
                    cp(carry[:, c, :], xr[:, W:W + 3], [r_xr], [r_carry])

        def phaseA_main(tok0, P, nsub, nseq, is_sample):
            W = P * nsub
            L = W // nseq
            def front(s):
                row0 = tok0 + s * P
                zs, r_zs = zs_bufs[s % 2]
                dtt, r_dtt = dtt_bufs[s % 2]
                ti = 16 if is_sample else (tok0 // 128 + s)
                cols = slice(s * P, (s + 1) * P)

                def proj(c0, c1, r_w):
                    bi_ = psM.next()
                    for k in range(8):
                        mm(banks[bi_][:P, 0:c1 - c0], hnFM[:, k, cols], Win[:, k, c0:c1], [r_hnFM, r_w], [bank_reg[bi_]],
                           start=(k == 0), stop=(k == 7))
                    return bi_
                bk_ = proj(512, 1024, r_WinA)
                bq_ = proj(0, 512, r_WinA)
                bv_ = proj(1024, 1536, r_WinA)
                bz_ = proj(1536, 2048, r_WinB)
                bi = bk_
                qk_post(banks[bi][:P, :], bank_reg[bi], P, ti, gk, r_gk, True, row0)
                if not is_sample:
                    bt = psT.next()
                    for h in range(8):
                        tp(banks_b[bt][0:64, h * 128:(h + 1) * 128], qb[:, h * 64:(h + 1) * 64], identB, [r_qb, r_idB],
                           [bank_reg[bt]])
                    kti = tok0 // 128 + s
                    cp(KT[0:64, :, kti * 128:(kti + 1) * 128], banks_b[bt][0:64, 0:1024].rearrange("p (h q) -> p h q", h=8),
                       [bank_reg[bt]], [r_KT[kti]], eng="act")
                else:
                    bt = psT.next()
                    for c in range(4):
                        tp(banks_b[bt][:, c * 32:(c + 1) * 32], qb[:32, c * 128:(c + 1) * 128], identB[:32, :32],
                           [r_qb, r_idB], [bank_reg[bt]])
                    cp(KnT, banks_b[bt][:, 0:128].rearrange("p (c q) -> p c q", c=4), [bank_reg[bt]], [r_KnT], eng="act")
                bi = bq_
                qk_post(banks[bi][:P, :], bank_reg[bi], P, ti, gq, r_gq, False, row0)
                if not is_sample:
                    bt = psT.next()
                    for h in range(8):
                        tp(banks_b[bt][0:64, h * 128:(h + 1) * 128], qb[:, h * 64:(h + 1) * 64], identB, [r_qb, r_idB],
                           [bank_reg[bt]])
                    cp(QT[0:64, :, cols], banks_b[bt][0:64, 0:1024].rearrange("p (h q) -> p h q", h=8),
                       [bank_reg[bt]], [r_QT], eng="act")
                else:
                    bt = psT.next()
                    for c in range(4):
                        tp(banks_b[bt][:, c * 32:(c + 1) * 32], qb[:32, c * 128:(c + 1) * 128], identB[:32, :32],
                           [r_qb, r_idB], [bank_reg[bt]])
                    cp(QsT, banks_b[bt][:, 0:128].rearrange("p (c q) -> p c q", c=4), [bank_reg[bt]], [r_QsT], eng="act")
                bi = bv_
                vs, r_vs = vs_ring.next()
                cp(vs[:P], banks[bi][:P, :], [bank_reg[bi]], [r_vs], eng="act")
                dma("sp", newv_d[row0:row0 + P, :], vs[:P], [r_vs], [])
                if not is_sample:
                    kti = tok0 // 128 + s
                    cp(VA[:, kti, :, 0:64], banks[bi][:, :].rearrange("p (h d) -> p h d", h=8), [bank_reg[bi]], [r_VA[kti]])
                else:
                    cp(vb, banks[bi][:32, :], [bank_reg[bi]], [r_vb])
                bi = bz_
                act(zs[:P], banks[bi][:P, :], AF.Silu, [bank_reg[bi]], [r_zs])
                bi = proj(3072, 3080, r_WinB)
                tt(dtt[:P, 0, :], banks[bi][:P, 0:8], dtb[:P], ALU.add, [bank_reg[bi], r_dtb], [r_dtt])
                act(dtt[:P, 0, :], dtt[:P, 0, :], AF.Exp, [r_dtt], [r_dtt])
                act(dtt[:P, 1, :], dtt[:P, 0, :], AF.Ln, [r_dtt, r_one], [r_dtt], bias=one_t[:P, :])
                tt(dtt[:P, 2, :], dtt[:P, 1, :], a_bc[:P], ALU.mult, [r_dtt, r_abc], [r_dtt])
                if is_sample or (tok0 == 1536 and s == 3):
                    for half in range(2):
                        bi = proj(2048 + half * 512, 2048 + (half + 1) * 512, r_WinB)
                        cp(xl[:P, half * 512:(half + 1) * 512], banks[bi][:P, :], [bank_reg[bi]], [r_xl], eng="act")
                    if is_sample:
                        for b_ in range(4):
                            dma("sp", newconv_d[3 + 3 * b_: 6 + 3 * b_, :], xl[8 * b_ + 5: 8 * b_ + 8, :], [r_xl], [])
                    else:
                        dma("sp", newconv_d[0:3, :], xl[125:128, :], [r_xl], [])
            def back(s):
                row0 = tok0 + s * P
                zs, r_zs = zs_bufs[s % 2]
                dtt, r_dtt = dtt_bufs[s % 2]
                if is_sample:
                    ssd_chunk(32, 4, 0, [1, 2, 3, 4], row0, triU_s, r_triUs, mneg_s, r_mnegs, onesbd_s, r_onesbds, zs, r_zs, dtt, r_dtt)
                else:
                    ssd_chunk(128, 1, s * 128, [0], row0, triU, r_triU, mneg, r_mneg, ones, r_ones, zs, r_zs, dtt, r_dtt)

            f_ops = S.record(lambda: front(0))
            S.replay(f_ops)
            for s in range(nsub):
                b_ops = S.record(lambda: back(s))
                if s + 1 < nsub:
                    f_ops = S.record(lambda: front(s + 1))
                    S.replay(f_ops, b_ops)
                else:
                    S.replay(b_ops)

        def attention_tile(i):
            tok0 = i * 512
            for j in (2 * i, 2 * i + 1):
                red(ksf[0:64, :], KT[0:64, :, j * 256:(j + 1) * 256], [r_KT[2 * j], r_KT[2 * j + 1]], [r_ksf])
                cp(ksumT[0:64, :, j], ksf[0:64, :], [r_ksf], [r_ksum])
            for s in range(4):
                own = (tok0 + s * 128) // 256
                cols = slice(s * 128, (s + 1) * 128)
                if own <= 3:
                    cp(QT[64:72, :, cols], bc(cbias0[64:72, own, :].unsqueeze(1), [8, 8, 128]), [r_cbias0], [r_QTb])
                else:
                    bi = ringAtt.next()
                    for h in range(8):
                        mm(banks[bi][:, h * 8:h * 8 + own], QT[0:64, h, cols], ksumT[0:64, h, 0:own], [r_QT, r_ksum],
                           [bank_reg[bi]])
                    cp(gsb[:, :, 0:own], banks[bi][:, 0:64].rearrange("p (h j) -> p h j", h=8)[:, :, 0:own],
                       [bank_reg[bi]], [r_gsb])
                    for h in range(8):
                        S.op("dve", (lambda o_, i_: (lambda e: e.max(out=o_, in_=i_)))(m8[:, h, :], gsb[:, h, :]),
                             [r_gsb], [r_m8])
                    tt(bsel, gsb, bc(m8[:, :, 2:3], [128, 8, 8]), ALU.is_lt, [r_gsb, r_m8], [r_bsel])
                    tsc(biasm, bsel, NEG, ALU.mult, [r_bsel], [r_biasm])
                    mset(biasm[:, :, own:own + 1], 0.0, [r_biasm])
                    bt = ringAtt.next()
                    for h in range(8):
                        tp(banks_b[bt][64:72, h * 128:(h + 1) * 128], biasm[:, h, :], identB, [r_biasm, r_idB],
                           [bank_reg[bt]])
                    cp(QT[64:72, :, cols], banks_b[bt][64:72, 0:1024].rearrange("p (h q) -> p h q", h=8),
                       [bank_reg[bt]], [r_QTb], eng="act")
            obank = [4, 5, 6, 7]
            nk = 4 * i + 4
            units = [(h, kc) for h in range(8) for kc in range(nk)]

            def emit_qk(h, kc):
                diag = kc >= 4 * i
                bi = ringAtt.next()
                mm(banks[bi][:, :], KT[0:72, h, kc * 128:(kc + 1) * 128], QT[0:72, h, :],
                   [r_KT[kc], r_KTind, r_QT, r_QTb], [bank_reg[bi]], start=True, stop=not diag)
                if diag:
                    mm(banks[bi][:, :], identB, negmask[:, kc - 4 * i, :], [r_idB, r_negmask], [bank_reg[bi]],
                       start=False, stop=True)
                PT, r_PT = PT_ring.next()
                act(PT, banks[bi][:, :], AF.Exp, [bank_reg[bi]], [r_PT], scale=0.125)
                return PT, r_PT

            def emit_pv(h, kc, PT, r_PT):
                for s in range(4):
                    last = 4 * i + s
                    if kc > last:
                        continue
                    ob = obank[s]
                    mm(banks[ob][:, 0:65], PT[:, s * 128:(s + 1) * 128], VA[:, kc, h, 0:65], [r_PT, r_VA[kc], r_VAone],
                       [bank_reg[ob]], start=(kc == 0), stop=(kc == last))
                if kc == nk - 1:
                    for s in range(4):
                        ob = obank[s]
                        rcp(rinv[:, s:s + 1], banks[ob][:, 64:65], [bank_reg[ob]], [r_rinv])
                        tsc(attb[:, s, h * 64:(h + 1) * 64], banks[ob][:, 0:64], rinv[:, s:s + 1], ALU.mult,
                            [bank_reg[ob], r_rinv], [r_attb[s]])

            pend = None
            for (h, kc) in units:
                cur = emit_qk(h, kc)
                if pend is not None:
                    emit_pv(*pend)
                pend = (h, kc) + cur
            emit_pv(*pend)
            for s in range(4):
                dma("sp", mix_d[tok0 + s * 128: tok0 + (s + 1) * 128, 0:512], attb[:, s, :], [r_attb[s]], [])

        if "A" in stages:
            NT_ = int(os.environ.get('K_NTILES', '4'))
            if NT_ > 0:
                phaseA_tile(0, 128, 4, 1, False, 0, part="pre")
            for i in range(NT_):
                phaseA_tile(i * 512, 128, 4, 1, False, i, part="main")
                att_ops = S.record(lambda: attention_tile(i)) if "B" in stages else []
                pre_ops = S.record(lambda: phaseA_tile((i + 1) * 512, 128, 4, 1, False, i + 1, part="pre")) if i + 1 < NT_ else []
                S.replay(att_ops, pre_ops)
            ssm_out(0)
            if os.environ.get('K_SAMPLE', '1') == '1':
                S.barrier()
                A.top = mark_attn
                for _ in range(4):
                    hT.append(A.alloc([128, 512], F32))
                    hTb.append(A.alloc([128, 512], BF16))
                sconvT = A.alloc([128, 8, 12], F32)
                sct, r_sct = xt_ring.next()
                dma("sp", sct[:12], sconv_d, [], [r_sct])
                bi = psM.next()
                for c in range(8):
                    tp(banks[bi][:, c * 12:(c + 1) * 12], sct[:12, c * 128:(c + 1) * 128], identF[:12, :12], [r_sct, r_idF],
                       [bank_reg[bi]])
                cp(sconvT, banks[bi][:, 0:96].rearrange("p (c r) -> p c r", c=8), [bank_reg[bi]], [r_sconvT], eng="act")
                for b_ in range(4):
                    dma("sp", hout, sssm_d[b_].rearrange("(c p) n -> p c n", p=128), [], [r_hout])
                    bi = psM.next()
                    for c in range(4):
                        tp(banks[bi][:, c * 128:(c + 1) * 128], hout[:, c, :], identF, [r_hout, r_idF], [bank_reg[bi]])
                    cp(hT[1 + b_], banks[bi][:, :], [bank_reg[bi]], [r_hT[1 + b_]], eng="act")
                    cp(hTb[1 + b_], banks[bi][:, :], [bank_reg[bi]], [r_hTb[1 + b_]])
                phaseA_tile(2048, 32, 1, 4, True, 4)
                for b_ in range(4):
                    ssm_out(1 + b_)


        if "S" in stages:
            S.barrier()
            A.top = markA
            psT = Ring([0, 1, 2, 3]); psM = psT
            KTall = A.alloc([128, 4, 8192], BF16); r_KTall = [Reg() for _ in range(64)]
            Kp_ring = Ring([(A.alloc([128, 512], BF16), Reg()) for _ in range(6)])
            Vp_ring = Ring([(A.alloc([128, 512], BF16), Reg()) for _ in range(6)])
            Zc, r_Zc = load_const("c_Z", [128, 63], BF16)
            selE, r_selE = load_const("c_selE", [32, 32, 128], BF16)
            bdmask, r_bd = load_const("c_bdmask", [64, 512], F32)
            negcs, r_negcs = load_const("c_negcs", [32, 4, 64], BF16)
            iota, r_iota = load_const("c_iota", [128, 1], F32)
            onesB = A.alloc([128, 2], BF16); r_onesB = Reg()
            mset(onesB, 1.0, [r_onesB])
            pti = A.alloc([128, 64], I32); r_pti = Reg()
            ptf = A.alloc([128, 64], F32); r_ptf = Reg()
            idxi = A.alloc([128, 64], I32); r_idx = Reg()
            Qbd = A.alloc([128, 4, 64], BF16); r_Qbd = Reg()
            mset(Qbd, 0.0, [r_Qbd])
            km = A.alloc([32, 512], F32); r_km = Reg()
            kmT = A.alloc([128, 4, 32], BF16); r_kmT = Reg()
            gs = A.alloc([64, 32], F32); r_gs = Reg()
            m8s = A.alloc([64, 8], F32); r_m8s = Reg()
            bm = A.alloc([64, 32], BF16); r_bm = Reg()
            biasT = A.alloc([32, 64], BF16); r_biasT = Reg()
            PTs_ring = Ring([(A.alloc([128, 64], BF16), Reg()) for _ in range(4)])
            PTn = A.alloc([32, 64], BF16); r_PTn = Reg()
            om = A.alloc([64, 512], F32); r_om = Reg()
            osm = A.alloc([64, 64], F32); r_osm = Reg()
            rl = A.alloc([64, 1], F32); r_rl = Reg()
            o2 = A.alloc([64, 64], BF16); r_o2 = Reg()
            for b in range(4):
                dma("sp", pti, pt_d[b:b + 1, :].to_broadcast([128, 64]), [], [r_pti])
                cp(ptf, pti, [r_pti], [r_ptf])
                tsc(ptf, ptf, 128.0, ALU.mult, [r_ptf, r_iota], [r_ptf], s2=iota[:, 0:1], op1=ALU.add)
                cp(idxi, ptf, [r_ptf], [r_idx])
                for c in range(4):
                    cp(Qbd[0:64, c, (2 * c) * 8:(2 * c) * 8 + 8], QsT[0:64, c, b * 8:(b + 1) * 8], [r_QsT], [r_Qbd])
                    cp(Qbd[64:128, c, (2 * c + 1) * 8:(2 * c + 1) * 8 + 8], QsT[64:128, c, b * 8:(b + 1) * 8], [r_QsT], [r_Qbd])
                for pg in range(64):
                    j = pg // 2
                    Kp, r_Kp = Kp_ring.next()
                    gather(Kp, ck_d, idxi[:, pg:pg + 1], [r_idx], [r_Kp])
                    mm(banks[4][0:32, :], Zc[:, 31 - j:63 - j], Kp, [r_Zc, r_Kp], [bank_reg[4]], start=(pg == 0), stop=(pg == 63))
                    bt = psT.next()
                    for c in range(4):
                        tp(banks_b[bt][:, c * 128:(c + 1) * 128], Kp[:, c * 128:(c + 1) * 128], identB, [r_Kp, r_idB], [bank_reg[bt]])
                    cp(KTall[:, :, pg * 128:(pg + 1) * 128], banks_b[bt][:, 0:512].rearrange("p (c q) -> p c q", c=4),
                       [bank_reg[bt]], [r_KTall[pg]], eng=("act" if pg % 2 == 0 else "dve"))
                cp(km, banks[4][0:32, :], [bank_reg[4]], [r_km], eng="act")
                for c in range(4):
                    tp(banks[7][:, c * 32:(c + 1) * 32], km[0:32, c * 128:(c + 1) * 128], identF[0:32, 0:32], [r_km, r_idF], [bank_reg[7]])
                cp(kmT, banks[7][:, 0:128].rearrange("p (c j) -> p c j", c=4), [bank_reg[7]], [r_kmT], eng="act")
                for c in range(4):
                    mm(banks[7][0:64, 256:288], Qbd[:, c, :], kmT[:, c, :], [r_Qbd, r_kmT], [bank_reg[7]], start=(c == 0), stop=(c == 3))
                cp(gs, banks[7][0:64, 256:288], [bank_reg[7]], [r_gs])
                S.op("dve", lambda e: e.max(out=m8s, in_=gs), [r_gs], [r_m8s])
                tsc(bm, gs, m8s[:, 2:3], ALU.is_lt, [r_gs, r_m8s], [r_bm], s2=NEG, op1=ALU.mult)
                bt = psT.next()
                tp(banks_b[bt][0:32, 0:64], bm[0:64, 0:32], identB[0:64, 0:64], [r_bm, r_idB], [bank_reg[bt]])
                cp(biasT, banks_b[bt][0:32, 0:64], [bank_reg[bt]], [r_biasT], eng="act")
                def s_qk(pg):
                    j = pg // 2
                    bi = psM.next()
                    mm(banks[bi][:, 0:64], selE[0:32, j, :], biasT[0:32, :], [r_selE, r_biasT], [bank_reg[bi]], start=True, stop=False)
                    for c in range(4):
                        mm(banks[bi][:, c * 16:(c + 1) * 16], KTall[:, c, pg * 128:(pg + 1) * 128], Qbd[:, c, c * 16:(c + 1) * 16],
                           [r_KTall[pg], r_Qbd], [bank_reg[bi]], start=False, stop=(c == 3))
                    PTs, r_PTs = PTs_ring.next()
                    act(PTs, banks[bi][:, 0:64], AF.Exp, [bank_reg[bi]], [r_PTs], scale=0.125)
                    return PTs, r_PTs

                def s_pv(pg, PTs, r_PTs):
                    Vp, r_Vp = Vp_ring.next()
                    gather(Vp, cv_d, idxi[:, pg:pg + 1], [r_idx], [r_Vp])
                    mm(banks[5][0:64, :], PTs, Vp, [r_PTs, r_Vp], [bank_reg[5]], start=(pg == 0), stop=False)
                    mm(banks[6][0:64, 0:2], PTs, onesB[:, 0:2], [r_PTs, r_onesB], [bank_reg[6]], start=(pg == 0), stop=False)

                pend = None
                for pg in range(64):
                    cur = s_qk(pg)
                    if pend is not None:
                        s_pv(*pend)
                    pend = (pg,) + cur
                s_pv(*pend)
                bi = psM.next()
                mm(banks[bi][0:32, 0:64], identB[0:32, 0:32], negcs[0:32, b, :], [r_idB, r_negcs], [bank_reg[bi]], start=True, stop=False)
                for c in range(4):
                    mm(banks[bi][0:32, c * 16:(c + 1) * 16], KnT[:, c, 0:32], Qbd[:, c, c * 16:(c + 1) * 16], [r_KnT, r_Qbd],
                       [bank_reg[bi]], start=False, stop=(c == 3))
                act(PTn, banks[bi][0:32, 0:64], AF.Exp, [bank_reg[bi]], [r_PTn], scale=0.125)
                mm(banks[5][0:64, :], PTn, vb[0:32, :], [r_PTn, r_vb], [bank_reg[5]], start=False, stop=True)
                mm(banks[6][0:64, 0:2], PTn, onesB[0:32, 0:2], [r_PTn, r_onesB], [bank_reg[6]], start=False, stop=True)
                tt(om, banks[5][0:64, :], bdmask, ALU.mult, [bank_reg[5], r_bd], [r_om])
                red(osm, om.rearrange("p (h d) -> p d h", h=8), [r_om], [r_osm])
                rcp(rl, banks[6][0:64, 0:1], [bank_reg[6]], [r_rl])
                tsc(o2, osm, rl[:, 0:1], ALU.mult, [r_osm, r_rl], [r_o2])
                for h in range(8):
                    dma("sp", mix_d[2048 + 8 * b: 2048 + 8 * b + 8, h * 64:(h + 1) * 64], o2[h * 8:(h + 1) * 8, :], [r_o2], [])

        if "C" in stages:
            S.barrier()
            A.top = markA
            psT = Ring([0, 1, 2, 3]); psM = psT
            wout = A.alloc([128, 8, 1024], BF16); r_wout = Reg()
            dma("pool", wout, w_out_d.rearrange("(k p) c -> p k c", p=128), [], [r_wout])
            wq = A.alloc([128, 8, 512], BF16); r_wq = Reg()
            dma("pool", wq, wq_d.rearrange("(k p) c -> p k c", p=128), [], [r_wq])
            wo = A.alloc([128, 4, 1024], BF16); r_wo = Reg()
            dma("pool", wo, wo_d.rearrange("(k p) c -> p k c", p=128), [], [r_wo])
            g_x, r_gx = load_in(g_x_d, [128, 1024])
            g_mlp, r_gmlp = load_in(g_mlp_d, [128, 1024])
            gqx, r_gqx = load_in(gqx_d, [128, 128])
            gkx, r_gkx = load_in(gkx_d, [128, 128])
            MKT = A.alloc([128, 5, 4, 256], BF16); r_MKT = Reg()
            MV = A.alloc([128, 5, 2, 4, 130], BF16); r_MV = Reg(); r_MVone = Reg()
            mset(MV[:, :, :, :, 128:129], 1.0, [r_MVone])
            xt = A.alloc([128, 1024], F32); r_xt = Reg()
            junk = A.alloc([128, 1024], BF16); r_junk = Reg()
            hn = A.alloc([128, 1024], BF16); r_hn = Reg()
            ssq = A.alloc([128, 4], F32); r_ssq = Reg()
            nwk = (junk, r_junk, hn, r_hn, ssq, r_ssq, psT)
            qs = A.alloc([128, 512], F32); r_qs = Reg()
            sq = A.alloc([128, 512], F32); r_sq = Reg()
            st8 = A.alloc([128, 3, 8], F32); r_st8 = Reg()
            mb = A.alloc([128, 512], BF16); r_mb = Reg()
            markM = A.top
            wk = A.alloc([128, 8, 512], BF16); r_wk = Reg()
            dma("pool", wk, wk_d.rearrange("(k p) c -> p k c", p=128), [], [r_wk])
            wv = A.alloc([128, 8, 512], BF16); r_wv = Reg()
            dma("pool", wv, wv_d.rearrange("(k p) c -> p k c", p=128), [], [r_wv])
            g_mem, r_gmem = load_in(g_mem_d, [128, 1024])
            mnFM = A.alloc([128, 8, 256], BF16); r_mnFM = Reg()
            vs = A.alloc([128, 512], F32); r_vs = Reg()
            for mc in range(2):
                dma("sp", xt, mem_d[mc * 128:(mc + 1) * 128, :], [], [r_xt])
                rmsnorm_fm(xt, r_xt, 128, g_mem, r_gmem, mnFM, r_mnFM, mc * 128, nwk)
            for mc in range(2):
                bi = psM.next()
                for k in range(8):
                    mm(banks[bi][:, :], mnFM[:, k, mc * 128:(mc + 1) * 128], wk[:, k, :], [r_mnFM, r_wk], [bank_reg[bi]],
                       start=(k == 0), stop=(k == 7))
                headnorm(banks[bi][:, :], bank_reg[bi], 128, 4, 128, qs, r_qs, sq, r_sq, st8, r_st8)
                q3 = qs.rearrange("p (h d) -> p h d", h=4)
                tt(q3, q3, bc(gkx.unsqueeze(1), [128, 4, 128]), ALU.mult, [r_qs, r_gkx], [r_qs])
                dma("sp", newmk_d[mc * 128:(mc + 1) * 128, :], qs, [r_qs], [])
                cp(mb, qs, [r_qs], [r_mb])
                bt = psT.next()
                for h in range(4):
                    tp(banks_b[bt][:, h * 128:(h + 1) * 128], mb[:, h * 128:(h + 1) * 128], identB, [r_mb, r_idB], [bank_reg[bt]])
                cp(MKT[:, 0, :, mc * 128:(mc + 1) * 128], banks_b[bt][:, 0:512].rearrange("p (h q) -> p h q", h=4),
                   [bank_reg[bt]], [r_MKT], eng="act")
                bi = psM.next()
                for k in range(8):
                    mm(banks[bi][:, :], mnFM[:, k, mc * 128:(mc + 1) * 128], wv[:, k, :], [r_mnFM, r_wv], [bank_reg[bi]],
                       start=(k == 0), stop=(k == 7))
                cp(vs, banks[bi][:, :], [bank_reg[bi]], [r_vs], eng="act")
                dma("sp", newmv_d[mc * 128:(mc + 1) * 128, :], vs, [r_vs], [])
                cp(MV[:, 0, mc, :, 0:128], banks[bi][:, :].rearrange("p (h d) -> p h d", h=4), [bank_reg[bi]], [r_MV])
            for b in range(4):
                for mc in range(2):
                    dma("pool", mb, cmk_d[b, mc * 128:(mc + 1) * 128, :], [], [r_mb])
                    bt = psT.next()
                    for h in range(4):
                        tp(banks_b[bt][:, h * 128:(h + 1) * 128], mb[:, h * 128:(h + 1) * 128], identB, [r_mb, r_idB], [bank_reg[bt]])
                    cp(MKT[:, 1 + b, :, mc * 128:(mc + 1) * 128], banks_b[bt][:, 0:512].rearrange("p (h q) -> p h q", h=4),
                       [bank_reg[bt]], [r_MKT], eng="act")
                    dma("pool", MV[:, 1 + b, mc, :, 0:128], cmv_d[b, mc * 128:(mc + 1) * 128, :].rearrange("p (h d) -> p h d", h=4),
                        [], [r_MV])
            S.barrier()
            A.top = markM
            xr4 = A.alloc([128, 4, 1024], F32); r_xr4 = [Reg() for _ in range(4)]
            mt = A.alloc([128, 1024], BF16); r_mt = Reg()
            mixFM = A.alloc([128, 8, 512], BF16); r_mixFM = Reg()
            hxFM = A.alloc([128, 8, 512], BF16); r_hxFM = Reg()
            QxT = A.alloc([128, 4, 512], BF16); r_QxT = Reg()
            oxb = A.alloc([128, 4, 512], BF16); r_oxb = [Reg() for _ in range(4)]
            oxFM = A.alloc([128, 4, 512], BF16); r_oxFM = Reg()
            PTx_ring = Ring([(A.alloc([128, 512], BF16), Reg()) for _ in range(2)])
            PTxb = A.alloc([128, 4, 32], BF16); r_PTxb = Reg()
            mset(PTxb, 0.0, [r_PTxb])
            rinv = A.alloc([128, 4], F32); r_rinv = Reg()
            print("phase C arena top", A.top)
            obank = [4, 5, 6, 7]
            xscale = float(128 ** -0.5)
            wup_v = wup_d.rearrange("(k p) c -> p k c", p=128)

            def phaseC_tile(tok0, P, nsub, is_sample):
                W = P * nsub
                for s in range(nsub):
                    rows = slice(tok0 + s * P, tok0 + (s + 1) * P)
                    cols = slice(s * P, (s + 1) * P)
                    xs_ = xr4[:P, s, :]
                    dma("sp", mt[:P], mix_d[rows, :], [], [r_mt])
                    dma("sp", xs_, x_d[rows, :], [], [r_xr4[s]])
                    bt = psT.next()
                    for k in range(8):
                        tp(banks_b[bt][:, k * P:(k + 1) * P], mt[:P, k * 128:(k + 1) * 128], identB[:P, :P], [r_mt, r_idB], [bank_reg[bt]])
                    cp(mixFM[:, :, cols], banks_b[bt][:, 0:8 * P].rearrange("p (k q) -> p k q", k=8), [bank_reg[bt]], [r_mixFM], eng="act")
                    for half in range(2):
                        hc = slice(half * 512, (half + 1) * 512)
                        bi = psM.next()
                        for k in range(8):
                            mm(banks[bi][:P, :], mixFM[:, k, cols], wout[:, k, hc], [r_mixFM, r_wout], [bank_reg[bi]],
                               start=(k == 0), stop=(k == 7))
                        tt(xr4[:P, s, hc], banks[bi][:P, :], xr4[:P, s, hc], ALU.add, [bank_reg[bi], r_xr4[s]], [r_xr4[s]])
                    rmsnorm_fm(xs_, r_xr4[s], P, g_x, r_gx, hxFM, r_hxFM, s * P, nwk)
                    bi = psM.next()
                    for k in range(8):
                        mm(banks[bi][:P, :], hxFM[:, k, cols], wq[:, k, :], [r_hxFM, r_wq], [bank_reg[bi]], start=(k == 0), stop=(k == 7))
                    headnorm(banks[bi][:P, :], bank_reg[bi], P, 4, 128, qs, r_qs, sq, r_sq, st8, r_st8)
                    q3 = qs[:P].rearrange("p (h d) -> p h d", h=4)
                    tt(q3, q3, bc(gqx[:P].unsqueeze(1), [P, 4, 128]), ALU.mult, [r_qs, r_gqx], [r_qs])
                    cp(mb[:P], qs[:P], [r_qs], [r_mb])
                    bt = psT.next()
                    for h in range(4):
                        tp(banks_b[bt][:, h * P:(h + 1) * P], mb[:P, h * 128:(h + 1) * 128], identB[:P, :P], [r_mb, r_idB], [bank_reg[bt]])
                    cp(QxT[:, :, cols], banks_b[bt][:, 0:4 * P].rearrange("p (h q) -> p h q", h=4), [bank_reg[bt]], [r_QxT], eng="act")
                for h in range(4):
                    for mc in range(2):
                        mcs = slice(mc * 128, (mc + 1) * 128)
                        bi = psM.next()
                        if not is_sample:
                            mm(banks[bi][:, 0:W], MKT[:, 0, h, mcs], QxT[:, h, 0:W], [r_MKT, r_QxT], [bank_reg[bi]])
                            PTx, r_PTx = PTx_ring.next()
                            act(PTx[:, 0:W], banks[bi][:, 0:W], AF.Exp, [bank_reg[bi]], [r_PTx], scale=xscale)
                            for s in range(nsub):
                                mm(banks[obank[s]][:P, 0:129], PTx[:, s * P:(s + 1) * P], MV[:, 0, mc, h, 0:129], [r_PTx, r_MV, r_MVone],
                                   [bank_reg[obank[s]]], start=(mc == 0), stop=(mc == 1))
                        else:
                            for b in range(4):
                                mm(banks[bi][:, b * 8:(b + 1) * 8], MKT[:, 1 + b, h, mcs], QxT[:, h, b * 8:(b + 1) * 8], [r_MKT, r_QxT],
                                   [bank_reg[bi]])
                            for b in range(4):
                                act(PTxb[:, b, b * 8:(b + 1) * 8], banks[bi][:, b * 8:(b + 1) * 8], AF.Exp, [bank_reg[bi]], [r_PTxb],
                                    scale=xscale)
                            for b in range(4):
                                mm(banks[obank[0]][:32, 0:129], PTxb[:, b, :], MV[:, 1 + b, mc, h, 0:129], [r_PTxb, r_MV, r_MVone],
                                   [bank_reg[obank[0]]], start=(mc == 0 and b == 0), stop=(mc == 1 and b == 3))
                    for s in range(nsub):
                        ob = obank[s]
                        rcp(rinv[:P, s:s + 1], banks[ob][:P, 128:129], [bank_reg[ob]], [r_rinv])
                        tsc(oxb[:P, s, h * 128:(h + 1) * 128], banks[ob][:P, 0:128], rinv[:P, s:s + 1], ALU.mult,
                            [bank_reg[ob], r_rinv], [r_oxb[s]])
                for s in range(nsub):
                    cols = slice(s * P, (s + 1) * P)
                    bt = psT.next()
                    for h in range(4):
                        tp(banks_b[bt][:, h * P:(h + 1) * P], oxb[:P, s, h * 128:(h + 1) * 128], identB[:P, :P], [r_oxb[s], r_idB],
                           [bank_reg[bt]])
                    cp(oxFM[:, :, cols], banks_b[bt][:, 0:4 * P].rearrange("p (h q) -> p h q", h=4), [bank_reg[bt]], [r_oxFM], eng="act")
                    for half in range(2):
                        hc = slice(half * 512, (half + 1) * 512)
                        bi = psM.next()
                        for k in range(4):
                            mm(banks[bi][:P, :], oxFM[:, k, cols], wo[:, k, hc], [r_oxFM, r_wo], [bank_reg[bi]], start=(k == 0), stop=(k == 3))
                        tt(xr4[:P, s, hc], banks[bi][:P, :], xr4[:P, s, hc], ALU.add, [bank_reg[bi], r_xr4[s]], [r_xr4[s]])
                    rmsnorm_fm(xr4[:P, s, :], r_xr4[s], P, g_mlp, r_gmlp, hxFM, r_hxFM, s * P, nwk)
                r_x2 = [Reg() for _ in range(nsub)]
                for s in range(nsub):
                    dma("sp", y_d[tok0 + s * P: tok0 + (s + 1) * P, :], xr4[:P, s, :], [r_xr4[s]], [r_y2[(tok0 // 128) + s]])
                dma("sp", hm_d[:, :, tok0:tok0 + W], hxFM[:, :, 0:W], [r_hxFM], [r_hmd[tok0 // 512]])

            r_y2 = [Reg() for _ in range(17)]
            r_hmd = [Reg() for _ in range(5)]
            for i in range(int(os.environ.get('K_NTILES', '4'))):
                phaseC_tile(i * 512, 128, 4, False)
            if os.environ.get('K_SAMPLE', '1') == '1':
                phaseC_tile(2048, 32, 1, True)

            S.barrier()
            A.top = markA
            psM = Ring([0, 1, 2, 3])
            wup_sb = A.alloc([128, 8, 4096], BF16); r_wup = [Reg() for _ in range(8)]
            wdn_sb = A.alloc([128, 32, 1024], BF16); r_wdn = [Reg() for _ in range(8)]
            wdn_v = wdn_d.rearrange("(f p) c -> p f c", p=128)
            for fg in range(8):
                dma("pool", wup_sb[:, :, fg * 512:(fg + 1) * 512], wup_v[:, :, fg * 512:(fg + 1) * 512], [], [r_wup[fg]])
                dma("pool", wdn_sb[:, fg * 4:(fg + 1) * 4, :], wdn_v[:, fg * 4:(fg + 1) * 4, :], [], [r_wdn[fg]])
            hm_ring = Ring([(A.alloc([128, 8, 512], BF16), Reg()) for _ in range(2)])
            x2_ring = Ring([(A.alloc([128, 4, 1024], F32), [Reg() for _ in range(4)]) for _ in range(1)])
            hF_ring = Ring([(A.alloc([128, 4, 512], BF16), Reg()) for _ in range(2)])
            tmp_ring = Ring([(A.alloc([128, 512], BF16), Reg()) for _ in range(2)])
            print("phase D arena top", A.top)

            def phaseD_tile(tok0, P, nsub):
                W = P * nsub
                hm, r_hm = hm_ring.next()
                dma("sp", hm[:, :, 0:W], hm_d[:, :, tok0:tok0 + W], [r_hmd[tok0 // 512]], [r_hm])
                x2, r_x2 = x2_ring.next()
                for s in range(nsub):
                    dma("sp", x2[:P, s, :], y_d[tok0 + s * P: tok0 + (s + 1) * P, :], [r_y2[(tok0 // 128) + s]], [r_x2[s]])
                for fg in range(8):
                    hF, r_hF = hF_ring.next()
                    for f4 in range(4):
                        f = fg * 4 + f4
                        bi = psM.next()
                        for k in range(8):
                            mm(banks[bi][:, 0:W], wup_sb[:, k, f * 128:(f + 1) * 128], hm[:, k, 0:W], [r_wup[fg], r_hm], [bank_reg[bi]],
                               start=(k == 0), stop=(k == 7))
                        tmpr, r_tmpr = tmp_ring.next()
                        act(tmpr[:, 0:W], banks[bi][:, 0:W], AF.Relu, [bank_reg[bi]], [r_tmpr])
                        tt(hF[:, f4, 0:W], tmpr[:, 0:W], tmpr[:, 0:W], ALU.mult, [r_tmpr], [r_hF])
                    for s in range(nsub):
                        for half in range(2):
                            hc = slice(half * 512, (half + 1) * 512)
                            bi = obank[(s * 2 + half) % 4]
                            for f4 in range(4):
                                mm(banks[bi][:P, :], hF[:, f4, s * P:(s + 1) * P], wdn_sb[:, fg * 4 + f4, hc], [r_hF, r_wdn[fg]],
                                   [bank_reg[bi]], start=(f4 == 0), stop=(f4 == 3))
                            tt(x2[:P, s, hc], banks[bi][:P, :], x2[:P, s, hc], ALU.add, [bank_reg[bi], r_x2[s]], [r_x2[s]])
                for s in range(nsub):
                    dma("sp", y_d[tok0 + s * P: tok0 + (s + 1) * P, :], x2[:P, s, :], [r_x2[s]], [r_y2[(tok0 // 128) + s]])

            for i in range(int(os.environ.get('K_NTILES', '4'))):
                phaseD_tile(i * 512, 128, 4)
            if os.environ.get('K_SAMPLE', '1') == '1':
                phaseD_tile(2048, 32, 1)

        S.run(st)
    return nc


def prep_inputs(inputs, consts):
    f32 = np.float32
    xp = np.asarray(inputs["x_prompt"], f32)
    xs = np.asarray(inputs["x_sample"], f32)
    ck = np.ascontiguousarray(np.asarray(inputs["cache_k"], f32)[0].reshape(NPOOL * 128, 512))
    cv = np.ascontiguousarray(np.asarray(inputs["cache_v"], f32)[0].reshape(NPOOL * 128, 512))
    pt = np.asarray(inputs["page_table"], np.int32)

    def bcast(a, n=128):
        a = np.asarray(a, f32).reshape(1, -1)
        return np.ascontiguousarray(np.broadcast_to(a, (n, a.shape[1])))

    shared = {
        "ck": ck, "cv": cv,
        "w_in": np.ascontiguousarray(inputs["w_in"][0], f32), "w_out": np.ascontiguousarray(inputs["w_out"][0], f32),
        "wq_x": np.ascontiguousarray(inputs["wq_x"][0], f32), "wk_x": np.ascontiguousarray(inputs["wk_x"][0], f32),
        "wv_x": np.ascontiguousarray(inputs["wv_x"][0], f32), "wo_x": np.ascontiguousarray(inputs["wo_x"][0], f32),
        "w_up": np.ascontiguousarray(inputs["w_up"][0], f32), "w_down": np.ascontiguousarray(inputs["w_down"][0], f32),
        "g_mix": bcast(inputs["ln_mix_g"][0]), "g_x": bcast(inputs["ln_x_g"][0]),
        "g_mem": bcast(inputs["ln_mem_g"][0]), "g_mlp": bcast(inputs["ln_mlp_g"][0]),
        "gq": bcast(inputs["q_norm_g"][0]), "gk": bcast(inputs["k_norm_g"][0]),
        "gqx": bcast(inputs["qx_norm_g"][0]), "gkx": bcast(inputs["kx_norm_g"][0]),
        "gssm": bcast(inputs["ssm_norm_g"][0]), "dtb": bcast(inputs["dt_bias"][0]),
        "alog": bcast(inputs["a_log"][0]), "dskip": bcast(inputs["d_skip"][0]),
        "cw": np.ascontiguousarray(np.asarray(inputs["conv_w"][0], f32).reshape(4, 8, 128).transpose(2, 1, 0)),
        "cb": np.ascontiguousarray(np.asarray(inputs["conv_b"][0], f32).reshape(8, 128).T),
    }
    shared.update(consts)
    in_maps = []
    for c in range(8):
        m = dict(shared)
        m["x"] = np.ascontiguousarray(np.concatenate([xp[c], xs[4 * c:4 * c + 4].reshape(32, 1024)], axis=0))
        m["mem"] = np.ascontiguousarray(inputs["mem_prompt"][c], f32)
        m["pt"] = np.ascontiguousarray(pt[4 * c:4 * c + 4])
        m["sconv"] = np.ascontiguousarray(np.asarray(inputs["state_conv"], f32)[0, 4 * c:4 * c + 4].reshape(12, 1024))
        m["sssm"] = np.ascontiguousarray(np.asarray(inputs["state_ssm"], f32)[0, 4 * c:4 * c + 4].reshape(4, 512, 128))
        m["cmk"] = np.ascontiguousarray(np.asarray(inputs["cache_mem_k"], f32)[0, 4 * c:4 * c + 4].reshape(4, 256, 512))
        m["cmv"] = np.ascontiguousarray(np.asarray(inputs["cache_mem_v"], f32)[0, 4 * c:4 * c + 4].reshape(4, 256, 512))
        in_maps.append(m)
    return in_maps


def assemble(results):
    f32 = np.float32
    y_p = np.stack([r["y"][:2048] for r in results]).astype(f32)
    y_s = np.concatenate([r["y"][2048:].reshape(4, 8, 1024) for r in results]).astype(f32)
    nk_p = np.stack([r["newk"][:2048].reshape(2048, 8, 64) for r in results])[None].astype(f32)
    nv_p = np.stack([r["newv"][:2048].reshape(2048, 8, 64) for r in results])[None].astype(f32)
    nc_p = np.stack([r["newconv"][0:3] for r in results])[None].astype(f32)
    ns_p = np.stack([r["newssm"][0].reshape(8, 64, 128) for r in results])[None].astype(f32)
    mk_p = np.stack([r["newmk"].reshape(256, 4, 128) for r in results])[None].astype(f32)
    mv_p = np.stack([r["newmv"].reshape(256, 4, 128) for r in results])[None].astype(f32)
    nk_s = np.concatenate([r["newk"][2048:].reshape(4, 8, 8, 64) for r in results])[None].astype(f32)
    nv_s = np.concatenate([r["newv"][2048:].reshape(4, 8, 8, 64) for r in results])[None].astype(f32)
    nc_s = np.concatenate([r["newconv"][3:15].reshape(4, 3, 1024) for r in results])[None].astype(f32)
    ns_s = np.concatenate([r["newssm"][1:5].reshape(4, 8, 64, 128) for r in results])[None].astype(f32)
    return (y_p, y_s, nk_p, nv_p, nc_p, ns_p, mk_p, mv_p, nk_s, nv_s, nc_s, ns_s)


def kernel(**inputs):
    consts = host_consts()
    nc = build_program(consts, stages=("A", "B", "S", "M", "C"))
    in_maps = prep_inputs(inputs, consts)
    res = run_bass_kernel_spmd(nc, in_maps, core_ids=list(range(8)))
    return assemble(res.results)
```

```python
import os
import numpy as np
import ml_dtypes
from contextlib import ExitStack
import concourse.bass as bass
import concourse.mybir as mybir
from concourse.bass_utils import run_bass_kernel_spmd

F32 = mybir.dt.float32
BF16 = mybir.dt.bfloat16
I32 = mybir.dt.int32
AF = mybir.ActivationFunctionType
ALU = mybir.AluOpType
AX = mybir.AxisListType

NEG = -30000.0
EPS = 1e-6
NTOK = 2080
NPOOL = 2560


class Reg:
    __slots__ = ("w", "rs", "name", "excl")

    def __init__(self, name="", excl=False):
        self.w = None
        self.rs = []
        self.name = name
        self.excl = excl


class Op:
    __slots__ = ("eng", "emit", "deps", "signal", "isdma", "sem", "target", "prev_target", "bar")

    def __init__(self, eng, emit, isdma):
        self.eng = eng
        self.emit = emit
        self.deps = []
        self.signal = False
        self.isdma = isdma
        self.sem = None
        self.target = 0
        self.prev_target = 0
        self.bar = 0


ENGS = ["pe", "act", "dve", "pool", "sp"]


class Sched:
    def __init__(self, nc, n_dma_sems=12):
        self.nc = nc
        self.ops = {e: [] for e in ENGS}
        self.n_dma_sems = n_dma_sems
        self.barriers = [[]]
        self.since_bar_dma = []

    def op(self, eng, emit, r=(), w=(), dma=False):
        if getattr(self, "capture", None) is not None:
            self.capture.append((eng, emit, list(r), list(w), dma))
            return None
        o = Op(eng, emit, dma)
        self.count = getattr(self, "count", 0) + 1
        if self.count > int(os.environ.get("K_MAXOPS", "100000000")):
            return o
        if os.environ.get("K_TRACE"):
            import sys as _sys
            f = _sys._getframe(2)
            print("OP", self.count, eng, "dma" if dma else "", f.f_lineno, f.f_code.co_name, "<-", f.f_back.f_lineno)
        o.bar = len(self.barriers) - 1
        deps = {}
        if any(reg.excl for reg in r):
            w = list(w) + [reg for reg in r if reg.excl and reg not in w]
            r = [reg for reg in r if not reg.excl]
        for reg in r:
            if reg.w is not None:
                deps[id(reg.w)] = reg.w
        for reg in w:
            if reg.w is not None:
                deps[id(reg.w)] = reg.w
            for x in reg.rs:
                deps[id(x)] = x
        for d in deps.values():
            if d.eng == "pe" and eng == "pe" and not d.isdma and not dma:
                continue
            o.deps.append(d)
            d.signal = True
        for reg in r:
            reg.rs.append(o)
        for reg in w:
            reg.w = o
            reg.rs = []
        self.ops[eng].append(o)
        if dma:
            self.since_bar_dma.append(o)
        return o

    def record(self, fn):
        assert getattr(self, "capture", None) is None
        self.capture = []
        fn()
        items = self.capture
        self.capture = None
        return items

    def replay(self, a, b=()):
        i = j = 0
        while i < len(a) or j < len(b):
            if j >= len(b) or (i < len(a) and i * len(b) <= j * len(a)):
                self.op(*a[i]); i += 1
            else:
                self.op(*b[j]); j += 1

    def barrier(self):
        deps = list(self.since_bar_dma)
        for e in ENGS:
            for o in reversed(self.ops[e]):
                if not o.isdma:
                    o.signal = True
                    deps.append(o)
                    break
        self.barriers.append(deps)
        self.since_bar_dma = []

    def finalize(self, stack):
        nc = self.nc
        self.esem = {}
        self.dsems = {}
        for e in ENGS:
            self.esem[e] = stack.enter_context(nc.semaphore("s_" + e))
            self.dsems[e] = [stack.enter_context(nc.semaphore("d_%s_%d" % (e, i)))
                             for i in range(self.n_dma_sems)]
        self.final_dma = {}
        for e in ENGS:
            cnt = 0
            dcnt = [0] * self.n_dma_sems
            k = 0
            for o in self.ops[e]:
                if o.isdma:
                    i = k % self.n_dma_sems
                    k += 1
                    o.sem = self.dsems[e][i]
                    o.prev_target = dcnt[i]
                    dcnt[i] += 16
                    o.target = dcnt[i]
                elif o.signal:
                    cnt += 1
                    o.sem = self.esem[e]
                    o.target = cnt
            self.final_dma[e] = [o for o in self.ops[e] if o.isdma]

    def emit_engine(self, ename, e):
        waited = {}

        def wait(sem, val):
            key = id(sem)
            if waited.get(key, 0) >= val:
                return
            e.wait_ge(sem, val)
            waited[key] = val

        cur_bar = 0
        for o in self.ops[ename]:
            while cur_bar < o.bar:
                cur_bar += 1
                for d in self.barriers[cur_bar]:
                    wait(d.sem, d.target)
            for d in o.deps:
                wait(d.sem, d.target)
            if o.isdma and o.prev_target > 0:
                wait(o.sem, o.prev_target)
            ins = o.emit(e)
            if o.isdma:
                ins.then_inc(o.sem, 16)
            elif o.signal:
                ins.then_inc(o.sem, 1)
        for o in self.final_dma[ename]:
            wait(o.sem, o.target)

    def run(self, stack):
        self.finalize(stack)
        block = stack.enter_context(self.nc.Block())
        S = self

        @block.tensor
        def _(e):
            S.emit_engine("pe", e)

        @block.scalar
        def _(e):
            S.emit_engine("act", e)

        @block.vector
        def _(e):
            S.emit_engine("dve", e)

        @block.gpsimd
        def _(e):
            S.emit_engine("pool", e)

        @block.sync
        def _(e):
            S.emit_engine("sp", e)


class Arena:
    def __init__(self, t, nbytes):
        self.t = t
        self.cap = nbytes
        self.top = 0

    def alloc(self, shape, dt):
        n = 1
        for d in shape[1:]:
            n *= d
        esz = 2 if dt == BF16 else 4
        size = (n * esz + 31) // 32 * 32
        off = self.top
        self.top += size
        assert self.top <= self.cap, ("SBUF arena overflow", self.top, self.cap)
        v = self.t[0:shape[0], off // 2: off // 2 + (n * esz) // 2]
        if dt != BF16:
            v = v.bitcast(dt)
        if len(shape) == 3:
            v = v.rearrange("p (a b) -> p a b", a=shape[1])
        elif len(shape) == 4:
            v = v.rearrange("p (a b c) -> p a b c", a=shape[1], b=shape[2])
        elif len(shape) == 5:
            v = v.rearrange("p (a b c d) -> p a b c d", a=shape[1], b=shape[2], c=shape[3])
        return v


class Ring:
    def __init__(self, items):
        self.items = items
        self.i = 0

    def next(self):
        x = self.items[self.i % len(self.items)]
        self.i += 1
        return x


def host_consts():
    c = {}
    c["c_ident"] = np.eye(128, dtype=np.float32)
    half = 32
    inv_freq = (10000.0 ** (-np.arange(half, dtype=np.float32) / half)).astype(np.float32)
    cs = np.zeros((128, 17, 64), np.float32)
    p = np.arange(128)
    for ti in range(16):
        pos = (ti * 128 + p).astype(np.float32)
        ang = pos[:, None] * inv_freq[None, :]
        cs[:, ti, 0:32] = np.cos(ang)
        cs[:, ti, 32:64] = np.sin(ang)
    pos = (8192 + (p % 8)).astype(np.float32)
    ang = pos[:, None] * inv_freq[None, :]
    cs[:, 16, 0:32] = np.cos(ang)
    cs[:, 16, 32:64] = np.sin(ang)
    c["c_cs"] = cs
    nm = np.zeros((128, 4, 512), np.float32)
    f = np.arange(512)
    for r in range(4):
        nm[:, r, :] = np.where((r * 128 + p)[:, None] > f[None, :], NEG, 0.0)
    c["c_negmask"] = nm
    ind = np.zeros((8, 2048), np.float32)
    for j in range(8):
        ind[j, j * 256:(j + 1) * 256] = 1.0
    c["c_ind"] = ind
    cb0 = np.zeros((8, 8, 128), np.float32)
    for j in range(8):
        for own in range(8):
            cb0[j, own, :] = 0.0 if j <= own else NEG
    c["c_bias0"] = cb0
    t = np.arange(128)
    c["c_triU"] = (t[:, None] <= t[None, :]).astype(np.float32)
    c["c_mneg"] = np.where(t[None, :] < t[:, None], NEG, 0.0).astype(np.float32)
    c["c_ones"] = np.ones((128, 128), np.float32)
    t32 = np.arange(32)
    same = (t32[:, None] // 8) == (t32[None, :] // 8)
    c["c_triU_s"] = ((t32[:, None] <= t32[None, :]) & same).astype(np.float32)
    c["c_mneg_s"] = np.where((t32[None, :] < t32[:, None]) | (~same), NEG, 0.0).astype(np.float32)
    c["c_onesbd_s"] = same.astype(np.float32)
    seqsel = np.zeros((32, 4, 128), np.float32)
    for b in range(4):
        seqsel[b * 8:(b + 1) * 8, b, :] = 1.0
    c["c_seqsel"] = seqsel
    colmask = np.zeros((128, 4, 32), np.float32)
    for b in range(4):
        colmask[:, b, b * 8:(b + 1) * 8] = 1.0
    c["c_colmask"] = colmask
    rowmask = np.zeros((32, 4), np.float32)
    for b in range(4):
        rowmask[b * 8:(b + 1) * 8, b] = 1.0
    c["c_rowmask"] = rowmask
    Z = np.zeros((128, 63), np.float32)
    Z[:, 31] = 1.0
    c["c_Z"] = Z
    selE = np.zeros((32, 32, 128), np.float32)
    for j in range(32):
        selE[j, j, :] = 1.0
    c["c_selE"] = selE
    bd = np.zeros((64, 512), np.float32)
    for h in range(8):
        bd[h * 8:(h + 1) * 8, h * 64:(h + 1) * 64] = 1.0
    c["c_bdmask"] = bd
    negcs = np.zeros((32, 4, 64), np.float32)
    for key in range(32):
        for b in range(4):
            for tq in range(8):
                ok = (key // 8 == b) and (key % 8 <= tq)
                negcs[key, b, tq::8] = 0.0 if ok else NEG
    negcs2 = np.zeros((32, 4, 64), np.float32)
    for key in range(32):
        for b in range(4):
            for h in range(8):
                for tq in range(8):
                    ok = (key // 8 == b) and (key % 8 <= tq)
                    negcs2[key, b, h * 8 + tq] = 0.0 if ok else NEG
    c["c_negcs"] = negcs2
    c["c_iota"] = np.arange(128, dtype=np.float32).reshape(128, 1)
    return c


CONST_SHAPES = None


def build_program(consts, stages=("A", "S", "M", "C")):
    nc = bass.Bass("TRN2", target_bir_lowering=False)

    def din(name, shape, dt=F32):
        return nc.dram_tensor(name, list(shape), dt, kind="ExternalInput").ap()

    def dout(name, shape):
        return nc.dram_tensor(name, list(shape), F32, kind="ExternalOutput").ap()

    x_d = din("x", [NTOK, 1024])
    mem_d = din("mem", [256, 1024])
    ck_d = din("ck", [NPOOL * 128, 512])
    cv_d = din("cv", [NPOOL * 128, 512])
    pt_d = din("pt", [4, 64], I32)
    sconv_d = din("sconv", [12, 1024])
    sssm_d = din("sssm", [4, 512, 128])
    cmk_d = din("cmk", [4, 256, 512])
    cmv_d = din("cmv", [4, 256, 512])
    w_in_d = din("w_in", [1024, 3080])
    w_out_d = din("w_out", [1024, 1024])
    wq_d = din("wq_x", [1024, 512])
    wk_d = din("wk_x", [1024, 512])
    wv_d = din("wv_x", [1024, 512])
    wo_d = din("wo_x", [512, 1024])
    wup_d = din("w_up", [1024, 4096])
    wdn_d = din("w_down", [4096, 1024])
    g_mix_d = din("g_mix", [128, 1024])
    g_x_d = din("g_x", [128, 1024])
    g_mem_d = din("g_mem", [128, 1024])
    g_mlp_d = din("g_mlp", [128, 1024])
    gq_d = din("gq", [128, 64])
    gk_d = din("gk", [128, 64])
    gqx_d = din("gqx", [128, 128])
    gkx_d = din("gkx", [128, 128])
    gssm_d = din("gssm", [128, 512])
    dtb_d = din("dtb", [128, 8])
    alog_d = din("alog", [128, 8])
    dskip_d = din("dskip", [128, 8])
    cw_d = din("cw", [128, 8, 4])
    cb_d = din("cb", [128, 8])
    cd = {k: din(k, v.shape) for k, v in consts.items()}

    y_d = dout("y", [NTOK, 1024])
    newk_d = dout("newk", [NTOK, 512])
    newv_d = dout("newv", [NTOK, 512])
    newconv_d = dout("newconv", [15, 1024])
    newssm_d = dout("newssm", [5, 512, 128])
    newmk_d = dout("newmk", [256, 512])
    newmv_d = dout("newmv", [256, 512])
    mix_d = nc.dram_tensor("mixscr", [NTOK, 1024], BF16, kind="Internal").ap()
    hm_d = nc.dram_tensor("hmscr", [128, 8, NTOK], BF16, kind="Internal").ap()

    with ExitStack() as st:
        S = Sched(nc)
        ARENA_BYTES = 192 * 1024
        arena_t = st.enter_context(nc.sbuf_tensor("arena", [128, ARENA_BYTES // 2], BF16))
        A = Arena(arena_t, ARENA_BYTES)
        banks = [st.enter_context(nc.psum_tensor("bank%d" % i, [128, 512], F32)) for i in range(8)]
        banks_b = [b.bitcast(BF16) for b in banks]
        bank_reg = [Reg("bank%d" % i, excl=True) for i in range(8)]

        def mm(out, lhsT, rhs, r, w, start=True, stop=True):
            S.op("pe", lambda e: e.matmul(out, lhsT=lhsT, rhs=rhs, start=start, stop=stop), r, w)

        def tp(out, in_, idn, r, w):
            S.op("pe", lambda e: e.transpose(out=out, in_=in_, identity=idn), r, w)

        def act(out, in_, func, r, w, **kw):
            S.op("act", lambda e: e.activation(out=out, in_=in_, func=func, **kw), r, w)

        def tt(out, a, b, op, r, w, eng="dve"):
            S.op(eng, lambda e: e.tensor_tensor(out=out, in0=a, in1=b, op=op), r, w)

        def tsc(out, a, s1, op0, r, w, s2=None, op1=None):
            if op1 is None:
                S.op("dve", lambda e: e.tensor_scalar(out=out, in0=a, scalar1=s1, scalar2=None, op0=op0), r, w)
            else:
                S.op("dve", lambda e: e.tensor_scalar(out=out, in0=a, scalar1=s1, scalar2=s2, op0=op0, op1=op1), r, w)

        def stt(out, a, s, b, op0, op1, r, w, accum=None):
            if accum is None:
                S.op("dve", lambda e: e.scalar_tensor_tensor(out=out, in0=a, scalar=s, in1=b, op0=op0, op1=op1), r, w)
            else:
                S.op("dve", lambda e: e.scalar_tensor_tensor(out=out, in0=a, scalar=s, in1=b, op0=op0, op1=op1,
                                                             accum_out=accum), r, w)

        def cp(out, in_, r, w, eng="dve"):
            if eng == "act":
                S.op("act", lambda e: e.activation(out=out, in_=in_, func=AF.Copy), r, w)
            else:
                S.op(eng, lambda e: e.tensor_copy(out=out, in_=in_), r, w)

        def red(out, in_, r, w, op=ALU.add):
            S.op("dve", lambda e: e.tensor_reduce(out=out, in_=in_, axis=AX.X, op=op), r, w)

        def rcp(out, in_, r, w):
            S.op("dve", lambda e: e.reciprocal(out=out, in_=in_), r, w)

        def mset(ap, val, w, eng="dve"):
            S.op(eng, lambda e: e.memset(ap, val), (), w)

        def dma(q, out, in_, r, w):
            S.op(q, lambda e: e.dma_start(out=out, in_=in_), r, w, dma=True)

        def gather(out, table, idx, r, w):
            S.op("pool", lambda e: e.indirect_dma_start(out=out, out_offset=None, in_=table,
                                                        in_offset=bass.IndirectOffsetOnAxis(ap=idx, axis=0)),
                 r, w, dma=True)

        def bc(ap, shape):
            return ap.to_broadcast(list(shape))

        identF = A.alloc([128, 128], F32); r_idF = Reg()
        identB = A.alloc([128, 128], BF16); r_idB = Reg()
        cs = A.alloc([128, 17, 64], F32); r_cs = Reg()
        dma("sp", identF, cd["c_ident"], [], [r_idF])
        dma("pool", identB, cd["c_ident"], [], [r_idB])
        dma("sp", cs, cd["c_cs"], [], [r_cs])
        eps_t = A.alloc([128, 1], F32); r_eps = Reg()
        mset(eps_t, EPS, [r_eps])
        one_t = A.alloc([128, 1], F32); r_one = Reg()
        mset(one_t, 1.0, [r_one])

        QsT = A.alloc([128, 4, 32], BF16)
        KnT = A.alloc([128, 4, 32], BF16)
        vb = A.alloc([32, 512], BF16)

        def load_const(name, shape, dt, q=None):
            t_ = A.alloc(shape, dt)
            r_ = Reg(name)
            if q is None:
                q = "pool" if dt == BF16 else "sp"
            dma(q, t_, name if not isinstance(name, str) else cd[name], [], [r_])
            return t_, r_

        def load_in(d_ap, shape, dt=F32, q="sp"):
            t_ = A.alloc(shape, dt)
            r_ = Reg()
            dma(q if dt == F32 or dt == I32 else "pool", t_, d_ap, [], [r_])
            return t_, r_

        def small(shape, dt=F32):
            return A.alloc(shape, dt), Reg()

        def rmsnorm_fm(x_ap, r_x, P, gbc, r_g, outFM, r_out, col0, wk):
            junk, r_junk, hn, r_hn, ssq, r_ssq, psb = wk
            stt(junk[:P], x_ap, 1.0, x_ap, ALU.mult, ALU.mult, [r_x], [r_junk, r_ssq], accum=ssq[:P, 0:1])
            act(ssq[:P, 1:2], ssq[:P, 0:1], AF.Sqrt, [r_ssq, r_eps], [r_ssq], scale=1.0 / 1024, bias=eps_t[:P, :])
            rcp(ssq[:P, 2:3], ssq[:P, 1:2], [r_ssq], [r_ssq])
            stt(hn[:P], x_ap, ssq[:P, 2:3], gbc[:P], ALU.mult, ALU.mult, [r_x, r_ssq, r_g], [r_hn])
            bi = psb.next()
            for k in range(8):
                tp(banks_b[bi][:, k * P:(k + 1) * P], hn[:P, k * 128:(k + 1) * 128], identB[:P, :P],
                   [r_hn, r_idB], [bank_reg[bi]])
            cp(outFM[:, :, col0:col0 + P], banks_b[bi][:, 0:8 * P].rearrange("p (k q) -> p k q", k=8),
               [bank_reg[bi]], [r_out], eng="act")

        def headnorm(ps_ap, r_ps, P, nh, hd, qs, r_qs, sq, r_sq, st8, r_st8):
            cp(qs[:P], ps_ap, [r_ps], [r_qs], eng="act")
            act(sq[:P], ps_ap, AF.Square, [r_ps], [r_sq])
            red(st8[:P, 0, 0:nh], sq[:P].rearrange("p (h d) -> p h d", h=nh), [r_sq], [r_st8])
            act(st8[:P, 1, 0:nh], st8[:P, 0, 0:nh], AF.Sqrt, [r_st8, r_eps], [r_st8], scale=1.0 / hd, bias=eps_t[:P, :])
            rcp(st8[:P, 2, 0:nh], st8[:P, 1, 0:nh], [r_st8], [r_st8])
            q3 = qs[:P].rearrange("p (h d) -> p h d", h=nh)
            tt(q3, q3, bc(st8[:P, 2, 0:nh].unsqueeze(2), [P, nh, hd]), ALU.mult, [r_qs, r_st8], [r_qs])

        markA = A.top
        Win = A.alloc([128, 8, 3080], BF16); r_WinA = Reg(); r_WinB = Reg()
        w_in_v = w_in_d.rearrange("(k p) c -> p k c", p=128)
        dma("pool", Win[:, :, 0:1536], w_in_v[:, :, 0:1536], [], [r_WinA])
        dma("pool", Win[:, :, 1536:3080], w_in_v[:, :, 1536:3080], [], [r_WinB])
        g_mix, r_gmix = load_in(g_mix_d, [128, 1024])
        gq, r_gq = load_in(gq_d, [128, 64])
        gk, r_gk = load_in(gk_d, [128, 64])
        gssm, r_gssm = load_in(gssm_d, [128, 512])
        dtb, r_dtb = load_in(dtb_d, [128, 8])
        a_bc, r_abc = load_in(alog_d, [128, 8])
        dskip, r_dskip = load_in(dskip_d, [128, 8])
        cw, r_cw = load_in(cw_d, [128, 8, 4])
        cb, r_cb = load_in(cb_d, [128, 8])
        act(a_bc, a_bc, AF.Exp, [r_abc], [r_abc])
        tsc(a_bc, a_bc, -1.0, ALU.mult, [r_abc], [r_abc])
        triU, r_triU = load_const("c_triU", [128, 128], F32)
        mneg, r_mneg = load_const("c_mneg", [128, 128], F32)
        ones, r_ones = load_const("c_ones", [128, 128], F32)
        triU_s, r_triUs = load_const("c_triU_s", [32, 32], F32)
        mneg_s, r_mnegs = load_const("c_mneg_s", [32, 32], F32)
        onesbd_s, r_onesbds = load_const("c_onesbd_s", [32, 32], F32)
        seqsel, r_seqsel = load_const("c_seqsel", [32, 4, 128], F32)
        colmask, r_colmask = load_const("c_colmask", [128, 4, 32], BF16)
        rowmask, r_rowmask = load_const("c_rowmask", [32, 4], F32)

        KT = A.alloc([128, 8, 2048], BF16); r_KT = [Reg() for _ in range(16)]; r_KTind = Reg()
        for h in range(8):
            dma("pool", KT[64:72, h, :], cd["c_ind"], [], [r_KTind])
        VA = A.alloc([128, 16, 8, 66], BF16); r_VA = [Reg() for _ in range(16)]; r_VAone = Reg()
        mset(VA[:, :, :, 64:65], 1.0, [r_VAone])
        hnFM = A.alloc([128, 8, 512], BF16); r_hnFM = Reg()
        xc = A.alloc([128, 8, 512], BF16); r_xc = Reg()
        carry = A.alloc([128, 8, 3], F32); r_carry = Reg()
        mset(carry, 0.0, [r_carry])
        hT = [A.alloc([128, 512], F32)]
        hTb = [A.alloc([128, 512], BF16)]
        r_hT = [Reg() for _ in range(5)]
        r_hTb = [Reg() for _ in range(5)]
        mset(hT[0], 0.0, [r_hT[0]])
        mset(hTb[0], 0.0, [r_hTb[0]])

        xt_ring = Ring([(A.alloc([128, 1024], F32), Reg()) for _ in range(1)])
        junk = A.alloc([128, 1024], BF16); r_junk = Reg()
        hn = A.alloc([128, 1024], BF16); r_hn = Reg()
        ssq = A.alloc([128, 4], F32); r_ssq = Reg()
        psT = Ring([0, 1, 2, 3])
        psM = psT
        nwk = (junk, r_junk, hn, r_hn, ssq, r_ssq, psT)
        qs_ring = Ring([(A.alloc([128, 512], F32), Reg()) for _ in range(1)])
        sq = A.alloc([128, 512], F32); r_sq = Reg()
        st8 = A.alloc([128, 3, 8], F32); r_st8 = Reg()
        tab = A.alloc([128, 4, 32], F32); r_tab = Reg()
        rt = [A.alloc([128, 8, 32], F32) for _ in range(2)]; r_rt = [Reg() for _ in range(2)]
        rt = rt + rt; r_rt = r_rt + r_rt
        ko_ring = Ring([(A.alloc([128, 512], F32), Reg()) for _ in range(1)])
        qb = A.alloc([128, 512], BF16); r_qb = Reg()
        vs_ring = ko_ring
        zs = A.alloc([128, 512], F32); r_zs = Reg()
        dtt = A.alloc([128, 4, 8], F32); r_dtt = Reg()
        dtt2 = A.alloc([128, 4, 8], F32); r_dtt2 = Reg()
        xr_ring = Ring([(A.alloc([128, 3 + 512], F32), Reg()) for _ in range(1)])
        acc = A.alloc([128, 512], F32); r_acc = Reg()
        zs_bufs = [(zs, r_zs), (acc, r_acc)]
        dtt_bufs = [(dtt, r_dtt), (dtt2, r_dtt2)]
        xl, r_xl = xt_ring.items[0]
        xdt = A.alloc([128, 8, 64], BF16); r_xdt = Reg()
        xsT = A.alloc([128, 512], F32); r_xsT = Reg()
        Btm = A.alloc([128, 2, 128], BF16); r_Btm = Reg()
        Bm = A.alloc([128, 2, 128], BF16); r_Bm = Reg()
        cumt = A.alloc([128, 6, 8], F32); r_cumt = Reg()
        etot = A.alloc([128, 4, 8], F32); r_etot = Reg()
        dab_ring = Ring([(A.alloc([128, 128], F32), Reg()) for _ in range(2)])
        dec_ring = Ring([(A.alloc([128, 128], BF16), Reg()) for _ in range(2)])
        LT_ring = Ring([(A.alloc([128, 128], BF16), Reg()) for _ in range(2)])
        CTm = A.alloc([128, 4, 2, 32], BF16); r_CTm = Reg()
        y2s = A.alloc([128, 512], F32); r_y2s = Reg()
        yy = A.alloc([128, 512], F32); r_yy = Reg()
        ysq = sq; r_ysq = r_sq
        ss2 = A.alloc([128, 3, 2], F32); r_ss2 = Reg()
        mixs = A.alloc([128, 512], BF16); r_mixs = Reg()
        xdtt = A.alloc([128, 512], BF16); r_xdtt = Reg()
        hout = yy.rearrange("p (c n) -> p c n", c=4); r_hout = r_yy
        mark_attn = A.top
        negmask, r_negmask = load_const("c_negmask", [128, 4, 512], BF16)
        cbias0, r_cbias0 = A.alloc([128, 8, 128], BF16), Reg()
        dma("pool", cbias0[64:72], cd["c_bias0"], [], [r_cbias0])
        QT = A.alloc([128, 8, 512], BF16); r_QT = Reg(); r_QTb = Reg()
        PT_ring = Ring([(A.alloc([128, 512], BF16), Reg()) for _ in range(3)])
        attb = A.alloc([128, 4, 512], BF16); r_attb = [Reg() for _ in range(4)]
        rinv = A.alloc([128, 4], F32); r_rinv = Reg()
        m8 = A.alloc([128, 8, 8], F32); r_m8 = Reg()
        bsel = A.alloc([128, 8, 8], F32); r_bsel = Reg()
        biasm = A.alloc([128, 8, 8], BF16); r_biasm = Reg()
        ksumT = A.alloc([128, 8, 8], BF16); r_ksum = Reg()
        ksf = A.alloc([128, 8], F32); r_ksf = Reg()
        gsb = A.alloc([128, 8, 8], F32); r_gsb = Reg()
        mset(gsb, -1e30, [r_gsb])
        r_sconvT = Reg(); r_QsT = Reg(); r_KnT = Reg(); r_vb = Reg()
        print("phase A arena top", A.top)

        def qk_post(ps_ap, r_ps, P, ti, g_bc, r_g, is_k, row0):
            qs, r_qs = qs_ring.next()
            headnorm(ps_ap, r_ps, P, 8, 64, qs, r_qs, sq, r_sq, st8, r_st8)
            cos = cs[:P, ti, 0:32]
            sin = cs[:P, ti, 32:64]
            tt(tab[:P, 0, :], cos, g_bc[:P, 0:32], ALU.mult, [r_cs, r_g], [r_tab])
            tt(tab[:P, 1, :], sin, g_bc[:P, 32:64], ALU.mult, [r_cs, r_g], [r_tab])
            tt(tab[:P, 2, :], cos, g_bc[:P, 32:64], ALU.mult, [r_cs, r_g], [r_tab])
            tt(tab[:P, 3, :], sin, g_bc[:P, 0:32], ALU.mult, [r_cs, r_g], [r_tab])
            q3 = qs[:P].rearrange("p (h d) -> p h d", h=8)
            x1 = q3[:, :, 0:32]
            x2 = q3[:, :, 32:64]
            tb = [bc(tab[:P, i, :].unsqueeze(1), [P, 8, 32]) for i in range(4)]
            ko, r_ko = ko_ring.next()
            o3 = ko[:P].rearrange("p (h d) -> p h d", h=8)
            tt(rt[0][:P], x1, tb[0], ALU.mult, [r_qs, r_tab], [r_rt[0]])
            tt(rt[1][:P], x2, tb[1], ALU.mult, [r_qs, r_tab], [r_rt[1]])
            tt(o3[:, :, 0:32], rt[0][:P], rt[1][:P], ALU.subtract, [r_rt[0], r_rt[1]], [r_ko])
            tt(rt[2][:P], x2, tb[2], ALU.mult, [r_qs, r_tab], [r_rt[2]])
            tt(rt[3][:P], x1, tb[3], ALU.mult, [r_qs, r_tab], [r_rt[3]])
            tt(o3[:, :, 32:64], rt[2][:P], rt[3][:P], ALU.add, [r_rt[2], r_rt[3]], [r_ko])
            if is_k:
                dma("sp", newk_d[row0:row0 + P, :], ko[:P], [r_ko], [])
            cp(qb[:P], ko[:P], [r_ko], [r_qb], eng="act")
            return ko, r_ko

        ssd_ring = Ring([5, 7])

        def ssd_chunk(L, nseq, c0, seqs, row0, tri_c, r_tri, mneg_c, r_mn, ones_c, r_on, zs, r_zs, dtt, r_dtt):
            dt_ = dtt[:L, 1, :]
            da_ = dtt[:L, 2, :]
            bi = 7
            for c in range(4):
                tp(banks_b[bi][:L, c * 128:(c + 1) * 128], xc[:, c, c0:c0 + L], identB, [r_xc, r_idB], [bank_reg[bi]])
            psv = banks_b[bi][:L, 0:512].rearrange("p (h d) -> p h d", h=8)
            tt(xdt[:L], psv, bc(dt_.unsqueeze(2), [L, 8, 64]), ALU.mult, [bank_reg[bi], r_dtt], [r_xdt])
            cp(xsT[:L], banks_b[bi][:L, 0:512], [bank_reg[bi]], [r_xsT], eng="act")
            bi = 7
            for g in range(2):
                tp(banks_b[bi][:L, g * 128:(g + 1) * 128], xc[:, 4 + g, c0:c0 + L], identB, [r_xc, r_idB], [bank_reg[bi]])
            cp(Btm[:L], banks_b[bi][:L, 0:256].rearrange("p (g n) -> p g n", g=2), [bank_reg[bi]], [r_Btm], eng="act")
            b5 = 5
            mm(banks[b5][:L, 0:8], tri_c[:L, :L], da_, [r_tri, r_dtt], [bank_reg[b5]])
            mm(banks[b5][:L, 8:16], ones_c[:L, :L], da_, [r_on, r_dtt], [bank_reg[b5]])
            for bidx in range(nseq):
                lhs = ones[:L, :] if nseq == 1 else seqsel[:L, bidx, :]
                mm(banks[b5][:, 16 + bidx * 8:24 + bidx * 8], lhs, da_, [r_ones, r_seqsel, r_dtt], [bank_reg[b5]])
            cp(cumt[:L, 0, :], banks[b5][:L, 0:8], [bank_reg[b5]], [r_cumt])
            tsc(cumt[:L, 1, :], banks[b5][:L, 0:8], -1.0, ALU.mult, [bank_reg[b5]], [r_cumt])
            tt(cumt[:L, 2, :], banks[b5][:L, 8:16], cumt[:L, 0, :], ALU.subtract, [bank_reg[b5], r_cumt], [r_cumt])
            act(cumt[:L, 3, :], cumt[:L, 2, :], AF.Exp, [r_cumt], [r_cumt])
            act(cumt[:L, 4, :], cumt[:L, 0, :], AF.Exp, [r_cumt], [r_cumt])
            act(etot[:, 0:nseq, :], banks[b5][:, 16:16 + 8 * nseq].rearrange("p (b h) -> p b h", b=nseq), AF.Exp,
                [bank_reg[b5]], [r_etot])
            b4 = 4
            for g in range(2):
                mm(banks[b4][:L, g * 128:g * 128 + L], xc[:, 4 + g, c0:c0 + L], xc[:, 6 + g, c0:c0 + L],
                   [r_xc], [bank_reg[b4]])
            b6, b7 = 6, 7
            for h in range(8):
                g = h // 4
                bi = ssd_ring.next()
                dab, r_dab = dab_ring.next()
                cp(dab[:L, 0:L], bc(dtt[:L, 2, h:h + 1], [L, L]), [r_dtt], [r_dab])
                mm(banks[bi][:L, 0:L], dab[:L, 0:L], tri_c[:L, :L], [r_dab, r_tri], [bank_reg[bi]], start=True, stop=False)
                mm(banks[bi][:L, 0:L], identF[:L, :L], mneg_c[:L, :L], [r_idF, r_mn], [bank_reg[bi]], start=False, stop=True)
                dec, r_dec = dec_ring.next()
                act(dec[:L, :L], banks[bi][:L, 0:L], AF.Exp, [bank_reg[bi], r_cumt], [r_dec], bias=cumt[:L, 1, h:h + 1])
                LT, r_LT = LT_ring.next()
                tt(LT[:L, :L], banks[b4][:L, g * 128:g * 128 + L], dec[:L, :L], ALU.mult, [bank_reg[b4], r_dec], [r_LT])
                mm(banks[b6][:L, h * 64:(h + 1) * 64], LT[:L, :L], xdt[:L, h, :], [r_LT, r_xdt], [bank_reg[b6]])
            if nseq > 1:
                for bidx in range(nseq):
                    for g in range(2):
                        tt(CTm[:, bidx, g, :], xc[:, 6 + g, c0:c0 + L], colmask[:, bidx, :], ALU.mult,
                           [r_xc, r_colmask], [r_CTm])
            for g in range(2):
                for bidx, sq_ in enumerate(seqs):
                    lhs = xc[:, 6 + g, c0:c0 + L] if nseq == 1 else CTm[:, bidx, g, :]
                    mm(banks[b7][:L, g * 256:(g + 1) * 256], lhs, hTb[sq_][:, g * 256:(g + 1) * 256],
                       [r_xc, r_CTm, r_hTb[sq_]], [bank_reg[b7]], start=(bidx == 0), stop=(bidx == nseq - 1))
            tt(y2s[:L].rearrange("p (h d) -> p h d", h=8), banks[b7][:L, :].rearrange("p (h d) -> p h d", h=8),
               bc(cumt[:L, 4, :].unsqueeze(2), [L, 8, 64]), ALU.mult, [bank_reg[b7], r_cumt], [r_y2s])
            tt(yy[:L], banks[b6][:L, :], y2s[:L], ALU.add, [bank_reg[b6], r_y2s], [r_yy])
            tt(y2s[:L].rearrange("p (h d) -> p h d", h=8), xsT[:L].rearrange("p (h d) -> p h d", h=8),
               bc(dskip[:L, :].unsqueeze(2), [L, 8, 64]), ALU.mult, [r_xsT, r_dskip], [r_y2s])
            tt(yy[:L], yy[:L], y2s[:L], ALU.add, [r_yy, r_y2s], [r_yy])
            tt(yy[:L], yy[:L], zs[:L], ALU.mult, [r_yy, r_zs], [r_yy])
            tt(ysq[:L], yy[:L], yy[:L], ALU.mult, [r_yy], [r_ysq])
            red(ss2[:L, 0, :], ysq[:L].rearrange("p (g d) -> p g d", g=2), [r_ysq], [r_ss2])
            act(ss2[:L, 1, :], ss2[:L, 0, :], AF.Sqrt, [r_ss2, r_eps], [r_ss2], scale=1.0 / 256, bias=eps_t[:L, :])
            rcp(ss2[:L, 2, :], ss2[:L, 1, :], [r_ss2], [r_ss2])
            tt(yy[:L].rearrange("p (g d) -> p g d", g=2), yy[:L].rearrange("p (g d) -> p g d", g=2),
               bc(ss2[:L, 2, :].unsqueeze(2), [L, 2, 256]), ALU.mult, [r_yy, r_ss2], [r_yy])
            tt(mixs[:L], yy[:L], gssm[:L], ALU.mult, [r_yy, r_gssm], [r_mixs])
            dma("sp", mix_d[row0:row0 + L, 512:1024], mixs[:L], [r_mixs], [])
            tt(xdtt[:L].rearrange("p (h d) -> p h d", h=8), xdt[:L], bc(cumt[:L, 3, :].unsqueeze(2), [L, 8, 64]),
               ALU.mult, [r_xdt, r_cumt], [r_xdtt])
            for bidx, sq_ in enumerate(seqs):
                if nseq > 1:
                    tsc(Bm[:L].rearrange("p g n -> p (g n)"), Btm[:L].rearrange("p g n -> p (g n)"),
                        rowmask[:L, bidx:bidx + 1], ALU.mult, [r_Btm, r_rowmask], [r_Bm])
                    Bsrc, r_Bs = Bm, r_Bm
                else:
                    Bsrc, r_Bs = Btm, r_Btm
                bi = ssd_ring.next()
                for g in range(2):
                    mm(banks[bi][:, g * 256:(g + 1) * 256], Bsrc[:L, g, :], xdtt[:L, g * 256:(g + 1) * 256],
                       [r_Bs, r_xdtt], [bank_reg[bi]])
                h3 = hT[sq_].rearrange("p (h d) -> p h d", h=8)
                tt(h3, h3, bc(etot[:, bidx, :].unsqueeze(2), [128, 8, 64]), ALU.mult, [r_hT[sq_], r_etot], [r_hT[sq_]])
                tt(hT[sq_], hT[sq_], banks[bi][:, :], ALU.add, [r_hT[sq_], bank_reg[bi]], [r_hT[sq_]])
                cp(hTb[sq_], hT[sq_], [r_hT[sq_]], [r_hTb[sq_]], eng="act")

        def ssm_out(sq_):
            bi = psM.next()
            for c in range(4):
                tp(banks[bi][:, c * 128:(c + 1) * 128], hT[sq_][:, c * 128:(c + 1) * 128], identF, [r_hT[sq_], r_idF],
                   [bank_reg[bi]])
            cp(hout, banks[bi][:, :].rearrange("p (c n) -> p c n", c=4), [bank_reg[bi]], [r_hout], eng="act")
            dma("sp", newssm_d[sq_].rearrange("(c p) n -> p c n", p=128), hout, [r_hout], [])

        ringPre = Ring([2, 3])
        ringAtt = Ring([0, 1])
        nwk_pre = (junk, r_junk, hn, r_hn, ssq, r_ssq, ringPre)

        def phaseA_tile(tok0, P, nsub, nseq, is_sample, tidx, part="all"):
            W = P * nsub
            L = W // nseq
            if part in ("all", "pre"):
                phaseA_pre(tok0, P, nsub, nseq, is_sample)
            if part in ("all", "main"):
                phaseA_main(tok0, P, nsub, nseq, is_sample)

        def phaseA_pre(tok0, P, nsub, nseq, is_sample):
            W = P * nsub
            L = W // nseq
            for s in range(nsub):
                xt, r_xt = xt_ring.next()
                dma("sp", xt[:P], x_d[tok0 + s * P: tok0 + (s + 1) * P, :], [], [r_xt])
                rmsnorm_fm(xt[:P], r_xt, P, g_mix, r_gmix, hnFM, r_hnFM, s * P, nwk_pre)
            for c in range(8):
                bi = ringPre.next()
                for k in range(8):
                    mm(banks[bi][:, 0:W], Win[:, k, 2048 + c * 128: 2048 + (c + 1) * 128], hnFM[:, k, 0:W],
                       [r_WinB, r_hnFM], [bank_reg[bi]], start=(k == 0), stop=(k == 7))
                xr, r_xr = xr_ring.next()
                xr3 = xr[:, 0:nseq * (3 + L)].rearrange("p (b l) -> p b l", b=nseq)
                if is_sample:
                    cp(xr3[:, :, 0:3], sconvT[:, c, :].rearrange("p (b r) -> p b r", b=4), [r_sconvT], [r_xr])
                else:
                    cp(xr3[:, :, 0:3], carry[:, c, :].unsqueeze(1), [r_carry], [r_xr])
                cp(xr3[:, :, 3:3 + L], banks[bi][:, 0:W].rearrange("p (b l) -> p b l", b=nseq), [bank_reg[bi]], [r_xr],
                   eng="act")
                acc3 = acc[:, 0:W].rearrange("p (b l) -> p b l", b=nseq)
                tsc(acc3, xr3[:, :, 0:L], cw[:, c, 0:1], ALU.mult, [r_xr, r_cw, r_cb], [r_acc], s2=cb[:, c:c + 1], op1=ALU.add)
                for i in range(1, 4):
                    stt(acc3, xr3[:, :, i:i + L], cw[:, c, i:i + 1], acc3, ALU.mult, ALU.add, [r_xr, r_cw, r_acc], [r_acc])
                act(xc[:, c, 0:W], acc[:, 0:W], AF.Silu, [r_acc], [r_xc])
                if not is_sample:
                    cp(carry[:, c, :], xr[:, W:W + 3], [r_xr], [r_carry])

        def phaseA_main(tok0, P, nsub, nseq, is_sample):
            W = P * nsub
            L = W // nseq
            def front(s):
                row0 = tok0 + s * P
                zs, r_zs = zs_bufs[s % 2]
                dtt, r_dtt = dtt_bufs[s % 2]
                ti = 16 if is_sample else (tok0 // 128 + s)
                cols = slice(s * P, (s + 1) * P)

                def proj(c0, c1, r_w):
                    bi_ = psM.next()
                    for k in range(8):
                        mm(banks[bi_][:P, 0:c1 - c0], hnFM[:, k, cols], Win[:, k, c0:c1], [r_hnFM, r_w], [bank_reg[bi_]],
                           start=(k == 0), stop=(k == 7))
                    return bi_
                bk_ = proj(512, 1024, r_WinA)
                bq_ = proj(0, 512, r_WinA)
                bv_ = proj(1024, 1536, r_WinA)
                bz_ = proj(1536, 2048, r_WinB)
                bi = bk_
                qk_post(banks[bi][:P, :], bank_reg[bi], P, ti, gk, r_gk, True, row0)
                if not is_sample:
                    bt = psT.next()
                    for h in range(8):
                        tp(banks_b[bt][0:64, h * 128:(h + 1) * 128], qb[:, h * 64:(h + 1) * 64], identB, [r_qb, r_idB],
                           [bank_reg[bt]])
                    kti = tok0 // 128 + s
                    cp(KT[0:64, :, kti * 128:(kti + 1) * 128], banks_b[bt][0:64, 0:1024].rearrange("p (h q) -> p h q", h=8),
                       [bank_reg[bt]], [r_KT[kti]], eng="act")
                else:
                    bt = psT.next()
                    for c in range(4):
                        tp(banks_b[bt][:, c * 32:(c + 1) * 32], qb[:32, c * 128:(c + 1) * 128], identB[:32, :32],
                           [r_qb, r_idB], [bank_reg[bt]])
                    cp(KnT, banks_b[bt][:, 0:128].rearrange("p (c q) -> p c q", c=4), [bank_reg[bt]], [r_KnT], eng="act")
                bi = bq_
                qk_post(banks[bi][:P, :], bank_reg[bi], P, ti, gq, r_gq, False, row0)
                if not is_sample:
                    bt = psT.next()
                    for h in range(8):
                        tp(banks_b[bt][0:64, h * 128:(h + 1) * 128], qb[:, h * 64:(h + 1) * 64], identB, [r_qb, r_idB],
                           [bank_reg[bt]])
                    cp(QT[0:64, :, cols], banks_b[bt][0:64, 0:1024].rearrange("p (h q) -> p h q", h=8),
                       [bank_reg[bt]], [r_QT], eng="act")
                else:
                    bt = psT.next()
                    for c in range(4):
                        tp(banks_b[bt][:, c * 32:(c + 1) * 32], qb[:32, c * 128:(c + 1) * 128], identB[:32, :32],
                           [r_qb, r_idB], [bank_reg[bt]])
                    cp(QsT, banks_b[bt][:, 0:128].rearrange("p (c q) -> p c q", c=4), [bank_reg[bt]], [r_QsT], eng="act")
                bi = bv_
                vs, r_vs = vs_ring.next()
                cp(vs[:P], banks[bi][:P, :], [bank_reg[bi]], [r_vs], eng="act")
                dma("sp", newv_d[row0:row0 + P, :], vs[:P], [r_vs], [])
                if not is_sample:
                    kti = tok0 // 128 + s
                    cp(VA[:, kti, :, 0:64], banks[bi][:, :].rearrange("p (h d) -> p h d", h=8), [bank_reg[bi]], [r_VA[kti]])
                else:
                    cp(vb, banks[bi][:32, :], [bank_reg[bi]], [r_vb])
                bi = bz_
                act(zs[:P], banks[bi][:P, :], AF.Silu, [bank_reg[bi]], [r_zs])
                bi = proj(3072, 3080, r_WinB)
                tt(dtt[:P, 0, :], banks[bi][:P, 0:8], dtb[:P], ALU.add, [bank_reg[bi], r_dtb], [r_dtt])
                act(dtt[:P, 0, :], dtt[:P, 0, :], AF.Exp, [r_dtt], [r_dtt])
                act(dtt[:P, 1, :], dtt[:P, 0, :], AF.Ln, [r_dtt, r_one], [r_dtt], bias=one_t[:P, :])
                tt(dtt[:P, 2, :], dtt[:P, 1, :], a_bc[:P], ALU.mult, [r_dtt, r_abc], [r_dtt])
                if is_sample or (tok0 == 1536 and s == 3):
                    for half in range(2):
                        bi = proj(2048 + half * 512, 2048 + (half + 1) * 512, r_WinB)
                        cp(xl[:P, half * 512:(half + 1) * 512], banks[bi][:P, :], [bank_reg[bi]], [r_xl], eng="act")
                    if is_sample:
                        for b_ in range(4):
                            dma("sp", newconv_d[3 + 3 * b_: 6 + 3 * b_, :], xl[8 * b_ + 5: 8 * b_ + 8, :], [r_xl], [])
                    else:
                        dma("sp", newconv_d[0:3, :], xl[125:128, :], [r_xl], [])
            def back(s):
                row0 = tok0 + s * P
                zs, r_zs = zs_bufs[s % 2]
                dtt, r_dtt = dtt_bufs[s % 2]
                if is_sample:
                    ssd_chunk(32, 4, 0, [1, 2, 3, 4], row0, triU_s, r_triUs, mneg_s, r_mnegs, onesbd_s, r_onesbds, zs, r_zs, dtt, r_dtt)
                else:
                    ssd_chunk(128, 1, s * 128, [0], row0, triU, r_triU, mneg, r_mneg, ones, r_ones, zs, r_zs, dtt, r_dtt)

            f_ops = S.record(lambda: front(0))
            S.replay(f_ops)
            for s in range(nsub):
                b_ops = S.record(lambda: back(s))
                if s + 1 < nsub:
                    f_ops = S.record(lambda: front(s + 1))
                    S.replay(f_ops, b_ops)
                else:
                    S.replay(b_ops)

        def attention_tile(i):
            tok0 = i * 512
            for j in (2 * i, 2 * i + 1):
                red(ksf[0:64, :], KT[0:64, :, j * 256:(j + 1) * 256], [r_KT[2 * j], r_KT[2 * j + 1]], [r_ksf])
                cp(ksumT[0:64, :, j], ksf[0:64, :], [r_ksf], [r_ksum])
            for s in range(4):
                own = (tok0 + s * 128) // 256
                cols = slice(s * 128, (s + 1) * 128)
                if own <= 3:
                    cp(QT[64:72, :, cols], bc(cbias0[64:72, own, :].unsqueeze(1), [8, 8, 128]), [r_cbias0], [r_QTb])
                else:
                    bi = ringAtt.next()
                    for h in range(8):
                        mm(banks[bi][:, h * 8:h * 8 + own], QT[0:64, h, cols], ksumT[0:64, h, 0:own], [r_QT, r_ksum],
                           [bank_reg[bi]])
                    cp(gsb[:, :, 0:own], banks[bi][:, 0:64].rearrange("p (h j) -> p h j", h=8)[:, :, 0:own],
                       [bank_reg[bi]], [r_gsb])
                    for h in range(8):
                        S.op("dve", (lambda o_, i_: (lambda e: e.max(out=o_, in_=i_)))(m8[:, h, :], gsb[:, h, :]),
                             [r_gsb], [r_m8])
                    tt(bsel, gsb, bc(m8[:, :, 2:3], [128, 8, 8]), ALU.is_lt, [r_gsb, r_m8], [r_bsel])
                    tsc(biasm, bsel, NEG, ALU.mult, [r_bsel], [r_biasm])
                    mset(biasm[:, :, own:own + 1], 0.0, [r_biasm])
                    bt = ringAtt.next()
                    for h in range(8):
                        tp(banks_b[bt][64:72, h * 128:(h + 1) * 128], biasm[:, h, :], identB, [r_biasm, r_idB],
                           [bank_reg[bt]])
                    cp(QT[64:72, :, cols], banks_b[bt][64:72, 0:1024].rearrange("p (h q) -> p h q", h=8),
                       [bank_reg[bt]], [r_QTb], eng="act")
            obank = [4, 5, 6, 7]
            nk = 4 * i + 4
            units = [(h, kc) for h in range(8) for kc in range(nk)]

            def emit_qk(h, kc):
                diag = kc >= 4 * i
                bi = ringAtt.next()
                mm(banks[bi][:, :], KT[0:72, h, kc * 128:(kc + 1) * 128], QT[0:72, h, :],
                   [r_KT[kc], r_KTind, r_QT, r_QTb], [bank_reg[bi]], start=True, stop=not diag)
                if diag:
                    mm(banks[bi][:, :], identB, negmask[:, kc - 4 * i, :], [r_idB, r_negmask], [bank_reg[bi]],
                       start=False, stop=True)
                PT, r_PT = PT_ring.next()
                act(PT, banks[bi][:, :], AF.Exp, [bank_reg[bi]], [r_PT], scale=0.125)
                return PT, r_PT

            def emit_pv(h, kc, PT, r_PT):
                for s in range(4):
                    last = 4 * i + s
                    if kc > last:
                        continue
                    ob = obank[s]
                    mm(banks[ob][:, 0:65], PT[:, s * 128:(s + 1) * 128], VA[:, kc, h, 0:65], [r_PT, r_VA[kc], r_VAone],
                       [bank_reg[ob]], start=(kc == 0), stop=(kc == last))
                if kc == nk - 1:
                    for s in range(4):
                        ob = obank[s]
                        rcp(rinv[:, s:s + 1], banks[ob][:, 64:65], [bank_reg[ob]], [r_rinv])
                        tsc(attb[:, s, h * 64:(h + 1) * 64], banks[ob][:, 0:64], rinv[:, s:s + 1], ALU.mult,
                            [bank_reg[ob], r_rinv], [r_attb[s]])

            pend = None
            for (h, kc) in units:
                cur = emit_qk(h, kc)
                if pend is not None:
                    emit_pv(*pend)
                pend = (h, kc) + cur
            emit_pv(*pend)
            for s in range(4):
                dma("sp", mix_d[tok0 + s * 128: tok0 + (s + 1) * 128, 0:512], attb[:, s, :], [r_attb[s]], [])

        if "A" in stages:
            NT_ = int(os.environ.get('K_NTILES', '4'))
            if NT_ > 0:
                phaseA_tile(0, 128, 4, 1, False, 0, part="pre")
            for i in range(NT_):
                phaseA_tile(i * 512, 128, 4, 1, False, i, part="main")
                att_ops = S.record(lambda: attention_tile(i)) if "B" in stages else []
                pre_ops = S.record(lambda: phaseA_tile((i + 1) * 512, 128, 4, 1, False, i + 1, part="pre")) if i + 1 < NT_ else []
                S.replay(att_ops, pre_ops)
            ssm_out(0)
            if os.environ.get('K_SAMPLE', '1') == '1':
                S.barrier()
                A.top = mark_attn
                for _ in range(4):
                    hT.append(A.alloc([128, 512], F32))
                    hTb.append(A.alloc([128, 512], BF16))
                sconvT = A.alloc([128, 8, 12], F32)
                sct, r_sct = xt_ring.next()
                dma("sp", sct[:12], sconv_d, [], [r_sct])
                bi = psM.next()
                for c in range(8):
                    tp(banks[bi][:, c * 12:(c + 1) * 12], sct[:12, c * 128:(c + 1) * 128], identF[:12, :12], [r_sct, r_idF],
                       [bank_reg[bi]])
                cp(sconvT, banks[bi][:, 0:96].rearrange("p (c r) -> p c r", c=8), [bank_reg[bi]], [r_sconvT], eng="act")
                for b_ in range(4):
                    dma("sp", hout, sssm_d[b_].rearrange("(c p) n -> p c n", p=128), [], [r_hout])
                    bi = psM.next()
                    for c in range(4):
                        tp(banks[bi][:, c * 128:(c + 1) * 128], hout[:, c, :], identF, [r_hout, r_idF], [bank_reg[bi]])
                    cp(hT[1 + b_], banks[bi][:, :], [bank_reg[bi]], [r_hT[1 + b_]], eng="act")
                    cp(hTb[1 + b_], banks[bi][:, :], [bank_reg[bi]], [r_hTb[1 + b_]])
                phaseA_tile(2048, 32, 1, 4, True, 4)
                for b_ in range(4):
                    ssm_out(1 + b_)


        if "S" in stages:
            S.barrier()
            A.top = markA
            psT = Ring([0, 1, 2, 3]); psM = psT
            KTall = A.alloc([128, 4, 8192], BF16); r_KTall = [Reg() for _ in range(64)]
            Kp_ring = Ring([(A.alloc([128, 512], BF16), Reg()) for _ in range(6)])
            Vp_ring = Ring([(A.alloc([128, 512], BF16), Reg()) for _ in range(6)])
            Zc, r_Zc = load_const("c_Z", [128, 63], BF16)
            selE, r_selE = load_const("c_selE", [32, 32, 128], BF16)
            bdmask, r_bd = load_const("c_bdmask", [64, 512], F32)
            negcs, r_negcs = load_const("c_negcs", [32, 4, 64], BF16)
            iota, r_iota = load_const("c_iota", [128, 1], F32)
            onesB = A.alloc([128, 2], BF16); r_onesB = Reg()
            mset(onesB, 1.0, [r_onesB])
            pti = A.alloc([128, 64], I32); r_pti = Reg()
            ptf = A.alloc([128, 64], F32); r_ptf = Reg()
            idxi = A.alloc([128, 64], I32); r_idx = Reg()
            Qbd = A.alloc([128, 4, 64], BF16); r_Qbd = Reg()
            mset(Qbd, 0.0, [r_Qbd])
            km = A.alloc([32, 512], F32); r_km = Reg()
            kmT = A.alloc([128, 4, 32], BF16); r_kmT = Reg()
            gs = A.alloc([64, 32], F32); r_gs = Reg()
            m8s = A.alloc([64, 8], F32); r_m8s = Reg()
            bm = A.alloc([64, 32], BF16); r_bm = Reg()
            biasT = A.alloc([32, 64], BF16); r_biasT = Reg()
            PTs_ring = Ring([(A.alloc([128, 64], BF16), Reg()) for _ in range(4)])
            PTn = A.alloc([32, 64], BF16); r_PTn = Reg()
            om = A.alloc([64, 512], F32); r_om = Reg()
            osm = A.alloc([64, 64], F32); r_osm = Reg()
            rl = A.alloc([64, 1], F32); r_rl = Reg()
            o2 = A.alloc([64, 64], BF16); r_o2 = Reg()
            for b in range(4):
                dma("sp", pti, pt_d[b:b + 1, :].to_broadcast([128, 64]), [], [r_pti])
                cp(ptf, pti, [r_pti], [r_ptf])
                tsc(ptf, ptf, 128.0, ALU.mult, [r_ptf, r_iota], [r_ptf], s2=iota[:, 0:1], op1=ALU.add)
                cp(idxi, ptf, [r_ptf], [r_idx])
                for c in range(4):
                    cp(Qbd[0:64, c, (2 * c) * 8:(2 * c) * 8 + 8], QsT[0:64, c, b * 8:(b + 1) * 8], [r_QsT], [r_Qbd])
                    cp(Qbd[64:128, c, (2 * c + 1) * 8:(2 * c + 1) * 8 + 8], QsT[64:128, c, b * 8:(b + 1) * 8], [r_QsT], [r_Qbd])
                for pg in range(64):
                    j = pg // 2
                    Kp, r_Kp = Kp_ring.next()
                    gather(Kp, ck_d, idxi[:, pg:pg + 1], [r_idx], [r_Kp])
                    mm(banks[4][0:32, :], Zc[:, 31 - j:63 - j], Kp, [r_Zc, r_Kp], [bank_reg[4]], start=(pg == 0), stop=(pg == 63))
                    bt = psT.next()
                    for c in range(4):
                        tp(banks_b[bt][:, c * 128:(c + 1) * 128], Kp[:, c * 128:(c + 1) * 128], identB, [r_Kp, r_idB], [bank_reg[bt]])
                    cp(KTall[:, :, pg * 128:(pg + 1) * 128], banks_b[bt][:, 0:512].rearrange("p (c q) -> p c q", c=4),
                       [bank_reg[bt]], [r_KTall[pg]], eng=("act" if pg % 2 == 0 else "dve"))
                cp(km, banks[4][0:32, :], [bank_reg[4]], [r_km], eng="act")
                for c in range(4):
                    tp(banks[7][:, c * 32:(c + 1) * 32], km[0:32, c * 128:(c + 1) * 128], identF[0:32, 0:32], [r_km, r_idF], [bank_reg[7]])
                cp(kmT, banks[7][:, 0:128].rearrange("p (c j) -> p c j", c=4), [bank_reg[7]], [r_kmT], eng="act")
                for c in range(4):
                    mm(banks[7][0:64, 256:288], Qbd[:, c, :], kmT[:, c, :], [r_Qbd, r_kmT], [bank_reg[7]], start=(c == 0), stop=(c == 3))
                cp(gs, banks[7][0:64, 256:288], [bank_reg[7]], [r_gs])
                S.op("dve", lambda e: e.max(out=m8s, in_=gs), [r_gs], [r_m8s])
                tsc(bm, gs, m8s[:, 2:3], ALU.is_lt, [r_gs, r_m8s], [r_bm], s2=NEG, op1=ALU.mult)
                bt = psT.next()
                tp(banks_b[bt][0:32, 0:64], bm[0:64, 0:32], identB[0:64, 0:64], [r_bm, r_idB], [bank_reg[bt]])
                cp(biasT, banks_b[bt][0:32, 0:64], [bank_reg[bt]], [r_biasT], eng="act")
                def s_qk(pg):
                    j = pg // 2
                    bi = psM.next()
                    mm(banks[bi][:, 0:64], selE[0:32, j, :], biasT[0:32, :], [r_selE, r_biasT], [bank_reg[bi]], start=True, stop=False)
                    for c in range(4):
                        mm(banks[bi][:, c * 16:(c + 1) * 16], KTall[:, c, pg * 128:(pg + 1) * 128], Qbd[:, c, c * 16:(c + 1) * 16],
                           [r_KTall[pg], r_Qbd], [bank_reg[bi]], start=False, stop=(c == 3))
                    PTs, r_PTs = PTs_ring.next()
                    act(PTs, banks[bi][:, 0:64], AF.Exp, [bank_reg[bi]], [r_PTs], scale=0.125)
                    return PTs, r_PTs

                def s_pv(pg, PTs, r_PTs):
                    Vp, r_Vp = Vp_ring.next()
                    gather(Vp, cv_d, idxi[:, pg:pg + 1], [r_idx], [r_Vp])
                    mm(banks[5][0:64, :], PTs, Vp, [r_PTs, r_Vp], [bank_reg[5]], start=(pg == 0), stop=False)
                    mm(banks[6][0:64, 0:2], PTs, onesB[:, 0:2], [r_PTs, r_onesB], [bank_reg[6]], start=(pg == 0), stop=False)

                pend = None
                for pg in range(64):
                    cur = s_qk(pg)
                    if pend is not None:
                        s_pv(*pend)
                    pend = (pg,) + cur
                s_pv(*pend)
                bi = psM.next()
                mm(banks[bi][0:32, 0:64], identB[0:32, 0:32], negcs[0:32, b, :], [r_idB, r_negcs], [bank_reg[bi]], start=True, stop=False)
                for c in range(4):
                    mm(banks[bi][0:32, c * 16:(c + 1) * 16], KnT[:, c, 0:32], Qbd[:, c, c * 16:(c + 1) * 16], [r_KnT, r_Qbd],
                       [bank_reg[bi]], start=False, stop=(c == 3))
                act(PTn, banks[bi][0:32, 0:64], AF.Exp, [bank_reg[bi]], [r_PTn], scale=0.125)
                mm(banks[5][0:64, :], PTn, vb[0:32, :], [r_PTn, r_vb], [bank_reg[5]], start=False, stop=True)
                mm(banks[6][0:64, 0:2], PTn, onesB[0:32, 0:2], [r_PTn, r_onesB], [bank_reg[6]], start=False, stop=True)
                tt(om, banks[5][0:64, :], bdmask, ALU.mult, [bank_reg[5], r_bd], [r_om])
                red(osm, om.rearrange("p (h d) -> p d h", h=8), [r_om], [r_osm])
                rcp(rl, banks[6][0:64, 0:1], [bank_reg[6]], [r_rl])
                tsc(o2, osm, rl[:, 0:1], ALU.mult, [r_osm, r_rl], [r_o2])
                for h in range(8):
                    dma("sp", mix_d[2048 + 8 * b: 2048 + 8 * b + 8, h * 64:(h + 1) * 64], o2[h * 8:(h + 1) * 8, :], [r_o2], [])

        if "C" in stages:
            S.barrier()
            A.top = markA
            psT = Ring([0, 1, 2, 3]); psM = psT
            wout = A.alloc([128, 8, 1024], BF16); r_wout = Reg()
            dma("pool", wout, w_out_d.rearrange("(k p) c -> p k c", p=128), [], [r_wout])
            wq = A.alloc([128, 8, 512], BF16); r_wq = Reg()
            dma("pool", wq, wq_d.rearrange("(k p) c -> p k c", p=128), [], [r_wq])
            wo = A.alloc([128, 4, 1024], BF16); r_wo = Reg()
            dma("pool", wo, wo_d.rearrange("(k p) c -> p k c", p=128), [], [r_wo])
            g_x, r_gx = load_in(g_x_d, [128, 1024])
            g_mlp, r_gmlp = load_in(g_mlp_d, [128, 1024])
            gqx, r_gqx = load_in(gqx_d, [128, 128])
            gkx, r_gkx = load_in(gkx_d, [128, 128])
            MKT = A.alloc([128, 5, 4, 256], BF16); r_MKT = Reg()
            MV = A.alloc([128, 5, 2, 4, 130], BF16); r_MV = Reg(); r_MVone = Reg()
            mset(MV[:, :, :, :, 128:129], 1.0, [r_MVone])
            xt = A.alloc([128, 1024], F32); r_xt = Reg()
            junk = A.alloc([128, 1024], BF16); r_junk = Reg()
            hn = A.alloc([128, 1024], BF16); r_hn = Reg()
            ssq = A.alloc([128, 4], F32); r_ssq = Reg()
            nwk = (junk, r_junk, hn, r_hn, ssq, r_ssq, psT)
            qs = A.alloc([128, 512], F32); r_qs = Reg()
            sq = A.alloc([128, 512], F32); r_sq = Reg()
            st8 = A.alloc([128, 3, 8], F32); r_st8 = Reg()
            mb = A.alloc([128, 512], BF16); r_mb = Reg()
            markM = A.top
            wk = A.alloc([128, 8, 512], BF16); r_wk = Reg()
            dma("pool", wk, wk_d.rearrange("(k p) c -> p k c", p=128), [], [r_wk])
            wv = A.alloc([128, 8, 512], BF16); r_wv = Reg()
            dma("pool", wv, wv_d.rearrange("(k p) c -> p k c", p=128), [], [r_wv])
            g_mem, r_gmem = load_in(g_mem_d, [128, 1024])
            mnFM = A.alloc([128, 8, 256], BF16); r_mnFM = Reg()
            vs = A.alloc([128, 512], F32); r_vs = Reg()
            for mc in range(2):
                dma("sp", xt, mem_d[mc * 128:(mc + 1) * 128, :], [], [r_xt])
                rmsnorm_fm(xt, r_xt, 128, g_mem, r_gmem, mnFM, r_mnFM, mc * 128, nwk)
            for mc in range(2):
                bi = psM.next()
                for k in range(8):
                    mm(banks[bi][:, :], mnFM[:, k, mc * 128:(mc + 1) * 128], wk[:, k, :], [r_mnFM, r_wk], [bank_reg[bi]],
                       start=(k == 0), stop=(k == 7))
                headnorm(banks[bi][:, :], bank_reg[bi], 128, 4, 128, qs, r_qs, sq, r_sq, st8, r_st8)
                q3 = qs.rearrange("p (h d) -> p h d", h=4)
                tt(q3, q3, bc(gkx.unsqueeze(1), [128, 4, 128]), ALU.mult, [r_qs, r_gkx], [r_qs])
                dma("sp", newmk_d[mc * 128:(mc + 1) * 128, :], qs, [r_qs], [])
                cp(mb, qs, [r_qs], [r_mb])
                bt = psT.next()
                for h in range(4):
                    tp(banks_b[bt][:, h * 128:(h + 1) * 128], mb[:, h * 128:(h + 1) * 128], identB, [r_mb, r_idB], [bank_reg[bt]])
                cp(MKT[:, 0, :, mc * 128:(mc + 1) * 128], banks_b[bt][:, 0:512].rearrange("p (h q) -> p h q", h=4),
                   [bank_reg[bt]], [r_MKT], eng="act")
                bi = psM.next()
                for k in range(8):
                    mm(banks[bi][:, :], mnFM[:, k, mc * 128:(mc + 1) * 128], wv[:, k, :], [r_mnFM, r_wv], [bank_reg[bi]],
                       start=(k == 0), stop=(k == 7))
                cp(vs, banks[bi][:, :], [bank_reg[bi]], [r_vs], eng="act")
                dma("sp", newmv_d[mc * 128:(mc + 1) * 128, :], vs, [r_vs], [])
                cp(MV[:, 0, mc, :, 0:128], banks[bi][:, :].rearrange("p (h d) -> p h d", h=4), [bank_reg[bi]], [r_MV])
            for b in range(4):
                for mc in range(2):
                    dma("pool", mb, cmk_d[b, mc * 128:(mc + 1) * 128, :], [], [r_mb])
                    bt = psT.next()
                    for h in range(4):
                        tp(banks_b[bt][:, h * 128:(h + 1) * 128], mb[:, h * 128:(h + 1) * 128], identB, [r_mb, r_idB], [bank_reg[bt]])
                    cp(MKT[:, 1 + b, :, mc * 128:(mc + 1) * 128], banks_b[bt][:, 0:512].rearrange("p (h q) -> p h q", h=4),
                       [bank_reg[bt]], [r_MKT], eng="act")
                    dma("pool", MV[:, 1 + b, mc, :, 0:128], cmv_d[b, mc * 128:(mc + 1) * 128, :].rearrange("p (h d) -> p h d", h=4),
                        [], [r_MV])
            S.barrier()
            A.top = markM
            xr4 = A.alloc([128, 4, 1024], F32); r_xr4 = [Reg() for _ in range(4)]
            mt = A.alloc([128, 1024], BF16); r_mt = Reg()
            mixFM = A.alloc([128, 8, 512], BF16); r_mixFM = Reg()
            hxFM = A.alloc([128, 8, 512], BF16); r_hxFM = Reg()
            QxT = A.alloc([128, 4, 512], BF16); r_QxT = Reg()
            oxb = A.alloc([128, 4, 512], BF16); r_oxb = [Reg() for _ in range(4)]
            oxFM = A.alloc([128, 4, 512], BF16); r_oxFM = Reg()
            PTx_ring = Ring([(A.alloc([128, 512], BF16), Reg()) for _ in range(2)])
            PTxb = A.alloc([128, 4, 32], BF16); r_PTxb = Reg()
            mset(PTxb, 0.0, [r_PTxb])
            rinv = A.alloc([128, 4], F32); r_rinv = Reg()
            print("phase C arena top", A.top)
            r_mixFMs = [Reg() for _ in range(4)]
            r_hxFMs = [Reg() for _ in range(4)]
            r_QxTs = [Reg() for _ in range(4)]
            ws0 = (mt, r_mt, (junk, r_junk, hn, r_hn, ssq, r_ssq, Ring([0, 1])), qs, r_qs, sq, r_sq, st8, r_st8, mb, r_mb, Ring([0, 1]))
            mt_b = A.alloc([128, 1024], BF16); junk_b = A.alloc([128, 1024], BF16); hn_b = A.alloc([128, 1024], BF16)
            ssq_b = A.alloc([128, 4], F32); qs_b = A.alloc([128, 512], F32); sq_b = A.alloc([128, 512], F32)
            st8_b = A.alloc([128, 3, 8], F32); mb_b = A.alloc([128, 512], BF16)
            ws1 = (mt_b, Reg(), (junk_b, Reg(), hn_b, Reg(), ssq_b, Reg(), Ring([2, 3])), qs_b, Reg(), sq_b, Reg(), st8_b, Reg(),
                   mb_b, Reg(), Ring([2, 3]))
            obank = [4, 5, 6, 7]
            xscale = float(128 ** -0.5)
            wup_v = wup_d.rearrange("(k p) c -> p k c", p=128)

            def phaseC_tile(tok0, P, nsub, is_sample):
                W = P * nsub

                def c1_a(s, ws):
                    mt, r_mt, nwk, qs, r_qs, sq, r_sq, st8, r_st8, mb, r_mb, psT = ws
                    psM = psT
                    rows = slice(tok0 + s * P, tok0 + (s + 1) * P)
                    cols = slice(s * P, (s + 1) * P)
                    xs_ = xr4[:P, s, :]
                    dma("sp", mt[:P], mix_d[rows, :], [], [r_mt])
                    dma("sp", xs_, x_d[rows, :], [], [r_xr4[s]])
                    bt = psT.next()
                    for k in range(8):
                        tp(banks_b[bt][:, k * P:(k + 1) * P], mt[:P, k * 128:(k + 1) * 128], identB[:P, :P], [r_mt, r_idB], [bank_reg[bt]])
                    cp(mixFM[:, :, cols], banks_b[bt][:, 0:8 * P].rearrange("p (k q) -> p k q", k=8), [bank_reg[bt]], [r_mixFMs[s]], eng="act")
                    for half in range(2):
                        hc = slice(half * 512, (half + 1) * 512)
                        bi = psM.next()
                        for k in range(8):
                            mm(banks[bi][:P, :], mixFM[:, k, cols], wout[:, k, hc], [r_mixFMs[s], r_wout], [bank_reg[bi]],
                               start=(k == 0), stop=(k == 7))
                        tt(xr4[:P, s, hc], banks[bi][:P, :], xr4[:P, s, hc], ALU.add, [bank_reg[bi], r_xr4[s]], [r_xr4[s]])
                    rmsnorm_fm(xs_, r_xr4[s], P, g_x, r_gx, hxFM, r_hxFMs[s], s * P, nwk)
                    bi = psM.next()
                    for k in range(8):
                        mm(banks[bi][:P, :], hxFM[:, k, cols], wq[:, k, :], [r_hxFMs[s], r_wq], [bank_reg[bi]], start=(k == 0), stop=(k == 7))
                    headnorm(banks[bi][:P, :], bank_reg[bi], P, 4, 128, qs, r_qs, sq, r_sq, st8, r_st8)
                    q3 = qs[:P].rearrange("p (h d) -> p h d", h=4)
                    tt(q3, q3, bc(gqx[:P].unsqueeze(1), [P, 4, 128]), ALU.mult, [r_qs, r_gqx], [r_qs])
                    cp(mb[:P], qs[:P], [r_qs], [r_mb])
                    bt = psT.next()
                    for h in range(4):
                        tp(banks_b[bt][:, h * P:(h + 1) * P], mb[:P, h * 128:(h + 1) * 128], identB[:P, :P], [r_mb, r_idB], [bank_reg[bt]])
                    cp(QxT[:, :, cols], banks_b[bt][:, 0:4 * P].rearrange("p (h q) -> p h q", h=4), [bank_reg[bt]], [r_QxTs[s]], eng="act")
                if nsub == 4:
                    for s0 in (0, 2):
                        oa = S.record(lambda: c1_a(s0, ws0))
                        ob_ = S.record(lambda: c1_a(s0 + 1, ws1))
                        S.replay(oa, ob_)
                else:
                    c1_a(0, ws0)
                for h in range(4):
                    for mc in range(2):
                        mcs = slice(mc * 128, (mc + 1) * 128)
                        bi = psM.next()
                        if not is_sample:
                            mm(banks[bi][:, 0:W], MKT[:, 0, h, mcs], QxT[:, h, 0:W], [r_MKT] + r_QxTs, [bank_reg[bi]])
                            PTx, r_PTx = PTx_ring.next()
                            act(PTx[:, 0:W], banks[bi][:, 0:W], AF.Exp, [bank_reg[bi]], [r_PTx], scale=xscale)
                            for s in range(nsub):
                                mm(banks[obank[s]][:P, 0:129], PTx[:, s * P:(s + 1) * P], MV[:, 0, mc, h, 0:129], [r_PTx, r_MV, r_MVone],
                                   [bank_reg[obank[s]]], start=(mc == 0), stop=(mc == 1))
                        else:
                            for b in range(4):
                                mm(banks[bi][:, b * 8:(b + 1) * 8], MKT[:, 1 + b, h, mcs], QxT[:, h, b * 8:(b + 1) * 8], [r_MKT] + r_QxTs,
                                   [bank_reg[bi]])
                            for b in range(4):
                                act(PTxb[:, b, b * 8:(b + 1) * 8], banks[bi][:, b * 8:(b + 1) * 8], AF.Exp, [bank_reg[bi]], [r_PTxb],
                                    scale=xscale)
                            for b in range(4):
                                mm(banks[obank[0]][:32, 0:129], PTxb[:, b, :], MV[:, 1 + b, mc, h, 0:129], [r_PTxb, r_MV, r_MVone],
                                   [bank_reg[obank[0]]], start=(mc == 0 and b == 0), stop=(mc == 1 and b == 3))
                    for s in range(nsub):
                        ob = obank[s]
                        rcp(rinv[:P, s:s + 1], banks[ob][:P, 128:129], [bank_reg[ob]], [r_rinv])
                        tsc(oxb[:P, s, h * 128:(h + 1) * 128], banks[ob][:P, 0:128], rinv[:P, s:s + 1], ALU.mult,
                            [bank_reg[ob], r_rinv], [r_oxb[s]])
                for s in range(nsub):
                    cols = slice(s * P, (s + 1) * P)
                    bt = psT.next()
                    for h in range(4):
                        tp(banks_b[bt][:, h * P:(h + 1) * P], oxb[:P, s, h * 128:(h + 1) * 128], identB[:P, :P], [r_oxb[s], r_idB],
                           [bank_reg[bt]])
                    cp(oxFM[:, :, cols], banks_b[bt][:, 0:4 * P].rearrange("p (h q) -> p h q", h=4), [bank_reg[bt]], [r_oxFM], eng="act")
                    for half in range(2):
                        hc = slice(half * 512, (half + 1) * 512)
                        bi = psM.next()
                        for k in range(4):
                            mm(banks[bi][:P, :], oxFM[:, k, cols], wo[:, k, hc], [r_oxFM, r_wo], [bank_reg[bi]], start=(k == 0), stop=(k == 3))
                        tt(xr4[:P, s, hc], banks[bi][:P, :], xr4[:P, s, hc], ALU.add, [bank_reg[bi], r_xr4[s]], [r_xr4[s]])
                    rmsnorm_fm(xr4[:P, s, :], r_xr4[s], P, g_mlp, r_gmlp, hxFM, r_hxFMs[s], s * P, nwk)
                r_x2 = [Reg() for _ in range(nsub)]
                for s in range(nsub):
                    dma("sp", y_d[tok0 + s * P: tok0 + (s + 1) * P, :], xr4[:P, s, :], [r_xr4[s]], [r_y2[(tok0 // 128) + s]])
                dma("sp", hm_d[:, :, tok0:tok0 + W], hxFM[:, :, 0:W], r_hxFMs, [r_hmd[tok0 // 512]])

            r_y2 = [Reg() for _ in range(17)]
            r_hmd = [Reg() for _ in range(5)]
            for i in range(int(os.environ.get('K_NTILES', '4'))):
                phaseC_tile(i * 512, 128, 4, False)
            if os.environ.get('K_SAMPLE', '1') == '1':
                phaseC_tile(2048, 32, 1, True)

            S.barrier()
            A.top = markA
            psM = Ring([0, 1, 2, 3])
            wup_sb = A.alloc([128, 8, 4096], BF16); r_wup = [Reg() for _ in range(8)]
            wdn_sb = A.alloc([128, 32, 1024], BF16); r_wdn = [Reg() for _ in range(8)]
            wdn_v = wdn_d.rearrange("(f p) c -> p f c", p=128)
            for fg in range(8):
                dma("pool", wup_sb[:, :, fg * 512:(fg + 1) * 512], wup_v[:, :, fg * 512:(fg + 1) * 512], [], [r_wup[fg]])
                dma("pool", wdn_sb[:, fg * 4:(fg + 1) * 4, :], wdn_v[:, fg * 4:(fg + 1) * 4, :], [], [r_wdn[fg]])
            hm_ring = Ring([(A.alloc([128, 8, 512], BF16), Reg()) for _ in range(2)])
            x2_ring = Ring([(A.alloc([128, 4, 1024], F32), [Reg() for _ in range(4)]) for _ in range(1)])
            hF_ring = Ring([(A.alloc([128, 4, 512], BF16), Reg()) for _ in range(2)])
            tmp_ring = Ring([(A.alloc([128, 512], BF16), Reg()) for _ in range(2)])
            print("phase D arena top", A.top)

            def phaseD_tile(tok0, P, nsub):
                W = P * nsub
                hm, r_hm = hm_ring.next()
                dma("sp", hm[:, :, 0:W], hm_d[:, :, tok0:tok0 + W], [r_hmd[tok0 // 512]], [r_hm])
                x2, r_x2 = x2_ring.next()
                for s in range(nsub):
                    dma("sp", x2[:P, s, :], y_d[tok0 + s * P: tok0 + (s + 1) * P, :], [r_y2[(tok0 // 128) + s]], [r_x2[s]])
                for fg in range(8):
                    hF, r_hF = hF_ring.next()
                    for f4 in range(4):
                        f = fg * 4 + f4
                        bi = psM.next()
                        for k in range(8):
                            mm(banks[bi][:, 0:W], wup_sb[:, k, f * 128:(f + 1) * 128], hm[:, k, 0:W], [r_wup[fg], r_hm], [bank_reg[bi]],
                               start=(k == 0), stop=(k == 7))
                        tmpr, r_tmpr = tmp_ring.next()
                        act(tmpr[:, 0:W], banks[bi][:, 0:W], AF.Relu, [bank_reg[bi]], [r_tmpr])
                        tt(hF[:, f4, 0:W], tmpr[:, 0:W], tmpr[:, 0:W], ALU.mult, [r_tmpr], [r_hF])
                    for s in range(nsub):
                        for half in range(2):
                            hc = slice(half * 512, (half + 1) * 512)
                            bi = obank[(s * 2 + half) % 4]
                            for f4 in range(4):
                                mm(banks[bi][:P, :], hF[:, f4, s * P:(s + 1) * P], wdn_sb[:, fg * 4 + f4, hc], [r_hF, r_wdn[fg]],
                                   [bank_reg[bi]], start=(f4 == 0), stop=(f4 == 3))
                            tt(x2[:P, s, hc], banks[bi][:P, :], x2[:P, s, hc], ALU.add, [bank_reg[bi], r_x2[s]], [r_x2[s]])
                for s in range(nsub):
                    dma("sp", y_d[tok0 + s * P: tok0 + (s + 1) * P, :], x2[:P, s, :], [r_x2[s]], [r_y2[(tok0 // 128) + s]])

            for i in range(int(os.environ.get('K_NTILES', '4'))):
                phaseD_tile(i * 512, 128, 4)
            if os.environ.get('K_SAMPLE', '1') == '1':
                phaseD_tile(2048, 32, 1)

        S.run(st)
    return nc


def prep_inputs(inputs, consts):
    f32 = np.float32
    xp = np.asarray(inputs["x_prompt"], f32)
    xs = np.asarray(inputs["x_sample"], f32)
    ck = np.ascontiguousarray(np.asarray(inputs["cache_k"], f32)[0].reshape(NPOOL * 128, 512))
    cv = np.ascontiguousarray(np.asarray(inputs["cache_v"], f32)[0].reshape(NPOOL * 128, 512))
    pt = np.asarray(inputs["page_table"], np.int32)

    def bcast(a, n=128):
        a = np.asarray(a, f32).reshape(1, -1)
        return np.ascontiguousarray(np.broadcast_to(a, (n, a.shape[1])))

    shared = {
        "ck": ck, "cv": cv,
        "w_in": np.ascontiguousarray(inputs["w_in"][0], f32), "w_out": np.ascontiguousarray(inputs["w_out"][0], f32),
        "wq_x": np.ascontiguousarray(inputs["wq_x"][0], f32), "wk_x": np.ascontiguousarray(inputs["wk_x"][0], f32),
        "wv_x": np.ascontiguousarray(inputs["wv_x"][0], f32), "wo_x": np.ascontiguousarray(inputs["wo_x"][0], f32),
        "w_up": np.ascontiguousarray(inputs["w_up"][0], f32), "w_down": np.ascontiguousarray(inputs["w_down"][0], f32),
        "g_mix": bcast(inputs["ln_mix_g"][0]), "g_x": bcast(inputs["ln_x_g"][0]),
        "g_mem": bcast(inputs["ln_mem_g"][0]), "g_mlp": bcast(inputs["ln_mlp_g"][0]),
        "gq": bcast(inputs["q_norm_g"][0]), "gk": bcast(inputs["k_norm_g"][0]),
        "gqx": bcast(inputs["qx_norm_g"][0]), "gkx": bcast(inputs["kx_norm_g"][0]),
        "gssm": bcast(inputs["ssm_norm_g"][0]), "dtb": bcast(inputs["dt_bias"][0]),
        "alog": bcast(inputs["a_log"][0]), "dskip": bcast(inputs["d_skip"][0]),
        "cw": np.ascontiguousarray(np.asarray(inputs["conv_w"][0], f32).reshape(4, 8, 128).transpose(2, 1, 0)),
        "cb": np.ascontiguousarray(np.asarray(inputs["conv_b"][0], f32).reshape(8, 128).T),
    }
    shared.update(consts)
    in_maps = []
    for c in range(8):
        m = dict(shared)
        m["x"] = np.ascontiguousarray(np.concatenate([xp[c], xs[4 * c:4 * c + 4].reshape(32, 1024)], axis=0))
        m["mem"] = np.ascontiguousarray(inputs["mem_prompt"][c], f32)
        m["pt"] = np.ascontiguousarray(pt[4 * c:4 * c + 4])
        m["sconv"] = np.ascontiguousarray(np.asarray(inputs["state_conv"], f32)[0, 4 * c:4 * c + 4].reshape(12, 1024))
        m["sssm"] = np.ascontiguousarray(np.asarray(inputs["state_ssm"], f32)[0, 4 * c:4 * c + 4].reshape(4, 512, 128))
        m["cmk"] = np.ascontiguousarray(np.asarray(inputs["cache_mem_k"], f32)[0, 4 * c:4 * c + 4].reshape(4, 256, 512))
        m["cmv"] = np.ascontiguousarray(np.asarray(inputs["cache_mem_v"], f32)[0, 4 * c:4 * c + 4].reshape(4, 256, 512))
        in_maps.append(m)
    return in_maps


def assemble(results):
    f32 = np.float32
    y_p = np.stack([r["y"][:2048] for r in results]).astype(f32)
    y_s = np.concatenate([r["y"][2048:].reshape(4, 8, 1024) for r in results]).astype(f32)
    nk_p = np.stack([r["newk"][:2048].reshape(2048, 8, 64) for r in results])[None].astype(f32)
    nv_p = np.stack([r["newv"][:2048].reshape(2048, 8, 64) for r in results])[None].astype(f32)
    nc_p = np.stack([r["newconv"][0:3] for r in results])[None].astype(f32)
    ns_p = np.stack([r["newssm"][0].reshape(8, 64, 128) for r in results])[None].astype(f32)
    mk_p = np.stack([r["newmk"].reshape(256, 4, 128) for r in results])[None].astype(f32)
    mv_p = np.stack([r["newmv"].reshape(256, 4, 128) for r in results])[None].astype(f32)
    nk_s = np.concatenate([r["newk"][2048:].reshape(4, 8, 8, 64) for r in results])[None].astype(f32)
    nv_s = np.concatenate([r["newv"][2048:].reshape(4, 8, 8, 64) for r in results])[None].astype(f32)
    nc_s = np.concatenate([r["newconv"][3:15].reshape(4, 3, 1024) for r in results])[None].astype(f32)
    ns_s = np.concatenate([r["newssm"][1:5].reshape(4, 8, 64, 128) for r in results])[None].astype(f32)
    return (y_p, y_s, nk_p, nv_p, nc_p, ns_p, mk_p, mv_p, nk_s, nv_s, nc_s, ns_s)


def kernel(**inputs):
    consts = host_consts()
    nc = build_program(consts, stages=("A", "B", "S", "M", "C"))
    in_maps = prep_inputs(inputs, consts)
    res = run_bass_kernel_spmd(nc, in_maps, core_ids=list(range(8)))
    return assemble(res.results)
```
